# Optimizing a Trainium2 kernel written in Bass

```python
import math
import jax, jax.numpy as jnp
from jax import lax
import numpy as np

D_MODEL = 2048
BATCH = 2
SEQ = 4096
DEPTH = 2

D_MIX = D_MODEL
RW_HEADS = 12
RW_HD = 64
RW_W = RW_HEADS * RW_HD
RW_DECAY_LORA = 64
RW_AAA_LORA = 64
RW_GATE_LORA = 128
RW_GN_EPS = 64e-5
DA_HEADS = 6
DA_HD = 64
DA_W = DA_HEADS * 2 * DA_HD
DA_SUBLN_EPS = 1e-5
Q_BLOCK = 128
SSM_W = D_MIX - RW_W - DA_W
SSM_GROUP = 16
SSM_GROUPS = SSM_W // SSM_GROUP
SSM_STATE = 64
RW_COLS = 3 * RW_W + RW_DECAY_LORA + RW_AAA_LORA + RW_GATE_LORA
DA_COLS = 3 * DA_W
IN_COLS = RW_COLS + DA_COLS + SSM_W
D_FF = 5504
NORM_EPS = 1e-6

kernel_name = "hybrid_rwkv7_diffattn_s5_macaron"


def rms_norm(x, g):
    xf = x.astype(jnp.float32)
    y = xf * lax.rsqrt(jnp.mean(xf * xf, axis=-1, keepdims=True) + NORM_EPS)
    return y.astype(x.dtype) * g


def swiglu(h, w_gu, w_down):
    gate, up = jnp.split(h @ w_gu, 2, axis=-1)
    return (jax.nn.silu(gate) * up) @ w_down


def rwkv7_mix(z, mu, w0, w2, a0, a2, g2, k_k, k_a, r_k, gn_w, gn_b):
    f32 = jnp.float32
    b_, t_, _ = z.shape
    z_prev = jnp.pad(z, ((0, 0), (1, 0), (0, 0)))[:, :-1]
    z = z + (z_prev - z) * mu
    o1, o2, o3 = RW_W, 2 * RW_W, 3 * RW_W
    o4, o5 = o3 + RW_DECAY_LORA, o3 + RW_DECAY_LORA + RW_AAA_LORA
    r, k, v = z[..., :o1], z[..., o1:o2], z[..., o2:o3]
    zw, za, zg = z[..., o3:o4], z[..., o4:o5], z[..., o5:]
    w = -jax.nn.softplus(-(w0 + jnp.tanh(zw) @ w2)) - 0.5
    decay = jnp.exp(-jnp.exp(w.astype(f32)))
    a = jax.nn.sigmoid(a0 + za @ a2)
    g = jax.nn.sigmoid(zg) @ g2
    hs = lambda t: t.reshape(b_, t_, RW_HEADS, RW_HD).astype(f32)
    r, k, v, decay, a = hs(r), hs(k), hs(v), hs(decay), hs(a)
    kk = k * k_k.reshape(RW_HEADS, RW_HD).astype(f32)
    kk = kk / jnp.maximum(jnp.sqrt(jnp.sum(kk * kk, axis=-1, keepdims=True)), 1e-12)
    k = k * (1.0 + (a - 1.0) * k_a.reshape(RW_HEADS, RW_HD).astype(f32))
    rem_a = -kk
    rem_b = kk * a

    def step(S, inp):
        r_t, w_t, k_t, v_t, a_t, b_t = inp
        sa = jnp.einsum('bhvk,bhk->bhv', S, a_t)
        S = S * w_t[:, :, None, :] + sa[..., None] * b_t[:, :, None, :] + v_t[..., None] * k_t[:, :, None, :]
        return S, jnp.einsum('bhvk,bhk->bhv', S, r_t)

    S0 = jnp.zeros((b_, RW_HEADS, RW_HD, RW_HD), f32)
    seq_first = tuple(t.transpose(1, 0, 2, 3) for t in (r, decay, k, v, rem_a, rem_b))
    _, y = lax.scan(step, S0, seq_first)
    y = y.transpose(1, 0, 2, 3)
    mean = jnp.mean(y, axis=-1, keepdims=True)
    var = jnp.mean(jnp.square(y - mean), axis=-1, keepdims=True)
    y = ((y - mean) * lax.rsqrt(var + RW_GN_EPS)).reshape(b_, t_, RW_W) * gn_w + gn_b
    bonus = jnp.sum(r * k * r_k.astype(f32), axis=-1, keepdims=True) * v
    y = (y + bonus.reshape(b_, t_, RW_W)) * g
    return y.astype(z.dtype)


def diff_attn_mix(z, lq1, lk1, lq2, lk2, subln_w, lam_init):
    f32 = jnp.float32
    b_, t_, _ = z.shape
    q = z[..., :DA_W].reshape(b_, t_, DA_HEADS, 2, DA_HD).astype(f32)
    k = z[..., DA_W:2 * DA_W].reshape(b_, t_, DA_HEADS, 2, DA_HD).astype(f32)
    v = z[..., 2 * DA_W:].reshape(b_, t_, DA_HEADS, 2 * DA_HD).astype(f32)
    lam = (jnp.exp(jnp.sum(lq1.astype(f32) * lk1.astype(f32)))
           - jnp.exp(jnp.sum(lq2.astype(f32) * lk2.astype(f32))) + lam_init)
    nb = t_ // Q_BLOCK
    qb = q.reshape(b_, nb, Q_BLOCK, DA_HEADS, 2, DA_HD).transpose(1, 0, 2, 3, 4, 5) * (DA_HD ** -0.5)
    kpos = jnp.arange(t_)

    def block(args):
        q_blk, blk = args
        s = jnp.einsum('bqhcd,bkhcd->bhcqk', q_blk, k)
        qpos = blk * Q_BLOCK + jnp.arange(Q_BLOCK)
        s = jnp.where(kpos[None, :] <= qpos[:, None], s, -jnp.inf)
        p = jax.nn.softmax(s, axis=-1)
        attn = p[:, :, 0] - lam * p[:, :, 1]
        return jnp.einsum('bhqk,bkhe->bqhe', attn, v)

    o = lax.map(block, (qb, jnp.arange(nb)))
    o = o.transpose(1, 0, 2, 3, 4).reshape(b_, t_, DA_HEADS, 2 * DA_HD)
    o = o * lax.rsqrt(jnp.mean(o * o, axis=-1, keepdims=True) + DA_SUBLN_EPS) * subln_w.astype(f32)
    o = o * (1.0 - lam_init)
    return o.reshape(b_, t_, DA_W).astype(z.dtype)


def _complex_linear_combine(e1, e2):
    a1r, a1i, b1r, b1i = e1
    a2r, a2i, b2r, b2i = e2
    return (a2r * a1r - a2i * a1i,
            a2r * a1i + a2i * a1r,
            a2r * b1r - a2i * b1i + b2r,
            a2r * b1i + a2i * b1r + b2i)


def s5_mix(u, a_re, a_im, log_dt, b_re, b_im, c_re, c_im, d_skip, w_glu, b_glu):
    f32 = jnp.float32
    b_, t_, _ = u.shape
    uf = u.astype(f32).reshape(b_, t_, SSM_GROUPS, SSM_GROUP)
    dt = jnp.exp(log_dt.astype(f32))[:, None]
    ar, ai = a_re.astype(f32), a_im.astype(f32)
    mag = jnp.exp(dt * ar)
    abar_r, abar_i = mag * jnp.cos(dt * ai), mag * jnp.sin(dt * ai)
    den = ar * ar + ai * ai
    nr, ni = abar_r - 1.0, abar_i
    coef_r, coef_i = (nr * ar + ni * ai) / den, (ni * ar - nr * ai) / den
    br, bi = b_re.astype(f32), b_im.astype(f32)
    bbar_r = coef_r[..., None] * br - coef_i[..., None] * bi
    bbar_i = coef_r[..., None] * bi + coef_i[..., None] * br
    bu_r = jnp.einsum('btgc,gnc->btgn', uf, bbar_r)
    bu_i = jnp.einsum('btgc,gnc->btgn', uf, bbar_i)
    shape = bu_r.shape
    elems = (jnp.broadcast_to(abar_r, shape), jnp.broadcast_to(abar_i, shape), bu_r, bu_i)
    _, _, xr, xi = lax.associative_scan(_complex_linear_combine, elems, axis=1)
    y = (jnp.einsum('btgn,gcn->btgc', xr, c_re.astype(f32))
         - jnp.einsum('btgn,gcn->btgc', xi, c_im.astype(f32)))
    y = y.reshape(b_, t_, SSM_W) + d_skip.astype(f32) * uf.reshape(b_, t_, SSM_W)
    y = jax.nn.gelu(y)
    y = y * jax.nn.sigmoid(y @ w_glu.astype(f32) + b_glu.astype(f32))
    return y.astype(u.dtype)


def setup_inputs(seed: int = 0) -> dict:
    key = jax.random.key(seed)
    keys = iter(jax.random.split(key, 64))
    f32 = jnp.float32
    L = DEPTH
    nrm = lambda shape, s: s * jax.random.normal(next(keys), shape, f32)
    uni = lambda shape, lo, hi: jax.random.uniform(next(keys), shape, f32, lo, hi)
    gain = lambda shape: 1.0 + 0.05 * jax.random.normal(next(keys), shape, f32)
    a_im0 = jnp.broadcast_to(jnp.pi * jnp.arange(SSM_STATE, dtype=f32), (L, SSM_GROUPS, SSM_STATE))
    return {
        "x": nrm((BATCH, SEQ, D_MODEL), 1.0),
        "ffn1_pre_g": gain((L, D_MODEL)),
        "ffn1_w_gu": nrm((L, D_MODEL, 2 * D_FF), D_MODEL ** -0.5),
        "ffn1_w_down": nrm((L, D_FF, D_MODEL), D_FF ** -0.5),
        "ffn1_post_g": gain((L, D_MODEL)),
        "mix_pre_g": gain((L, D_MODEL)),
        "w_in": nrm((L, D_MODEL, IN_COLS), D_MODEL ** -0.5),
        "rw_mu": uni((L, RW_COLS), 0.0, 1.0),
        "rw_w0": uni((L, RW_W), -5.0, -0.5),
        "rw_w2": nrm((L, RW_DECAY_LORA, RW_W), 0.1),
        "rw_a0": nrm((L, RW_W), 0.1),
        "rw_a2": nrm((L, RW_AAA_LORA, RW_W), 0.5 * RW_AAA_LORA ** -0.5),
        "rw_g2": nrm((L, RW_GATE_LORA, RW_W), RW_GATE_LORA ** -0.5),
        "rw_k_k": 0.85 + nrm((L, RW_W), 0.05),
        "rw_k_a": 1.0 + nrm((L, RW_W), 0.05),
        "rw_r_k": nrm((L, RW_HEADS, RW_HD), 0.1),
        "rw_gn_w": gain((L, RW_W)),
        "rw_gn_b": nrm((L, RW_W), 0.02),
        "da_lq1": nrm((L, DA_HD), 0.1),
        "da_lk1": nrm((L, DA_HD), 0.1),
        "da_lq2": nrm((L, DA_HD), 0.1),
        "da_lk2": nrm((L, DA_HD), 0.1),
        "da_subln_w": gain((L, 2 * DA_HD)),
        "ssm_a_re": -0.5 + nrm((L, SSM_GROUPS, SSM_STATE), 0.01),
        "ssm_a_im": a_im0 + nrm((L, SSM_GROUPS, SSM_STATE), 0.01),
        "ssm_log_dt": uni((L, SSM_GROUPS), math.log(0.001), math.log(0.1)),
        "ssm_b_re": nrm((L, SSM_GROUPS, SSM_STATE, SSM_GROUP), (0.5 / SSM_GROUP) ** 0.5),
        "ssm_b_im": nrm((L, SSM_GROUPS, SSM_STATE, SSM_GROUP), (0.5 / SSM_GROUP) ** 0.5),
        "ssm_c_re": nrm((L, SSM_GROUPS, SSM_GROUP, SSM_STATE), (1.0 / SSM_STATE) ** 0.5),
        "ssm_c_im": nrm((L, SSM_GROUPS, SSM_GROUP, SSM_STATE), (1.0 / SSM_STATE) ** 0.5),
        "ssm_d": nrm((L, SSM_W), 1.0),
        "ssm_w_glu": nrm((L, SSM_W, SSM_W), SSM_W ** -0.5),
        "ssm_b_glu": nrm((L, SSM_W), 0.02),
        "w_out": nrm((L, D_MIX, D_MODEL), D_MIX ** -0.5),
        "mix_post_g": gain((L, D_MODEL)),
        "ffn2_pre_g": gain((L, D_MODEL)),
        "ffn2_w_gu": nrm((L, D_MODEL, 2 * D_FF), D_MODEL ** -0.5),
        "ffn2_w_down": nrm((L, D_FF, D_MODEL), D_FF ** -0.5),
        "ffn2_post_g": gain((L, D_MODEL)),
    }


def reference(x, ffn1_pre_g, ffn1_w_gu, ffn1_w_down, ffn1_post_g, mix_pre_g, w_in,
              rw_mu, rw_w0, rw_w2, rw_a0, rw_a2, rw_g2, rw_k_k, rw_k_a, rw_r_k, rw_gn_w, rw_gn_b,
              da_lq1, da_lk1, da_lq2, da_lk2, da_subln_w,
              ssm_a_re, ssm_a_im, ssm_log_dt, ssm_b_re, ssm_b_im, ssm_c_re, ssm_c_im, ssm_d,
              ssm_w_glu, ssm_b_glu, w_out, mix_post_g,
              ffn2_pre_g, ffn2_w_gu, ffn2_w_down, ffn2_post_g):
    for l in range(DEPTH):
        h = swiglu(rms_norm(x, ffn1_pre_g[l]), ffn1_w_gu[l], ffn1_w_down[l])
        x = x + 0.5 * rms_norm(h, ffn1_post_g[l])
        z = rms_norm(x, mix_pre_g[l]) @ w_in[l]
        y_rw = rwkv7_mix(z[..., :RW_COLS], rw_mu[l], rw_w0[l], rw_w2[l], rw_a0[l], rw_a2[l],
                         rw_g2[l], rw_k_k[l], rw_k_a[l], rw_r_k[l], rw_gn_w[l], rw_gn_b[l])
        lam_init = 0.8 - 0.6 * math.exp(-0.3 * l)
        y_da = diff_attn_mix(z[..., RW_COLS:RW_COLS + DA_COLS], da_lq1[l], da_lk1[l],
                             da_lq2[l], da_lk2[l], da_subln_w[l], lam_init)
        y_ss = s5_mix(z[..., RW_COLS + DA_COLS:], ssm_a_re[l], ssm_a_im[l], ssm_log_dt[l],
                      ssm_b_re[l], ssm_b_im[l], ssm_c_re[l], ssm_c_im[l], ssm_d[l],
                      ssm_w_glu[l], ssm_b_glu[l])
        y = jnp.concatenate([y_rw, y_da, y_ss], axis=-1) @ w_out[l]
        x = x + rms_norm(y, mix_post_g[l])
        h = swiglu(rms_norm(x, ffn2_pre_g[l]), ffn2_w_gu[l], ffn2_w_down[l])
        x = x + 0.5 * rms_norm(h, ffn2_post_g[l])
    return x
```

```python
import math
from contextlib import ExitStack
import numpy as np
import ml_dtypes
import concourse.bass as bass
import concourse.mybir as mybir
from concourse.bass_utils import run_bass_kernel_spmd

F32 = mybir.dt.float32
BF16 = mybir.dt.bfloat16
AF = mybir.ActivationFunctionType
ALU = mybir.AluOpType
AX = mybir.AxisListType

D = 2048
KC = 16
NT = 1024
DFF = 5504
NF = 43
SEQ = 4096
NCORES = 8
EPS = 1e-6


class Buf:
    __slots__ = ("w", "r")

    def __init__(self):
        self.w = None
        self.r = {}


class Prog:
    def __init__(self, nc, es, n_dma_sems=24):
        self.nc = nc
        self.es = es
        self.eng = {"pe": nc.tensor, "act": nc.scalar, "dve": nc.vector, "pool": nc.gpsimd, "sp": nc.sync}
        self.sem = {e: es.enter_context(nc.semaphore("s_" + e)) for e in self.eng}
        self.sem["cc"] = es.enter_context(nc.semaphore("s_cc"))
        self.cc_cnt = 0
        self.cnt = {e: 0 for e in self.eng}
        self.dsem = [es.enter_context(nc.semaphore("d%d" % i)) for i in range(n_dma_sems)]
        self.dval = [0] * n_dma_sems
        self.dpool = {"sp": list(range(0, 12)), "pool": list(range(12, n_dma_sems - 2)),
                      "cc": list(range(n_dma_sems - 2, n_dma_sems))}
        self.dnext = {"sp": 0, "pool": 0, "cc": 0}
        self.waited = {e: {} for e in self.eng}
        self.nins = 0

    def _wait(self, e, tok):
        key, val = tok
        if val <= 0:
            return
        if self.waited[e].get(key, 0) >= val:
            return
        sem = self.sem[key] if isinstance(key, str) else self.dsem[key]
        self.eng[e].wait_ge(sem, val)
        self.waited[e][key] = val

    def _sync(self, e, reads, writes):
        for b in reads:
            if b.w is not None:
                self._wait(e, b.w)
        for b in writes:
            if b.w is not None and (b.w[0] != e or e != "pe"):
                self._wait(e, b.w)
            for k, v in b.r.items():
                if k != e or e != "pe":
                    self._wait(e, (k, v))

    def op(self, e, reads, writes, fn):
        self._sync(e, reads, writes)
        ins = fn(self.eng[e])
        self.cnt[e] += 1
        self.nins += 1
        ins.then_inc(self.sem[e], 1)
        v = self.cnt[e]
        for b in reads:
            b.r[e] = v
        for b in writes:
            b.w = (e, v)
            b.r = {}

    def dma(self, q, out, in_, reads, writes):
        self._sync(q, reads, writes)
        pool = self.dpool[q]
        i = pool[self.dnext[q] % len(pool)]
        self.dnext[q] += 1
        self._wait(q, (i, self.dval[i]))
        ins = self.eng[q].dma_start(out=out, in_=in_)
        self.dval[i] += 16
        self.nins += 1
        ins.then_inc(self.dsem[i], 16)
        v = self.dval[i]
        for b in reads:
            b.r[i] = v
        for b in writes:
            b.w = (i, v)
            b.r = {}

    def barrier(self):
        for e in self.eng:
            for k in self.eng:
                if k != e:
                    self._wait(e, (k, self.cnt[k]))
            for i in range(len(self.dsem)):
                self._wait(e, (i, self.dval[i]))

    def final_wait(self, bufs):
        for b in bufs:
            if b.w is not None:
                self._wait("sp", b.w)


_UNIQ = [0]


def sb(nc, es, name, shape, dt):
    _UNIQ[0] += 1
    return es.enter_context(nc.sbuf_tensor("%s_%d" % (name, _UNIQ[0]), list(shape), dt))


def ps(nc, es, name, shape, dt=F32):
    _UNIQ[0] += 1
    return es.enter_context(nc.psum_tensor("%s_%d" % (name, _UNIQ[0]), list(shape), dt))


def emit_rstd(pg, consts, src, src_bufs, nchunks, ncols, rstd, rstd_buf, sq, sq_bufs, pss, pss_bufs, dim):
    ones = consts["ones_f"]
    nh = ncols // 512
    for c in range(nchunks):
        k = c % 2
        pg.op("act", [src_bufs[c]], [sq_bufs[k]],
              lambda e, c=c, k=k: e.activation(out=sq[:, k, :], in_=src[:, c, :], func=AF.Square))
        for h in range(nh):
            pg.op("pe", [sq_bufs[k], consts["buf"]], [pss_bufs[h]],
                  lambda e, c=c, k=k, h=h: e.matmul(pss[:, h * 512:(h + 1) * 512], ones[:, :],
                                                   sq[:, k, h * 512:(h + 1) * 512],
                                                   start=(c == 0), stop=(c == nchunks - 1)))
    for h in range(nh):
        pg.op("act", [pss_bufs[h]], [rstd_buf],
              lambda e, h=h: e.activation(out=rstd[:, h * 512:(h + 1) * 512], in_=pss[:, h * 512:(h + 1) * 512],
                                          func=AF.Sqrt, scale=1.0 / dim, bias=consts["eps"][:, 0:1]))
    pg.op("dve", [rstd_buf], [rstd_buf],
          lambda e: e.reciprocal(out=rstd[:, :], in_=rstd[:, :]))


def phase_prenorm(pg, nc, consts, x_d, x_dbuf, g_sb, xnT, xn_bufs):
    with ExitStack() as es:
        xT = sb(nc, es, "pn_xT", [128, KC, NT], F32)
        sq = sb(nc, es, "pn_sq", [128, 2, NT], F32)
        rstd = sb(nc, es, "pn_rstd", [128, NT], F32)
        pss = ps(nc, es, "pn_pss", [128, NT])
        xb = [Buf() for _ in range(KC)]
        sqb = [Buf(), Buf()]
        pssb = [Buf(), Buf()]
        rb = Buf()
        for c in range(KC):
            pg.dma("sp", xT[:, c, :], x_d[c], [x_dbuf], [xb[c]])
        emit_rstd(pg, consts, xT, xb, KC, NT, rstd, rb, sq, sqb, pss, pssb, D)
        for c in range(KC):
            pg.op("dve", [xb[c], rb, consts["buf"]], [xn_bufs[c]],
                  lambda e, c=c: e.scalar_tensor_tensor(out=xnT[:, c, :], in0=xT[:, c, :], scalar=g_sb[:, c:c + 1],
                                                        in1=rstd[:, :], op0=ALU.mult, op1=ALU.mult))
        pg.barrier()


def phase_ffn(pg, nc, consts, x_d, x_dbuf, wgu_d, wd_d, gpre_sb, gpost_sb):
    with ExitStack() as es0:
        hT = sb(nc, es0, "f_hT", [128, NF, NT], BF16)
        hb = [[Buf(), Buf()] for _ in range(NF)]
        with ExitStack() as es1:
            xnT = sb(nc, es1, "f_xnT", [128, KC, NT], BF16)
            xnb = [Buf() for _ in range(KC)]
            phase_prenorm(pg, nc, consts, x_d, x_dbuf, gpre_sb, xnT, xnb)
            with ExitStack() as es2:
                NW = 3
                wgu = sb(nc, es2, "f_wgu", [128, NW, 2 * KC * 128], BF16)
                wb = [Buf() for _ in range(NW)]
                sg = sb(nc, es2, "f_sg", [128, 2, 512], BF16)
                sgb = [Buf(), Buf()]
                psg = [ps(nc, es2, "f_psg%d" % i, [128, 512]) for i in range(2)]
                psu = [ps(nc, es2, "f_psu%d" % i, [128, 512]) for i in range(2)]
                psgb = [Buf(), Buf()]
                psub = [Buf(), Buf()]

                def load_w(f):
                    s = f % NW
                    pg.dma("pool", wgu[:, s, :], wgu_d[f], [], [wb[s]])

                for f in range(min(NW - 1, NF)):
                    load_w(f)
                it = 0
                for f in range(NF):
                    if f + NW - 1 < NF:
                        load_w(f + NW - 1)
                    s = f % NW
                    for half in range(2):
                        k = it % 2
                        it += 1
                        tsl = slice(half * 512, (half + 1) * 512)
                        for c in range(KC):
                            pg.op("pe", [wb[s], xnb[c]], [psgb[k]],
                                  lambda e, c=c, s=s, k=k, tsl=tsl: e.matmul(
                                      psg[k][:, :], wgu[:, s, c * 128:(c + 1) * 128], xnT[:, c, tsl],
                                      start=(c == 0), stop=(c == KC - 1)))
                        for c in range(KC):
                            pg.op("pe", [wb[s], xnb[c]], [psub[k]],
                                  lambda e, c=c, s=s, k=k, tsl=tsl: e.matmul(
                                      psu[k][:, :], wgu[:, s, (KC + c) * 128:(KC + c + 1) * 128], xnT[:, c, tsl],
                                      start=(c == 0), stop=(c == KC - 1)))
                        pg.op("act", [psgb[k]], [sgb[k]],
                              lambda e, k=k: e.activation(out=sg[:, k, :], in_=psg[k][:, :], func=AF.Silu))
                        pg.op("dve", [sgb[k], psub[k]], [hb[f][half]],
                              lambda e, k=k, f=f, tsl=tsl: e.tensor_tensor(out=hT[:, f, tsl], in0=sg[:, k, :],
                                                                          in1=psu[k][:, :], op=ALU.mult))
                pg.barrier()
        with ExitStack() as es3:
            oT = sb(nc, es3, "f_oT", [128, KC, NT], F32)
            ob = [Buf() for _ in range(KC)]
            with ExitStack() as es4:
                NWD = 2
                wd = sb(nc, es4, "f_wd", [128, NWD, NF * 128], BF16)
                wdb = [Buf() for _ in range(NWD)]
                pso = [ps(nc, es4, "f_pso%d" % i, [128, 512]) for i in range(4)]
                psob = [Buf() for _ in range(4)]

                def load_wd(c):
                    s = c % NWD
                    pg.dma("pool", wd[:, s, :], wd_d[c], [], [wdb[s]])

                load_wd(0)
                it = 0
                for c in range(KC):
                    if c + 1 < KC:
                        load_wd(c + 1)
                    s = c % NWD
                    for half in range(2):
                        k = it % 4
                        it += 1
                        tsl = slice(half * 512, (half + 1) * 512)
                        for f in range(NF):
                            pg.op("pe", [wdb[s], hb[f][half]], [psob[k]],
                                  lambda e, f=f, s=s, k=k, tsl=tsl: e.matmul(
                                      pso[k][:, :], wd[:, s, f * 128:(f + 1) * 128], hT[:, f, tsl],
                                      start=(f == 0), stop=(f == NF - 1)))
                        pg.op("act", [psob[k]], [ob[c]],
                              lambda e, k=k, c=c, tsl=tsl: e.activation(out=oT[:, c, tsl], in_=pso[k][:, :],
                                                                       func=AF.Copy))
                pg.barrier()
            emit_postnorm_residual(pg, nc, consts, oT, ob, x_d, x_dbuf, gpost_sb)


def emit_postnorm_residual(pg, nc, consts, oT, ob, x_d, x_dbuf, gpost_sb):
    with ExitStack() as es5:
        sq = sb(nc, es5, "f_sq", [128, 2, NT], F32)
        rstd = sb(nc, es5, "f_rstd", [128, NT], F32)
        xc = sb(nc, es5, "f_xc", [128, 3, NT], F32)
        pss = ps(nc, es5, "f_pss", [128, NT])
        sqb = [Buf(), Buf()]
        pssb = [Buf(), Buf()]
        rb = Buf()
        xcb = [Buf() for _ in range(3)]
        emit_rstd(pg, consts, oT, ob, KC, NT, rstd, rb, sq, sqb, pss, pssb, D)
        newbuf = Buf()
        for c in range(KC):
            k = c % 3
            pg.dma("sp", xc[:, k, :], x_d[c], [x_dbuf], [xcb[k]])
            pg.op("dve", [ob[c], rb, consts["buf"]], [ob[c]],
                  lambda e, c=c: e.scalar_tensor_tensor(out=oT[:, c, :], in0=oT[:, c, :],
                                                        scalar=gpost_sb[:, c:c + 1], in1=rstd[:, :],
                                                        op0=ALU.mult, op1=ALU.mult))
            pg.op("pool", [ob[c], xcb[k]], [xcb[k]],
                  lambda e, c=c, k=k: e.tensor_tensor(out=xc[:, k, :], in0=xc[:, k, :], in1=oT[:, c, :],
                                                     op=ALU.add))
            pg.dma("sp", x_d[c], xc[:, k, :], [xcb[k]], [newbuf])
        pg.barrier()
        x_dbuf.w = newbuf.w
        x_dbuf.r = {}


PAY_CHUNK_ROWS = [512, 512, 512, 512, 512, 256, 480, 288]
HD = 128


def pay_xn_rows(pay, c):
    return pay[c // 4][(c % 4) * 128:(c % 4 + 1) * 128, :]


def pay_k_rows(pay, h):
    return pay[4 + h // 4][(h % 4) * 128:(h % 4 + 1) * 128, :]


def _vflat(rows_ap):
    return rows_ap.rearrange("a b -> (a b)").rearrange("(t e) -> t e", e=768)


def pay_v_block(pay, mm):
    k, loc = (6, mm) if mm < 5 else (7, mm - 5)
    return _vflat(pay[k][loc * 96:(loc + 1) * 96, :])


def g1_view(g1):
    return [g.rearrange("(r a) t -> r a t", r=4) for g in g1]


def g1_v_block(g1v, r, mm):
    k, loc = (6, mm) if mm < 5 else (7, mm - 5)
    return _vflat(g1v[k][r, loc * 96:(loc + 1) * 96, :])


def phase_p(pg, nc, consts, x_d, x_dbuf, g_sb, wqk_d, wv_d, pay_d, pay_buf, q_d, q_buf):
    with ExitStack() as es0:
        xnT = sb(nc, es0, "p_xnT", [128, KC, NT], BF16)
        xnb = [Buf() for _ in range(KC)]
        phase_prenorm(pg, nc, consts, x_d, x_dbuf, g_sb, xnT, xnb)
        for c in range(KC):
            pg.dma("sp", pay_xn_rows(pay_d, c), xnT[:, c, :], [xnb[c]], [pay_buf])
        with ExitStack() as es:
            wv = sb(nc, es, "p_wv", [128, KC * 768], BF16)
            wvb = Buf()
            pg.dma("pool", wv[:, :], wv_d, [], [wvb])
            NW = 3
            wq = sb(nc, es, "p_wq", [128, NW, KC * 128], BF16)
            wqb = [Buf() for _ in range(NW)]
            ot = sb(nc, es, "p_ot", [128, 4, 512], BF16)
            otb = [Buf() for _ in range(4)]
            vt = sb(nc, es, "p_vt", [128, 2, 768], BF16)
            vtb = [Buf(), Buf()]
            pp = [ps(nc, es, "p_pp%d" % i, [128, 512]) for i in range(4)]
            ppb = [Buf() for _ in range(4)]

            def load_w(i):
                pg.dma("pool", wq[:, i % NW, :], wqk_d[i], [], [wqb[i % NW]])

            load_w(0)
            load_w(1)
            it = 0
            for i in range(12):
                if i + 2 < 12:
                    load_w(i + 2)
                s = i % NW
                for half in range(2):
                    k = it % 4
                    it += 1
                    tsl = slice(half * 512, (half + 1) * 512)
                    for c in range(KC):
                        pg.op("pe", [wqb[s], xnb[c]], [ppb[k]],
                              lambda e, c=c, s=s, k=k, tsl=tsl: e.matmul(pp[k][:, :], wq[:, s, c * 128:(c + 1) * 128],
                                                                        xnT[:, c, tsl], start=(c == 0), stop=(c == KC - 1)))
                    pg.op("act", [ppb[k]], [otb[k]],
                          lambda e, k=k: e.activation(out=ot[:, k, :], in_=pp[k][:, :], func=AF.Copy))
                    if i < 6:
                        pg.dma("sp", q_d[i][:, tsl], ot[:, k, :], [otb[k]], [q_buf])
                    else:
                        pg.dma("sp", pay_k_rows(pay_d, i - 6)[:, tsl], ot[:, k, :], [otb[k]], [pay_buf])
            for tb in range(8):
                vk = tb % 2
                for (c0, cn) in ((0, 512), (512, 256)):
                    k = it % 4
                    it += 1
                    for c in range(KC):
                        pg.op("pe", [wvb, xnb[c]], [ppb[k]],
                              lambda e, c=c, k=k, tb=tb, c0=c0, cn=cn: e.matmul(
                                  pp[k][:, 0:cn], xnT[:, c, tb * 128:(tb + 1) * 128],
                                  wv[:, c * 768 + c0:c * 768 + c0 + cn], start=(c == 0), stop=(c == KC - 1)))
                    pg.op("act", [ppb[k]], [vtb[vk]],
                          lambda e, k=k, vk=vk, c0=c0, cn=cn: e.activation(out=vt[:, vk, c0:c0 + cn], in_=pp[k][:, 0:cn],
                                                                           func=AF.Copy))
                pg.dma("sp", pay_v_block(pay_d, tb), vt[:, vk, :], [vtb[vk]], [pay_buf])
            pg.barrier()


def phase_da(pg, nc, consts, g1_d, g1_buf, q_d, q_buf, mask_d, lamp_d, subw_d, lam_init, yda_d, yda_buf):
    with ExitStack() as es:
        KT = sb(nc, es, "da_KT", [128, 6, 4, NT], BF16)
        Vt = sb(nc, es, "da_V", [128, 4, 8, 6, 129], BF16)
        QT = sb(nc, es, "da_QT", [128, 6, NT], BF16)
        mask = sb(nc, es, "da_mask", [128, 4, 128], BF16)
        lamp = sb(nc, es, "da_lamp", [128, 4, 64], F32)
        subw = sb(nc, es, "da_subw", [128, 128], F32)
        lam = sb(nc, es, "da_lam", [128, 4], F32)
        ydaT = sb(nc, es, "da_ydaT", [128, 6, NT], BF16)
        PT = sb(nc, es, "da_PT", [128, 2, 2, 4, 128], BF16)
        fin = sb(nc, es, "da_fin", [128, 8, 128], F32)
        sm = sb(nc, es, "da_sm", [128, 16], F32)
        ybf = sb(nc, es, "da_ybf", [128, 2, 128], BF16)
        pS = [ps(nc, es, "da_pS%d" % i, [128, 2, 4, 128]) for i in range(2)]
        pO = [ps(nc, es, "da_pO", [128, 2, 512])]
        pT = ps(nc, es, "da_pT", [128, 2, 128], BF16)
        ktb = [Buf() for _ in range(6)]
        vb = Buf()
        qb = Buf()
        mb = Buf()
        lb = Buf()
        ptb = [Buf(), Buf()]
        psb = [Buf(), Buf()]
        pob = [Buf(), Buf()]
        ptrb = [Buf(), Buf()]
        ybb = [Buf(), Buf()]
        finb = Buf()
        ydb = [Buf() for _ in range(6)]
        ident = consts["ident_b"]
        for h in range(6):
            pg.dma("sp", KT[:, h, :, :], g1_d[4 + h // 4][:, (h % 4) * 128:(h % 4 + 1) * 128, :].rearrange("r p t -> p r t"),
                   [g1_buf], [ktb[h]])
        pg.op("pool", [], [vb], lambda e: e.memset(Vt[:, :, :, :, 128:129], 1.0))
        for r in range(4):
            for mm in range(8):
                pg.dma("sp", Vt[:, r, mm, :, 0:128],
                       g1_v_block(g1_d, r, mm).rearrange("t (h e) -> t h e", h=6), [g1_buf], [vb])
        pg.dma("sp", QT[:, :, :], q_d.rearrange("h p t -> p h t"), [q_buf], [qb])
        pg.dma("pool", mask[:, :, :], mask_d, [], [mb])
        pg.dma("sp", lamp[:, :, :], lamp_d, [], [lb])
        pg.dma("sp", subw[:, :], subw_d, [], [lb])
        pg.op("dve", [lb], [lb], lambda e: e.tensor_tensor(out=lamp[:, 0, :], in0=lamp[:, 0, :], in1=lamp[:, 1, :], op=ALU.mult))
        pg.op("dve", [lb], [lb], lambda e: e.tensor_tensor(out=lamp[:, 2, :], in0=lamp[:, 2, :], in1=lamp[:, 3, :], op=ALU.mult))
        pg.op("dve", [lb], [lb], lambda e: e.tensor_reduce(out=lam[:, 0:1], in_=lamp[:, 0, :], axis=AX.X, op=ALU.add))
        pg.op("dve", [lb], [lb], lambda e: e.tensor_reduce(out=lam[:, 1:2], in_=lamp[:, 2, :], axis=AX.X, op=ALU.add))
        pg.op("act", [lb], [lb], lambda e: e.activation(out=lam[:, 0:2], in_=lam[:, 0:2], func=AF.Exp))
        pg.op("dve", [lb], [lb], lambda e: e.tensor_tensor(out=lam[:, 2:3], in0=lam[:, 0:1], in1=lam[:, 1:2], op=ALU.subtract))
        pg.op("dve", [lb], [lb], lambda e: e.tensor_scalar(out=lam[:, 2:3], in0=lam[:, 2:3], scalar1=float(lam_init), scalar2=None, op0=ALU.add))
        pg.op("dve", [lb], [lb], lambda e: e.tensor_scalar(out=subw[:, :], in0=subw[:, :], scalar1=float(1.0 - lam_init), scalar2=None, op0=ALU.mult))
        git = 0
        oit = 0
        stage = globals().get("DA_STAGE", "full")
        if stage == "lam":
            pg.barrier()
            return
        for h in range(6 if stage == "full" else 1):
            for m in range(8 if stage == "full" else 1):
                ok = 0
                oit += 1
                ngroups = m + 1
                for mp in range(ngroups):
                    k = git % 2
                    git += 1
                    for r in range(4):
                        for c in range(2):
                            pg.op("pe", [ktb[h], qb], [psb[k]],
                                  lambda e, h=h, r=r, c=c, k=k, mp=mp, m=m: e.matmul(
                                      pS[k][:, c, r, :], KT[c * 64:(c + 1) * 64, h, r, mp * 128:(mp + 1) * 128],
                                      QT[c * 64:(c + 1) * 64, h, m * 128:(m + 1) * 128], start=True, stop=True))
                    if stage == "S":
                        continue
                    for half in range(2):
                        pg.op("act", [psb[k]], [ptb[k]],
                              lambda e, k=k, half=half: e.activation(out=PT[:, k, half, :, :],
                                                                     in_=pS[k][:, half, :, :],
                                                                     func=AF.Exp, scale=0.125))
                    if stage == "SE":
                        continue
                    if mp == m:
                        pg.op("dve", [ptb[k], mb], [ptb[k]],
                              lambda e, k=k: e.tensor_tensor(out=PT[:, k, :, :, :], in0=PT[:, k, :, :, :],
                                                             in1=mask[:, :, :].unsqueeze(1).broadcast_to([128, 2, 4, 128]),
                                                             op=ALU.mult))
                    if stage == "SEM":
                        continue
                    for r in range(4):
                        for c in range(2):
                            pg.op("pe", [ptb[k], vb], [pob[ok]],
                                  lambda e, h=h, r=r, c=c, k=k, mp=mp, ok=ok, ng=ngroups: e.matmul(
                                      pO[ok][:, c, 0:129], PT[:, k, c, r, :], Vt[:, r, mp, h, :],
                                      start=(mp == 0 and r == 0), stop=(mp == ng - 1 and r == 3)))
                if stage in ("nofin", "S", "SE", "SEM"):
                    continue
                O = pO[ok]
                pg.op("dve", [pob[ok]], [finb], lambda e, O=O: e.reciprocal(out=sm[:, 0:2], in_=O[:, :, 128]))
                pg.op("dve", [finb, lb], [finb],
                      lambda e: e.tensor_tensor(out=sm[:, 2:3], in0=sm[:, 1:2], in1=lam[:, 2:3], op=ALU.mult))
                pg.op("dve", [pob[ok], finb], [finb],
                      lambda e, O=O: e.tensor_scalar(out=fin[:, 0, :], in0=O[:, 1, 0:128], scalar1=sm[:, 2:3], scalar2=None,
                                                     op0=ALU.mult))
                pg.op("dve", [pob[ok], finb], [finb],
                      lambda e, O=O: e.scalar_tensor_tensor(out=fin[:, 1, :], in0=O[:, 0, 0:128], scalar=sm[:, 0:1],
                                                            in1=fin[:, 0, :], op0=ALU.mult, op1=ALU.subtract))
                pg.op("dve", [finb], [finb],
                      lambda e: e.tensor_tensor(out=fin[:, 2, :], in0=fin[:, 1, :], in1=fin[:, 1, :], op=ALU.mult))
                pg.op("dve", [finb], [finb],
                      lambda e: e.tensor_reduce(out=sm[:, 4:5], in_=fin[:, 2, :], axis=AX.X, op=ALU.add))
                pg.op("act", [finb, consts["buf"]], [finb],
                      lambda e: e.activation(out=sm[:, 5:6], in_=sm[:, 4:5], func=AF.Sqrt, scale=1.0 / 128,
                                             bias=consts["eps5"][:, 0:1]))
                pg.op("dve", [finb], [finb], lambda e: e.reciprocal(out=sm[:, 6:7], in_=sm[:, 5:6]))
                yk = oit % 2
                pg.op("dve", [finb, lb], [ybb[yk]],
                      lambda e, yk=yk: e.scalar_tensor_tensor(out=ybf[:, yk, :], in0=fin[:, 1, :], scalar=sm[:, 6:7],
                                                              in1=subw[:, :], op0=ALU.mult, op1=ALU.mult))
                pg.op("pe", [ybb[yk], consts["buf"]], [ptrb[yk]],
                      lambda e, yk=yk: e.transpose(pT[:, yk, :], ybf[:, yk, :], ident[:, :]))
                pg.op("act", [ptrb[yk]], [ydb[h]],
                      lambda e, yk=yk, h=h, m=m: e.activation(out=ydaT[:, h, m * 128:(m + 1) * 128], in_=pT[:, yk, :],
                                                             func=AF.Copy))
            pg.dma("sp", yda_d[h], ydaT[:, h, :], [ydb[h]], [yda_buf])
        pg.barrier()


def setup_consts(pg, nc, es, ident_d):
    cb = Buf()
    ones_f = sb(nc, es, "c_ones_f", [128, 128], F32)
    ones_b = sb(nc, es, "c_ones_b", [128, 128], BF16)
    ident_f = sb(nc, es, "c_ident_f", [128, 128], F32)
    ident_b = sb(nc, es, "c_ident_b", [128, 128], BF16)
    eps = sb(nc, es, "c_eps", [128, 4], F32)
    pg.op("dve", [], [cb], lambda e: e.memset(ones_f[:, :], 1.0))
    pg.op("dve", [], [cb], lambda e: e.memset(ones_b[:, :], 1.0))
    pg.op("dve", [], [cb], lambda e: e.memset(eps[:, 0:1], EPS))
    pg.op("dve", [], [cb], lambda e: e.memset(eps[:, 1:2], 1e-5))
    pg.op("dve", [], [cb], lambda e: e.memset(eps[:, 2:3], 64e-5))
    pg.op("dve", [], [cb], lambda e: e.memset(eps[:, 3:4], 0.0))
    pg.dma("sp", ident_f[:, :], ident_d, [], [cb])
    pg.dma("pool", ident_b[:, :], ident_d, [], [cb])
    return {"ones_f": ones_f, "ones_b": ones_b, "ident_f": ident_f, "ident_b": ident_b, "buf": cb,
            "eps": eps[:, 0:1], "eps5": eps[:, 1:2], "epsgn": eps[:, 2:3], "zero": eps[:, 3:4]}


I32 = mybir.dt.int32
TT = 512


def load_xn_tile(pg, g1_d, g1_buf, xt, xtb, m, rs=(0, 1, 2, 3)):
    for i, r in enumerate(rs):
        for k4 in range(4):
            pg.dma("sp", xt[:, k4 * 4:(k4 + 1) * 4, i, :],
                   g1_d[k4][r, :, m * 128:(m + 1) * 128].rearrange("(kc p) t -> p kc t", p=128),
                   [g1_buf], [xtb])


def emit_sin(pg, out, x, tmpf, tmpi, bufs):
    pg.op("dve", bufs, bufs, lambda e: e.tensor_scalar(out=tmpf, in0=x, scalar1=1.0 / (2 * math.pi), scalar2=None, op0=ALU.mult))
    pg.op("dve", bufs, bufs, lambda e: e.tensor_copy(out=tmpi, in_=tmpf))
    pg.op("dve", bufs, bufs, lambda e: e.tensor_copy(out=tmpf, in_=tmpi))
    pg.op("dve", bufs, bufs, lambda e: e.scalar_tensor_tensor(out=tmpf, in0=tmpf, scalar=-2 * math.pi, in1=x, op0=ALU.mult, op1=ALU.add))
    pg.op("dve", bufs, bufs, lambda e: e.tensor_scalar(out=tmpf, in0=tmpf, scalar1=math.pi, scalar2=-math.pi, op0=ALU.min, op1=ALU.max))
    pg.op("act", bufs, bufs, lambda e: e.activation(out=out, in_=tmpf, func=AF.Sin))


def phase_s5(pg, nc, consts, g1_d, g1_buf, wss_d, s5p_d, s5b_d, s5c_d, s5d_d, ramp_d, pay2_d, pay2_buf):
    TC = TT
    with ExitStack() as es:
        wss = sb(nc, es, "s5_w", [128, KC * 128], BF16)
        prm = sb(nc, es, "s5_prm", [128, 4, 3], F32)
        bp = sb(nc, es, "s5_bp", [128, 4, 2, 16], F32)
        cp = sb(nc, es, "s5_cp", [128, 4, 2, 16], F32)
        dsk = sb(nc, es, "s5_d", [128, 1], F32)
        ramp = sb(nc, es, "s5_ramp", [128, TC + 1], F32)
        sm = sb(nc, es, "s5_sm", [128, 4, 16], F32)
        bbar = sb(nc, es, "s5_bbar", [128, 4, 2, 16], F32)
        tmp16 = sb(nc, es, "s5_t16", [128, 4, 2, 16], F32)
        pad = sb(nc, es, "s5_pad", [128, 4, 4, 128], F32)
        BT = sb(nc, es, "s5_BT", [128, 4, 2, 128], BF16)
        padb = sb(nc, es, "s5_padb", [128, 4, 2, 128], BF16)
        CT = sb(nc, es, "s5_CT", [128, 4, 2, 128], BF16)
        cosT = sb(nc, es, "s5_cos", [128, 4, TC + 1], F32)
        sinT = sb(nc, es, "s5_sin", [128, 4, TC + 1], F32)
        targ = sb(nc, es, "s5_targ", [128, 4, TC + 1], F32)
        ttmp = sb(nc, es, "s5_ttmp", [128, 4, TC + 1], F32)
        tint = sb(nc, es, "s5_tint", [128, 4, TC + 1], I32)
        init = sb(nc, es, "s5_init", [128, 4, 2], F32)
        itmp = sb(nc, es, "s5_itmp", [128, 4, 4], F32)
        xt = sb(nc, es, "s5_xt", [128, 2, KC, 4, 128], BF16)
        uf = sb(nc, es, "s5_uf", [128, TC], F32)
        ub = sb(nc, es, "s5_ub", [128, TC], BF16)
        w1 = sb(nc, es, "s5_w1", [128, 6, TC], F32)
        xs = sb(nc, es, "s5_xs", [128, 4, 2, TC], BF16)
        yo = sb(nc, es, "s5_yo", [128, 3, TC], F32)
        yb16 = sb(nc, es, "s5_yb", [128, 2, TC], BF16)
        pu = ps(nc, es, "s5_pu", [128, TC])
        pbu = [ps(nc, es, "s5_pbu%d" % i, [128, 2, TC]) for i in range(2)]
        py = ps(nc, es, "s5_py", [128, TC])
        ptr = ps(nc, es, "s5_ptr", [128, 4, 128], BF16)
        pb = Buf()
        pg.dma("pool", wss[:, :], wss_d, [], [pb])
        pg.dma("sp", prm[:, :, :], s5p_d, [], [pb])
        pg.dma("sp", bp[:, :, :, :], s5b_d, [], [pb])
        pg.dma("sp", cp[:, :, :, :], s5c_d, [], [pb])
        pg.dma("sp", dsk[:, :], s5d_d, [], [pb])
        pg.dma("sp", ramp[:, :], ramp_d, [], [pb])
        P = [pb]
        ar, ai, ldt = prm[:, :, 0], prm[:, :, 1], prm[:, :, 2]
        dt, mag, ang = sm[:, :, 0], sm[:, :, 1], sm[:, :, 2]
        cosa, sina = sm[:, :, 3], sm[:, :, 4]
        nr, ni, den, cr, ci = sm[:, :, 5], sm[:, :, 6], sm[:, :, 7], sm[:, :, 8], sm[:, :, 9]
        t0, t1, angc = sm[:, :, 10], sm[:, :, 11], sm[:, :, 12]

        def V(fn):
            pg.op("dve", P, P, fn)

        pg.op("act", P, P, lambda e: e.activation(out=dt, in_=ldt, func=AF.Exp))
        V(lambda e: e.tensor_tensor(out=t0, in0=dt, in1=ar, op=ALU.mult))
        pg.op("act", P, P, lambda e: e.activation(out=mag, in_=t0, func=AF.Exp))
        V(lambda e: e.tensor_tensor(out=ang, in0=dt, in1=ai, op=ALU.mult))
        emit_sin(pg, sina, ang, t0, tint[:, :, 0], P)
        V(lambda e: e.tensor_scalar(out=angc, in0=ang, scalar1=math.pi / 2, scalar2=None, op0=ALU.add))
        emit_sin(pg, cosa, angc, t0, tint[:, :, 0], P)
        V(lambda e: e.tensor_tensor(out=nr, in0=mag, in1=cosa, op=ALU.mult))
        V(lambda e: e.tensor_scalar(out=nr, in0=nr, scalar1=-1.0, scalar2=None, op0=ALU.add))
        V(lambda e: e.tensor_tensor(out=ni, in0=mag, in1=sina, op=ALU.mult))
        V(lambda e: e.tensor_tensor(out=den, in0=ar, in1=ar, op=ALU.mult))
        V(lambda e: e.tensor_tensor(out=t0, in0=ai, in1=ai, op=ALU.mult))
        V(lambda e: e.tensor_tensor(out=den, in0=den, in1=t0, op=ALU.add))
        V(lambda e: e.reciprocal(out=den, in_=den))
        V(lambda e: e.tensor_tensor(out=cr, in0=nr, in1=ar, op=ALU.mult))
        V(lambda e: e.tensor_tensor(out=t0, in0=ni, in1=ai, op=ALU.mult))
        V(lambda e: e.tensor_tensor(out=cr, in0=cr, in1=t0, op=ALU.add))
        V(lambda e: e.tensor_tensor(out=cr, in0=cr, in1=den, op=ALU.mult))
        V(lambda e: e.tensor_tensor(out=ci, in0=ni, in1=ar, op=ALU.mult))
        V(lambda e: e.tensor_tensor(out=t0, in0=nr, in1=ai, op=ALU.mult))
        V(lambda e: e.tensor_tensor(out=ci, in0=ci, in1=t0, op=ALU.subtract))
        V(lambda e: e.tensor_tensor(out=ci, in0=ci, in1=den, op=ALU.mult))
        s5stage = globals().get("S5_STAGE", "full")
        if s5stage == "A":
            pg.barrier()
            return
        crb = cr.unsqueeze(2).broadcast_to([128, 4, 16])
        cib = ci.unsqueeze(2).broadcast_to([128, 4, 16])
        V(lambda e: e.tensor_tensor(out=bbar[:, :, 0, :], in0=bp[:, :, 0, :], in1=crb, op=ALU.mult))
        V(lambda e: e.tensor_tensor(out=tmp16[:, :, 0, :], in0=bp[:, :, 1, :], in1=cib, op=ALU.mult))
        V(lambda e: e.tensor_tensor(out=bbar[:, :, 0, :], in0=bbar[:, :, 0, :], in1=tmp16[:, :, 0, :], op=ALU.subtract))
        V(lambda e: e.tensor_tensor(out=bbar[:, :, 1, :], in0=bp[:, :, 1, :], in1=crb, op=ALU.mult))
        V(lambda e: e.tensor_tensor(out=tmp16[:, :, 1, :], in0=bp[:, :, 0, :], in1=cib, op=ALU.mult))
        V(lambda e: e.tensor_tensor(out=bbar[:, :, 1, :], in0=bbar[:, :, 1, :], in1=tmp16[:, :, 1, :], op=ALU.add))
        V(lambda e: e.memset(pad[:, :, :, :], 0.0))
        for q in range(4):
            for half in range(2):
                psl = slice(half * 64, (half + 1) * 64)
                csl = slice((2 * q + half) * 16, (2 * q + half + 1) * 16)
                V(lambda e, q=q, psl=psl, csl=csl: e.tensor_copy(out=pad[psl, q, 0, csl], in_=bbar[psl, q, 0, :]))
                V(lambda e, q=q, psl=psl, csl=csl: e.tensor_copy(out=pad[psl, q, 1, csl], in_=bbar[psl, q, 1, :]))
                V(lambda e, q=q, psl=psl, csl=csl: e.tensor_copy(out=pad[psl, q, 2, csl], in_=cp[psl, q, 0, :]))
                V(lambda e, q=q, psl=psl, csl=csl: e.tensor_scalar(out=pad[psl, q, 3, csl], in0=cp[psl, q, 1, :], scalar1=-1.0,
                                                                   scalar2=None, op0=ALU.mult))
        V(lambda e: e.tensor_copy(out=CT[:, :, :, :], in_=pad[:, :, 2:4, :]))
        V(lambda e: e.tensor_copy(out=padb[:, :, :, :], in_=pad[:, :, 0:2, :]))
        trb = Buf()
        for q in range(4):
            for ri in range(2):
                pg.op("pe", P + [consts["buf"]], [trb],
                      lambda e, q=q, ri=ri: e.transpose(ptr[:, q, :], padb[:, q, ri, :], consts["ident_b"][:, :]))
                pg.op("act", [trb], P, lambda e, q=q, ri=ri: e.activation(out=BT[:, q, ri, :], in_=ptr[:, q, :], func=AF.Copy))
        if s5stage == "B":
            pg.barrier()
            return
        for q in range(4):
            V(lambda e, q=q: e.tensor_scalar(out=targ[:, q, :], in0=ramp[:, :], scalar1=ang[:, q:q + 1], scalar2=None, op0=ALU.mult))
        emit_sin(pg, sinT[:, :, :], targ[:, :, :], ttmp[:, :, :], tint[:, :, :], P)
        V(lambda e: e.tensor_scalar(out=targ[:, :, :], in0=targ[:, :, :], scalar1=math.pi / 2, scalar2=None, op0=ALU.add))
        emit_sin(pg, cosT[:, :, :], targ[:, :, :], ttmp[:, :, :], tint[:, :, :], P)
        V(lambda e: e.memset(init[:, :, :], 0.0))
        if s5stage == "C":
            pg.barrier()
            return
        xtb = [Buf(), Buf()]
        ufb, ubb, pub = Buf(), Buf(), Buf()
        pbub = [Buf(), Buf()]
        w1b = [Buf() for _ in range(6)]
        xsb = [Buf() for _ in range(4)]
        pyb = Buf()
        yob = Buf()
        ybb = [Buf(), Buf()]
        ib = Buf()
        ib.w = pb.w
        load_xn_tile(pg, g1_d, g1_buf, xt[:, 0], xtb[0], 0)
        nt = globals().get("S5_TILES", SEQ // TT)
        for m in range(nt):
            k = m % 2
            if m + 1 < nt:
                load_xn_tile(pg, g1_d, g1_buf, xt[:, (m + 1) % 2], xtb[(m + 1) % 2], m + 1)
            if s5stage == "D0":
                continue
            for c in range(KC):
                pg.op("pe", [xtb[k], pb], [pub],
                      lambda e, c=c, k=k: e.matmul(pu[:, :], wss[:, c * 128:(c + 1) * 128],
                                                   xt[:, k, c, :, :].rearrange("p r t -> p (r t)"),
                                                   start=(c == 0), stop=(c == KC - 1)))
            pg.op("act", [pub], [ufb], lambda e: e.activation(out=uf[:, :], in_=pu[:, :], func=AF.Copy))
            pg.op("dve", [ufb], [ubb], lambda e: e.tensor_copy(out=ub[:, :], in_=uf[:, :]))
            if s5stage == "D":
                continue
            for q in range(4):
                kb = q % 2
                for ri in range(2):
                    pg.op("pe", [ubb, pb], [pbub[kb]],
                          lambda e, q=q, ri=ri, kb=kb: e.matmul(pbu[kb][:, ri, :], BT[:, q, ri, :], ub[:, :], start=True, stop=True))
                cs, sn = cosT[:, q, 0:TC], sinT[:, q, 0:TC]
                bur, bui = pbu[kb][:, 0, :], pbu[kb][:, 1, :]
                pg.op("dve", [pbub[kb], pb], [w1b[0]], lambda e, cs=cs, bur=bur: e.tensor_tensor(out=w1[:, 0, :], in0=bur, in1=cs, op=ALU.mult))
                pg.op("dve", [pbub[kb], pb], [w1b[1]], lambda e, sn=sn, bui=bui: e.tensor_tensor(out=w1[:, 1, :], in0=bui, in1=sn, op=ALU.mult))
                pg.op("pool", [w1b[0], w1b[1]], [w1b[0]], lambda e: e.tensor_tensor(out=w1[:, 0, :], in0=w1[:, 0, :], in1=w1[:, 1, :], op=ALU.add))
                pg.op("dve", [pbub[kb], pb], [w1b[2]], lambda e, cs=cs, bui=bui: e.tensor_tensor(out=w1[:, 2, :], in0=bui, in1=cs, op=ALU.mult))
                pg.op("dve", [pbub[kb], pb], [w1b[1]], lambda e, sn=sn, bur=bur: e.tensor_tensor(out=w1[:, 1, :], in0=bur, in1=sn, op=ALU.mult))
                pg.op("pool", [w1b[2], w1b[1]], [w1b[2]], lambda e: e.tensor_tensor(out=w1[:, 2, :], in0=w1[:, 2, :], in1=w1[:, 1, :], op=ALU.subtract))
                rho = mag[:, q:q + 1].to_broadcast([128, TC])
                pg.op("dve", [w1b[0], ib, pb], [w1b[3]],
                      lambda e, q=q, rho=rho: e.tensor_tensor_scan(out=w1[:, 3, :], data0=rho, data1=w1[:, 0, :],
                                                                   initial=init[:, q, 0:1], op0=ALU.mult, op1=ALU.add))
                pg.op("dve", [w1b[2], ib, pb], [w1b[4]],
                      lambda e, q=q, rho=rho: e.tensor_tensor_scan(out=w1[:, 4, :], data0=rho, data1=w1[:, 2, :],
                                                                   initial=init[:, q, 1:2], op0=ALU.mult, op1=ALU.add))
                wl_r, wl_i = w1[:, 3, TC - 1:TC], w1[:, 4, TC - 1:TC]
                cT, sT = cosT[:, q, TC:TC + 1], sinT[:, q, TC:TC + 1]
                pg.op("dve", [w1b[3], w1b[4], pb], [ib], lambda e, q=q, wl_r=wl_r, cT=cT: e.tensor_tensor(out=itmp[:, q, 0:1], in0=wl_r, in1=cT, op=ALU.mult))
                pg.op("dve", [w1b[3], w1b[4], pb], [ib], lambda e, q=q, wl_i=wl_i, sT=sT: e.tensor_tensor(out=itmp[:, q, 1:2], in0=wl_i, in1=sT, op=ALU.mult))
                pg.op("dve", [w1b[3], w1b[4], pb], [ib], lambda e, q=q, wl_r=wl_r, sT=sT: e.tensor_tensor(out=itmp[:, q, 2:3], in0=wl_r, in1=sT, op=ALU.mult))
                pg.op("dve", [w1b[3], w1b[4], pb], [ib], lambda e, q=q, wl_i=wl_i, cT=cT: e.tensor_tensor(out=itmp[:, q, 3:4], in0=wl_i, in1=cT, op=ALU.mult))
                pg.op("dve", [ib], [ib], lambda e, q=q: e.tensor_tensor(out=init[:, q, 0:1], in0=itmp[:, q, 0:1], in1=itmp[:, q, 1:2], op=ALU.subtract))
                pg.op("dve", [ib], [ib], lambda e, q=q: e.tensor_tensor(out=init[:, q, 1:2], in0=itmp[:, q, 2:3], in1=itmp[:, q, 3:4], op=ALU.add))
                pg.op("pool", [w1b[3], pb], [w1b[0]], lambda e, cs=cs: e.tensor_tensor(out=w1[:, 0, :], in0=w1[:, 3, :], in1=cs, op=ALU.mult))
                pg.op("pool", [w1b[4], pb], [w1b[1]], lambda e, sn=sn: e.tensor_tensor(out=w1[:, 1, :], in0=w1[:, 4, :], in1=sn, op=ALU.mult))
                pg.op("dve", [w1b[0], w1b[1]], [xsb[q]], lambda e, q=q: e.tensor_tensor(out=xs[:, q, 0, :], in0=w1[:, 0, :], in1=w1[:, 1, :], op=ALU.subtract))
                pg.op("pool", [w1b[3], pb], [w1b[2]], lambda e, sn=sn: e.tensor_tensor(out=w1[:, 2, :], in0=w1[:, 3, :], in1=sn, op=ALU.mult))
                pg.op("pool", [w1b[4], pb], [w1b[5]], lambda e, cs=cs: e.tensor_tensor(out=w1[:, 5, :], in0=w1[:, 4, :], in1=cs, op=ALU.mult))
                pg.op("dve", [w1b[2], w1b[5]], [xsb[q]], lambda e, q=q: e.tensor_tensor(out=xs[:, q, 1, :], in0=w1[:, 2, :], in1=w1[:, 5, :], op=ALU.add))
            for q in range(4):
                for ri in range(2):
                    pg.op("pe", [xsb[q], pb], [pyb],
                          lambda e, q=q, ri=ri: e.matmul(py[:, :], CT[:, q, ri, :], xs[:, q, ri, :],
                                                         start=(q == 0 and ri == 0), stop=(q == 3 and ri == 1)))
            pg.op("dve", [pyb, ufb, pb], [yob], lambda e: e.scalar_tensor_tensor(out=yo[:, 0, :], in0=uf[:, :], scalar=dsk[:, 0:1],
                                                                                in1=py[:, :], op0=ALU.mult, op1=ALU.add))
            pg.op("dve", [yob], [yob], lambda e: e.tensor_tensor(out=yo[:, 1, :], in0=yo[:, 0, :], in1=yo[:, 0, :], op=ALU.mult))
            pg.op("dve", [yob], [yob], lambda e: e.tensor_scalar(out=yo[:, 1, :], in0=yo[:, 1, :], scalar1=0.044715, scalar2=1.0,
                                                                 op0=ALU.mult, op1=ALU.add))
            pg.op("dve", [yob], [yob], lambda e: e.tensor_tensor(out=yo[:, 1, :], in0=yo[:, 1, :], in1=yo[:, 0, :], op=ALU.mult))
            pg.op("act", [yob], [yob], lambda e: e.activation(out=yo[:, 2, :], in_=yo[:, 1, :], func=AF.Sigmoid,
                                                              scale=2.0 * math.sqrt(2.0 / math.pi)))
            pg.op("dve", [yob], [ybb[k]], lambda e, k=k: e.tensor_tensor(out=yb16[:, k, :], in0=yo[:, 0, :], in1=yo[:, 2, :], op=ALU.mult))
            pg.dma("sp", pay2_d[2][:, m * TT:(m + 1) * TT], yb16[:, k, :], [ybb[k]], [pay2_buf])
        pg.barrier()


RT = 256
NCH = RT // 64
NEG_EH = -math.exp(-0.5)


def phase_rwkv(pg, nc, consts, g1_d, g1_buf, wrw_d, rwp_d, rwpg_d, rwl_d, rwgn_d, rwm_d, pay2_d, pay2_buf):
    with ExitStack() as es:
        wrw = sb(nc, es, "rw_w", [128, KC, 832], BF16)
        xt = sb(nc, es, "rw_xt", [128, 2, KC, 2, 128], BF16)
        prm = sb(nc, es, "rw_prm", [64, 32], F32)
        mug = sb(nc, es, "rw_mug", [128, 1], F32)
        lw = sb(nc, es, "rw_lw", [128, 3, 192], BF16)
        gn = sb(nc, es, "rw_gn", [64, 2, 3, 64], F32)
        msk = sb(nc, es, "rw_msk", [64, 3, 64], F32)
        rmask = sb(nc, es, "rw_rmask", [64, RT], F32)
        idb = sb(nc, es, "rw_idb", [64, 64], BF16)
        Z = sb(nc, es, "rw_Z", [64, 11, RT + 1], F32)
        ZG = sb(nc, es, "rw_ZG", [128, RT + 1], F32)
        ZS = sb(nc, es, "rw_ZS", [64, 11, RT], F32)
        E = sb(nc, es, "rw_E", [64, 12, RT], F32)
        zgs = sb(nc, es, "rw_zgs", [128, 2, RT], F32)
        tw = sb(nc, es, "rw_tw", [64, 2, RT], BF16)
        sgb = sb(nc, es, "rw_sgb", [128, RT], BF16)
        SIG = sb(nc, es, "rw_SIG", [64, 3, RT], F32)
        AL = sb(nc, es, "rw_AL", [64, 3, RT], F32)
        KKN = sb(nc, es, "rw_KKN", [64, 3, RT], F32)
        KP = sb(nc, es, "rw_KP", [64, 3, RT], F32)
        L = sb(nc, es, "rw_L", [64, 3, RT], F32)
        T1 = sb(nc, es, "rw_T1", [64, 3, RT], F32)
        T2 = sb(nc, es, "rw_T2", [128, 3, RT], F32)
        onesblk = sb(nc, es, "rw_onesblk", [128, 128], F32)
        PCt = sb(nc, es, "rw_PC", [64, 3, NCH], F32)
        arT = sb(nc, es, "rw_arT", [64, 3, NCH, 2, 64], BF16)
        bkT = sb(nc, es, "rw_bkT", [64, 3, NCH, 2, 64], BF16)
        FM = sb(nc, es, "rw_FM", [64, 3, 5, RT], BF16)
        TOK = sb(nc, es, "rw_TOK", [64, NCH, 3, 5, 64], BF16)
        SCm = sb(nc, es, "rw_SCm", [64, 3, NCH, 2, 2, 64], BF16)
        NLt = sb(nc, es, "rw_NL", [64, 3, NCH, 64], BF16)
        MJ = sb(nc, es, "rw_MJ", [64, 2, NCH, 64], BF16)
        NJ = sb(nc, es, "rw_NJ", [64, 2, NCH, 64], BF16)
        Tt = sb(nc, es, "rw_Tt", [64, 2, NCH, 64], BF16)
        AkVb = sb(nc, es, "rw_AkVb", [64, NCH, 64], BF16)
        UVs = sb(nc, es, "rw_UVs", [64, NCH, 3, 64], F32)
        ApT = sb(nc, es, "rw_ApT", [64, 3, NCH, 64], BF16)
        S = sb(nc, es, "rw_S", [64, 3, 64], F32)
        Sb_ = sb(nc, es, "rw_Sb", [64, 3, 64], BF16)
        Ubf = sb(nc, es, "rw_Ubf", [64, 3, 64], BF16)
        Yt = sb(nc, es, "rw_Yt", [64, NCH, 3, 64], F32)
        F1 = sb(nc, es, "rw_F1", [64, NCH, 3, 64], F32)
        F2 = sb(nc, es, "rw_F2", [64, NCH, 3, 64], F32)
        st = sb(nc, es, "rw_st", [64, 4, NCH, 3], F32)
        rk = sb(nc, es, "rw_rk", [64, NCH, 3], F32)
        yfb = sb(nc, es, "rw_yfb", [64, NCH, 192], BF16)
        yoA = sb(nc, es, "rw_yoA", [128, 2, RT], BF16)
        yoB = sb(nc, es, "rw_yoB", [64, 2, RT], BF16)
        bank = [ps(nc, es, "rw_b%d" % i, [128, 512]) for i in range(7)]
        bankT = ps(nc, es, "rw_bT", [128, 1024], BF16)
        pb = Buf()
        P = [pb]
        pg.dma("pool", wrw[:, :, :], wrw_d, [], P)
        pg.dma("sp", prm[:, 0:26], rwp_d, [], P)
        pg.dma("sp", mug[:, :], rwpg_d, [], P)
        pg.dma("pool", lw[:, :, :], rwl_d, [], P)
        pg.dma("sp", gn[:, :, :, :], rwgn_d, [], P)
        pg.dma("sp", msk[:, :, :], rwm_d, [], P)
        pg.op("dve", P + [consts["buf"]], P, lambda e: e.tensor_copy(out=idb[:, :], in_=consts["ident_b"][0:64, 0:64]))
        pg.op("dve", P, P, lambda e: e.memset(rmask[:, :], 1.0))
        pg.op("dve", P, P, lambda e: e.memset(rmask[:, :].rearrange("p (c t) -> p c t", t=64)[:, :, 0:1], 0.0))
        pg.op("dve", P, P, lambda e: e.tensor_scalar(out=prm[:, 29:32], in0=prm[:, 20:23], scalar1=-1.0, scalar2=1.0,
                                                     op0=ALU.mult, op1=ALU.add))
        pg.op("dve", P, P, lambda e: e.memset(T2[:, :, :], 0.0))
        pg.op("dve", P, P, lambda e: e.memset(onesblk[:, :], 0.0))
        pg.op("dve", P, P, lambda e: e.memset(onesblk[0:64, 0:64], 1.0))
        pg.op("dve", P, P, lambda e: e.memset(S[:, :, :], 0.0))
        pg.op("dve", P, P, lambda e: e.memset(Sb_[:, :, :], 0.0))
        pg.op("dve", P, P, lambda e: e.memset(Z[:, :, 0:1], 0.0))
        pg.op("dve", P, P, lambda e: e.memset(ZG[:, 0:1], 0.0))
        MU, W0, A0, KK_, KA, RK, OMKA = 0, 11, 14, 17, 20, 23, 29
        xtb = [Buf(), Buf()]
        zb, zgb, zsb, eb = Buf(), Buf(), Buf(), Buf()
        bb = [Buf() for _ in range(8)]
        pjb = [bb[0], bb[1]]
        plb = [bb[2], bb[3], bb[4]]
        b_tw, b_sg, b_sig, b_al, b_kkn, b_kp, b_L, b_t1, b_t2, b_pc = (Buf() for _ in range(10))
        b_ar, b_bk, b_fm, b_tok, b_ptr = Buf(), Buf(), Buf(), Buf(), bb[7]
        b_scp = [bb[5], bb[6]]
        b_np, b_scm, b_nl = bb[4], Buf(), Buf()
        b_mj, b_nj, b_tt = [Buf(), Buf()], [Buf(), Buf()], [Buf(), Buf()]
        b_pm, b_pn, b_pp = bb[5], bb[6], bb[4]
        b_akv, b_uvs, b_apt = Buf(), Buf(), Buf()
        b_S, b_Sb, b_ubf, b_yt = Buf(), Buf(), Buf(), Buf()
        b_pu, b_py, b_pd, b_pg, b_prk = bb[0], bb[1], bb[2], bb[3], bb[4]
        b_f1, b_f2, b_st, b_rk, b_yfb = Buf(), Buf(), Buf(), Buf(), Buf()
        b_yo = [Buf(), Buf()]
        zgsb = Buf()
        pj = [bank[0][:, 0:256], bank[1][:, 0:256]]
        PL = [bank[2][0:64, 0:256], bank[3][0:64, 0:256], bank[4][0:64, 0:256]]
        PLF = [bank[2][:, 0:256], bank[3][:, 0:256], bank[4][:, 0:256]]
        p_u = bank[0][0:64, 0:192].rearrange("p (h v) -> p h v", h=3)
        p_y = bank[1][0:64, 0:192].rearrange("p (h v) -> p h v", h=3)
        p_d = bank[2][0:64, 0:192].rearrange("p (h v) -> p h v", h=3)
        p_g = bank[3][0:64, 256:448]
        p_rk = bank[4][0:64, 256:256 + NCH * 3].rearrange("p (c h) -> p c h", h=3)
        p_sc = [bank[5][0:64, :].rearrange("p (c x) -> p c x", x=256), bank[6][0:64, :].rearrange("p (c x) -> p c x", x=256)]
        p_n = bank[4][0:64, 0:256].rearrange("p (c s) -> p c s", s=64)
        p_pp = bank[4][0:64, 256:512].rearrange("p (c s) -> p c s", s=64)
        p_m = bank[5][0:64, 0:256].rearrange("p (c s) -> p c s", s=64)
        p_nn = bank[6][0:64, 0:256].rearrange("p (c s) -> p c s", s=64)
        p_tok = bankT[0:64, 0:960].rearrange("p (h k f) -> p h k f", h=3, k=5)
        p_tA = bankT[:, 0:NCH * 64].rearrange("p (c t) -> p c t", t=64)
        p_tB = bankT[0:64, 512:512 + NCH * 64].rearrange("p (c t) -> p c t", t=64)

        def bc(ap, shape, axis):
            return ap.unsqueeze(axis).broadcast_to(shape)

        ntile = globals().get("RW_TILES", SEQ // RT)

        def load(i):
            load_xn_tile(pg, g1_d, g1_buf, xt[:, i % 2], xtb[i % 2], i // 2, rs=(2 * (i % 2), 2 * (i % 2) + 1))

        load(0)
        for i in range(ntile):
            k = i % 2
            if i + 1 < ntile:
                load(i + 1)
            xr = lambda c, k=k: xt[:, k, c, :, :].rearrange("p r t -> p (r t)")
            if i > 0:
                pg.op("dve", [zb], [zb], lambda e: e.tensor_copy(out=Z[:, :, 0:1], in_=Z[:, :, RT:RT + 1]))
                pg.op("dve", [zgb], [zgb], lambda e: e.tensor_copy(out=ZG[:, 0:1], in_=ZG[:, RT:RT + 1]))
            for g in range(12):
                kk = g % 2
                M = 64 if g < 11 else 128
                c0 = g * 64
                for c in range(KC):
                    pg.op("pe", [xtb[k], pb], [pjb[kk]],
                          lambda e, c=c, kk=kk, M=M, c0=c0, xr=xr: e.matmul(pj[kk][0:M, :], wrw[:, c, c0:c0 + M], xr(c),
                                                                           start=(c == 0), stop=(c == KC - 1)))
                if g < 11:
                    pg.op("act", [pjb[kk]], [zb], lambda e, g=g, kk=kk: e.activation(out=Z[:, g, 1:RT + 1], in_=pj[kk][0:64, :], func=AF.Copy))
                else:
                    pg.op("act", [pjb[kk]], [zgb], lambda e, kk=kk: e.activation(out=ZG[:, 1:RT + 1], in_=pj[kk][:, :], func=AF.Copy))
            Dt = E[:, 0:11, :]
            pg.op("dve", [zb], [eb], lambda e: e.tensor_tensor(out=Dt, in0=Z[:, :, 0:RT], in1=Z[:, :, 1:RT + 1], op=ALU.subtract))
            pg.op("dve", [eb, pb], [eb], lambda e: e.tensor_tensor(out=Dt, in0=Dt, in1=bc(prm[:, MU:MU + 11], [64, 11, RT], 2), op=ALU.mult))
            pg.op("dve", [eb, zb], [zsb], lambda e: e.tensor_tensor(out=ZS[:, :, :], in0=Dt, in1=Z[:, :, 1:RT + 1], op=ALU.add))
            pg.op("pool", [zgb], [zgsb], lambda e: e.tensor_tensor(out=zgs[:, 0, :], in0=ZG[:, 0:RT], in1=ZG[:, 1:RT + 1], op=ALU.subtract))
            pg.op("dve", [zgsb, zgb, pb], [zgsb], lambda e: e.scalar_tensor_tensor(out=zgs[:, 1, :], in0=zgs[:, 0, :], scalar=mug[:, 0:1],
                                                                                  in1=ZG[:, 1:RT + 1], op0=ALU.mult, op1=ALU.add))
            R, Kx, Vx = ZS[:, 0:3, :], ZS[:, 3:6, :], ZS[:, 6:9, :]
            pg.op("act", [zsb], [b_tw], lambda e: e.activation(out=tw[:, 0, :], in_=ZS[:, 9, :], func=AF.Tanh))
            pg.op("act", [zsb], [b_tw], lambda e: e.activation(out=tw[:, 1, :], in_=ZS[:, 10, :], func=AF.Copy))
            pg.op("act", [zgsb], [b_sg], lambda e: e.activation(out=sgb[:, :], in_=zgs[:, 1, :], func=AF.Sigmoid))
            for h in range(3):
                pg.op("pe", [b_tw, pb], [plb[h]], lambda e, h=h: e.matmul(PL[h], lw[0:64, 0, h * 64:(h + 1) * 64], tw[:, 0, :], start=True, stop=True))
                pg.op("act", [plb[h], pb], [b_sig], lambda e, h=h: e.activation(out=SIG[:, h, :], in_=PL[h], func=AF.Sigmoid, bias=prm[:, W0 + h:W0 + h + 1]))
            for h in range(3):
                pg.op("pe", [b_tw, pb], [plb[h]], lambda e, h=h: e.matmul(PL[h], lw[0:64, 1, h * 64:(h + 1) * 64], tw[:, 1, :], start=True, stop=True))
                pg.op("act", [plb[h], pb], [b_al], lambda e, h=h: e.activation(out=AL[:, h, :], in_=PL[h], func=AF.Sigmoid, bias=prm[:, A0 + h:A0 + h + 1]))
            pg.op("dve", [b_sig], [b_sig], lambda e: e.tensor_scalar(out=SIG[:, :, :], in0=SIG[:, :, :], scalar1=NEG_EH, scalar2=None, op0=ALU.mult))
            pg.op("dve", [zsb, pb], [b_t1], lambda e: e.tensor_tensor(out=T1[:, :, :], in0=Kx, in1=bc(prm[:, KK_:KK_ + 3], [64, 3, RT], 2), op=ALU.mult))
            pg.op("pool", [b_t1], [b_t2], lambda e: e.tensor_tensor(out=T2[0:64, :, :], in0=T1[:, :, :], in1=T1[:, :, :], op=ALU.mult))
            for h in range(3):
                pg.op("pe", [b_t2, pb], [plb[h]], lambda e, h=h: e.matmul(PLF[h], onesblk[:, :], T2[:, h, :], start=True, stop=True))
                pg.op("act", [plb[h]], [b_kkn], lambda e, h=h: e.activation(out=KKN[:, h, :], in_=PL[h], func=AF.Sqrt))
            pg.op("dve", [b_kkn], [b_kkn], lambda e: e.tensor_scalar(out=KKN[:, :, :], in0=KKN[:, :, :], scalar1=1e-12, scalar2=None, op0=ALU.max))
            pg.op("dve", [b_kkn], [b_kkn], lambda e: e.reciprocal(out=KKN[:, :, :], in_=KKN[:, :, :]))
            pg.op("dve", [b_kkn, b_t1], [b_kkn], lambda e: e.tensor_tensor(out=KKN[:, :, :], in0=KKN[:, :, :], in1=T1[:, :, :], op=ALU.mult))
            pg.op("dve", [b_al, pb], [b_kp], lambda e: e.tensor_tensor(out=KP[:, :, :], in0=AL[:, :, :], in1=bc(prm[:, KA:KA + 3], [64, 3, RT], 2), op=ALU.mult))
            pg.op("dve", [b_kp, pb], [b_kp], lambda e: e.tensor_tensor(out=KP[:, :, :], in0=KP[:, :, :], in1=bc(prm[:, OMKA:OMKA + 3], [64, 3, RT], 2), op=ALU.add))
            pg.op("dve", [b_kp, zsb], [b_kp], lambda e: e.tensor_tensor(out=KP[:, :, :], in0=KP[:, :, :], in1=Kx, op=ALU.mult))
            pg.op("pool", [zsb, b_kp, b_t2], [b_t2], lambda e: e.tensor_tensor(out=T2[0:64, :, :], in0=R, in1=KP[:, :, :], op=ALU.mult))
            pg.op("pool", [b_t2, pb], [b_fm], lambda e: e.tensor_tensor(out=FM[:, :, 4, :], in0=T2[0:64, :, :], in1=bc(prm[:, RK:RK + 3], [64, 3, RT], 2), op=ALU.mult))
            pg.op("dve", [b_kkn, b_al, b_t1], [b_t1], lambda e: e.tensor_tensor(out=T1[:, :, :], in0=KKN[:, :, :], in1=AL[:, :, :], op=ALU.mult))
            for h in range(3):
                pg.op("dve", [b_sig, pb], [b_L], lambda e, h=h: e.tensor_tensor_scan(out=L[:, h, :], data0=rmask[:, :], data1=SIG[:, h, :], initial=0.0,
                                                                                      op0=ALU.mult, op1=ALU.add))
            Pin, Pex, Pinv, PCs = E[:, 0:3, :], E[:, 3:6, :], E[:, 6:9, :], E[:, 9:12, :]
            Lc = L[:, :, :].rearrange("p h (c t) -> p h c t", t=64)
            pg.op("act", [b_L], [eb], lambda e: e.activation(out=Pin, in_=L[:, :, :], func=AF.Exp))
            pg.op("act", [b_L], [eb], lambda e: e.activation(out=Pinv, in_=L[:, :, :], func=AF.Exp, scale=-1.0))
            pg.op("dve", [b_L, b_sig, eb], [eb], lambda e: e.tensor_tensor(out=Pex, in0=L[:, :, :], in1=SIG[:, :, :], op=ALU.subtract))
            pg.op("act", [eb], [eb], lambda e: e.activation(out=Pex, in_=Pex, func=AF.Exp))
            pg.op("dve", [b_L, eb], [eb], lambda e: e.tensor_tensor(out=PCs.rearrange("p h (c t) -> p h c t", t=64),
                                                                   in0=Lc[:, :, :, 63:64].broadcast_to([64, 3, NCH, 64]), in1=Lc, op=ALU.subtract))
            pg.op("act", [eb], [eb], lambda e: e.activation(out=PCs, in_=PCs, func=AF.Exp))
            pg.op("act", [b_L], [b_pc], lambda e: e.activation(out=PCt[:, :, :], in_=Lc[:, :, :, 63], func=AF.Exp))
            arv = arT[:, :, :, :, :]
            pg.op("dve", [b_kkn, eb], [b_ar], lambda e: e.scalar_tensor_tensor(out=arT[:, :, :, 0, :], in0=KKN[:, :, :].rearrange("p h (c t) -> p h c t", t=64), scalar=-1.0,
                                                                             in1=Pex.rearrange("p h (c t) -> p h c t", t=64), op0=ALU.mult, op1=ALU.mult))
            pg.op("dve", [zsb, eb], [b_ar], lambda e: e.tensor_tensor(out=arT[:, :, :, 1, :], in0=R.rearrange("p h (c t) -> p h c t", t=64),
                                                                    in1=Pin.rearrange("p h (c t) -> p h c t", t=64), op=ALU.mult))
            pg.op("dve", [b_t1, eb], [b_bk], lambda e: e.tensor_tensor(out=bkT[:, :, :, 0, :], in0=T1[:, :, :].rearrange("p h (c t) -> p h c t", t=64),
                                                                     in1=Pinv.rearrange("p h (c t) -> p h c t", t=64), op=ALU.mult))
            pg.op("dve", [b_kp, eb], [b_bk], lambda e: e.tensor_tensor(out=bkT[:, :, :, 1, :], in0=KP[:, :, :].rearrange("p h (c t) -> p h c t", t=64),
                                                                     in1=Pinv.rearrange("p h (c t) -> p h c t", t=64), op=ALU.mult))
            pg.op("pool", [b_t1, eb], [b_fm], lambda e: e.tensor_tensor(out=FM[:, :, 0, :], in0=T1[:, :, :], in1=PCs, op=ALU.mult))
            pg.op("pool", [b_kp, eb], [b_fm], lambda e: e.tensor_tensor(out=FM[:, :, 1, :], in0=KP[:, :, :], in1=PCs, op=ALU.mult))
            pg.op("pool", [b_ar], [b_fm], lambda e: e.tensor_copy(out=FM[:, :, 2, :].rearrange("p h (c t) -> p h c t", t=64), in_=arT[:, :, :, 0, :]))
            pg.op("pool", [zsb], [b_fm], lambda e: e.tensor_copy(out=FM[:, :, 3, :], in_=Vx))
            for c in range(NCH):
                for h in range(3):
                    for kd in range(5):
                        pg.op("pe", [b_fm, pb], [b_ptr], lambda e, c=c, h=h, kd=kd: e.transpose(p_tok[:, h, kd, :], FM[:, h, kd, c * 64:(c + 1) * 64], idb[:, :]))
                pg.op("act", [b_ptr], [b_tok], lambda e, c=c: e.activation(out=TOK[:, c, :, :, :], in_=p_tok, func=AF.Copy))
            pg.op("dve", [b_tok], [b_rk], lambda e: e.tensor_reduce(out=rk[:, :, :], in_=TOK[:, :, :, 4, :], axis=AX.X, op=ALU.add))
            for h in range(3):
                for half in range(NCH // 2):
                    for cc in range(2):
                        c = half * 2 + cc
                        for kd in range(2):
                            pg.op("pe", [b_bk, b_ar], [b_scp[half]],
                                  lambda e, h=h, c=c, cc=cc, kd=kd, half=half: e.matmul(p_sc[half][:, cc, kd * 128:(kd + 1) * 128], bkT[:, h, c, kd, :],
                                                                                        arT[:, h, c, :, :].rearrange("p a t -> p (a t)"), start=True, stop=True))
                    pg.op("dve", [b_scp[half], pb], [b_scm],
                          lambda e, h=h, half=half: e.tensor_tensor(out=SCm[:, h, half * 2:half * 2 + 2, :, :, :].rearrange("p c k a t -> p (c k) a t"),
                                                                    in0=p_sc[half].rearrange("p c (k a t) -> p (c k) a t", k=2, a=2),
                                                                    in1=msk[:, 0:2, :].unsqueeze(1).broadcast_to([64, 4, 2, 64]), op=ALU.mult))
                for c in range(NCH):
                    pg.op("pe", [b_bk, b_ar], [b_np], lambda e, h=h, c=c: e.matmul(p_n[:, c, :], arT[:, h, c, 0, :], bkT[:, h, c, 0, :], start=True, stop=True))
                pg.op("dve", [b_np, pb], [b_nl], lambda e, h=h: e.tensor_tensor(out=NLt[:, h, :, :], in0=p_n, in1=bc(msk[:, 2, :], [64, NCH, 64], 1), op=ALU.mult))
                Mc = lambda c, h=h: SCm[:, h, c, 0, 0, :]
                Nc = lambda c, h=h: NLt[:, h, c, :]
                pg.op("dve", [b_scm, pb], [b_tt[0]], lambda e, h=h: e.tensor_tensor(out=Tt[:, 0, :, :], in0=SCm[:, h, :, 0, 0, :], in1=bc(idb[:, :], [64, NCH, 64], 1), op=ALU.add))
                tcur = 0
                mrd, nrd = [b_scm], [b_nl]
                for lev in range(5):
                    j = lev % 2
                    last = (lev == 4)
                    for c in range(NCH):
                        pg.op("pe", mrd + nrd, [b_pn], lambda e, c=c, Mc=Mc, Nc=Nc: e.matmul(p_nn[:, c, :], Mc(c), Nc(c), start=True, stop=True))
                    if not last:
                        for c in range(NCH):
                            pg.op("pe", mrd + nrd, [b_pm], lambda e, c=c, Mc=Mc, Nc=Nc: e.matmul(p_m[:, c, :], Nc(c), Mc(c), start=True, stop=True))
                    pg.op("act", [b_pn], [b_nj[j]], lambda e, j=j: e.activation(out=NJ[:, j, :, :], in_=p_nn, func=AF.Copy))
                    if not last:
                        pg.op("dve", [b_pm], [b_mj[j]], lambda e, j=j: e.tensor_copy(out=MJ[:, j, :, :], in_=p_m))
                    Mc = lambda c, j=j: MJ[:, j, c, :]
                    Nc = lambda c, j=j: NJ[:, j, c, :]
                    mrd, nrd = [b_mj[j]], [b_nj[j]]
                    for c in range(NCH):
                        pg.op("pe", [b_nj[j], b_tt[tcur]], [b_pp], lambda e, c=c, j=j, tcur=tcur: e.matmul(p_pp[:, c, :], NJ[:, j, c, :], Tt[:, tcur, c, :], start=True, stop=True))
                    pg.op("dve", [b_pp, b_tt[tcur]], [b_tt[1 - tcur]], lambda e, tcur=tcur: e.tensor_tensor(out=Tt[:, 1 - tcur, :, :], in0=p_pp, in1=Tt[:, tcur, :, :], op=ALU.add))
                    tcur = 1 - tcur
                for c in range(NCH):
                    pg.op("pe", [b_scm, b_tok], [b_pm], lambda e, c=c, h=h: e.matmul(p_m[:, c, :], SCm[:, h, c, 1, 0, :], TOK[:, c, h, 3, :], start=True, stop=True))
                pg.op("act", [b_pm], [b_akv], lambda e: e.activation(out=AkVb[:, :, :], in_=p_m, func=AF.Copy))
                for c in range(NCH):
                    pg.op("pe", [b_tt[tcur], b_akv], [b_pn], lambda e, c=c, tcur=tcur: e.matmul(p_nn[:, c, :], Tt[:, tcur, c, :], AkVb[:, c, :], start=True, stop=True))
                pg.op("act", [b_pn], [b_uvs], lambda e, h=h: e.activation(out=UVs[:, :, h, :], in_=p_nn, func=AF.Copy))
                for c in range(NCH):
                    pg.op("pe", [b_tt[tcur], b_tok], [b_pp], lambda e, c=c, h=h, tcur=tcur: e.matmul(p_pp[:, c, :], TOK[:, c, h, 2, :], Tt[:, tcur, c, :], start=True, stop=True))
                pg.op("dve", [b_pp], [b_apt], lambda e, h=h: e.tensor_copy(out=ApT[:, h, :, :], in_=p_pp))
            for c in range(NCH):
                for h in range(3):
                    pg.op("pe", [b_apt, b_Sb], [b_pu], lambda e, c=c, h=h: e.matmul(p_u[:, h, :], ApT[:, h, c, :], Sb_[:, h, :], start=True, stop=True))
                pg.op("dve", [b_pu, b_uvs], [b_ubf], lambda e, c=c: e.tensor_tensor(out=Ubf[:, :, :], in0=p_u, in1=UVs[:, c, :, :], op=ALU.add))
                for h in range(3):
                    pg.op("pe", [b_ar, b_Sb], [b_py], lambda e, c=c, h=h: e.matmul(p_y[:, h, :], arT[:, h, c, 1, :], Sb_[:, h, :], start=True, stop=False))
                    pg.op("pe", [b_scm, b_tok], [b_py], lambda e, c=c, h=h: e.matmul(p_y[:, h, :], SCm[:, h, c, 1, 1, :], TOK[:, c, h, 3, :], start=False, stop=False))
                    pg.op("pe", [b_scm, b_ubf], [b_py], lambda e, c=c, h=h: e.matmul(p_y[:, h, :], SCm[:, h, c, 0, 1, :], Ubf[:, h, :], start=False, stop=True))
                for h in range(3):
                    pg.op("pe", [b_tok, b_ubf], [b_pd], lambda e, c=c, h=h: e.matmul(p_d[:, h, :], TOK[:, c, h, 0, :], Ubf[:, h, :], start=True, stop=False))
                    pg.op("pe", [b_tok], [b_pd], lambda e, c=c, h=h: e.matmul(p_d[:, h, :], TOK[:, c, h, 1, :], TOK[:, c, h, 3, :], start=False, stop=True))
                pg.op("act", [b_py], [b_yt], lambda e, c=c: e.activation(out=Yt[:, c, :, :], in_=p_y, func=AF.Copy))
                pg.op("dve", [b_S, b_pc], [b_S], lambda e, c=c: e.tensor_tensor(out=S[:, :, :], in0=S[:, :, :], in1=bc(PCt[:, :, c], [64, 3, 64], 2), op=ALU.mult))
                pg.op("dve", [b_S, b_pd], [b_S], lambda e: e.tensor_tensor(out=S[:, :, :], in0=S[:, :, :], in1=p_d, op=ALU.add))
                pg.op("act", [b_S], [b_Sb], lambda e: e.activation(out=Sb_[:, :, :], in_=S[:, :, :], func=AF.Copy))
            Y3 = Yt[:, :, :, :]
            sh = [64, NCH, 3, 64]
            pg.op("dve", [b_yt], [b_st], lambda e: e.tensor_reduce(out=st[:, 0, :, :], in_=Y3, axis=AX.X, op=ALU.add))
            pg.op("dve", [b_st], [b_st], lambda e: e.tensor_scalar(out=st[:, 0, :, :], in0=st[:, 0, :, :], scalar1=1.0 / 64, scalar2=None, op0=ALU.mult))
            pg.op("dve", [b_yt, b_st], [b_f1], lambda e: e.tensor_tensor(out=F1[:, :, :, :], in0=Y3, in1=bc(st[:, 0, :, :], sh, 3), op=ALU.subtract))
            pg.op("pool", [b_f1], [b_f2], lambda e: e.tensor_tensor(out=F2[:, :, :, :], in0=F1[:, :, :, :], in1=F1[:, :, :, :], op=ALU.mult))
            pg.op("dve", [b_f2], [b_st], lambda e: e.tensor_reduce(out=st[:, 1, :, :], in_=F2[:, :, :, :], axis=AX.X, op=ALU.add))
            pg.op("act", [b_st, consts["buf"]], [b_st], lambda e: e.activation(out=st[:, 2, :, :], in_=st[:, 1, :, :], func=AF.Sqrt, scale=1.0 / 64, bias=consts["epsgn"][0:64, :]))
            pg.op("dve", [b_st], [b_st], lambda e: e.reciprocal(out=st[:, 3, :, :], in_=st[:, 2, :, :]))
            pg.op("dve", [b_f1, b_st], [b_f1], lambda e: e.tensor_tensor(out=F1[:, :, :, :], in0=F1[:, :, :, :], in1=bc(st[:, 3, :, :], sh, 3), op=ALU.mult))
            pg.op("dve", [b_f1, pb], [b_f1], lambda e: e.tensor_tensor(out=F1[:, :, :, :], in0=F1[:, :, :, :], in1=bc(gn[:, 0, :, :], sh, 1), op=ALU.mult))
            pg.op("dve", [b_f1, pb], [b_f1], lambda e: e.tensor_tensor(out=F1[:, :, :, :], in0=F1[:, :, :, :], in1=bc(gn[:, 1, :, :], sh, 1), op=ALU.add))
            pg.op("pool", [b_tok, b_rk, b_f2], [b_f2], lambda e: e.tensor_tensor(out=F2[:, :, :, :], in0=TOK[:, :, :, 3, :], in1=bc(rk[:, :, :], sh, 3), op=ALU.mult))
            pg.op("dve", [b_f1, b_f2], [b_f1], lambda e: e.tensor_tensor(out=F1[:, :, :, :], in0=F1[:, :, :, :], in1=F2[:, :, :, :], op=ALU.add))
            for c in range(NCH):
                pg.op("pe", [b_sg, pb], [b_pg], lambda e, c=c: e.matmul(p_g[:, 0:192], sgb[:, c * 64:(c + 1) * 64], lw[:, 2, :], start=True, stop=True))
                pg.op("dve", [b_pg, b_f1], [b_yfb], lambda e, c=c: e.tensor_tensor(out=yfb[:, c, :], in0=F1[:, c, :, :].rearrange("p h v -> p (h v)"), in1=p_g[:, 0:192], op=ALU.mult))
            for c in range(NCH):
                pg.op("pe", [b_yfb, pb], [b_ptr], lambda e, c=c: e.transpose(p_tA[:, c, :], yfb[:, c, 0:128], idb[:, :]))
                pg.op("pe", [b_yfb, pb], [b_ptr], lambda e, c=c: e.transpose(p_tB[:, c, :], yfb[:, c, 128:192], idb[:, :]))
            pg.op("act", [b_ptr], [b_yo[k]], lambda e, k=k: e.activation(out=yoA[:, k, :].rearrange("p (c t) -> p c t", t=64), in_=p_tA, func=AF.Copy))
            pg.op("act", [b_ptr], [b_yo[k]], lambda e, k=k: e.activation(out=yoB[:, k, :].rearrange("p (c t) -> p c t", t=64), in_=p_tB, func=AF.Copy))
            pg.dma("sp", pay2_d[0][:, i * RT:(i + 1) * RT], yoA[:, k, :], [b_yo[k]], [pay2_buf])
            pg.dma("sp", pay2_d[1][:, i * RT:(i + 1) * RT], yoB[:, k, :], [b_yo[k]], [pay2_buf])
        pg.barrier()


NPIECE = 18


def phase_o(pg, nc, consts, x_d, x_dbuf, yda_d, yda_buf, g2_d, g2_buf, sel_d, wglu_d, bglu_d, wout_d, gpost_sb):
    with ExitStack() as es0:
        oT = sb(nc, es0, "o_oT", [128, KC, NT], F32)
        ob = [Buf() for _ in range(KC)]
        with ExitStack() as es:
            yT = sb(nc, es, "o_yT", [128, NPIECE, NT], BF16)
            yb = [Buf() for _ in range(NPIECE)]
            ygT = sb(nc, es, "o_ygT", [128, 4, NT], BF16)
            ygb = [Buf() for _ in range(4)]
            stg = sb(nc, es, "o_stg", [128, 2, 8, 4, 128], BF16)
            stb = [Buf(), Buf()]
            acc = sb(nc, es, "o_acc", [128, 2, NT], F32)
            accb = [Buf(), Buf()]
            sel = sb(nc, es, "o_sel", [128, 4], F32)
            wglu = sb(nc, es, "o_wglu", [128, 4, 512], BF16)
            bglu = sb(nc, es, "o_bglu", [128, 4], F32)
            gate = sb(nc, es, "o_gate", [128, 2, 512], F32)
            gtb = [Buf(), Buf()]
            NW = 2
            wt = sb(nc, es, "o_wt", [128, NW, NPIECE * 128], BF16)
            wtb = [Buf() for _ in range(NW)]
            pp = [ps(nc, es, "o_pp%d" % i, [128, 512]) for i in range(4)]
            ppb = [Buf() for _ in range(4)]
            pb = Buf()
            pg.dma("sp", sel[:, :], sel_d, [], [pb])
            pg.dma("pool", wglu[:, :, :], wglu_d, [], [pb])
            pg.dma("sp", bglu[:, :], bglu_d, [], [pb])
            for h in range(6):
                pg.dma("sp", yT[:, 8 + h, :], yda_d[h], [yda_buf], [yb[8 + h]])
            it = 0

            def select(rp, kchunk, nrows, dst, dstb):
                nonlocal it
                k = it % 2
                it += 1
                pg.dma("sp", stg[0:nrows, k, :, :, :], g2_d[kchunk][rp].rearrange("p (m r t) -> p m r t", r=4, t=128),
                       [g2_buf], [stb[k]])
                a = acc[0:nrows, k, :].rearrange("p (m t) -> p m t", t=128)
                pg.op("dve", [stb[k], pb], [accb[k]],
                      lambda e, k=k, a=a, nrows=nrows: e.tensor_scalar(out=a, in0=stg[0:nrows, k, :, 0, :], scalar1=sel[0:nrows, 0:1], scalar2=None, op0=ALU.mult))
                for r in range(1, 4):
                    pg.op("dve", [stb[k], pb, accb[k]], [accb[k]],
                          lambda e, k=k, a=a, r=r, nrows=nrows: e.scalar_tensor_tensor(out=a, in0=stg[0:nrows, k, :, r, :], scalar=sel[0:nrows, r:r + 1],
                                                                                       in1=a, op0=ALU.mult, op1=ALU.add))
                pg.op("act", [accb[k]], [dstb], lambda e, k=k, nrows=nrows, dst=dst: e.activation(out=dst[0:nrows, :], in_=acc[0:nrows, k, :], func=AF.Copy))

            for rp in range(4):
                select(rp, 0, 128, yT[:, 2 * rp, :], yb[2 * rp])
                select(rp, 1, 64, yT[:, 2 * rp + 1, :], yb[2 * rp + 1])
                select(rp, 2, 128, ygT[:, rp, :], ygb[rp])
            git = 0
            for cc in range(4):
                for half in range(2):
                    k = git % 4
                    gk = git % 2
                    git += 1
                    tsl = slice(half * 512, (half + 1) * 512)
                    for rp in range(4):
                        pg.op("pe", [ygb[rp], pb], [ppb[k]],
                              lambda e, rp=rp, cc=cc, k=k, tsl=tsl: e.matmul(pp[k][:, :], wglu[:, rp, cc * 128:(cc + 1) * 128], ygT[:, rp, tsl],
                                                                            start=(rp == 0), stop=(rp == 3)))
                    pg.op("act", [ppb[k], pb], [gtb[gk]],
                          lambda e, k=k, gk=gk, cc=cc: e.activation(out=gate[:, gk, :], in_=pp[k][:, :], func=AF.Sigmoid, bias=bglu[:, cc:cc + 1]))
                    pg.op("dve", [gtb[gk], ygb[cc]], [yb[14 + cc]],
                          lambda e, gk=gk, cc=cc, tsl=tsl: e.tensor_tensor(out=yT[:, 14 + cc, tsl], in0=ygT[:, cc, tsl], in1=gate[:, gk, :], op=ALU.mult))
            def load_w(c):
                pg.dma("pool", wt[:, c % NW, :], wout_d[c], [], [wtb[c % NW]])

            load_w(0)
            for c in range(KC):
                if c + 1 < KC:
                    load_w(c + 1)
                s = c % NW
                for half in range(2):
                    k = git % 4
                    git += 1
                    tsl = slice(half * 512, (half + 1) * 512)
                    for pi in range(NPIECE):
                        kr = 64 if (pi < 8 and pi % 2 == 1) else 128
                        pg.op("pe", [wtb[s], yb[pi]], [ppb[k]],
                              lambda e, pi=pi, kr=kr, s=s, k=k, tsl=tsl: e.matmul(pp[k][:, :], wt[0:kr, s, pi * 128:(pi + 1) * 128], yT[0:kr, pi, tsl],
                                                                                 start=(pi == 0), stop=(pi == NPIECE - 1)))
                    pg.op("act", [ppb[k]], [ob[c]],
                          lambda e, k=k, c=c, tsl=tsl: e.activation(out=oT[:, c, tsl], in_=pp[k][:, :], func=AF.Copy))
            pg.barrier()
        emit_postnorm_residual(pg, nc, consts, oT, ob, x_d, x_dbuf, gpost_sb)


L_DEPTH = 2
GROUPS = [[0, 1, 2, 3], [4, 5, 6, 7]]
PAY2_CHUNK_ROWS = [128, 64, 128]
DATA_GROUPS = {
    "pay": [([r, NT], BF16) for r in PAY_CHUNK_ROWS],
    "g1": [([4 * r, NT], BF16) for r in PAY_CHUNK_ROWS],
    "pay2": [([r, SEQ], BF16) for r in PAY2_CHUNK_ROWS],
    "g2": [([4 * r, SEQ], BF16) for r in PAY2_CHUNK_ROWS],
    "q": [([6, 128, NT], BF16)],
    "yda": [([6, 128, NT], BF16)],
}
WEIGHT_SHAPES = {
    "wgu1": [NF, 128, 2 * KC * 128], "wd1": [KC, 128, NF * 128], "wgu2": [NF, 128, 2 * KC * 128], "wd2": [KC, 128, NF * 128],
    "wqk": [12, 128, KC * 128], "wv": [128, KC * 768], "wrw": [128, KC, 832], "rwp": [64, 26], "rwpg": [128, 1],
    "rwl": [128, 3, 192], "rwgn": [64, 2, 3, 64], "wss": [128, KC * 128], "s5p": [128, 4, 3], "s5b": [128, 4, 2, 16],
    "s5c": [128, 4, 2, 16], "s5d": [128, 1], "lamp": [128, 4, 64], "subw": [128, 128], "wglu": [128, 4, 512],
    "bglu": [128, 4], "wout": [KC, 128, NPIECE * 128],
}
CONST_SHAPES = {"ident": [128, 128], "ramp": [128, TT + 1], "rwm": [64, 3, 64], "sel": [128, 4], "damask": [128, 4, 128],
                "gains": [128, L_DEPTH * 6, KC]}


def lam_init_of(l):
    return 0.8 - 0.6 * math.exp(-0.3 * l)


def build_launch(seq, ins, outs, uses_x):
    nc = bass.Bass("TRN2", target_bir_lowering=False)
    declared = {}

    def ext_in(name, shape, dt=F32):
        if name not in declared:
            declared[name] = nc.dram_tensor(name, list(shape), dt, kind="ExternalInput").ap()
        return declared[name]

    def W(name, l):
        return ext_in("%s_%d" % (name, l), WEIGHT_SHAPES[name])

    def Cn(name):
        return ext_in(name, CONST_SHAPES[name])

    data = {}
    dbuf = {}
    data_in_names = []
    for name, members in DATA_GROUPS.items():
        kind = "ExternalInput" if name in ins else ("ExternalOutput" if name in outs else "Internal")
        data[name] = []
        for k, (shape, dt) in enumerate(members):
            nm = "%s%d" % (name, k)
            data[name].append(nc.dram_tensor(nm, list(shape), dt, kind=kind).ap())
            if name in ins:
                data_in_names.append(nm)
        dbuf[name] = Buf()
    g1v = g1_view(data["g1"])
    g2v = g1_view(data["g2"])
    q_d = data["q"][0]
    yda_d = data["yda"][0]
    with ExitStack() as es:
        pg = Prog(nc, es)
        consts = setup_consts(pg, nc, es, Cn("ident"))
        gains = sb(nc, es, "gains_sb", [128, L_DEPTH * 6, KC], F32)
        pg.dma("sp", gains[:, :, :], Cn("gains"), [], [consts["buf"]])
        for l in range(L_DEPTH):
            for i in (1, 5):
                pg.op("dve", [consts["buf"]], [consts["buf"]],
                      lambda e, l=l, i=i: e.tensor_scalar(out=gains[:, l * 6 + i, :], in0=gains[:, l * 6 + i, :], scalar1=0.5,
                                                          scalar2=None, op0=ALU.mult))
        xb = Buf()
        x_d = None
        if uses_x:
            x_in = ext_in("x_in", [KC, 128, NT])
            x_d = nc.dram_tensor("x_out", [KC, 128, NT], F32, kind="ExternalOutput").ap()
            pg.dma("sp", x_d, x_in, [], [xb])
        pg.barrier()
        G = lambda l, i: gains[:, l * 6 + i, :]
        for (ph, l) in seq:
            if ph == "ffn1":
                phase_ffn(pg, nc, consts, x_d, xb, W("wgu1", l), W("wd1", l), G(l, 0), G(l, 1))
            elif ph == "ffn2":
                phase_ffn(pg, nc, consts, x_d, xb, W("wgu2", l), W("wd2", l), G(l, 4), G(l, 5))
            elif ph == "p":
                phase_p(pg, nc, consts, x_d, xb, G(l, 2), W("wqk", l), W("wv", l), data["pay"], dbuf["pay"], q_d, dbuf["q"])
            elif ph == "gather1":
                emit_allgather(pg, nc, data["pay"], dbuf["pay"], data["g1"], dbuf["g1"])
            elif ph == "gather2":
                emit_allgather(pg, nc, data["pay2"], dbuf["pay2"], data["g2"], dbuf["g2"])
            elif ph == "da":
                phase_da(pg, nc, consts, g1v, dbuf["g1"], q_d, dbuf["q"], Cn("damask"), W("lamp", l), W("subw", l),
                         lam_init_of(l), yda_d, dbuf["yda"])
            elif ph == "rwkv":
                phase_rwkv(pg, nc, consts, g1v, dbuf["g1"], W("wrw", l), W("rwp", l), W("rwpg", l), W("rwl", l), W("rwgn", l),
                           Cn("rwm"), data["pay2"], dbuf["pay2"])
            elif ph == "s5":
                phase_s5(pg, nc, consts, g1v, dbuf["g1"], W("wss", l), W("s5p", l), W("s5b", l), W("s5c", l), W("s5d", l),
                         Cn("ramp"), data["pay2"], dbuf["pay2"])
            elif ph == "o":
                phase_o(pg, nc, consts, x_d, xb, yda_d, dbuf["yda"], g2v, dbuf["g2"], Cn("sel"), W("wglu", l), W("bglu", l),
                        W("wout", l), G(l, 3))
            else:
                raise ValueError(ph)
        pg.barrier()
    in_names = list(declared.keys()) + data_in_names
    return nc, in_names


def emit_allgather(pg, nc, src, src_buf, dst, dst_buf):
    pg.barrier()
    for s_ap, d_ap in zip(src, dst):
        ins = nc.gpsimd.collective_compute("AllGather", ALU.bypass, replica_groups=GROUPS, ins=[s_ap], outs=[d_ap])
        pg.cc_cnt += 1
        ins.then_inc(pg.sem["cc"], 1)
    tok = ("cc", pg.cc_cnt)
    dst_buf.w = tok
    dst_buf.r = {}
    for e in pg.eng:
        pg._wait(e, tok)


def _tile_gu(w_gu):
    return np.ascontiguousarray(w_gu.reshape(KC, 128, 2, NF, 128).transpose(3, 1, 2, 0, 4)).reshape(NF, 128, 2 * KC * 128)


def _tile_down(w_down):
    return np.ascontiguousarray(w_down.reshape(NF, 128, KC, 128).transpose(2, 1, 0, 3)).reshape(KC, 128, NF * 128)


def _pm(v):
    return np.ascontiguousarray(np.asarray(v).reshape(KC, 128).T)


def host_shared(inp, l):
    w_in = inp["w_in"][l]
    out = {}
    out["wgu1"] = _tile_gu(inp["ffn1_w_gu"][l])
    out["wd1"] = _tile_down(inp["ffn1_w_down"][l])
    out["wgu2"] = _tile_gu(inp["ffn2_w_gu"][l])
    out["wd2"] = _tile_down(inp["ffn2_w_down"][l])
    c0 = 2560
    wqk = np.empty((12, 128, KC * 128), np.float32)
    for i in range(12):
        col = c0 + i * 128
        wqk[i] = w_in[:, col:col + 128].reshape(KC, 128, 128).transpose(1, 0, 2).reshape(128, KC * 128)
    out["wqk"] = wqk
    out["wv"] = np.ascontiguousarray(w_in[:, c0 + 1536:c0 + 2304].reshape(KC, 128, 768).transpose(1, 0, 2)).reshape(128, KC * 768)
    out["lamp"] = np.ascontiguousarray(np.broadcast_to(
        np.stack([inp["da_lq1"][l], inp["da_lk1"][l], inp["da_lq2"][l], inp["da_lk2"][l]])[None], (128, 4, 64)))
    out["subw"] = np.ascontiguousarray(np.broadcast_to(inp["da_subln_w"][l][None], (128, 128)))
    out["wglu"] = np.ascontiguousarray(inp["ssm_w_glu"][l].reshape(4, 128, 512).transpose(1, 0, 2))
    out["bglu"] = np.ascontiguousarray(inp["ssm_b_glu"][l].reshape(4, 128).T)
    w_out = inp["w_out"][l]
    pieces = []
    for rp in range(4):
        pieces += [(rp * 192, 128), (rp * 192 + 128, 64)]
    pieces += [(768 + h * 128, 128) for h in range(6)] + [(1536 + c * 128, 128) for c in range(4)]
    wo = np.zeros((KC, 128, NPIECE, 128), np.float32)
    for pi, (r0, n) in enumerate(pieces):
        wo[:, :n, pi, :] = w_out[r0:r0 + n, :].reshape(n, KC, 128).transpose(1, 0, 2)
    out["wout"] = wo.reshape(KC, 128, NPIECE * 128)
    return out


def host_percore(inp, l, j):
    w_in = inp["w_in"][l]
    mu = inp["rw_mu"][l]
    hs = [3 * j, 3 * j + 1, 3 * j + 2]
    cols = []
    for base in (0, 768, 1536):
        for h in hs:
            cols += list(range(base + h * 64, base + (h + 1) * 64))
    cols += list(range(2304, 2560))
    cols = np.array(cols)
    out = {}
    out["wrw"] = np.ascontiguousarray(w_in[:, cols].reshape(KC, 128, 832).transpose(1, 0, 2))
    prm = np.zeros((64, 26), np.float32)
    for g in range(11):
        prm[:, g] = mu[cols[g * 64:(g + 1) * 64]]
    for i, h in enumerate(hs):
        sl = slice(h * 64, (h + 1) * 64)
        prm[:, 11 + i] = inp["rw_w0"][l][sl]
        prm[:, 14 + i] = inp["rw_a0"][l][sl]
        prm[:, 17 + i] = inp["rw_k_k"][l][sl]
        prm[:, 20 + i] = inp["rw_k_a"][l][sl]
        prm[:, 23 + i] = inp["rw_r_k"][l][h]
    out["rwp"] = prm
    out["rwpg"] = np.ascontiguousarray(mu[2432:2560].reshape(128, 1))
    own = np.array(sum([list(range(h * 64, (h + 1) * 64)) for h in hs], []))
    rwl = np.zeros((128, 3, 192), np.float32)
    rwl[:64, 0] = inp["rw_w2"][l][:, own]
    rwl[:64, 1] = inp["rw_a2"][l][:, own]
    rwl[:, 2] = inp["rw_g2"][l][:, own]
    out["rwl"] = rwl
    gn = np.zeros((64, 2, 3, 64), np.float32)
    gn[:, 0] = inp["rw_gn_w"][l][own].reshape(3, 64)[None]
    gn[:, 1] = inp["rw_gn_b"][l][own].reshape(3, 64)[None]
    out["rwgn"] = gn
    out["wss"] = np.ascontiguousarray(
        w_in[:, 4864 + j * 128:4864 + (j + 1) * 128].reshape(KC, 128, 128).transpose(1, 0, 2)).reshape(128, KC * 128)
    s5p = np.zeros((128, 4, 3), np.float32)
    s5b = np.zeros((128, 4, 2, 16), np.float32)
    s5c = np.zeros((128, 4, 2, 16), np.float32)
    for q in range(4):
        for half in range(2):
            g = 8 * j + 2 * q + half
            ps_ = slice(half * 64, (half + 1) * 64)
            s5p[ps_, q, 0] = inp["ssm_a_re"][l][g]
            s5p[ps_, q, 1] = inp["ssm_a_im"][l][g]
            s5p[ps_, q, 2] = inp["ssm_log_dt"][l][g]
            s5b[ps_, q, 0] = inp["ssm_b_re"][l][g]
            s5b[ps_, q, 1] = inp["ssm_b_im"][l][g]
            s5c[ps_, q, 0] = inp["ssm_c_re"][l][g].T
            s5c[ps_, q, 1] = inp["ssm_c_im"][l][g].T
    out["s5p"], out["s5b"], out["s5c"] = s5p, s5b, s5c
    out["s5d"] = np.ascontiguousarray(inp["ssm_d"][l][j * 128:(j + 1) * 128].reshape(128, 1))
    return out


def host_consts(inp, j):
    out = {"ident": np.eye(128, dtype=np.float32)}
    out["ramp"] = np.ascontiguousarray(np.broadcast_to(np.arange(TT + 1, dtype=np.float32)[None], (128, TT + 1)))
    s_ = np.arange(64)
    out["rwm"] = np.ascontiguousarray(np.stack([(s_[:, None] < s_[None, :]), (s_[:, None] <= s_[None, :]),
                                                 (s_[None, :] < s_[:, None])], axis=1).astype(np.float32))
    sel = np.zeros((128, 4), np.float32)
    sel[:, j] = 1.0
    out["sel"] = sel
    tri = (np.arange(128)[:, None] <= np.arange(128)[None, :]).astype(np.float32)
    mask = np.zeros((128, 4, 128), np.float32)
    for r in range(4):
        if r < j:
            mask[:, r, :] = 1.0
        elif r == j:
            mask[:, r, :] = tri
    out["damask"] = mask
    names = ["ffn1_pre_g", "ffn1_post_g", "mix_pre_g", "mix_post_g", "ffn2_pre_g", "ffn2_post_g"]
    gains = np.zeros((128, L_DEPTH * 6, KC), np.float32)
    for l in range(L_DEPTH):
        for i, n in enumerate(names):
            gains[:, l * 6 + i, :] = _pm(inp[n][l])
    out["gains"] = gains
    return out


FUSED = True


def kernel(**inputs):
    inp = {k: np.asarray(v) for k, v in inputs.items()}
    x = inp["x"].astype(np.float32, copy=False)
    pool = []
    for c in range(NCORES):
        b, j = c // 4, c % 4
        d = dict(host_consts(inp, j))
        pool.append(d)
    shared = [host_shared(inp, l) for l in range(L_DEPTH)]
    percore = [[host_percore(inp, l, j) for j in range(4)] for l in range(L_DEPTH)]
    for c in range(NCORES):
        j = c % 4
        for l in range(L_DEPTH):
            for k, v in shared[l].items():
                pool[c]["%s_%d" % (k, l)] = v
            for k, v in percore[l][j].items():
                pool[c]["%s_%d" % (k, l)] = v
    xs = []
    for c in range(NCORES):
        b, j = c // 4, c % 4
        t = x[b].reshape(8, 4, 128, D)[:, j].reshape(NT, D)
        xs.append(np.ascontiguousarray(t.T.reshape(KC, 128, NT)))

    def run(seq, ins, outs, uses_x, state):
        nc, in_names = build_launch(seq, ins, outs, uses_x)
        maps = []
        for c in range(NCORES):
            m = {}
            for n in in_names:
                if n == "x_in":
                    m[n] = state["x"][c]
                elif n in state:
                    m[n] = state[n][c]
                else:
                    m[n] = pool[c][n]
            maps.append(m)
        res = run_bass_kernel_spmd(nc, maps, core_ids=list(range(NCORES))).results
        if uses_x:
            state["x"] = [np.asarray(res[c]["x_out"]) for c in range(NCORES)]
        for g in outs:
            for k in range(len(DATA_GROUPS[g])):
                n = "%s%d" % (g, k)
                state[n] = [np.asarray(res[c][n]) for c in range(NCORES)]

    def gather(state, src, dst):
        for k in range(len(DATA_GROUPS[src])):
            sn, dn = "%s%d" % (src, k), "%s%d" % (dst, k)
            state[dn] = [None] * NCORES
            for c in range(NCORES):
                b = c // 4
                state[dn][c] = np.concatenate([state[sn][4 * b + r] for r in range(4)], axis=0)

    state = {"x": xs}
    if FUSED:
        seq = []
        for l in range(L_DEPTH):
            seq += [("ffn1", l), ("p", l), ("gather1", l), ("da", l), ("rwkv", l), ("s5", l), ("gather2", l), ("o", l), ("ffn2", l)]
        run(seq, set(), set(), True, state)
    else:
        run([("ffn1", 0), ("p", 0)], set(), {"pay", "q"}, True, state)
        for l in range(L_DEPTH):
            gather(state, "pay", "g1")
            run([("da", l), ("rwkv", l), ("s5", l)], {"g1", "q"}, {"yda", "pay2"}, False, state)
            gather(state, "pay2", "g2")
            seq = [("o", l), ("ffn2", l)]
            if l + 1 < L_DEPTH:
                seq += [("ffn1", l + 1), ("p", l + 1)]
                run(seq, {"yda", "g2"}, {"pay", "q"}, True, state)
            else:
                run(seq, {"yda", "g2"}, set(), True, state)
    out = np.empty((2, SEQ, D), np.float32)
    for c in range(NCORES):
        b, j = c // 4, c % 4
        t = state["x"][c].reshape(D, NT).T.reshape(8, 128, D)
        out[b].reshape(8, 4, 128, D)[:, j] = t
    return out
```

```python
import math
from contextlib import ExitStack
import numpy as np
import ml_dtypes
import concourse.bass as bass
import concourse.mybir as mybir
from concourse.bass_utils import run_bass_kernel_spmd

F32 = mybir.dt.float32
BF16 = mybir.dt.bfloat16
AF = mybir.ActivationFunctionType
ALU = mybir.AluOpType
AX = mybir.AxisListType

D = 2048
KC = 16
NT = 1024
DFF = 5504
NF = 43
SEQ = 4096
NCORES = 8
EPS = 1e-6


class Buf:
    __slots__ = ("w", "r")

    def __init__(self):
        self.w = None
        self.r = {}


class Prog:
    def __init__(self, nc, es, n_dma_sems=24):
        self.nc = nc
        self.es = es
        self.eng = {"pe": nc.tensor, "act": nc.scalar, "dve": nc.vector, "pool": nc.gpsimd, "sp": nc.sync}
        self.sem = {e: es.enter_context(nc.semaphore("s_" + e)) for e in self.eng}
        self.sem["cc"] = es.enter_context(nc.semaphore("s_cc"))
        self.cc_cnt = 0
        self.cnt = {e: 0 for e in self.eng}
        self.dsem = [es.enter_context(nc.semaphore("d%d" % i)) for i in range(n_dma_sems)]
        self.dval = [0] * n_dma_sems
        self.dpool = {"sp": list(range(0, 12)), "pool": list(range(12, n_dma_sems - 2)),
                      "cc": list(range(n_dma_sems - 2, n_dma_sems))}
        self.dnext = {"sp": 0, "pool": 0, "cc": 0}
        self.waited = {e: {} for e in self.eng}
        self.nins = 0

    def _wait(self, e, tok):
        key, val = tok
        if val <= 0:
            return
        if self.waited[e].get(key, 0) >= val:
            return
        sem = self.sem[key] if isinstance(key, str) else self.dsem[key]
        self.eng[e].wait_ge(sem, val)
        self.waited[e][key] = val

    def _sync(self, e, reads, writes):
        for b in reads:
            if b.w is not None:
                self._wait(e, b.w)
        for b in writes:
            if b.w is not None and (b.w[0] != e or e != "pe"):
                self._wait(e, b.w)
            for k, v in b.r.items():
                if k != e or e != "pe":
                    self._wait(e, (k, v))

    def op(self, e, reads, writes, fn):
        self._sync(e, reads, writes)
        ins = fn(self.eng[e])
        self.cnt[e] += 1
        self.nins += 1
        ins.then_inc(self.sem[e], 1)
        v = self.cnt[e]
        for b in reads:
            b.r[e] = v
        for b in writes:
            b.w = (e, v)
            b.r = {}

    def dma(self, q, out, in_, reads, writes):
        self._sync(q, reads, writes)
        pool = self.dpool[q]
        i = pool[self.dnext[q] % len(pool)]
        self.dnext[q] += 1
        self._wait(q, (i, self.dval[i]))
        ins = self.eng[q].dma_start(out=out, in_=in_)
        self.dval[i] += 16
        self.nins += 1
        ins.then_inc(self.dsem[i], 16)
        v = self.dval[i]
        for b in reads:
            b.r[i] = v
        for b in writes:
            b.w = (i, v)
            b.r = {}

    def barrier(self):
        for e in self.eng:
            for k in self.eng:
                if k != e:
                    self._wait(e, (k, self.cnt[k]))
            for i in range(len(self.dsem)):
                self._wait(e, (i, self.dval[i]))

    def final_wait(self, bufs):
        for b in bufs:
            if b.w is not None:
                self._wait("sp", b.w)


_UNIQ = [0]


def sb(nc, es, name, shape, dt):
    _UNIQ[0] += 1
    return es.enter_context(nc.sbuf_tensor("%s_%d" % (name, _UNIQ[0]), list(shape), dt))


def ps(nc, es, name, shape, dt=F32):
    _UNIQ[0] += 1
    return es.enter_context(nc.psum_tensor("%s_%d" % (name, _UNIQ[0]), list(shape), dt))


def emit_rstd(pg, consts, src, src_bufs, nchunks, ncols, rstd, rstd_buf, sq, sq_bufs, pss, pss_bufs, dim):
    ones = consts["ones_f"]
    nh = ncols // 512
    for c in range(nchunks):
        k = c % 2
        pg.op("act", [src_bufs[c]], [sq_bufs[k]],
              lambda e, c=c, k=k: e.activation(out=sq[:, k, :], in_=src[:, c, :], func=AF.Square))
        for h in range(nh):
            pg.op("pe", [sq_bufs[k], consts["buf"]], [pss_bufs[h]],
                  lambda e, c=c, k=k, h=h: e.matmul(pss[:, h * 512:(h + 1) * 512], ones[:, :],
                                                   sq[:, k, h * 512:(h + 1) * 512],
                                                   start=(c == 0), stop=(c == nchunks - 1)))
    for h in range(nh):
        pg.op("act", [pss_bufs[h]], [rstd_buf],
              lambda e, h=h: e.activation(out=rstd[:, h * 512:(h + 1) * 512], in_=pss[:, h * 512:(h + 1) * 512],
                                          func=AF.Sqrt, scale=1.0 / dim, bias=consts["eps"][:, 0:1]))
    pg.op("dve", [rstd_buf], [rstd_buf],
          lambda e: e.reciprocal(out=rstd[:, :], in_=rstd[:, :]))


def phase_prenorm(pg, nc, consts, x_d, x_dbuf, g_sb, xnT, xn_bufs):
    with ExitStack() as es:
        xT = sb(nc, es, "pn_xT", [128, KC, NT], F32)
        sq = sb(nc, es, "pn_sq", [128, 2, NT], F32)
        rstd = sb(nc, es, "pn_rstd", [128, NT], F32)
        pss = ps(nc, es, "pn_pss", [128, NT])
        xb = [Buf() for _ in range(KC)]
        sqb = [Buf(), Buf()]
        pssb = [Buf(), Buf()]
        rb = Buf()
        for c in range(KC):
            pg.dma("sp", xT[:, c, :], x_d[c], [x_dbuf], [xb[c]])
        emit_rstd(pg, consts, xT, xb, KC, NT, rstd, rb, sq, sqb, pss, pssb, D)
        for c in range(KC):
            pg.op("dve", [xb[c], rb, consts["buf"]], [xn_bufs[c]],
                  lambda e, c=c: e.scalar_tensor_tensor(out=xnT[:, c, :], in0=xT[:, c, :], scalar=g_sb[:, c:c + 1],
                                                        in1=rstd[:, :], op0=ALU.mult, op1=ALU.mult))
        pg.barrier()


def phase_ffn(pg, nc, consts, x_d, x_dbuf, wgu_d, wd_d, gpre_sb, gpost_sb):
    with ExitStack() as es0:
        hT = sb(nc, es0, "f_hT", [128, NF, NT], BF16)
        hb = [[Buf(), Buf()] for _ in range(NF)]
        with ExitStack() as es1:
            xnT = sb(nc, es1, "f_xnT", [128, KC, NT], BF16)
            xnb = [Buf() for _ in range(KC)]
            phase_prenorm(pg, nc, consts, x_d, x_dbuf, gpre_sb, xnT, xnb)
            with ExitStack() as es2:
                NW = 3
                wgu = sb(nc, es2, "f_wgu", [128, NW, 2 * KC * 128], BF16)
                wb = [Buf() for _ in range(NW)]
                sg = sb(nc, es2, "f_sg", [128, 2, 512], BF16)
                sgb = [Buf(), Buf()]
                psg = [ps(nc, es2, "f_psg%d" % i, [128, 512]) for i in range(2)]
                psu = [ps(nc, es2, "f_psu%d" % i, [128, 512]) for i in range(2)]
                psgb = [Buf(), Buf()]
                psub = [Buf(), Buf()]

                def load_w(f):
                    s = f % NW
                    pg.dma("pool", wgu[:, s, :], wgu_d[f], [], [wb[s]])

                for f in range(min(NW - 1, NF)):
                    load_w(f)
                it = 0
                for f in range(NF):
                    if f + NW - 1 < NF:
                        load_w(f + NW - 1)
                    s = f % NW
                    for half in range(2):
                        k = it % 2
                        it += 1
                        tsl = slice(half * 512, (half + 1) * 512)
                        for c in range(KC):
                            pg.op("pe", [wb[s], xnb[c]], [psgb[k]],
                                  lambda e, c=c, s=s, k=k, tsl=tsl: e.matmul(
                                      psg[k][:, :], wgu[:, s, c * 128:(c + 1) * 128], xnT[:, c, tsl],
                                      start=(c == 0), stop=(c == KC - 1)))
                        for c in range(KC):
                            pg.op("pe", [wb[s], xnb[c]], [psub[k]],
                                  lambda e, c=c, s=s, k=k, tsl=tsl: e.matmul(
                                      psu[k][:, :], wgu[:, s, (KC + c) * 128:(KC + c + 1) * 128], xnT[:, c, tsl],
                                      start=(c == 0), stop=(c == KC - 1)))
                        pg.op("act", [psgb[k]], [sgb[k]],
                              lambda e, k=k: e.activation(out=sg[:, k, :], in_=psg[k][:, :], func=AF.Silu))
                        pg.op("dve", [sgb[k], psub[k]], [hb[f][half]],
                              lambda e, k=k, f=f, tsl=tsl: e.tensor_tensor(out=hT[:, f, tsl], in0=sg[:, k, :],
                                                                          in1=psu[k][:, :], op=ALU.mult))
                pg.barrier()
        with ExitStack() as es3:
            oT = sb(nc, es3, "f_oT", [128, KC, NT], F32)
            ob = [Buf() for _ in range(KC)]
            with ExitStack() as es4:
                NWD = 2
                wd = sb(nc, es4, "f_wd", [128, NWD, NF * 128], BF16)
                wdb = [Buf() for _ in range(NWD)]
                pso = [ps(nc, es4, "f_pso%d" % i, [128, 512]) for i in range(4)]
                psob = [Buf() for _ in range(4)]

                def load_wd(c):
                    s = c % NWD
                    pg.dma("pool", wd[:, s, :], wd_d[c], [], [wdb[s]])

                load_wd(0)
                it = 0
                for c in range(KC):
                    if c + 1 < KC:
                        load_wd(c + 1)
                    s = c % NWD
                    for half in range(2):
                        k = it % 4
                        it += 1
                        tsl = slice(half * 512, (half + 1) * 512)
                        for f in range(NF):
                            pg.op("pe", [wdb[s], hb[f][half]], [psob[k]],
                                  lambda e, f=f, s=s, k=k, tsl=tsl: e.matmul(
                                      pso[k][:, :], wd[:, s, f * 128:(f + 1) * 128], hT[:, f, tsl],
                                      start=(f == 0), stop=(f == NF - 1)))
                        pg.op("act", [psob[k]], [ob[c]],
                              lambda e, k=k, c=c, tsl=tsl: e.activation(out=oT[:, c, tsl], in_=pso[k][:, :],
                                                                       func=AF.Copy))
                pg.barrier()
            emit_postnorm_residual(pg, nc, consts, oT, ob, x_d, x_dbuf, gpost_sb)


def emit_postnorm_residual(pg, nc, consts, oT, ob, x_d, x_dbuf, gpost_sb):
    with ExitStack() as es5:
        sq = sb(nc, es5, "f_sq", [128, 2, NT], F32)
        rstd = sb(nc, es5, "f_rstd", [128, NT], F32)
        xc = sb(nc, es5, "f_xc", [128, 3, NT], F32)
        pss = ps(nc, es5, "f_pss", [128, NT])
        sqb = [Buf(), Buf()]
        pssb = [Buf(), Buf()]
        rb = Buf()
        xcb = [Buf() for _ in range(3)]
        emit_rstd(pg, consts, oT, ob, KC, NT, rstd, rb, sq, sqb, pss, pssb, D)
        newbuf = Buf()
        for c in range(KC):
            k = c % 3
            pg.dma("sp", xc[:, k, :], x_d[c], [x_dbuf], [xcb[k]])
            pg.op("dve", [ob[c], rb, consts["buf"]], [ob[c]],
                  lambda e, c=c: e.scalar_tensor_tensor(out=oT[:, c, :], in0=oT[:, c, :],
                                                        scalar=gpost_sb[:, c:c + 1], in1=rstd[:, :],
                                                        op0=ALU.mult, op1=ALU.mult))
            pg.op("pool", [ob[c], xcb[k]], [xcb[k]],
                  lambda e, c=c, k=k: e.tensor_tensor(out=xc[:, k, :], in0=xc[:, k, :], in1=oT[:, c, :],
                                                     op=ALU.add))
            pg.dma("sp", x_d[c], xc[:, k, :], [xcb[k]], [newbuf])
        pg.barrier()
        x_dbuf.w = newbuf.w
        x_dbuf.r = {}


PAY_CHUNK_ROWS = [512, 512, 512, 512, 512, 256, 480, 288]
HD = 128


def pay_xn_rows(pay, c):
    return pay[c // 4][(c % 4) * 128:(c % 4 + 1) * 128, :]


def pay_k_rows(pay, h):
    return pay[4 + h // 4][(h % 4) * 128:(h % 4 + 1) * 128, :]


def _vflat(rows_ap):
    return rows_ap.rearrange("a b -> (a b)").rearrange("(t e) -> t e", e=768)


def pay_v_block(pay, mm):
    k, loc = (6, mm) if mm < 5 else (7, mm - 5)
    return _vflat(pay[k][loc * 96:(loc + 1) * 96, :])


def g1_view(g1):
    return [g.rearrange("(r a) t -> r a t", r=4) for g in g1]


def g1_v_block(g1v, r, mm):
    k, loc = (6, mm) if mm < 5 else (7, mm - 5)
    return _vflat(g1v[k][r, loc * 96:(loc + 1) * 96, :])


def phase_p(pg, nc, consts, x_d, x_dbuf, g_sb, wqk_d, wv_d, pay_d, pay_buf, q_d, q_buf, pay_xn_buf=None, after_xn=None):
    if pay_xn_buf is None:
        pay_xn_buf = pay_buf
    with ExitStack() as es0:
        xnT = sb(nc, es0, "p_xnT", [128, KC, NT], BF16)
        xnb = [Buf() for _ in range(KC)]
        phase_prenorm(pg, nc, consts, x_d, x_dbuf, g_sb, xnT, xnb)
        for c in range(KC):
            pg.dma("sp", pay_xn_rows(pay_d, c), xnT[:, c, :], [xnb[c]], [pay_xn_buf])
        if after_xn is not None:
            after_xn()
        with ExitStack() as es:
            wv = sb(nc, es, "p_wv", [128, KC * 768], BF16)
            wvb = Buf()
            pg.dma("pool", wv[:, :], wv_d, [], [wvb])
            NW = 3
            wq = sb(nc, es, "p_wq", [128, NW, KC * 128], BF16)
            wqb = [Buf() for _ in range(NW)]
            ot = sb(nc, es, "p_ot", [128, 4, 512], BF16)
            otb = [Buf() for _ in range(4)]
            vt = sb(nc, es, "p_vt", [128, 2, 768], BF16)
            vtb = [Buf(), Buf()]
            pp = [ps(nc, es, "p_pp%d" % i, [128, 512]) for i in range(4)]
            ppb = [Buf() for _ in range(4)]

            def load_w(i):
                pg.dma("pool", wq[:, i % NW, :], wqk_d[i], [], [wqb[i % NW]])

            load_w(0)
            load_w(1)
            it = 0
            for i in range(12):
                if i + 2 < 12:
                    load_w(i + 2)
                s = i % NW
                for half in range(2):
                    k = it % 4
                    it += 1
                    tsl = slice(half * 512, (half + 1) * 512)
                    for c in range(KC):
                        pg.op("pe", [wqb[s], xnb[c]], [ppb[k]],
                              lambda e, c=c, s=s, k=k, tsl=tsl: e.matmul(pp[k][:, :], wq[:, s, c * 128:(c + 1) * 128],
                                                                        xnT[:, c, tsl], start=(c == 0), stop=(c == KC - 1)))
                    pg.op("act", [ppb[k]], [otb[k]],
                          lambda e, k=k: e.activation(out=ot[:, k, :], in_=pp[k][:, :], func=AF.Copy))
                    if i < 6:
                        pg.dma("sp", q_d[i][:, tsl], ot[:, k, :], [otb[k]], [q_buf])
                    else:
                        pg.dma("sp", pay_k_rows(pay_d, i - 6)[:, tsl], ot[:, k, :], [otb[k]], [pay_buf])
            for tb in range(8):
                vk = tb % 2
                for (c0, cn) in ((0, 512), (512, 256)):
                    k = it % 4
                    it += 1
                    for c in range(KC):
                        pg.op("pe", [wvb, xnb[c]], [ppb[k]],
                              lambda e, c=c, k=k, tb=tb, c0=c0, cn=cn: e.matmul(
                                  pp[k][:, 0:cn], xnT[:, c, tb * 128:(tb + 1) * 128],
                                  wv[:, c * 768 + c0:c * 768 + c0 + cn], start=(c == 0), stop=(c == KC - 1)))
                    pg.op("act", [ppb[k]], [vtb[vk]],
                          lambda e, k=k, vk=vk, c0=c0, cn=cn: e.activation(out=vt[:, vk, c0:c0 + cn], in_=pp[k][:, 0:cn],
                                                                           func=AF.Copy))
                pg.dma("sp", pay_v_block(pay_d, tb), vt[:, vk, :], [vtb[vk]], [pay_buf])
            pg.barrier()


def phase_da(pg, nc, consts, g1_d, g1_buf, q_d, q_buf, mask_d, lamp_d, subw_d, lam_init, yda_d, yda_buf):
    with ExitStack() as es:
        KT = sb(nc, es, "da_KT", [128, 6, 4, NT], BF16)
        Vt = sb(nc, es, "da_V", [128, 4, 8, 6, 129], BF16)
        QT = sb(nc, es, "da_QT", [128, 6, NT], BF16)
        mask = sb(nc, es, "da_mask", [128, 4, 128], BF16)
        lamp = sb(nc, es, "da_lamp", [128, 4, 64], F32)
        subw = sb(nc, es, "da_subw", [128, 128], F32)
        lam = sb(nc, es, "da_lam", [128, 4], F32)
        ydaT = sb(nc, es, "da_ydaT", [128, 6, NT], BF16)
        PT = sb(nc, es, "da_PT", [128, 2, 2, 4, 128], BF16)
        fin = sb(nc, es, "da_fin", [128, 8, 128], F32)
        sm = sb(nc, es, "da_sm", [128, 16], F32)
        ybf = sb(nc, es, "da_ybf", [128, 2, 128], BF16)
        pS = [ps(nc, es, "da_pS%d" % i, [128, 2, 4, 128]) for i in range(2)]
        pO = [ps(nc, es, "da_pO", [128, 2, 512])]
        pT = ps(nc, es, "da_pT", [128, 2, 128], BF16)
        ktb = [Buf() for _ in range(6)]
        vb = Buf()
        qb = Buf()
        mb = Buf()
        lb = Buf()
        ptb = [Buf(), Buf()]
        psb = [Buf(), Buf()]
        pob = [Buf(), Buf()]
        _pt = Buf()
        ptrb = [_pt, _pt]
        ybb = [Buf(), Buf()]
        finb = Buf()
        ydb = [Buf() for _ in range(6)]
        ident = consts["ident_b"]
        for h in range(6):
            pg.dma("sp", KT[:, h, :, :], g1_d[4 + h // 4][:, (h % 4) * 128:(h % 4 + 1) * 128, :].rearrange("r p t -> p r t"),
                   [g1_buf], [ktb[h]])
        pg.op("pool", [], [vb], lambda e: e.memset(Vt[:, :, :, :, 128:129], 1.0))
        for r in range(4):
            for mm in range(8):
                pg.dma("sp", Vt[:, r, mm, :, 0:128],
                       g1_v_block(g1_d, r, mm).rearrange("t (h e) -> t h e", h=6), [g1_buf], [vb])
        pg.dma("sp", QT[:, :, :], q_d.rearrange("h p t -> p h t"), [q_buf], [qb])
        pg.dma("pool", mask[:, :, :], mask_d, [], [mb])
        pg.dma("sp", lamp[:, :, :], lamp_d, [], [lb])
        pg.dma("sp", subw[:, :], subw_d, [], [lb])
        pg.op("dve", [lb], [lb], lambda e: e.tensor_tensor(out=lamp[:, 0, :], in0=lamp[:, 0, :], in1=lamp[:, 1, :], op=ALU.mult))
        pg.op("dve", [lb], [lb], lambda e: e.tensor_tensor(out=lamp[:, 2, :], in0=lamp[:, 2, :], in1=lamp[:, 3, :], op=ALU.mult))
        pg.op("dve", [lb], [lb], lambda e: e.tensor_reduce(out=lam[:, 0:1], in_=lamp[:, 0, :], axis=AX.X, op=ALU.add))
        pg.op("dve", [lb], [lb], lambda e: e.tensor_reduce(out=lam[:, 1:2], in_=lamp[:, 2, :], axis=AX.X, op=ALU.add))
        pg.op("act", [lb], [lb], lambda e: e.activation(out=lam[:, 0:2], in_=lam[:, 0:2], func=AF.Exp))
        pg.op("dve", [lb], [lb], lambda e: e.tensor_tensor(out=lam[:, 2:3], in0=lam[:, 0:1], in1=lam[:, 1:2], op=ALU.subtract))
        pg.op("dve", [lb], [lb], lambda e: e.tensor_scalar(out=lam[:, 2:3], in0=lam[:, 2:3], scalar1=float(lam_init), scalar2=None, op0=ALU.add))
        pg.op("dve", [lb], [lb], lambda e: e.tensor_scalar(out=subw[:, :], in0=subw[:, :], scalar1=float(1.0 - lam_init), scalar2=None, op0=ALU.mult))
        groups = [(h, m, mp) for h in range(6) for m in range(8) for mp in range(m + 1)]

        def emit_S(idx):
            h, m, mp = groups[idx]
            k = idx % 2
            for r in range(4):
                for c in range(2):
                    pg.op("pe", [ktb[h], qb], [psb[k]],
                          lambda e, h=h, r=r, c=c, k=k, mp=mp, m=m: e.matmul(
                              pS[k][:, c, r, :], KT[c * 64:(c + 1) * 64, h, r, mp * 128:(mp + 1) * 128],
                              QT[c * 64:(c + 1) * 64, h, m * 128:(m + 1) * 128], start=True, stop=True))

        emit_S(0)
        oit = 0
        for idx, (h, m, mp) in enumerate(groups):
            k = idx % 2
            ok = 0
            ngroups = m + 1
            if idx + 1 < len(groups):
                emit_S(idx + 1)
            for half in range(2):
                pg.op("act", [psb[k]], [ptb[k]],
                      lambda e, k=k, half=half: e.activation(out=PT[:, k, half, :, :], in_=pS[k][:, half, :, :],
                                                             func=AF.Exp, scale=0.125))
            if mp == m:
                pg.op("dve", [ptb[k], mb], [ptb[k]],
                      lambda e, k=k: e.tensor_tensor(out=PT[:, k, :, :, :], in0=PT[:, k, :, :, :],
                                                     in1=mask[:, :, :].unsqueeze(1).broadcast_to([128, 2, 4, 128]),
                                                     op=ALU.mult))
            for r in range(4):
                for c in range(2):
                    pg.op("pe", [ptb[k], vb], [pob[ok]],
                          lambda e, h=h, r=r, c=c, k=k, mp=mp, ok=ok, ng=ngroups: e.matmul(
                              pO[ok][:, c, 0:129], PT[:, k, c, r, :], Vt[:, r, mp, h, :],
                              start=(mp == 0 and r == 0), stop=(mp == ng - 1 and r == 3)))
            if mp != m:
                continue
            oit += 1
            O = pO[ok]
            pg.op("dve", [pob[ok]], [finb], lambda e, O=O: e.reciprocal(out=sm[:, 0:2], in_=O[:, :, 128]))
            pg.op("dve", [finb, lb], [finb],
                  lambda e: e.tensor_tensor(out=sm[:, 2:3], in0=sm[:, 1:2], in1=lam[:, 2:3], op=ALU.mult))
            pg.op("dve", [pob[ok], finb], [finb],
                  lambda e, O=O: e.tensor_scalar(out=fin[:, 0, :], in0=O[:, 1, 0:128], scalar1=sm[:, 2:3], scalar2=None,
                                                 op0=ALU.mult))
            pg.op("dve", [pob[ok], finb], [finb],
                  lambda e, O=O: e.scalar_tensor_tensor(out=fin[:, 1, :], in0=O[:, 0, 0:128], scalar=sm[:, 0:1],
                                                        in1=fin[:, 0, :], op0=ALU.mult, op1=ALU.subtract))
            pg.op("dve", [finb], [finb],
                  lambda e: e.tensor_tensor(out=fin[:, 2, :], in0=fin[:, 1, :], in1=fin[:, 1, :], op=ALU.mult))
            pg.op("dve", [finb], [finb],
                  lambda e: e.tensor_reduce(out=sm[:, 4:5], in_=fin[:, 2, :], axis=AX.X, op=ALU.add))
            pg.op("act", [finb, consts["buf"]], [finb],
                  lambda e: e.activation(out=sm[:, 5:6], in_=sm[:, 4:5], func=AF.Sqrt, scale=1.0 / 128,
                                         bias=consts["eps5"][:, 0:1]))
            pg.op("dve", [finb], [finb], lambda e: e.reciprocal(out=sm[:, 6:7], in_=sm[:, 5:6]))
            yk = oit % 2
            pg.op("dve", [finb, lb], [ybb[yk]],
                  lambda e, yk=yk: e.scalar_tensor_tensor(out=ybf[:, yk, :], in0=fin[:, 1, :], scalar=sm[:, 6:7],
                                                          in1=subw[:, :], op0=ALU.mult, op1=ALU.mult))
            pg.op("pe", [ybb[yk], consts["buf"]], [ptrb[yk]],
                  lambda e, yk=yk: e.transpose(pT[:, yk, :], ybf[:, yk, :], ident[:, :]))
            pg.op("act", [ptrb[yk]], [ydb[h]],
                  lambda e, yk=yk, h=h, m=m: e.activation(out=ydaT[:, h, m * 128:(m + 1) * 128], in_=pT[:, yk, :],
                                                         func=AF.Copy))
            if m == 7:
                pg.dma("sp", yda_d[h], ydaT[:, h, :], [ydb[h]], [yda_buf])
        pg.barrier()


def setup_consts(pg, nc, es, ident_d):
    cb = Buf()
    ones_f = sb(nc, es, "c_ones_f", [128, 128], F32)
    ones_b = sb(nc, es, "c_ones_b", [128, 128], BF16)
    ident_f = sb(nc, es, "c_ident_f", [128, 128], F32)
    ident_b = sb(nc, es, "c_ident_b", [128, 128], BF16)
    eps = sb(nc, es, "c_eps", [128, 4], F32)
    pg.op("dve", [], [cb], lambda e: e.memset(ones_f[:, :], 1.0))
    pg.op("dve", [], [cb], lambda e: e.memset(ones_b[:, :], 1.0))
    pg.op("dve", [], [cb], lambda e: e.memset(eps[:, 0:1], EPS))
    pg.op("dve", [], [cb], lambda e: e.memset(eps[:, 1:2], 1e-5))
    pg.op("dve", [], [cb], lambda e: e.memset(eps[:, 2:3], 64e-5))
    pg.op("dve", [], [cb], lambda e: e.memset(eps[:, 3:4], 0.0))
    pg.dma("sp", ident_f[:, :], ident_d, [], [cb])
    pg.dma("pool", ident_b[:, :], ident_d, [], [cb])
    return {"ones_f": ones_f, "ones_b": ones_b, "ident_f": ident_f, "ident_b": ident_b, "buf": cb,
            "eps": eps[:, 0:1], "eps5": eps[:, 1:2], "epsgn": eps[:, 2:3], "zero": eps[:, 3:4]}


I32 = mybir.dt.int32
TT = 512


def load_xn_tile(pg, g1_d, g1_buf, xt, xtb, m, rs=(0, 1, 2, 3)):
    for i, r in enumerate(rs):
        for k4 in range(4):
            pg.dma("sp", xt[:, k4 * 4:(k4 + 1) * 4, i, :],
                   g1_d[k4][r, :, m * 128:(m + 1) * 128].rearrange("(kc p) t -> p kc t", p=128),
                   [g1_buf], [xtb])


def emit_sin(pg, out, x, tmpf, tmpi, bufs):
    pg.op("dve", bufs, bufs, lambda e: e.tensor_scalar(out=tmpf, in0=x, scalar1=1.0 / (2 * math.pi), scalar2=None, op0=ALU.mult))
    pg.op("dve", bufs, bufs, lambda e: e.tensor_copy(out=tmpi, in_=tmpf))
    pg.op("dve", bufs, bufs, lambda e: e.tensor_copy(out=tmpf, in_=tmpi))
    pg.op("dve", bufs, bufs, lambda e: e.scalar_tensor_tensor(out=tmpf, in0=tmpf, scalar=-2 * math.pi, in1=x, op0=ALU.mult, op1=ALU.add))
    pg.op("dve", bufs, bufs, lambda e: e.tensor_scalar(out=tmpf, in0=tmpf, scalar1=math.pi, scalar2=-math.pi, op0=ALU.min, op1=ALU.max))
    pg.op("act", bufs, bufs, lambda e: e.activation(out=out, in_=tmpf, func=AF.Sin))


def phase_s5(pg, nc, consts, g1_d, g1_buf, wss_d, s5p_d, s5b_d, s5c_d, s5d_d, ramp_d, pay2_d, pay2_buf):
    TC = TT
    with ExitStack() as es:
        wss = sb(nc, es, "s5_w", [128, KC * 128], BF16)
        prm = sb(nc, es, "s5_prm", [128, 4, 3], F32)
        bp = sb(nc, es, "s5_bp", [128, 4, 2, 16], F32)
        cp = sb(nc, es, "s5_cp", [128, 4, 2, 16], F32)
        dsk = sb(nc, es, "s5_d", [128, 1], F32)
        ramp = sb(nc, es, "s5_ramp", [128, TC + 1], F32)
        sm = sb(nc, es, "s5_sm", [128, 4, 16], F32)
        bbar = sb(nc, es, "s5_bbar", [128, 4, 2, 16], F32)
        tmp16 = sb(nc, es, "s5_t16", [128, 4, 2, 16], F32)
        pad = sb(nc, es, "s5_pad", [128, 4, 4, 128], F32)
        BT = sb(nc, es, "s5_BT", [128, 4, 2, 128], BF16)
        padb = sb(nc, es, "s5_padb", [128, 4, 2, 128], BF16)
        CT = sb(nc, es, "s5_CT", [128, 4, 2, 128], BF16)
        cosT = sb(nc, es, "s5_cos", [128, 4, TC + 1], F32)
        sinT = sb(nc, es, "s5_sin", [128, 4, TC + 1], F32)
        targ = sb(nc, es, "s5_targ", [128, 4, TC + 1], F32)
        ttmp = sb(nc, es, "s5_ttmp", [128, 4, TC + 1], F32)
        tint = sb(nc, es, "s5_tint", [128, 4, TC + 1], I32)
        init = sb(nc, es, "s5_init", [128, 4, 2], F32)
        itmp = sb(nc, es, "s5_itmp", [128, 4, 4], F32)
        xt = sb(nc, es, "s5_xt", [128, 2, KC, 4, 128], BF16)
        uf = sb(nc, es, "s5_uf", [128, TC], F32)
        ub = sb(nc, es, "s5_ub", [128, TC], BF16)
        w1 = sb(nc, es, "s5_w1", [128, 6, TC], F32)
        xs = sb(nc, es, "s5_xs", [128, 4, 2, TC], BF16)
        yo = sb(nc, es, "s5_yo", [128, 3, TC], F32)
        yb16 = sb(nc, es, "s5_yb", [128, 2, TC], BF16)
        pu = ps(nc, es, "s5_pu", [128, TC])
        pbu = [ps(nc, es, "s5_pbu%d" % i, [128, 2, TC]) for i in range(2)]
        py = ps(nc, es, "s5_py", [128, TC])
        ptr = ps(nc, es, "s5_ptr", [128, 4, 128], BF16)
        pb = Buf()
        pg.dma("pool", wss[:, :], wss_d, [], [pb])
        pg.dma("sp", prm[:, :, :], s5p_d, [], [pb])
        pg.dma("sp", bp[:, :, :, :], s5b_d, [], [pb])
        pg.dma("sp", cp[:, :, :, :], s5c_d, [], [pb])
        pg.dma("sp", dsk[:, :], s5d_d, [], [pb])
        pg.dma("sp", ramp[:, :], ramp_d, [], [pb])
        P = [pb]
        ar, ai, ldt = prm[:, :, 0], prm[:, :, 1], prm[:, :, 2]
        dt, mag, ang = sm[:, :, 0], sm[:, :, 1], sm[:, :, 2]
        cosa, sina = sm[:, :, 3], sm[:, :, 4]
        nr, ni, den, cr, ci = sm[:, :, 5], sm[:, :, 6], sm[:, :, 7], sm[:, :, 8], sm[:, :, 9]
        t0, t1, angc = sm[:, :, 10], sm[:, :, 11], sm[:, :, 12]

        def V(fn):
            pg.op("dve", P, P, fn)

        pg.op("act", P, P, lambda e: e.activation(out=dt, in_=ldt, func=AF.Exp))
        V(lambda e: e.tensor_tensor(out=t0, in0=dt, in1=ar, op=ALU.mult))
        pg.op("act", P, P, lambda e: e.activation(out=mag, in_=t0, func=AF.Exp))
        V(lambda e: e.tensor_tensor(out=ang, in0=dt, in1=ai, op=ALU.mult))
        emit_sin(pg, sina, ang, t0, tint[:, :, 0], P)
        V(lambda e: e.tensor_scalar(out=angc, in0=ang, scalar1=math.pi / 2, scalar2=None, op0=ALU.add))
        emit_sin(pg, cosa, angc, t0, tint[:, :, 0], P)
        V(lambda e: e.tensor_tensor(out=nr, in0=mag, in1=cosa, op=ALU.mult))
        V(lambda e: e.tensor_scalar(out=nr, in0=nr, scalar1=-1.0, scalar2=None, op0=ALU.add))
        V(lambda e: e.tensor_tensor(out=ni, in0=mag, in1=sina, op=ALU.mult))
        V(lambda e: e.tensor_tensor(out=den, in0=ar, in1=ar, op=ALU.mult))
        V(lambda e: e.tensor_tensor(out=t0, in0=ai, in1=ai, op=ALU.mult))
        V(lambda e: e.tensor_tensor(out=den, in0=den, in1=t0, op=ALU.add))
        V(lambda e: e.reciprocal(out=den, in_=den))
        V(lambda e: e.tensor_tensor(out=cr, in0=nr, in1=ar, op=ALU.mult))
        V(lambda e: e.tensor_tensor(out=t0, in0=ni, in1=ai, op=ALU.mult))
        V(lambda e: e.tensor_tensor(out=cr, in0=cr, in1=t0, op=ALU.add))
        V(lambda e: e.tensor_tensor(out=cr, in0=cr, in1=den, op=ALU.mult))
        V(lambda e: e.tensor_tensor(out=ci, in0=ni, in1=ar, op=ALU.mult))
        V(lambda e: e.tensor_tensor(out=t0, in0=nr, in1=ai, op=ALU.mult))
        V(lambda e: e.tensor_tensor(out=ci, in0=ci, in1=t0, op=ALU.subtract))
        V(lambda e: e.tensor_tensor(out=ci, in0=ci, in1=den, op=ALU.mult))
        s5stage = globals().get("S5_STAGE", "full")
        if s5stage == "A":
            pg.barrier()
            return
        crb = cr.unsqueeze(2).broadcast_to([128, 4, 16])
        cib = ci.unsqueeze(2).broadcast_to([128, 4, 16])
        V(lambda e: e.tensor_tensor(out=bbar[:, :, 0, :], in0=bp[:, :, 0, :], in1=crb, op=ALU.mult))
        V(lambda e: e.tensor_tensor(out=tmp16[:, :, 0, :], in0=bp[:, :, 1, :], in1=cib, op=ALU.mult))
        V(lambda e: e.tensor_tensor(out=bbar[:, :, 0, :], in0=bbar[:, :, 0, :], in1=tmp16[:, :, 0, :], op=ALU.subtract))
        V(lambda e: e.tensor_tensor(out=bbar[:, :, 1, :], in0=bp[:, :, 1, :], in1=crb, op=ALU.mult))
        V(lambda e: e.tensor_tensor(out=tmp16[:, :, 1, :], in0=bp[:, :, 0, :], in1=cib, op=ALU.mult))
        V(lambda e: e.tensor_tensor(out=bbar[:, :, 1, :], in0=bbar[:, :, 1, :], in1=tmp16[:, :, 1, :], op=ALU.add))
        V(lambda e: e.memset(pad[:, :, :, :], 0.0))
        for q in range(4):
            for half in range(2):
                psl = slice(half * 64, (half + 1) * 64)
                csl = slice((2 * q + half) * 16, (2 * q + half + 1) * 16)
                V(lambda e, q=q, psl=psl, csl=csl: e.tensor_copy(out=pad[psl, q, 0, csl], in_=bbar[psl, q, 0, :]))
                V(lambda e, q=q, psl=psl, csl=csl: e.tensor_copy(out=pad[psl, q, 1, csl], in_=bbar[psl, q, 1, :]))
                V(lambda e, q=q, psl=psl, csl=csl: e.tensor_copy(out=pad[psl, q, 2, csl], in_=cp[psl, q, 0, :]))
                V(lambda e, q=q, psl=psl, csl=csl: e.tensor_scalar(out=pad[psl, q, 3, csl], in0=cp[psl, q, 1, :], scalar1=-1.0,
                                                                   scalar2=None, op0=ALU.mult))
        V(lambda e: e.tensor_copy(out=CT[:, :, :, :], in_=pad[:, :, 2:4, :]))
        V(lambda e: e.tensor_copy(out=padb[:, :, :, :], in_=pad[:, :, 0:2, :]))
        trb = Buf()
        for q in range(4):
            for ri in range(2):
                pg.op("pe", P + [consts["buf"]], [trb],
                      lambda e, q=q, ri=ri: e.transpose(ptr[:, q, :], padb[:, q, ri, :], consts["ident_b"][:, :]))
                pg.op("act", [trb], P, lambda e, q=q, ri=ri: e.activation(out=BT[:, q, ri, :], in_=ptr[:, q, :], func=AF.Copy))
        if s5stage == "B":
            pg.barrier()
            return
        for q in range(4):
            V(lambda e, q=q: e.tensor_scalar(out=targ[:, q, :], in0=ramp[:, :], scalar1=ang[:, q:q + 1], scalar2=None, op0=ALU.mult))
        emit_sin(pg, sinT[:, :, :], targ[:, :, :], ttmp[:, :, :], tint[:, :, :], P)
        V(lambda e: e.tensor_scalar(out=targ[:, :, :], in0=targ[:, :, :], scalar1=math.pi / 2, scalar2=None, op0=ALU.add))
        emit_sin(pg, cosT[:, :, :], targ[:, :, :], ttmp[:, :, :], tint[:, :, :], P)
        V(lambda e: e.memset(init[:, :, :], 0.0))
        if s5stage == "C":
            pg.barrier()
            return
        xtb = [Buf(), Buf()]
        ufb, ubb, pub = Buf(), Buf(), Buf()
        pbub = [Buf(), Buf()]
        w1b = [Buf() for _ in range(6)]
        xsb = [Buf() for _ in range(4)]
        pyb = Buf()
        yob = Buf()
        ybb = [Buf(), Buf()]
        ib = Buf()
        ib.w = pb.w
        load_xn_tile(pg, g1_d, g1_buf, xt[:, 0], xtb[0], 0)
        nt = globals().get("S5_TILES", SEQ // TT)
        for m in range(nt):
            k = m % 2
            if m + 1 < nt:
                load_xn_tile(pg, g1_d, g1_buf, xt[:, (m + 1) % 2], xtb[(m + 1) % 2], m + 1)
            if s5stage == "D0":
                continue
            for c in range(KC):
                pg.op("pe", [xtb[k], pb], [pub],
                      lambda e, c=c, k=k: e.matmul(pu[:, :], wss[:, c * 128:(c + 1) * 128],
                                                   xt[:, k, c, :, :].rearrange("p r t -> p (r t)"),
                                                   start=(c == 0), stop=(c == KC - 1)))
            pg.op("act", [pub], [ufb], lambda e: e.activation(out=uf[:, :], in_=pu[:, :], func=AF.Copy))
            pg.op("dve", [ufb], [ubb], lambda e: e.tensor_copy(out=ub[:, :], in_=uf[:, :]))
            if s5stage == "D":
                continue
            for q in range(4):
                kb = q % 2
                for ri in range(2):
                    pg.op("pe", [ubb, pb], [pbub[kb]],
                          lambda e, q=q, ri=ri, kb=kb: e.matmul(pbu[kb][:, ri, :], BT[:, q, ri, :], ub[:, :], start=True, stop=True))
                cs, sn = cosT[:, q, 0:TC], sinT[:, q, 0:TC]
                bur, bui = pbu[kb][:, 0, :], pbu[kb][:, 1, :]
                pg.op("dve", [pbub[kb], pb], [w1b[0]], lambda e, cs=cs, bur=bur: e.tensor_tensor(out=w1[:, 0, :], in0=bur, in1=cs, op=ALU.mult))
                pg.op("dve", [pbub[kb], pb], [w1b[1]], lambda e, sn=sn, bui=bui: e.tensor_tensor(out=w1[:, 1, :], in0=bui, in1=sn, op=ALU.mult))
                pg.op("pool", [w1b[0], w1b[1]], [w1b[0]], lambda e: e.tensor_tensor(out=w1[:, 0, :], in0=w1[:, 0, :], in1=w1[:, 1, :], op=ALU.add))
                pg.op("dve", [pbub[kb], pb], [w1b[2]], lambda e, cs=cs, bui=bui: e.tensor_tensor(out=w1[:, 2, :], in0=bui, in1=cs, op=ALU.mult))
                pg.op("dve", [pbub[kb], pb], [w1b[1]], lambda e, sn=sn, bur=bur: e.tensor_tensor(out=w1[:, 1, :], in0=bur, in1=sn, op=ALU.mult))
                pg.op("pool", [w1b[2], w1b[1]], [w1b[2]], lambda e: e.tensor_tensor(out=w1[:, 2, :], in0=w1[:, 2, :], in1=w1[:, 1, :], op=ALU.subtract))
                rho = mag[:, q:q + 1].to_broadcast([128, TC])
                pg.op("dve", [w1b[0], ib, pb], [w1b[3]],
                      lambda e, q=q, rho=rho: e.tensor_tensor_scan(out=w1[:, 3, :], data0=rho, data1=w1[:, 0, :],
                                                                   initial=init[:, q, 0:1], op0=ALU.mult, op1=ALU.add))
                pg.op("dve", [w1b[2], ib, pb], [w1b[4]],
                      lambda e, q=q, rho=rho: e.tensor_tensor_scan(out=w1[:, 4, :], data0=rho, data1=w1[:, 2, :],
                                                                   initial=init[:, q, 1:2], op0=ALU.mult, op1=ALU.add))
                wl_r, wl_i = w1[:, 3, TC - 1:TC], w1[:, 4, TC - 1:TC]
                cT, sT = cosT[:, q, TC:TC + 1], sinT[:, q, TC:TC + 1]
                pg.op("dve", [w1b[3], w1b[4], pb], [ib], lambda e, q=q, wl_r=wl_r, cT=cT: e.tensor_tensor(out=itmp[:, q, 0:1], in0=wl_r, in1=cT, op=ALU.mult))
                pg.op("dve", [w1b[3], w1b[4], pb], [ib], lambda e, q=q, wl_i=wl_i, sT=sT: e.tensor_tensor(out=itmp[:, q, 1:2], in0=wl_i, in1=sT, op=ALU.mult))
                pg.op("dve", [w1b[3], w1b[4], pb], [ib], lambda e, q=q, wl_r=wl_r, sT=sT: e.tensor_tensor(out=itmp[:, q, 2:3], in0=wl_r, in1=sT, op=ALU.mult))
                pg.op("dve", [w1b[3], w1b[4], pb], [ib], lambda e, q=q, wl_i=wl_i, cT=cT: e.tensor_tensor(out=itmp[:, q, 3:4], in0=wl_i, in1=cT, op=ALU.mult))
                pg.op("dve", [ib], [ib], lambda e, q=q: e.tensor_tensor(out=init[:, q, 0:1], in0=itmp[:, q, 0:1], in1=itmp[:, q, 1:2], op=ALU.subtract))
                pg.op("dve", [ib], [ib], lambda e, q=q: e.tensor_tensor(out=init[:, q, 1:2], in0=itmp[:, q, 2:3], in1=itmp[:, q, 3:4], op=ALU.add))
                pg.op("pool", [w1b[3], pb], [w1b[0]], lambda e, cs=cs: e.tensor_tensor(out=w1[:, 0, :], in0=w1[:, 3, :], in1=cs, op=ALU.mult))
                pg.op("pool", [w1b[4], pb], [w1b[1]], lambda e, sn=sn: e.tensor_tensor(out=w1[:, 1, :], in0=w1[:, 4, :], in1=sn, op=ALU.mult))
                pg.op("dve", [w1b[0], w1b[1]], [xsb[q]], lambda e, q=q: e.tensor_tensor(out=xs[:, q, 0, :], in0=w1[:, 0, :], in1=w1[:, 1, :], op=ALU.subtract))
                pg.op("pool", [w1b[3], pb], [w1b[2]], lambda e, sn=sn: e.tensor_tensor(out=w1[:, 2, :], in0=w1[:, 3, :], in1=sn, op=ALU.mult))
                pg.op("pool", [w1b[4], pb], [w1b[5]], lambda e, cs=cs: e.tensor_tensor(out=w1[:, 5, :], in0=w1[:, 4, :], in1=cs, op=ALU.mult))
                pg.op("dve", [w1b[2], w1b[5]], [xsb[q]], lambda e, q=q: e.tensor_tensor(out=xs[:, q, 1, :], in0=w1[:, 2, :], in1=w1[:, 5, :], op=ALU.add))
            for q in range(4):
                for ri in range(2):
                    pg.op("pe", [xsb[q], pb], [pyb],
                          lambda e, q=q, ri=ri: e.matmul(py[:, :], CT[:, q, ri, :], xs[:, q, ri, :],
                                                         start=(q == 0 and ri == 0), stop=(q == 3 and ri == 1)))
            pg.op("dve", [pyb, ufb, pb], [yob], lambda e: e.scalar_tensor_tensor(out=yo[:, 0, :], in0=uf[:, :], scalar=dsk[:, 0:1],
                                                                                in1=py[:, :], op0=ALU.mult, op1=ALU.add))
            pg.op("dve", [yob], [yob], lambda e: e.tensor_tensor(out=yo[:, 1, :], in0=yo[:, 0, :], in1=yo[:, 0, :], op=ALU.mult))
            pg.op("dve", [yob], [yob], lambda e: e.tensor_scalar(out=yo[:, 1, :], in0=yo[:, 1, :], scalar1=0.044715, scalar2=1.0,
                                                                 op0=ALU.mult, op1=ALU.add))
            pg.op("dve", [yob], [yob], lambda e: e.tensor_tensor(out=yo[:, 1, :], in0=yo[:, 1, :], in1=yo[:, 0, :], op=ALU.mult))
            pg.op("act", [yob], [yob], lambda e: e.activation(out=yo[:, 2, :], in_=yo[:, 1, :], func=AF.Sigmoid,
                                                              scale=2.0 * math.sqrt(2.0 / math.pi)))
            pg.op("dve", [yob], [ybb[k]], lambda e, k=k: e.tensor_tensor(out=yb16[:, k, :], in0=yo[:, 0, :], in1=yo[:, 2, :], op=ALU.mult))
            pg.dma("sp", pay2_d[2][:, m * TT:(m + 1) * TT], yb16[:, k, :], [ybb[k]], [pay2_buf])
        pg.barrier()


RT = 256
NCH = RT // 64
NEG_EH = -math.exp(-0.5)


def phase_rwkv(pg, nc, consts, g1_d, g1_buf, wrw_d, rwp_d, rwpg_d, rwl_d, rwgn_d, rwm_d, pay2_d, pay2_buf):
    with ExitStack() as es:
        wrw = sb(nc, es, "rw_w", [128, KC, 832], BF16)
        xt = sb(nc, es, "rw_xt", [128, 2, KC, 2, 128], BF16)
        prm = sb(nc, es, "rw_prm", [64, 32], F32)
        mug = sb(nc, es, "rw_mug", [128, 1], F32)
        lw = sb(nc, es, "rw_lw", [128, 3, 192], BF16)
        gn = sb(nc, es, "rw_gn", [64, 2, 3, 64], F32)
        msk = sb(nc, es, "rw_msk", [64, 3, 64], F32)
        rmask = sb(nc, es, "rw_rmask", [64, RT], F32)
        idb = sb(nc, es, "rw_idb", [64, 64], BF16)
        Z = sb(nc, es, "rw_Z", [64, 11, RT + 1], F32)
        ZG = sb(nc, es, "rw_ZG", [128, RT + 1], F32)
        ZS = sb(nc, es, "rw_ZS", [64, 11, RT], F32)
        E = sb(nc, es, "rw_E", [64, 12, RT], F32)
        zgs = sb(nc, es, "rw_zgs", [128, 2, RT], F32)
        tw = sb(nc, es, "rw_tw", [64, 2, RT], BF16)
        sgb = sb(nc, es, "rw_sgb", [128, RT], BF16)
        SIG = sb(nc, es, "rw_SIG", [64, 3, RT], F32)
        AL = sb(nc, es, "rw_AL", [64, 3, RT], F32)
        KKN = sb(nc, es, "rw_KKN", [64, 3, RT], F32)
        KP = sb(nc, es, "rw_KP", [64, 3, RT], F32)
        L = sb(nc, es, "rw_L", [64, 3, RT], F32)
        T1 = sb(nc, es, "rw_T1", [64, 3, RT], F32)
        T2 = sb(nc, es, "rw_T2", [128, 3, RT], F32)
        onesblk = sb(nc, es, "rw_onesblk", [128, 128], F32)
        PCt = sb(nc, es, "rw_PC", [64, 3, NCH], F32)
        arT = sb(nc, es, "rw_arT", [64, 3, NCH, 2, 64], BF16)
        bkT = sb(nc, es, "rw_bkT", [64, 3, NCH, 2, 64], BF16)
        FM = sb(nc, es, "rw_FM", [64, 3, 5, RT], BF16)
        TOK = sb(nc, es, "rw_TOK", [64, NCH, 3, 5, 64], BF16)
        SCm = sb(nc, es, "rw_SCm", [64, 3, NCH, 2, 2, 64], BF16)
        NLt = sb(nc, es, "rw_NL", [64, 3, NCH, 64], BF16)
        MJ = sb(nc, es, "rw_MJ", [64, 2, 3, NCH, 64], BF16)
        NJ = sb(nc, es, "rw_NJ", [64, 2, 3, NCH, 64], BF16)
        Tt = sb(nc, es, "rw_Tt", [64, 2, 3, NCH, 64], BF16)
        AkVb = sb(nc, es, "rw_AkVb", [64, 3, NCH, 64], BF16)
        UVs = sb(nc, es, "rw_UVs", [64, NCH, 3, 64], F32)
        ApT = sb(nc, es, "rw_ApT", [64, 3, NCH, 64], BF16)
        S = sb(nc, es, "rw_S", [64, 3, 64], F32)
        Sb_ = sb(nc, es, "rw_Sb", [64, 3, 64], BF16)
        Ubf = sb(nc, es, "rw_Ubf", [64, 3, 64], BF16)
        Yt = sb(nc, es, "rw_Yt", [64, NCH, 3, 64], F32)
        F1 = sb(nc, es, "rw_F1", [64, NCH, 3, 64], F32)
        F2 = sb(nc, es, "rw_F2", [64, NCH, 3, 64], F32)
        st = sb(nc, es, "rw_st", [64, 4, NCH, 3], F32)
        rk = sb(nc, es, "rw_rk", [64, NCH, 3], F32)
        yfb = sb(nc, es, "rw_yfb", [64, NCH, 192], BF16)
        yoA = sb(nc, es, "rw_yoA", [128, 2, RT], BF16)
        yoB = sb(nc, es, "rw_yoB", [64, 2, RT], BF16)
        B01 = ps(nc, es, "rw_b01", [128, 1024])
        B23 = ps(nc, es, "rw_b23", [128, 1024])
        B4 = ps(nc, es, "rw_b4", [128, 512])
        B56 = ps(nc, es, "rw_b56", [128, 1024])
        bankT = ps(nc, es, "rw_bT", [128, 1024], BF16)
        pb = Buf()
        P = [pb]
        pg.dma("pool", wrw[:, :, :], wrw_d, [], P)
        pg.dma("sp", prm[:, 0:26], rwp_d, [], P)
        pg.dma("sp", mug[:, :], rwpg_d, [], P)
        pg.dma("pool", lw[:, :, :], rwl_d, [], P)
        pg.dma("sp", gn[:, :, :, :], rwgn_d, [], P)
        pg.dma("sp", msk[:, :, :], rwm_d, [], P)
        pg.op("dve", P + [consts["buf"]], P, lambda e: e.tensor_copy(out=idb[:, :], in_=consts["ident_b"][0:64, 0:64]))
        pg.op("dve", P, P, lambda e: e.memset(rmask[:, :], 1.0))
        pg.op("dve", P, P, lambda e: e.memset(rmask[:, :].rearrange("p (c t) -> p c t", t=64)[:, :, 0:1], 0.0))
        pg.op("dve", P, P, lambda e: e.tensor_scalar(out=prm[:, 29:32], in0=prm[:, 20:23], scalar1=-1.0, scalar2=1.0,
                                                     op0=ALU.mult, op1=ALU.add))
        pg.op("dve", P, P, lambda e: e.memset(T2[:, :, :], 0.0))
        pg.op("dve", P, P, lambda e: e.memset(onesblk[:, :], 0.0))
        pg.op("dve", P, P, lambda e: e.memset(onesblk[0:64, 0:64], 1.0))
        pg.op("dve", P, P, lambda e: e.memset(S[:, :, :], 0.0))
        pg.op("dve", P, P, lambda e: e.memset(Sb_[:, :, :], 0.0))
        pg.op("dve", P, P, lambda e: e.memset(Z[:, :, 0:1], 0.0))
        pg.op("dve", P, P, lambda e: e.memset(ZG[:, 0:1], 0.0))
        MU, W0, A0, KK_, KA, RK, OMKA = 0, 11, 14, 17, 20, 23, 29
        xtb = [Buf(), Buf()]
        zb, zgb, zsb, eb = Buf(), Buf(), Buf(), Buf()
        bb = [Buf() for _ in range(8)]
        pjb = [bb[0], bb[1]]
        plb = [bb[2], bb[3], bb[4]]
        b_tw, b_sg, b_sig, b_al, b_kkn, b_kp, b_L, b_t1, b_t2, b_pc = (Buf() for _ in range(10))
        b_ar, b_bk, b_fm, b_tok, b_ptr = Buf(), Buf(), Buf(), Buf(), bb[7]
        b_scp = [bb[5], bb[6]]
        b_np, b_scm, b_nl = bb[4], Buf(), Buf()
        b_mj, b_nj, b_tt = [Buf(), Buf()], [Buf(), Buf()], [Buf(), Buf()]
        b_pm, b_pn, b_pp = [bb[0], bb[1]], [bb[2], bb[3]], [bb[5], bb[6]]
        b_akv, b_uvs, b_apt = Buf(), Buf(), Buf()
        b_S, b_Sb, b_ubf, b_yt = Buf(), Buf(), Buf(), Buf()
        b_pu, b_py, b_pd, b_pg, b_prk = bb[0], bb[1], bb[2], bb[3], bb[4]
        b_f1, b_f2, b_st, b_rk, b_yfb = Buf(), Buf(), Buf(), Buf(), Buf()
        b_yo = [Buf(), Buf()]
        zgsb = Buf()
        pj = [B01[:, 0:256], B01[:, 512:768]]
        PL = [B23[0:64, 0:256], B23[0:64, 512:768], B4[0:64, 0:256]]
        PLF = [B23[:, 0:256], B23[:, 512:768], B4[:, 0:256]]
        p_u = B01[0:64, 0:192].rearrange("p (h v) -> p h v", h=3)
        p_y = B01[0:64, 512:704].rearrange("p (h v) -> p h v", h=3)
        p_d = B23[0:64, 0:192].rearrange("p (h v) -> p h v", h=3)
        p_g = B23[0:64, 768:960]
        p_sc = [B56[0:64, 0:512].rearrange("p (c x) -> p c x", x=256), B56[0:64, 512:1024].rearrange("p (c x) -> p c x", x=256)]
        p_n = B4[0:64, 0:256].rearrange("p (c s) -> p c s", s=64)
        HC = 3 * NCH * 64
        p_m3 = B01[0:64, 0:HC].rearrange("p (h c s) -> p h c s", h=3, s=64)
        p_nn3 = B23[0:64, 0:HC].rearrange("p (h c s) -> p h c s", h=3, s=64)
        p_pp3 = B56[0:64, 0:HC].rearrange("p (h c s) -> p h c s", h=3, s=64)
        p_tok = bankT[0:64, 0:960].rearrange("p (h k f) -> p h k f", h=3, k=5)
        p_tA = bankT[:, 0:NCH * 64].rearrange("p (c t) -> p c t", t=64)
        p_tB = bankT[0:64, 512:512 + NCH * 64].rearrange("p (c t) -> p c t", t=64)

        def bc(ap, shape, axis):
            return ap.unsqueeze(axis).broadcast_to(shape)

        ntile = globals().get("RW_TILES", SEQ // RT)

        def load(i):
            load_xn_tile(pg, g1_d, g1_buf, xt[:, i % 2], xtb[i % 2], i // 2, rs=(2 * (i % 2), 2 * (i % 2) + 1))

        load(0)
        for i in range(ntile):
            k = i % 2
            if i + 1 < ntile:
                load(i + 1)
            xr = lambda c, k=k: xt[:, k, c, :, :].rearrange("p r t -> p (r t)")
            if i > 0:
                pg.op("dve", [zb], [zb], lambda e: e.tensor_copy(out=Z[:, :, 0:1], in_=Z[:, :, RT:RT + 1]))
                pg.op("dve", [zgb], [zgb], lambda e: e.tensor_copy(out=ZG[:, 0:1], in_=ZG[:, RT:RT + 1]))
            for g in range(12):
                kk = g % 2
                M = 64 if g < 11 else 128
                c0 = g * 64
                for c in range(KC):
                    pg.op("pe", [xtb[k], pb], [pjb[kk]],
                          lambda e, c=c, kk=kk, M=M, c0=c0, xr=xr: e.matmul(pj[kk][0:M, :], wrw[:, c, c0:c0 + M], xr(c),
                                                                           start=(c == 0), stop=(c == KC - 1)))
                if g < 11:
                    pg.op("act", [pjb[kk]], [zb], lambda e, g=g, kk=kk: e.activation(out=Z[:, g, 1:RT + 1], in_=pj[kk][0:64, :], func=AF.Copy))
                else:
                    pg.op("act", [pjb[kk]], [zgb], lambda e, kk=kk: e.activation(out=ZG[:, 1:RT + 1], in_=pj[kk][:, :], func=AF.Copy))
            Dt = E[:, 0:11, :]
            pg.op("dve", [zb], [eb], lambda e: e.tensor_tensor(out=Dt, in0=Z[:, :, 0:RT], in1=Z[:, :, 1:RT + 1], op=ALU.subtract))
            pg.op("dve", [eb, pb], [eb], lambda e: e.tensor_tensor(out=Dt, in0=Dt, in1=bc(prm[:, MU:MU + 11], [64, 11, RT], 2), op=ALU.mult))
            pg.op("dve", [eb, zb], [zsb], lambda e: e.tensor_tensor(out=ZS[:, :, :], in0=Dt, in1=Z[:, :, 1:RT + 1], op=ALU.add))
            pg.op("pool", [zgb], [zgsb], lambda e: e.tensor_tensor(out=zgs[:, 0, :], in0=ZG[:, 0:RT], in1=ZG[:, 1:RT + 1], op=ALU.subtract))
            pg.op("dve", [zgsb, zgb, pb], [zgsb], lambda e: e.scalar_tensor_tensor(out=zgs[:, 1, :], in0=zgs[:, 0, :], scalar=mug[:, 0:1],
                                                                                  in1=ZG[:, 1:RT + 1], op0=ALU.mult, op1=ALU.add))
            R, Kx, Vx = ZS[:, 0:3, :], ZS[:, 3:6, :], ZS[:, 6:9, :]
            pg.op("act", [zsb], [b_tw], lambda e: e.activation(out=tw[:, 0, :], in_=ZS[:, 9, :], func=AF.Tanh))
            pg.op("act", [zsb], [b_tw], lambda e: e.activation(out=tw[:, 1, :], in_=ZS[:, 10, :], func=AF.Copy))
            pg.op("act", [zgsb], [b_sg], lambda e: e.activation(out=sgb[:, :], in_=zgs[:, 1, :], func=AF.Sigmoid))
            for h in range(3):
                pg.op("pe", [b_tw, pb], [plb[h]], lambda e, h=h: e.matmul(PL[h], lw[0:64, 0, h * 64:(h + 1) * 64], tw[:, 0, :], start=True, stop=True))
                pg.op("act", [plb[h], pb], [b_sig], lambda e, h=h: e.activation(out=SIG[:, h, :], in_=PL[h], func=AF.Sigmoid, bias=prm[:, W0 + h:W0 + h + 1]))
            for h in range(3):
                pg.op("pe", [b_tw, pb], [plb[h]], lambda e, h=h: e.matmul(PL[h], lw[0:64, 1, h * 64:(h + 1) * 64], tw[:, 1, :], start=True, stop=True))
                pg.op("act", [plb[h], pb], [b_al], lambda e, h=h: e.activation(out=AL[:, h, :], in_=PL[h], func=AF.Sigmoid, bias=prm[:, A0 + h:A0 + h + 1]))
            pg.op("dve", [b_sig], [b_sig], lambda e: e.tensor_scalar(out=SIG[:, :, :], in0=SIG[:, :, :], scalar1=NEG_EH, scalar2=None, op0=ALU.mult))
            pg.op("dve", [zsb, pb], [b_t1], lambda e: e.tensor_tensor(out=T1[:, :, :], in0=Kx, in1=bc(prm[:, KK_:KK_ + 3], [64, 3, RT], 2), op=ALU.mult))
            pg.op("pool", [b_t1], [b_t2], lambda e: e.tensor_tensor(out=T2[0:64, :, :], in0=T1[:, :, :], in1=T1[:, :, :], op=ALU.mult))
            for h in range(3):
                pg.op("pe", [b_t2, pb], [plb[h]], lambda e, h=h: e.matmul(PLF[h], onesblk[:, :], T2[:, h, :], start=True, stop=True))
                pg.op("act", [plb[h]], [b_kkn], lambda e, h=h: e.activation(out=KKN[:, h, :], in_=PL[h], func=AF.Sqrt))
            pg.op("dve", [b_kkn], [b_kkn], lambda e: e.tensor_scalar(out=KKN[:, :, :], in0=KKN[:, :, :], scalar1=1e-12, scalar2=None, op0=ALU.max))
            pg.op("dve", [b_kkn], [b_kkn], lambda e: e.reciprocal(out=KKN[:, :, :], in_=KKN[:, :, :]))
            pg.op("dve", [b_kkn, b_t1], [b_kkn], lambda e: e.tensor_tensor(out=KKN[:, :, :], in0=KKN[:, :, :], in1=T1[:, :, :], op=ALU.mult))
            pg.op("dve", [b_al, pb], [b_kp], lambda e: e.tensor_tensor(out=KP[:, :, :], in0=AL[:, :, :], in1=bc(prm[:, KA:KA + 3], [64, 3, RT], 2), op=ALU.mult))
            pg.op("dve", [b_kp, pb], [b_kp], lambda e: e.tensor_tensor(out=KP[:, :, :], in0=KP[:, :, :], in1=bc(prm[:, OMKA:OMKA + 3], [64, 3, RT], 2), op=ALU.add))
            pg.op("dve", [b_kp, zsb], [b_kp], lambda e: e.tensor_tensor(out=KP[:, :, :], in0=KP[:, :, :], in1=Kx, op=ALU.mult))
            pg.op("pool", [zsb, b_kp, b_t2], [b_t2], lambda e: e.tensor_tensor(out=T2[0:64, :, :], in0=R, in1=KP[:, :, :], op=ALU.mult))
            pg.op("pool", [b_t2, pb], [b_fm], lambda e: e.tensor_tensor(out=FM[:, :, 4, :], in0=T2[0:64, :, :], in1=bc(prm[:, RK:RK + 3], [64, 3, RT], 2), op=ALU.mult))
            pg.op("dve", [b_kkn, b_al, b_t1], [b_t1], lambda e: e.tensor_tensor(out=T1[:, :, :], in0=KKN[:, :, :], in1=AL[:, :, :], op=ALU.mult))
            for h in range(3):
                pg.op("dve", [b_sig, pb], [b_L], lambda e, h=h: e.tensor_tensor_scan(out=L[:, h, :], data0=rmask[:, :], data1=SIG[:, h, :], initial=0.0,
                                                                                      op0=ALU.mult, op1=ALU.add))
            Pin, Pex, Pinv, PCs = E[:, 0:3, :], E[:, 3:6, :], E[:, 6:9, :], E[:, 9:12, :]
            Lc = L[:, :, :].rearrange("p h (c t) -> p h c t", t=64)
            pg.op("act", [b_L], [eb], lambda e: e.activation(out=Pin, in_=L[:, :, :], func=AF.Exp))
            pg.op("act", [b_L], [eb], lambda e: e.activation(out=Pinv, in_=L[:, :, :], func=AF.Exp, scale=-1.0))
            pg.op("dve", [b_L, b_sig, eb], [eb], lambda e: e.tensor_tensor(out=Pex, in0=L[:, :, :], in1=SIG[:, :, :], op=ALU.subtract))
            pg.op("act", [eb], [eb], lambda e: e.activation(out=Pex, in_=Pex, func=AF.Exp))
            pg.op("dve", [b_L, eb], [eb], lambda e: e.tensor_tensor(out=PCs.rearrange("p h (c t) -> p h c t", t=64),
                                                                   in0=Lc[:, :, :, 63:64].broadcast_to([64, 3, NCH, 64]), in1=Lc, op=ALU.subtract))
            pg.op("act", [eb], [eb], lambda e: e.activation(out=PCs, in_=PCs, func=AF.Exp))
            pg.op("act", [b_L], [b_pc], lambda e: e.activation(out=PCt[:, :, :], in_=Lc[:, :, :, 63], func=AF.Exp))
            arv = arT[:, :, :, :, :]
            pg.op("dve", [b_kkn, eb], [b_ar], lambda e: e.scalar_tensor_tensor(out=arT[:, :, :, 0, :], in0=KKN[:, :, :].rearrange("p h (c t) -> p h c t", t=64), scalar=-1.0,
                                                                             in1=Pex.rearrange("p h (c t) -> p h c t", t=64), op0=ALU.mult, op1=ALU.mult))
            pg.op("dve", [zsb, eb], [b_ar], lambda e: e.tensor_tensor(out=arT[:, :, :, 1, :], in0=R.rearrange("p h (c t) -> p h c t", t=64),
                                                                    in1=Pin.rearrange("p h (c t) -> p h c t", t=64), op=ALU.mult))
            pg.op("dve", [b_t1, eb], [b_bk], lambda e: e.tensor_tensor(out=bkT[:, :, :, 0, :], in0=T1[:, :, :].rearrange("p h (c t) -> p h c t", t=64),
                                                                     in1=Pinv.rearrange("p h (c t) -> p h c t", t=64), op=ALU.mult))
            pg.op("dve", [b_kp, eb], [b_bk], lambda e: e.tensor_tensor(out=bkT[:, :, :, 1, :], in0=KP[:, :, :].rearrange("p h (c t) -> p h c t", t=64),
                                                                     in1=Pinv.rearrange("p h (c t) -> p h c t", t=64), op=ALU.mult))
            pg.op("pool", [b_t1, eb], [b_fm], lambda e: e.tensor_tensor(out=FM[:, :, 0, :], in0=T1[:, :, :], in1=PCs, op=ALU.mult))
            pg.op("pool", [b_kp, eb], [b_fm], lambda e: e.tensor_tensor(out=FM[:, :, 1, :], in0=KP[:, :, :], in1=PCs, op=ALU.mult))
            pg.op("pool", [b_ar], [b_fm], lambda e: e.tensor_copy(out=FM[:, :, 2, :].rearrange("p h (c t) -> p h c t", t=64), in_=arT[:, :, :, 0, :]))
            pg.op("pool", [zsb], [b_fm], lambda e: e.tensor_copy(out=FM[:, :, 3, :], in_=Vx))
            for c in range(NCH):
                for h in range(3):
                    for kd in range(5):
                        pg.op("pe", [b_fm, pb], [b_ptr], lambda e, c=c, h=h, kd=kd: e.transpose(p_tok[:, h, kd, :], FM[:, h, kd, c * 64:(c + 1) * 64], idb[:, :]))
                pg.op("act", [b_ptr], [b_tok], lambda e, c=c: e.activation(out=TOK[:, c, :, :, :], in_=p_tok, func=AF.Copy))
            pg.op("dve", [b_tok], [b_rk], lambda e: e.tensor_reduce(out=rk[:, :, :], in_=TOK[:, :, :, 4, :], axis=AX.X, op=ALU.add))
            for h in range(3):
                for half in range(NCH // 2):
                    for cc in range(2):
                        c = half * 2 + cc
                        for kd in range(2):
                            pg.op("pe", [b_bk, b_ar], [b_scp[half]],
                                  lambda e, h=h, c=c, cc=cc, kd=kd, half=half: e.matmul(p_sc[half][:, cc, kd * 128:(kd + 1) * 128], bkT[:, h, c, kd, :],
                                                                                        arT[:, h, c, :, :].rearrange("p a t -> p (a t)"), start=True, stop=True))
                    pg.op("dve", [b_scp[half], pb], [b_scm],
                          lambda e, h=h, half=half: e.tensor_tensor(out=SCm[:, h, half * 2:half * 2 + 2, :, :, :].rearrange("p c k a t -> p (c k) a t"),
                                                                    in0=p_sc[half].rearrange("p c (k a t) -> p (c k) a t", k=2, a=2),
                                                                    in1=msk[:, 0:2, :].unsqueeze(1).broadcast_to([64, 4, 2, 64]), op=ALU.mult))
                for c in range(NCH):
                    pg.op("pe", [b_bk, b_ar], [b_np], lambda e, h=h, c=c: e.matmul(p_n[:, c, :], arT[:, h, c, 0, :], bkT[:, h, c, 0, :], start=True, stop=True))
                pg.op("dve", [b_np, pb], [b_nl], lambda e, h=h: e.tensor_tensor(out=NLt[:, h, :, :], in0=p_n, in1=bc(msk[:, 2, :], [64, NCH, 64], 1), op=ALU.mult))
            hcs = [(h, c) for h in range(3) for c in range(NCH)]
            Mc = lambda h, c: SCm[:, h, c, 0, 0, :]
            Nc = lambda h, c: NLt[:, h, c, :]
            pg.op("dve", [b_scm, pb], [b_tt[0]],
                  lambda e: e.tensor_tensor(out=Tt[:, 0, :, :, :], in0=SCm[:, :, :, 0, 0, :], in1=idb[:, :].unsqueeze(1).unsqueeze(1).broadcast_to([64, 3, NCH, 64]), op=ALU.add))
            tcur = 0
            mrd, nrd = [b_scm], [b_nl]
            for lev in range(5):
                j = lev % 2
                last = (lev == 4)
                for (h, c) in hcs:
                    pg.op("pe", mrd + nrd, b_pn, lambda e, h=h, c=c, Mc=Mc, Nc=Nc: e.matmul(p_nn3[:, h, c, :], Mc(h, c), Nc(h, c), start=True, stop=True))
                if not last:
                    for (h, c) in hcs:
                        pg.op("pe", mrd + nrd, b_pm, lambda e, h=h, c=c, Mc=Mc, Nc=Nc: e.matmul(p_m3[:, h, c, :], Nc(h, c), Mc(h, c), start=True, stop=True))
                pg.op("act", b_pn, [b_nj[j]], lambda e, j=j: e.activation(out=NJ[:, j, :, :, :], in_=p_nn3, func=AF.Copy))
                if not last:
                    pg.op("dve", b_pm, [b_mj[j]], lambda e, j=j: e.tensor_copy(out=MJ[:, j, :, :, :], in_=p_m3))
                Mc = lambda h, c, j=j: MJ[:, j, h, c, :]
                Nc = lambda h, c, j=j: NJ[:, j, h, c, :]
                mrd, nrd = [b_mj[j]], [b_nj[j]]
                for (h, c) in hcs:
                    pg.op("pe", [b_nj[j], b_tt[tcur]], b_pp, lambda e, h=h, c=c, j=j, tcur=tcur: e.matmul(p_pp3[:, h, c, :], NJ[:, j, h, c, :], Tt[:, tcur, h, c, :], start=True, stop=True))
                pg.op("dve", b_pp + [b_tt[tcur]], [b_tt[1 - tcur]], lambda e, tcur=tcur: e.tensor_tensor(out=Tt[:, 1 - tcur, :, :, :], in0=p_pp3, in1=Tt[:, tcur, :, :, :], op=ALU.add))
                tcur = 1 - tcur
            for (h, c) in hcs:
                pg.op("pe", [b_scm, b_tok], b_pm, lambda e, c=c, h=h: e.matmul(p_m3[:, h, c, :], SCm[:, h, c, 1, 0, :], TOK[:, c, h, 3, :], start=True, stop=True))
            pg.op("act", b_pm, [b_akv], lambda e: e.activation(out=AkVb[:, :, :, :], in_=p_m3, func=AF.Copy))
            for (h, c) in hcs:
                pg.op("pe", [b_tt[tcur], b_akv], b_pn, lambda e, c=c, h=h, tcur=tcur: e.matmul(p_nn3[:, h, c, :], Tt[:, tcur, h, c, :], AkVb[:, h, c, :], start=True, stop=True))
            pg.op("act", b_pn, [b_uvs], lambda e: e.activation(out=UVs[:, :, :, :].rearrange("p c h v -> p h c v"), in_=p_nn3, func=AF.Copy))
            for (h, c) in hcs:
                pg.op("pe", [b_tt[tcur], b_tok], b_pp, lambda e, c=c, h=h, tcur=tcur: e.matmul(p_pp3[:, h, c, :], TOK[:, c, h, 2, :], Tt[:, tcur, h, c, :], start=True, stop=True))
            pg.op("dve", b_pp, [b_apt], lambda e: e.tensor_copy(out=ApT[:, :, :, :], in_=p_pp3))
            for c in range(NCH):
                for h in range(3):
                    pg.op("pe", [b_apt, b_Sb], [b_pu], lambda e, c=c, h=h: e.matmul(p_u[:, h, :], ApT[:, h, c, :], Sb_[:, h, :], start=True, stop=True))
                pg.op("dve", [b_pu, b_uvs], [b_ubf], lambda e, c=c: e.tensor_tensor(out=Ubf[:, :, :], in0=p_u, in1=UVs[:, c, :, :], op=ALU.add))
                for h in range(3):
                    pg.op("pe", [b_ar, b_Sb], [b_py], lambda e, c=c, h=h: e.matmul(p_y[:, h, :], arT[:, h, c, 1, :], Sb_[:, h, :], start=True, stop=False))
                    pg.op("pe", [b_scm, b_tok], [b_py], lambda e, c=c, h=h: e.matmul(p_y[:, h, :], SCm[:, h, c, 1, 1, :], TOK[:, c, h, 3, :], start=False, stop=False))
                    pg.op("pe", [b_scm, b_ubf], [b_py], lambda e, c=c, h=h: e.matmul(p_y[:, h, :], SCm[:, h, c, 0, 1, :], Ubf[:, h, :], start=False, stop=True))
                for h in range(3):
                    pg.op("pe", [b_tok, b_ubf], [b_pd], lambda e, c=c, h=h: e.matmul(p_d[:, h, :], TOK[:, c, h, 0, :], Ubf[:, h, :], start=True, stop=False))
                    pg.op("pe", [b_tok], [b_pd], lambda e, c=c, h=h: e.matmul(p_d[:, h, :], TOK[:, c, h, 1, :], TOK[:, c, h, 3, :], start=False, stop=True))
                pg.op("act", [b_py], [b_yt], lambda e, c=c: e.activation(out=Yt[:, c, :, :], in_=p_y, func=AF.Copy))
                pg.op("dve", [b_S, b_pc], [b_S], lambda e, c=c: e.tensor_tensor(out=S[:, :, :], in0=S[:, :, :], in1=bc(PCt[:, :, c], [64, 3, 64], 2), op=ALU.mult))
                pg.op("dve", [b_S, b_pd], [b_S], lambda e: e.tensor_tensor(out=S[:, :, :], in0=S[:, :, :], in1=p_d, op=ALU.add))
                pg.op("act", [b_S], [b_Sb], lambda e: e.activation(out=Sb_[:, :, :], in_=S[:, :, :], func=AF.Copy))
            Y3 = Yt[:, :, :, :]
            sh = [64, NCH, 3, 64]
            pg.op("dve", [b_yt], [b_st], lambda e: e.tensor_reduce(out=st[:, 0, :, :], in_=Y3, axis=AX.X, op=ALU.add))
            pg.op("dve", [b_st], [b_st], lambda e: e.tensor_scalar(out=st[:, 0, :, :], in0=st[:, 0, :, :], scalar1=1.0 / 64, scalar2=None, op0=ALU.mult))
            pg.op("dve", [b_yt, b_st], [b_f1], lambda e: e.tensor_tensor(out=F1[:, :, :, :], in0=Y3, in1=bc(st[:, 0, :, :], sh, 3), op=ALU.subtract))
            pg.op("pool", [b_f1], [b_f2], lambda e: e.tensor_tensor(out=F2[:, :, :, :], in0=F1[:, :, :, :], in1=F1[:, :, :, :], op=ALU.mult))
            pg.op("dve", [b_f2], [b_st], lambda e: e.tensor_reduce(out=st[:, 1, :, :], in_=F2[:, :, :, :], axis=AX.X, op=ALU.add))
            pg.op("act", [b_st, consts["buf"]], [b_st], lambda e: e.activation(out=st[:, 2, :, :], in_=st[:, 1, :, :], func=AF.Sqrt, scale=1.0 / 64, bias=consts["epsgn"][0:64, :]))
            pg.op("dve", [b_st], [b_st], lambda e: e.reciprocal(out=st[:, 3, :, :], in_=st[:, 2, :, :]))
            pg.op("dve", [b_f1, b_st], [b_f1], lambda e: e.tensor_tensor(out=F1[:, :, :, :], in0=F1[:, :, :, :], in1=bc(st[:, 3, :, :], sh, 3), op=ALU.mult))
            pg.op("dve", [b_f1, pb], [b_f1], lambda e: e.tensor_tensor(out=F1[:, :, :, :], in0=F1[:, :, :, :], in1=bc(gn[:, 0, :, :], sh, 1), op=ALU.mult))
            pg.op("dve", [b_f1, pb], [b_f1], lambda e: e.tensor_tensor(out=F1[:, :, :, :], in0=F1[:, :, :, :], in1=bc(gn[:, 1, :, :], sh, 1), op=ALU.add))
            pg.op("pool", [b_tok, b_rk, b_f2], [b_f2], lambda e: e.tensor_tensor(out=F2[:, :, :, :], in0=TOK[:, :, :, 3, :], in1=bc(rk[:, :, :], sh, 3), op=ALU.mult))
            pg.op("dve", [b_f1, b_f2], [b_f1], lambda e: e.tensor_tensor(out=F1[:, :, :, :], in0=F1[:, :, :, :], in1=F2[:, :, :, :], op=ALU.add))
            for c in range(NCH):
                pg.op("pe", [b_sg, pb], [b_pg], lambda e, c=c: e.matmul(p_g[:, 0:192], sgb[:, c * 64:(c + 1) * 64], lw[:, 2, :], start=True, stop=True))
                pg.op("dve", [b_pg, b_f1], [b_yfb], lambda e, c=c: e.tensor_tensor(out=yfb[:, c, :], in0=F1[:, c, :, :].rearrange("p h v -> p (h v)"), in1=p_g[:, 0:192], op=ALU.mult))
            for c in range(NCH):
                pg.op("pe", [b_yfb, pb], [b_ptr], lambda e, c=c: e.transpose(p_tA[:, c, :], yfb[:, c, 0:128], idb[:, :]))
                pg.op("pe", [b_yfb, pb], [b_ptr], lambda e, c=c: e.transpose(p_tB[:, c, :], yfb[:, c, 128:192], idb[:, :]))
            pg.op("act", [b_ptr], [b_yo[k]], lambda e, k=k: e.activation(out=yoA[:, k, :].rearrange("p (c t) -> p c t", t=64), in_=p_tA, func=AF.Copy))
            pg.op("act", [b_ptr], [b_yo[k]], lambda e, k=k: e.activation(out=yoB[:, k, :].rearrange("p (c t) -> p c t", t=64), in_=p_tB, func=AF.Copy))
            pg.dma("sp", pay2_d[0][:, i * RT:(i + 1) * RT], yoA[:, k, :], [b_yo[k]], [pay2_buf])
            pg.dma("sp", pay2_d[1][:, i * RT:(i + 1) * RT], yoB[:, k, :], [b_yo[k]], [pay2_buf])
        pg.barrier()


NPIECE = 18


def phase_o(pg, nc, consts, x_d, x_dbuf, yda_d, yda_buf, g2_d, g2_buf, sel_d, wglu_d, bglu_d, wout_d, gpost_sb):
    with ExitStack() as es0:
        oT = sb(nc, es0, "o_oT", [128, KC, NT], F32)
        ob = [Buf() for _ in range(KC)]
        with ExitStack() as es:
            yT = sb(nc, es, "o_yT", [128, NPIECE, NT], BF16)
            yb = [Buf() for _ in range(NPIECE)]
            ygT = sb(nc, es, "o_ygT", [128, 4, NT], BF16)
            ygb = [Buf() for _ in range(4)]
            stg = sb(nc, es, "o_stg", [128, 2, 8, 4, 128], BF16)
            stb = [Buf(), Buf()]
            acc = sb(nc, es, "o_acc", [128, 2, NT], F32)
            accb = [Buf(), Buf()]
            sel = sb(nc, es, "o_sel", [128, 4], F32)
            wglu = sb(nc, es, "o_wglu", [128, 4, 512], BF16)
            bglu = sb(nc, es, "o_bglu", [128, 4], F32)
            gate = sb(nc, es, "o_gate", [128, 2, 512], F32)
            gtb = [Buf(), Buf()]
            NW = 2
            wt = sb(nc, es, "o_wt", [128, NW, NPIECE * 128], BF16)
            wtb = [Buf() for _ in range(NW)]
            pp = [ps(nc, es, "o_pp%d" % i, [128, 512]) for i in range(4)]
            ppb = [Buf() for _ in range(4)]
            pb = Buf()
            pg.dma("sp", sel[:, :], sel_d, [], [pb])
            pg.dma("pool", wglu[:, :, :], wglu_d, [], [pb])
            pg.dma("sp", bglu[:, :], bglu_d, [], [pb])
            for h in range(6):
                pg.dma("sp", yT[:, 8 + h, :], yda_d[h], [yda_buf], [yb[8 + h]])
            it = 0

            def select(rp, kchunk, nrows, dst, dstb):
                nonlocal it
                k = it % 2
                it += 1
                pg.dma("sp", stg[0:nrows, k, :, :, :], g2_d[kchunk][rp].rearrange("p (m r t) -> p m r t", r=4, t=128),
                       (list(g2_buf) if isinstance(g2_buf, (list, tuple)) else [g2_buf]), [stb[k]])
                a = acc[0:nrows, k, :].rearrange("p (m t) -> p m t", t=128)
                pg.op("dve", [stb[k], pb], [accb[k]],
                      lambda e, k=k, a=a, nrows=nrows: e.tensor_scalar(out=a, in0=stg[0:nrows, k, :, 0, :], scalar1=sel[0:nrows, 0:1], scalar2=None, op0=ALU.mult))
                for r in range(1, 4):
                    pg.op("dve", [stb[k], pb, accb[k]], [accb[k]],
                          lambda e, k=k, a=a, r=r, nrows=nrows: e.scalar_tensor_tensor(out=a, in0=stg[0:nrows, k, :, r, :], scalar=sel[0:nrows, r:r + 1],
                                                                                       in1=a, op0=ALU.mult, op1=ALU.add))
                pg.op("act", [accb[k]], [dstb], lambda e, k=k, nrows=nrows, dst=dst: e.activation(out=dst[0:nrows, :], in_=acc[0:nrows, k, :], func=AF.Copy))

            for rp in range(4):
                select(rp, 0, 128, yT[:, 2 * rp, :], yb[2 * rp])
                select(rp, 1, 64, yT[:, 2 * rp + 1, :], yb[2 * rp + 1])
                select(rp, 2, 128, ygT[:, rp, :], ygb[rp])
            git = 0
            for cc in range(4):
                for half in range(2):
                    k = git % 4
                    gk = git % 2
                    git += 1
                    tsl = slice(half * 512, (half + 1) * 512)
                    for rp in range(4):
                        pg.op("pe", [ygb[rp], pb], [ppb[k]],
                              lambda e, rp=rp, cc=cc, k=k, tsl=tsl: e.matmul(pp[k][:, :], wglu[:, rp, cc * 128:(cc + 1) * 128], ygT[:, rp, tsl],
                                                                            start=(rp == 0), stop=(rp == 3)))
                    pg.op("act", [ppb[k], pb], [gtb[gk]],
                          lambda e, k=k, gk=gk, cc=cc: e.activation(out=gate[:, gk, :], in_=pp[k][:, :], func=AF.Sigmoid, bias=bglu[:, cc:cc + 1]))
                    pg.op("dve", [gtb[gk], ygb[cc]], [yb[14 + cc]],
                          lambda e, gk=gk, cc=cc, tsl=tsl: e.tensor_tensor(out=yT[:, 14 + cc, tsl], in0=ygT[:, cc, tsl], in1=gate[:, gk, :], op=ALU.mult))
            def load_w(c):
                pg.dma("pool", wt[:, c % NW, :], wout_d[c], [], [wtb[c % NW]])

            load_w(0)
            for c in range(KC):
                if c + 1 < KC:
                    load_w(c + 1)
                s = c % NW
                for half in range(2):
                    k = git % 4
                    git += 1
                    tsl = slice(half * 512, (half + 1) * 512)
                    for pi in range(NPIECE):
                        kr = 64 if (pi < 8 and pi % 2 == 1) else 128
                        pg.op("pe", [wtb[s], yb[pi]], [ppb[k]],
                              lambda e, pi=pi, kr=kr, s=s, k=k, tsl=tsl: e.matmul(pp[k][:, :], wt[0:kr, s, pi * 128:(pi + 1) * 128], yT[0:kr, pi, tsl],
                                                                                 start=(pi == 0), stop=(pi == NPIECE - 1)))
                    pg.op("act", [ppb[k]], [ob[c]],
                          lambda e, k=k, c=c, tsl=tsl: e.activation(out=oT[:, c, tsl], in_=pp[k][:, :], func=AF.Copy))
            pg.barrier()
        emit_postnorm_residual(pg, nc, consts, oT, ob, x_d, x_dbuf, gpost_sb)


L_DEPTH = 2
GROUPS = [[0, 1, 2, 3], [4, 5, 6, 7]]
PAY2_CHUNK_ROWS = [128, 64, 128]
DATA_GROUPS = {
    "pay": [([r, NT], BF16) for r in PAY_CHUNK_ROWS],
    "g1": [([4 * r, NT], BF16) for r in PAY_CHUNK_ROWS],
    "pay2": [([r, SEQ], BF16) for r in PAY2_CHUNK_ROWS],
    "g2": [([4 * r, SEQ], BF16) for r in PAY2_CHUNK_ROWS],
    "q": [([6, 128, NT], BF16)],
    "yda": [([6, 128, NT], BF16)],
}
WEIGHT_SHAPES = {
    "wgu1": [NF, 128, 2 * KC * 128], "wd1": [KC, 128, NF * 128], "wgu2": [NF, 128, 2 * KC * 128], "wd2": [KC, 128, NF * 128],
    "wqk": [12, 128, KC * 128], "wv": [128, KC * 768], "wrw": [128, KC, 832], "rwp": [64, 26], "rwpg": [128, 1],
    "rwl": [128, 3, 192], "rwgn": [64, 2, 3, 64], "wss": [128, KC * 128], "s5p": [128, 4, 3], "s5b": [128, 4, 2, 16],
    "s5c": [128, 4, 2, 16], "s5d": [128, 1], "lamp": [128, 4, 64], "subw": [128, 128], "wglu": [128, 4, 512],
    "bglu": [128, 4], "wout": [KC, 128, NPIECE * 128],
}
CONST_SHAPES = {"ident": [128, 128], "ramp": [128, TT + 1], "rwm": [64, 3, 64], "sel": [128, 4], "damask": [128, 4, 128],
                "gains": [128, L_DEPTH * 6, KC]}


def lam_init_of(l):
    return 0.8 - 0.6 * math.exp(-0.3 * l)


def build_launch(seq, ins, outs, uses_x):
    nc = bass.Bass("TRN2", target_bir_lowering=False)
    declared = {}

    def ext_in(name, shape, dt=F32):
        if name not in declared:
            declared[name] = nc.dram_tensor(name, list(shape), dt, kind="ExternalInput").ap()
        return declared[name]

    def W(name, l):
        return ext_in("%s_%d" % (name, l), WEIGHT_SHAPES[name])

    def Cn(name):
        return ext_in(name, CONST_SHAPES[name])

    data = {}
    dbuf = {}
    data_in_names = []
    for name, members in DATA_GROUPS.items():
        kind = "ExternalInput" if name in ins else ("ExternalOutput" if name in outs else "Internal")
        data[name] = []
        for k, (shape, dt) in enumerate(members):
            nm = "%s%d" % (name, k)
            data[name].append(nc.dram_tensor(nm, list(shape), dt, kind=kind).ap())
            if name in ins:
                data_in_names.append(nm)
        dbuf[name] = Buf()
    for nm in ("pay_xn", "pay_kv", "g1_xn", "g1_kv", "pay2_rw", "pay2_ss", "g2_rw", "g2_ss"):
        dbuf[nm] = Buf()
    fused = any(ph.startswith("gather") for ph, _ in seq)
    g1v = g1_view(data["g1"])
    g2v = g1_view(data["g2"])
    q_d = data["q"][0]
    yda_d = data["yda"][0]
    with ExitStack() as es:
        pg = Prog(nc, es)
        consts = setup_consts(pg, nc, es, Cn("ident"))
        gains = sb(nc, es, "gains_sb", [128, L_DEPTH * 6, KC], F32)
        pg.dma("sp", gains[:, :, :], Cn("gains"), [], [consts["buf"]])
        for l in range(L_DEPTH):
            for i in (1, 5):
                pg.op("dve", [consts["buf"]], [consts["buf"]],
                      lambda e, l=l, i=i: e.tensor_scalar(out=gains[:, l * 6 + i, :], in0=gains[:, l * 6 + i, :], scalar1=0.5,
                                                          scalar2=None, op0=ALU.mult))
        xb = Buf()
        x_d = None
        if uses_x:
            x_in = ext_in("x_in", [KC, 128, NT])
            x_d = nc.dram_tensor("x_out", [KC, 128, NT], F32, kind="ExternalOutput").ap()
            pg.dma("sp", x_d, x_in, [], [xb])
        pg.barrier()
        G = lambda l, i: gains[:, l * 6 + i, :]
        for (ph, l) in seq:
            if ph == "ffn1":
                phase_ffn(pg, nc, consts, x_d, xb, W("wgu1", l), W("wd1", l), G(l, 0), G(l, 1))
            elif ph == "ffn2":
                phase_ffn(pg, nc, consts, x_d, xb, W("wgu2", l), W("wd2", l), G(l, 4), G(l, 5))
            elif ph == "p":
                if fused:
                    phase_p(pg, nc, consts, x_d, xb, G(l, 2), W("wqk", l), W("wv", l), data["pay"], dbuf["pay_kv"], q_d, dbuf["q"],
                            pay_xn_buf=dbuf["pay_xn"],
                            after_xn=lambda: emit_allgather(pg, nc, data["pay"][0:4], dbuf["pay_xn"], data["g1"][0:4], dbuf["g1_xn"]))
                    emit_allgather(pg, nc, data["pay"][4:8], dbuf["pay_kv"], data["g1"][4:8], dbuf["g1_kv"])
                else:
                    phase_p(pg, nc, consts, x_d, xb, G(l, 2), W("wqk", l), W("wv", l), data["pay"], dbuf["pay"], q_d, dbuf["q"])
            elif ph == "gather":
                pass
            elif ph == "da":
                phase_da(pg, nc, consts, g1v, dbuf["g1_kv"] if fused else dbuf["g1"], q_d, dbuf["q"], Cn("damask"), W("lamp", l), W("subw", l),
                         lam_init_of(l), yda_d, dbuf["yda"])
            elif ph == "rwkv":
                phase_rwkv(pg, nc, consts, g1v, dbuf["g1_xn"] if fused else dbuf["g1"], W("wrw", l), W("rwp", l), W("rwpg", l), W("rwl", l), W("rwgn", l),
                           Cn("rwm"), data["pay2"], dbuf["pay2_rw"] if fused else dbuf["pay2"])
                if fused:
                    emit_allgather(pg, nc, data["pay2"][0:2], dbuf["pay2_rw"], data["g2"][0:2], dbuf["g2_rw"])
            elif ph == "s5":
                phase_s5(pg, nc, consts, g1v, dbuf["g1_xn"] if fused else dbuf["g1"], W("wss", l), W("s5p", l), W("s5b", l), W("s5c", l), W("s5d", l),
                         Cn("ramp"), data["pay2"], dbuf["pay2_ss"] if fused else dbuf["pay2"])
                if fused:
                    emit_allgather(pg, nc, data["pay2"][2:3], dbuf["pay2_ss"], data["g2"][2:3], dbuf["g2_ss"])
            elif ph == "o":
                phase_o(pg, nc, consts, x_d, xb, yda_d, dbuf["yda"], g2v, [dbuf["g2_rw"], dbuf["g2_ss"]] if fused else dbuf["g2"],
                        Cn("sel"), W("wglu", l), W("bglu", l),
                        W("wout", l), G(l, 3))
            else:
                raise ValueError(ph)
        pg.barrier()
    in_names = list(declared.keys()) + data_in_names
    return nc, in_names


def emit_allgather(pg, nc, src, src_buf, dst, dst_buf):
    pg._sync("pool", [src_buf], [dst_buf])
    for s_ap, d_ap in zip(src, dst):
        ins = nc.gpsimd.collective_compute("AllGather", ALU.bypass, replica_groups=GROUPS, ins=[s_ap], outs=[d_ap])
        pg.cc_cnt += 1
        ins.then_inc(pg.sem["cc"], 1)
    src_buf.r["cc"] = pg.cc_cnt
    dst_buf.w = ("cc", pg.cc_cnt)
    dst_buf.r = {}


def _tile_gu(w_gu):
    return np.ascontiguousarray(w_gu.reshape(KC, 128, 2, NF, 128).transpose(3, 1, 2, 0, 4)).reshape(NF, 128, 2 * KC * 128)


def _tile_down(w_down):
    return np.ascontiguousarray(w_down.reshape(NF, 128, KC, 128).transpose(2, 1, 0, 3)).reshape(KC, 128, NF * 128)


def _pm(v):
    return np.ascontiguousarray(np.asarray(v).reshape(KC, 128).T)


def host_shared(inp, l):
    w_in = inp["w_in"][l]
    out = {}
    out["wgu1"] = _tile_gu(inp["ffn1_w_gu"][l])
    out["wd1"] = _tile_down(inp["ffn1_w_down"][l])
    out["wgu2"] = _tile_gu(inp["ffn2_w_gu"][l])
    out["wd2"] = _tile_down(inp["ffn2_w_down"][l])
    c0 = 2560
    wqk = np.empty((12, 128, KC * 128), np.float32)
    for i in range(12):
        col = c0 + i * 128
        wqk[i] = w_in[:, col:col + 128].reshape(KC, 128, 128).transpose(1, 0, 2).reshape(128, KC * 128)
    out["wqk"] = wqk
    out["wv"] = np.ascontiguousarray(w_in[:, c0 + 1536:c0 + 2304].reshape(KC, 128, 768).transpose(1, 0, 2)).reshape(128, KC * 768)
    out["lamp"] = np.ascontiguousarray(np.broadcast_to(
        np.stack([inp["da_lq1"][l], inp["da_lk1"][l], inp["da_lq2"][l], inp["da_lk2"][l]])[None], (128, 4, 64)))
    out["subw"] = np.ascontiguousarray(np.broadcast_to(inp["da_subln_w"][l][None], (128, 128)))
    out["wglu"] = np.ascontiguousarray(inp["ssm_w_glu"][l].reshape(4, 128, 512).transpose(1, 0, 2))
    out["bglu"] = np.ascontiguousarray(inp["ssm_b_glu"][l].reshape(4, 128).T)
    w_out = inp["w_out"][l]
    pieces = []
    for rp in range(4):
        pieces += [(rp * 192, 128), (rp * 192 + 128, 64)]
    pieces += [(768 + h * 128, 128) for h in range(6)] + [(1536 + c * 128, 128) for c in range(4)]
    wo = np.zeros((KC, 128, NPIECE, 128), np.float32)
    for pi, (r0, n) in enumerate(pieces):
        wo[:, :n, pi, :] = w_out[r0:r0 + n, :].reshape(n, KC, 128).transpose(1, 0, 2)
    out["wout"] = wo.reshape(KC, 128, NPIECE * 128)
    return out


def host_percore(inp, l, j):
    w_in = inp["w_in"][l]
    mu = inp["rw_mu"][l]
    hs = [3 * j, 3 * j + 1, 3 * j + 2]
    cols = []
    for base in (0, 768, 1536):
        for h in hs:
            cols += list(range(base + h * 64, base + (h + 1) * 64))
    cols += list(range(2304, 2560))
    cols = np.array(cols)
    out = {}
    out["wrw"] = np.ascontiguousarray(w_in[:, cols].reshape(KC, 128, 832).transpose(1, 0, 2))
    prm = np.zeros((64, 26), np.float32)
    for g in range(11):
        prm[:, g] = mu[cols[g * 64:(g + 1) * 64]]
    for i, h in enumerate(hs):
        sl = slice(h * 64, (h + 1) * 64)
        prm[:, 11 + i] = inp["rw_w0"][l][sl]
        prm[:, 14 + i] = inp["rw_a0"][l][sl]
        prm[:, 17 + i] = inp["rw_k_k"][l][sl]
        prm[:, 20 + i] = inp["rw_k_a"][l][sl]
        prm[:, 23 + i] = inp["rw_r_k"][l][h]
    out["rwp"] = prm
    out["rwpg"] = np.ascontiguousarray(mu[2432:2560].reshape(128, 1))
    own = np.array(sum([list(range(h * 64, (h + 1) * 64)) for h in hs], []))
    rwl = np.zeros((128, 3, 192), np.float32)
    rwl[:64, 0] = inp["rw_w2"][l][:, own]
    rwl[:64, 1] = inp["rw_a2"][l][:, own]
    rwl[:, 2] = inp["rw_g2"][l][:, own]
    out["rwl"] = rwl
    gn = np.zeros((64, 2, 3, 64), np.float32)
    gn[:, 0] = inp["rw_gn_w"][l][own].reshape(3, 64)[None]
    gn[:, 1] = inp["rw_gn_b"][l][own].reshape(3, 64)[None]
    out["rwgn"] = gn
    out["wss"] = np.ascontiguousarray(
        w_in[:, 4864 + j * 128:4864 + (j + 1) * 128].reshape(KC, 128, 128).transpose(1, 0, 2)).reshape(128, KC * 128)
    s5p = np.zeros((128, 4, 3), np.float32)
    s5b = np.zeros((128, 4, 2, 16), np.float32)
    s5c = np.zeros((128, 4, 2, 16), np.float32)
    for q in range(4):
        for half in range(2):
            g = 8 * j + 2 * q + half
            ps_ = slice(half * 64, (half + 1) * 64)
            s5p[ps_, q, 0] = inp["ssm_a_re"][l][g]
            s5p[ps_, q, 1] = inp["ssm_a_im"][l][g]
            s5p[ps_, q, 2] = inp["ssm_log_dt"][l][g]
            s5b[ps_, q, 0] = inp["ssm_b_re"][l][g]
            s5b[ps_, q, 1] = inp["ssm_b_im"][l][g]
            s5c[ps_, q, 0] = inp["ssm_c_re"][l][g].T
            s5c[ps_, q, 1] = inp["ssm_c_im"][l][g].T
    out["s5p"], out["s5b"], out["s5c"] = s5p, s5b, s5c
    out["s5d"] = np.ascontiguousarray(inp["ssm_d"][l][j * 128:(j + 1) * 128].reshape(128, 1))
    return out


def host_consts(inp, j):
    out = {"ident": np.eye(128, dtype=np.float32)}
    out["ramp"] = np.ascontiguousarray(np.broadcast_to(np.arange(TT + 1, dtype=np.float32)[None], (128, TT + 1)))
    s_ = np.arange(64)
    out["rwm"] = np.ascontiguousarray(np.stack([(s_[:, None] < s_[None, :]), (s_[:, None] <= s_[None, :]),
                                                 (s_[None, :] < s_[:, None])], axis=1).astype(np.float32))
    sel = np.zeros((128, 4), np.float32)
    sel[:, j] = 1.0
    out["sel"] = sel
    tri = (np.arange(128)[:, None] <= np.arange(128)[None, :]).astype(np.float32)
    mask = np.zeros((128, 4, 128), np.float32)
    for r in range(4):
        if r < j:
            mask[:, r, :] = 1.0
        elif r == j:
            mask[:, r, :] = tri
    out["damask"] = mask
    names = ["ffn1_pre_g", "ffn1_post_g", "mix_pre_g", "mix_post_g", "ffn2_pre_g", "ffn2_post_g"]
    gains = np.zeros((128, L_DEPTH * 6, KC), np.float32)
    for l in range(L_DEPTH):
        for i, n in enumerate(names):
            gains[:, l * 6 + i, :] = _pm(inp[n][l])
    out["gains"] = gains
    return out


FUSED = True


def kernel(**inputs):
    inp = {k: np.asarray(v) for k, v in inputs.items()}
    x = inp["x"].astype(np.float32, copy=False)
    pool = []
    for c in range(NCORES):
        b, j = c // 4, c % 4
        d = dict(host_consts(inp, j))
        pool.append(d)
    shared = [host_shared(inp, l) for l in range(L_DEPTH)]
    percore = [[host_percore(inp, l, j) for j in range(4)] for l in range(L_DEPTH)]
    for c in range(NCORES):
        j = c % 4
        for l in range(L_DEPTH):
            for k, v in shared[l].items():
                pool[c]["%s_%d" % (k, l)] = v
            for k, v in percore[l][j].items():
                pool[c]["%s_%d" % (k, l)] = v
    xs = []
    for c in range(NCORES):
        b, j = c // 4, c % 4
        t = x[b].reshape(8, 4, 128, D)[:, j].reshape(NT, D)
        xs.append(np.ascontiguousarray(t.T.reshape(KC, 128, NT)))

    def run(seq, ins, outs, uses_x, state):
        nc, in_names = build_launch(seq, ins, outs, uses_x)
        maps = []
        for c in range(NCORES):
            m = {}
            for n in in_names:
                if n == "x_in":
                    m[n] = state["x"][c]
                elif n in state:
                    m[n] = state[n][c]
                else:
                    m[n] = pool[c][n]
            maps.append(m)
        res = run_bass_kernel_spmd(nc, maps, core_ids=list(range(NCORES))).results
        if uses_x:
            state["x"] = [np.asarray(res[c]["x_out"]) for c in range(NCORES)]
        for g in outs:
            for k in range(len(DATA_GROUPS[g])):
                n = "%s%d" % (g, k)
                state[n] = [np.asarray(res[c][n]) for c in range(NCORES)]

    def gather(state, src, dst):
        for k in range(len(DATA_GROUPS[src])):
            sn, dn = "%s%d" % (src, k), "%s%d" % (dst, k)
            state[dn] = [None] * NCORES
            for c in range(NCORES):
                b = c // 4
                state[dn][c] = np.concatenate([state[sn][4 * b + r] for r in range(4)], axis=0)

    state = {"x": xs}
    if FUSED:
        seq = []
        for l in range(L_DEPTH):
            seq += [("ffn1", l), ("p", l), ("gather", l), ("rwkv", l), ("s5", l), ("da", l), ("o", l), ("ffn2", l)]
        run(seq, set(), set(), True, state)
    else:
        run([("ffn1", 0), ("p", 0)], set(), {"pay", "q"}, True, state)
        for l in range(L_DEPTH):
            gather(state, "pay", "g1")
            run([("da", l), ("rwkv", l), ("s5", l)], {"g1", "q"}, {"yda", "pay2"}, False, state)
            gather(state, "pay2", "g2")
            seq = [("o", l), ("ffn2", l)]
            if l + 1 < L_DEPTH:
                seq += [("ffn1", l + 1), ("p", l + 1)]
                run(seq, {"yda", "g2"}, {"pay", "q"}, True, state)
            else:
                run(seq, {"yda", "g2"}, set(), True, state)
    out = np.empty((2, SEQ, D), np.float32)
    for c in range(NCORES):
        b, j = c // 4, c % 4
        t = state["x"][c].reshape(D, NT).T.reshape(8, 128, D)
        out[b].reshape(8, 4, 128, D)[:, j] = t
    return out
```

```python
import math
from contextlib import ExitStack
import numpy as np
import ml_dtypes
import concourse.bass as bass
import concourse.mybir as mybir
from concourse.bass_utils import run_bass_kernel_spmd

F32 = mybir.dt.float32
BF16 = mybir.dt.bfloat16
AF = mybir.ActivationFunctionType
ALU = mybir.AluOpType
AX = mybir.AxisListType

D = 2048
KC = 16
NT = 1024
DFF = 5504
NF = 43
SEQ = 4096
NCORES = 8
EPS = 1e-6


class Buf:
    __slots__ = ("w", "r")

    def __init__(self):
        self.w = None
        self.r = {}


class Prog:
    def __init__(self, nc, es, n_dma_sems=24):
        self.nc = nc
        self.es = es
        self.eng = {"pe": nc.tensor, "act": nc.scalar, "dve": nc.vector, "pool": nc.gpsimd, "sp": nc.sync}
        self.sem = {e: es.enter_context(nc.semaphore("s_" + e)) for e in self.eng}
        self.sem["cc"] = es.enter_context(nc.semaphore("s_cc"))
        self.cc_cnt = 0
        self.cnt = {e: 0 for e in self.eng}
        self.dsem = [es.enter_context(nc.semaphore("d%d" % i)) for i in range(n_dma_sems)]
        self.dval = [0] * n_dma_sems
        self.dpool = {"sp": list(range(0, 12)), "pool": list(range(12, n_dma_sems - 2)),
                      "cc": list(range(n_dma_sems - 2, n_dma_sems))}
        self.dnext = {"sp": 0, "pool": 0, "cc": 0}
        self.waited = {e: {} for e in self.eng}
        self.nins = 0

    def _wait(self, e, tok):
        key, val = tok
        if val <= 0:
            return
        if self.waited[e].get(key, 0) >= val:
            return
        sem = self.sem[key] if isinstance(key, str) else self.dsem[key]
        self.eng[e].wait_ge(sem, val)
        self.waited[e][key] = val

    @staticmethod
    def _toks(w):
        if w is None:
            return ()
        if isinstance(w, list):
            return w
        return (w,)

    def _sync(self, e, reads, writes):
        for b in reads:
            for t in self._toks(b.w):
                self._wait(e, t)
        for b in writes:
            for t in self._toks(b.w):
                if t[0] != e or e != "pe":
                    self._wait(e, t)
            for k, v in b.r.items():
                if k != e or e != "pe":
                    self._wait(e, (k, v))

    def dma_multi(self, q, pairs, reads, writes):
        self._sync(q, reads, writes)
        toks = []
        pool = self.dpool[q]
        for (out, in_) in pairs:
            i = pool[self.dnext[q] % len(pool)]
            self.dnext[q] += 1
            self._wait(q, (i, self.dval[i]))
            ins = self.eng[q].dma_start(out=out, in_=in_)
            self.dval[i] += 16
            self.nins += 1
            ins.then_inc(self.dsem[i], 16)
            toks.append((i, self.dval[i]))
            for b in reads:
                b.r[i] = self.dval[i]
        for b in writes:
            b.w = list(toks)
            b.r = {}

    def op(self, e, reads, writes, fn):
        self._sync(e, reads, writes)
        ins = fn(self.eng[e])
        self.cnt[e] += 1
        self.nins += 1
        ins.then_inc(self.sem[e], 1)
        v = self.cnt[e]
        for b in reads:
            b.r[e] = v
        for b in writes:
            b.w = (e, v)
            b.r = {}

    def dma(self, q, out, in_, reads, writes):
        self._sync(q, reads, writes)
        pool = self.dpool[q]
        i = pool[self.dnext[q] % len(pool)]
        self.dnext[q] += 1
        self._wait(q, (i, self.dval[i]))
        ins = self.eng[q].dma_start(out=out, in_=in_)
        self.dval[i] += 16
        self.nins += 1
        ins.then_inc(self.dsem[i], 16)
        v = self.dval[i]
        for b in reads:
            b.r[i] = v
        for b in writes:
            b.w = (i, v)
            b.r = {}

    def barrier(self):
        for e in self.eng:
            for k in self.eng:
                if k != e:
                    self._wait(e, (k, self.cnt[k]))
            for i in range(len(self.dsem)):
                self._wait(e, (i, self.dval[i]))

    def final_wait(self, bufs):
        for b in bufs:
            for t in self._toks(b.w):
                self._wait("sp", t)


_UNIQ = [0]


def sb(nc, es, name, shape, dt):
    _UNIQ[0] += 1
    return es.enter_context(nc.sbuf_tensor("%s_%d" % (name, _UNIQ[0]), list(shape), dt))


def ps(nc, es, name, shape, dt=F32):
    _UNIQ[0] += 1
    return es.enter_context(nc.psum_tensor("%s_%d" % (name, _UNIQ[0]), list(shape), dt))


def emit_rstd(pg, consts, src, src_bufs, nchunks, ncols, rstd, rstd_buf, sq, sq_bufs, pss, pss_bufs, dim):
    ones = consts["ones_f"]
    nh = ncols // 512
    for c in range(nchunks):
        k = c % 2
        pg.op("act", [src_bufs[c]], [sq_bufs[k]],
              lambda e, c=c, k=k: e.activation(out=sq[:, k, :], in_=src[:, c, :], func=AF.Square))
        for h in range(nh):
            pg.op("pe", [sq_bufs[k], consts["buf"]], [pss_bufs[h]],
                  lambda e, c=c, k=k, h=h: e.matmul(pss[:, h * 512:(h + 1) * 512], ones[:, :],
                                                   sq[:, k, h * 512:(h + 1) * 512],
                                                   start=(c == 0), stop=(c == nchunks - 1)))
    for h in range(nh):
        pg.op("act", [pss_bufs[h]], [rstd_buf],
              lambda e, h=h: e.activation(out=rstd[:, h * 512:(h + 1) * 512], in_=pss[:, h * 512:(h + 1) * 512],
                                          func=AF.Sqrt, scale=1.0 / dim, bias=consts["eps"][:, 0:1]))
    pg.op("dve", [rstd_buf], [rstd_buf],
          lambda e: e.reciprocal(out=rstd[:, :], in_=rstd[:, :]))


def phase_prenorm(pg, nc, consts, x_d, x_dbuf, g_sb, xnT, xn_bufs):
    with ExitStack() as es:
        xT = sb(nc, es, "pn_xT", [128, KC, NT], F32)
        sq = sb(nc, es, "pn_sq", [128, 2, NT], F32)
        rstd = sb(nc, es, "pn_rstd", [128, NT], F32)
        pss = ps(nc, es, "pn_pss", [128, NT])
        xb = [Buf() for _ in range(KC)]
        sqb = [Buf(), Buf()]
        pssb = [Buf(), Buf()]
        rb = Buf()
        for c in range(KC):
            pg.dma("sp", xT[:, c, :], x_d[c], [x_dbuf], [xb[c]])
        emit_rstd(pg, consts, xT, xb, KC, NT, rstd, rb, sq, sqb, pss, pssb, D)
        for c in range(KC):
            pg.op("dve", [xb[c], rb, consts["buf"]], [xn_bufs[c]],
                  lambda e, c=c: e.scalar_tensor_tensor(out=xnT[:, c, :], in0=xT[:, c, :], scalar=g_sb[:, c:c + 1],
                                                        in1=rstd[:, :], op0=ALU.mult, op1=ALU.mult))
        pg.barrier()


def phase_ffn(pg, nc, consts, x_d, x_dbuf, wgu_d, wd_d, gpre_sb, gpost_sb):
    with ExitStack() as es0:
        hT = sb(nc, es0, "f_hT", [128, NF, NT], BF16)
        hb = [[Buf(), Buf()] for _ in range(NF)]
        with ExitStack() as es1:
            xnT = sb(nc, es1, "f_xnT", [128, KC, NT], BF16)
            xnb = [Buf() for _ in range(KC)]
            phase_prenorm(pg, nc, consts, x_d, x_dbuf, gpre_sb, xnT, xnb)
            with ExitStack() as es2:
                NW = 3
                wgu = sb(nc, es2, "f_wgu", [128, NW, 2 * KC * 128], BF16)
                wb = [Buf() for _ in range(NW)]
                sg = sb(nc, es2, "f_sg", [128, 2, 512], BF16)
                sgb = [Buf(), Buf()]
                psg = [ps(nc, es2, "f_psg%d" % i, [128, 512]) for i in range(2)]
                psu = [ps(nc, es2, "f_psu%d" % i, [128, 512]) for i in range(2)]
                psgb = [Buf(), Buf()]
                psub = [Buf(), Buf()]

                def load_w(f):
                    s = f % NW
                    pg.dma("pool", wgu[:, s, :], wgu_d[f], [], [wb[s]])

                for f in range(min(NW - 1, NF)):
                    load_w(f)
                it = 0
                for f in range(NF):
                    if f + NW - 1 < NF:
                        load_w(f + NW - 1)
                    s = f % NW
                    for half in range(2):
                        k = it % 2
                        it += 1
                        tsl = slice(half * 512, (half + 1) * 512)
                        for c in range(KC):
                            pg.op("pe", [wb[s], xnb[c]], [psgb[k]],
                                  lambda e, c=c, s=s, k=k, tsl=tsl: e.matmul(
                                      psg[k][:, :], wgu[:, s, c * 128:(c + 1) * 128], xnT[:, c, tsl],
                                      start=(c == 0), stop=(c == KC - 1)))
                        for c in range(KC):
                            pg.op("pe", [wb[s], xnb[c]], [psub[k]],
                                  lambda e, c=c, s=s, k=k, tsl=tsl: e.matmul(
                                      psu[k][:, :], wgu[:, s, (KC + c) * 128:(KC + c + 1) * 128], xnT[:, c, tsl],
                                      start=(c == 0), stop=(c == KC - 1)))
                        pg.op("act", [psgb[k]], [sgb[k]],
                              lambda e, k=k: e.activation(out=sg[:, k, :], in_=psg[k][:, :], func=AF.Silu))
                        pg.op("dve", [sgb[k], psub[k]], [hb[f][half]],
                              lambda e, k=k, f=f, tsl=tsl: e.tensor_tensor(out=hT[:, f, tsl], in0=sg[:, k, :],
                                                                          in1=psu[k][:, :], op=ALU.mult))
                pg.barrier()
        with ExitStack() as es3:
            oT = sb(nc, es3, "f_oT", [128, KC, NT], F32)
            ob = [Buf() for _ in range(KC)]
            with ExitStack() as es4:
                NWD = 2
                wd = sb(nc, es4, "f_wd", [128, NWD, NF * 128], BF16)
                wdb = [Buf() for _ in range(NWD)]
                pso = [ps(nc, es4, "f_pso%d" % i, [128, 512]) for i in range(4)]
                psob = [Buf() for _ in range(4)]

                def load_wd(c):
                    s = c % NWD
                    pg.dma("pool", wd[:, s, :], wd_d[c], [], [wdb[s]])

                load_wd(0)
                it = 0
                for c in range(KC):
                    if c + 1 < KC:
                        load_wd(c + 1)
                    s = c % NWD
                    for half in range(2):
                        k = it % 4
                        it += 1
                        tsl = slice(half * 512, (half + 1) * 512)
                        for f in range(NF):
                            pg.op("pe", [wdb[s], hb[f][half]], [psob[k]],
                                  lambda e, f=f, s=s, k=k, tsl=tsl: e.matmul(
                                      pso[k][:, :], wd[:, s, f * 128:(f + 1) * 128], hT[:, f, tsl],
                                      start=(f == 0), stop=(f == NF - 1)))
                        pg.op("act", [psob[k]], [ob[c]],
                              lambda e, k=k, c=c, tsl=tsl: e.activation(out=oT[:, c, tsl], in_=pso[k][:, :],
                                                                       func=AF.Copy))
                pg.barrier()
            emit_postnorm_residual(pg, nc, consts, oT, ob, x_d, x_dbuf, gpost_sb)


def emit_postnorm_residual(pg, nc, consts, oT, ob, x_d, x_dbuf, gpost_sb):
    with ExitStack() as es5:
        sq = sb(nc, es5, "f_sq", [128, 2, NT], F32)
        rstd = sb(nc, es5, "f_rstd", [128, NT], F32)
        xc = sb(nc, es5, "f_xc", [128, 3, NT], F32)
        pss = ps(nc, es5, "f_pss", [128, NT])
        sqb = [Buf(), Buf()]
        pssb = [Buf(), Buf()]
        rb = Buf()
        xcb = [Buf() for _ in range(3)]
        emit_rstd(pg, consts, oT, ob, KC, NT, rstd, rb, sq, sqb, pss, pssb, D)
        newbuf = Buf()
        for c in range(KC):
            k = c % 3
            pg.dma("sp", xc[:, k, :], x_d[c], [x_dbuf], [xcb[k]])
            pg.op("dve", [ob[c], rb, consts["buf"]], [ob[c]],
                  lambda e, c=c: e.scalar_tensor_tensor(out=oT[:, c, :], in0=oT[:, c, :],
                                                        scalar=gpost_sb[:, c:c + 1], in1=rstd[:, :],
                                                        op0=ALU.mult, op1=ALU.mult))
            pg.op("pool", [ob[c], xcb[k]], [xcb[k]],
                  lambda e, c=c, k=k: e.tensor_tensor(out=xc[:, k, :], in0=xc[:, k, :], in1=oT[:, c, :],
                                                     op=ALU.add))
            pg.dma("sp", x_d[c], xc[:, k, :], [xcb[k]], [newbuf])
        pg.barrier()
        x_dbuf.w = newbuf.w
        x_dbuf.r = {}


PAY_CHUNK_ROWS = [512, 512, 512, 512, 512, 256, 480, 288]
HD = 128


def pay_xn_rows(pay, c):
    return pay[c // 4][(c % 4) * 128:(c % 4 + 1) * 128, :]


def pay_k_rows(pay, h):
    return pay[4 + h // 4][(h % 4) * 128:(h % 4 + 1) * 128, :]


def _vflat(rows_ap):
    return rows_ap.rearrange("a b -> (a b)").rearrange("(t e) -> t e", e=768)


def pay_v_block(pay, mm):
    k, loc = (6, mm) if mm < 5 else (7, mm - 5)
    return _vflat(pay[k][loc * 96:(loc + 1) * 96, :])


def g1_view(g1):
    return [g.rearrange("(r a) t -> r a t", r=4) for g in g1]


def g1_v_block(g1v, r, mm):
    k, loc = (6, mm) if mm < 5 else (7, mm - 5)
    return _vflat(g1v[k][r, loc * 96:(loc + 1) * 96, :])


def phase_p(pg, nc, consts, x_d, x_dbuf, g_sb, wqk_d, wv_d, pay_d, pay_buf, q_d, q_buf, pay_xn_buf=None, after_xn=None):
    if pay_xn_buf is None:
        pay_xn_buf = pay_buf
    with ExitStack() as es0:
        xnT = sb(nc, es0, "p_xnT", [128, KC, NT], BF16)
        xnb = [Buf() for _ in range(KC)]
        phase_prenorm(pg, nc, consts, x_d, x_dbuf, g_sb, xnT, xnb)
        for c in range(KC):
            pg.dma("sp", pay_xn_rows(pay_d, c), xnT[:, c, :], [xnb[c]], [pay_xn_buf])
        if after_xn is not None:
            after_xn()
        with ExitStack() as es:
            wv = sb(nc, es, "p_wv", [128, KC * 768], BF16)
            wvb = Buf()
            pg.dma("pool", wv[:, :], wv_d, [], [wvb])
            NW = 3
            wq = sb(nc, es, "p_wq", [128, NW, KC * 128], BF16)
            wqb = [Buf() for _ in range(NW)]
            ot = sb(nc, es, "p_ot", [128, 4, 512], BF16)
            otb = [Buf() for _ in range(4)]
            vt = sb(nc, es, "p_vt", [128, 2, 768], BF16)
            vtb = [Buf(), Buf()]
            pp = [ps(nc, es, "p_pp%d" % i, [128, 512]) for i in range(4)]
            ppb = [Buf() for _ in range(4)]

            def load_w(i):
                pg.dma("pool", wq[:, i % NW, :], wqk_d[i], [], [wqb[i % NW]])

            load_w(0)
            load_w(1)
            it = 0
            for i in range(12):
                if i + 2 < 12:
                    load_w(i + 2)
                s = i % NW
                for half in range(2):
                    k = it % 4
                    it += 1
                    tsl = slice(half * 512, (half + 1) * 512)
                    for c in range(KC):
                        pg.op("pe", [wqb[s], xnb[c]], [ppb[k]],
                              lambda e, c=c, s=s, k=k, tsl=tsl: e.matmul(pp[k][:, :], wq[:, s, c * 128:(c + 1) * 128],
                                                                        xnT[:, c, tsl], start=(c == 0), stop=(c == KC - 1)))
                    pg.op("act", [ppb[k]], [otb[k]],
                          lambda e, k=k: e.activation(out=ot[:, k, :], in_=pp[k][:, :], func=AF.Copy))
                    if i < 6:
                        pg.dma("sp", q_d[i][:, tsl], ot[:, k, :], [otb[k]], [q_buf])
                    else:
                        pg.dma("sp", pay_k_rows(pay_d, i - 6)[:, tsl], ot[:, k, :], [otb[k]], [pay_buf])
            for tb in range(8):
                vk = tb % 2
                for (c0, cn) in ((0, 512), (512, 256)):
                    k = it % 4
                    it += 1
                    for c in range(KC):
                        pg.op("pe", [wvb, xnb[c]], [ppb[k]],
                              lambda e, c=c, k=k, tb=tb, c0=c0, cn=cn: e.matmul(
                                  pp[k][:, 0:cn], xnT[:, c, tb * 128:(tb + 1) * 128],
                                  wv[:, c * 768 + c0:c * 768 + c0 + cn], start=(c == 0), stop=(c == KC - 1)))
                    pg.op("act", [ppb[k]], [vtb[vk]],
                          lambda e, k=k, vk=vk, c0=c0, cn=cn: e.activation(out=vt[:, vk, c0:c0 + cn], in_=pp[k][:, 0:cn],
                                                                           func=AF.Copy))
                pg.dma("sp", pay_v_block(pay_d, tb), vt[:, vk, :], [vtb[vk]], [pay_buf])
            pg.barrier()


def phase_da(pg, nc, consts, g1_d, g1_buf, q_d, q_buf, mask_d, lamp_d, subw_d, lam_init, yda_d, yda_buf):
    with ExitStack() as es:
        KT = sb(nc, es, "da_KT", [128, 6, 4, NT], BF16)
        Vt = sb(nc, es, "da_V", [128, 4, 8, 6, 129], BF16)
        QT = sb(nc, es, "da_QT", [128, 6, NT], BF16)
        mask = sb(nc, es, "da_mask", [128, 4, 128], BF16)
        lamp = sb(nc, es, "da_lamp", [128, 4, 64], F32)
        subw = sb(nc, es, "da_subw", [128, 128], F32)
        lam = sb(nc, es, "da_lam", [128, 4], F32)
        ydaT = sb(nc, es, "da_ydaT", [128, 6, NT], BF16)
        PT = sb(nc, es, "da_PT", [128, 2, 2, 4, 128], BF16)
        fin = sb(nc, es, "da_fin", [128, 8, 128], F32)
        sm = sb(nc, es, "da_sm", [128, 16], F32)
        ybf = sb(nc, es, "da_ybf", [128, 2, 128], BF16)
        pS = [ps(nc, es, "da_pS%d" % i, [128, 2, 4, 128]) for i in range(2)]
        pO = [ps(nc, es, "da_pO", [128, 2, 512])]
        pT = ps(nc, es, "da_pT", [128, 2, 128], BF16)
        ktb = [Buf() for _ in range(6)]
        vb = Buf()
        qb = Buf()
        mb = Buf()
        lb = Buf()
        ptb = [Buf(), Buf()]
        psb = [Buf(), Buf()]
        pob = [Buf(), Buf()]
        _pt = Buf()
        ptrb = [_pt, _pt]
        ybb = [Buf(), Buf()]
        finb = Buf()
        ydb = [Buf() for _ in range(6)]
        ident = consts["ident_b"]
        for h in range(6):
            pg.dma("sp", KT[:, h, :, :], g1_d[4 + h // 4][:, (h % 4) * 128:(h % 4 + 1) * 128, :].rearrange("r p t -> p r t"),
                   [g1_buf], [ktb[h]])
        pg.op("pool", [], [vb], lambda e: e.memset(Vt[:, :, :, :, 128:129], 1.0))
        pg.dma_multi("sp", [(Vt[:, r, mm, :, 0:128], g1_v_block(g1_d, r, mm).rearrange("t (h e) -> t h e", h=6))
                            for r in range(4) for mm in range(8)], [g1_buf], [vb])
        pg.dma("sp", QT[:, :, :], q_d.rearrange("h p t -> p h t"), [q_buf], [qb])
        pg.dma("pool", mask[:, :, :], mask_d, [], [mb])
        pg.dma("sp", lamp[:, :, :], lamp_d, [], [lb])
        pg.dma("sp", subw[:, :], subw_d, [], [lb])
        pg.op("dve", [lb], [lb], lambda e: e.tensor_tensor(out=lamp[:, 0, :], in0=lamp[:, 0, :], in1=lamp[:, 1, :], op=ALU.mult))
        pg.op("dve", [lb], [lb], lambda e: e.tensor_tensor(out=lamp[:, 2, :], in0=lamp[:, 2, :], in1=lamp[:, 3, :], op=ALU.mult))
        pg.op("dve", [lb], [lb], lambda e: e.tensor_reduce(out=lam[:, 0:1], in_=lamp[:, 0, :], axis=AX.X, op=ALU.add))
        pg.op("dve", [lb], [lb], lambda e: e.tensor_reduce(out=lam[:, 1:2], in_=lamp[:, 2, :], axis=AX.X, op=ALU.add))
        pg.op("act", [lb], [lb], lambda e: e.activation(out=lam[:, 0:2], in_=lam[:, 0:2], func=AF.Exp))
        pg.op("dve", [lb], [lb], lambda e: e.tensor_tensor(out=lam[:, 2:3], in0=lam[:, 0:1], in1=lam[:, 1:2], op=ALU.subtract))
        pg.op("dve", [lb], [lb], lambda e: e.tensor_scalar(out=lam[:, 2:3], in0=lam[:, 2:3], scalar1=float(lam_init), scalar2=None, op0=ALU.add))
        pg.op("dve", [lb], [lb], lambda e: e.tensor_scalar(out=subw[:, :], in0=subw[:, :], scalar1=float(1.0 - lam_init), scalar2=None, op0=ALU.mult))
        groups = [(h, m, mp) for h in range(6) for m in range(8) for mp in range(m + 1)]

        def emit_S(idx):
            h, m, mp = groups[idx]
            k = idx % 2
            for r in range(4):
                for c in range(2):
                    pg.op("pe", [ktb[h], qb], [psb[k]],
                          lambda e, h=h, r=r, c=c, k=k, mp=mp, m=m: e.matmul(
                              pS[k][:, c, r, :], KT[c * 64:(c + 1) * 64, h, r, mp * 128:(mp + 1) * 128],
                              QT[c * 64:(c + 1) * 64, h, m * 128:(m + 1) * 128], start=True, stop=True))

        emit_S(0)
        oit = 0
        for idx, (h, m, mp) in enumerate(groups):
            k = idx % 2
            ok = 0
            ngroups = m + 1
            if idx + 1 < len(groups):
                emit_S(idx + 1)
            for half in range(2):
                pg.op("act", [psb[k]], [ptb[k]],
                      lambda e, k=k, half=half: e.activation(out=PT[:, k, half, :, :], in_=pS[k][:, half, :, :],
                                                             func=AF.Exp, scale=0.125))
            if mp == m:
                pg.op("dve", [ptb[k], mb], [ptb[k]],
                      lambda e, k=k: e.tensor_tensor(out=PT[:, k, :, :, :], in0=PT[:, k, :, :, :],
                                                     in1=mask[:, :, :].unsqueeze(1).broadcast_to([128, 2, 4, 128]),
                                                     op=ALU.mult))
            for r in range(4):
                for c in range(2):
                    pg.op("pe", [ptb[k], vb], [pob[ok]],
                          lambda e, h=h, r=r, c=c, k=k, mp=mp, ok=ok, ng=ngroups: e.matmul(
                              pO[ok][:, c, 0:129], PT[:, k, c, r, :], Vt[:, r, mp, h, :],
                              start=(mp == 0 and r == 0), stop=(mp == ng - 1 and r == 3)))
            if mp != m:
                continue
            oit += 1
            O = pO[ok]
            pg.op("dve", [pob[ok]], [finb], lambda e, O=O: e.reciprocal(out=sm[:, 0:2], in_=O[:, :, 128]))
            pg.op("dve", [finb, lb], [finb],
                  lambda e: e.tensor_tensor(out=sm[:, 2:3], in0=sm[:, 1:2], in1=lam[:, 2:3], op=ALU.mult))
            pg.op("dve", [pob[ok], finb], [finb],
                  lambda e, O=O: e.tensor_scalar(out=fin[:, 0, :], in0=O[:, 1, 0:128], scalar1=sm[:, 2:3], scalar2=None,
                                                 op0=ALU.mult))
            pg.op("dve", [pob[ok], finb], [finb],
                  lambda e, O=O: e.scalar_tensor_tensor(out=fin[:, 1, :], in0=O[:, 0, 0:128], scalar=sm[:, 0:1],
                                                        in1=fin[:, 0, :], op0=ALU.mult, op1=ALU.subtract))
            pg.op("dve", [finb], [finb],
                  lambda e: e.tensor_tensor(out=fin[:, 2, :], in0=fin[:, 1, :], in1=fin[:, 1, :], op=ALU.mult))
            pg.op("dve", [finb], [finb],
                  lambda e: e.tensor_reduce(out=sm[:, 4:5], in_=fin[:, 2, :], axis=AX.X, op=ALU.add))
            pg.op("act", [finb, consts["buf"]], [finb],
                  lambda e: e.activation(out=sm[:, 5:6], in_=sm[:, 4:5], func=AF.Sqrt, scale=1.0 / 128,
                                         bias=consts["eps5"][:, 0:1]))
            pg.op("dve", [finb], [finb], lambda e: e.reciprocal(out=sm[:, 6:7], in_=sm[:, 5:6]))
            yk = oit % 2
            pg.op("dve", [finb, lb], [ybb[yk]],
                  lambda e, yk=yk: e.scalar_tensor_tensor(out=ybf[:, yk, :], in0=fin[:, 1, :], scalar=sm[:, 6:7],
                                                          in1=subw[:, :], op0=ALU.mult, op1=ALU.mult))
            pg.op("pe", [ybb[yk], consts["buf"]], [ptrb[yk]],
                  lambda e, yk=yk: e.transpose(pT[:, yk, :], ybf[:, yk, :], ident[:, :]))
            pg.op("act", [ptrb[yk]], [ydb[h]],
                  lambda e, yk=yk, h=h, m=m: e.activation(out=ydaT[:, h, m * 128:(m + 1) * 128], in_=pT[:, yk, :],
                                                         func=AF.Copy))
            if m == 7:
                pg.dma("sp", yda_d[h], ydaT[:, h, :], [ydb[h]], [yda_buf])
        pg.barrier()


def setup_consts(pg, nc, es, ident_d):
    cb = Buf()
    ones_f = sb(nc, es, "c_ones_f", [128, 128], F32)
    ones_b = sb(nc, es, "c_ones_b", [128, 128], BF16)
    ident_f = sb(nc, es, "c_ident_f", [128, 128], F32)
    ident_b = sb(nc, es, "c_ident_b", [128, 128], BF16)
    eps = sb(nc, es, "c_eps", [128, 4], F32)
    pg.op("dve", [], [cb], lambda e: e.memset(ones_f[:, :], 1.0))
    pg.op("dve", [], [cb], lambda e: e.memset(ones_b[:, :], 1.0))
    pg.op("dve", [], [cb], lambda e: e.memset(eps[:, 0:1], EPS))
    pg.op("dve", [], [cb], lambda e: e.memset(eps[:, 1:2], 1e-5))
    pg.op("dve", [], [cb], lambda e: e.memset(eps[:, 2:3], 64e-5))
    pg.op("dve", [], [cb], lambda e: e.memset(eps[:, 3:4], 0.0))
    pg.dma("sp", ident_f[:, :], ident_d, [], [cb])
    pg.dma("pool", ident_b[:, :], ident_d, [], [cb])
    return {"ones_f": ones_f, "ones_b": ones_b, "ident_f": ident_f, "ident_b": ident_b, "buf": cb,
            "eps": eps[:, 0:1], "eps5": eps[:, 1:2], "epsgn": eps[:, 2:3], "zero": eps[:, 3:4]}


I32 = mybir.dt.int32
TT = 512


def load_xn_tile(pg, g1_d, g1_buf, xt, xtb, m, rs=(0, 1, 2, 3)):
    pairs = []
    for i, r in enumerate(rs):
        for k4 in range(4):
            pairs.append((xt[:, k4 * 4:(k4 + 1) * 4, i, :],
                          g1_d[k4][r, :, m * 128:(m + 1) * 128].rearrange("(kc p) t -> p kc t", p=128)))
    pg.dma_multi("sp", pairs, [g1_buf], [xtb])


def emit_sin(pg, out, x, tmpf, tmpi, bufs):
    pg.op("dve", bufs, bufs, lambda e: e.tensor_scalar(out=tmpf, in0=x, scalar1=1.0 / (2 * math.pi), scalar2=None, op0=ALU.mult))
    pg.op("dve", bufs, bufs, lambda e: e.tensor_copy(out=tmpi, in_=tmpf))
    pg.op("dve", bufs, bufs, lambda e: e.tensor_copy(out=tmpf, in_=tmpi))
    pg.op("dve", bufs, bufs, lambda e: e.scalar_tensor_tensor(out=tmpf, in0=tmpf, scalar=-2 * math.pi, in1=x, op0=ALU.mult, op1=ALU.add))
    pg.op("dve", bufs, bufs, lambda e: e.tensor_scalar(out=tmpf, in0=tmpf, scalar1=math.pi, scalar2=-math.pi, op0=ALU.min, op1=ALU.max))
    pg.op("act", bufs, bufs, lambda e: e.activation(out=out, in_=tmpf, func=AF.Sin))


def phase_s5(pg, nc, consts, g1_d, g1_buf, wss_d, s5p_d, s5b_d, s5c_d, s5d_d, ramp_d, pay2_d, pay2_buf):
    TC = TT
    with ExitStack() as es:
        wss = sb(nc, es, "s5_w", [128, KC * 128], BF16)
        prm = sb(nc, es, "s5_prm", [128, 4, 3], F32)
        bp = sb(nc, es, "s5_bp", [128, 4, 2, 16], F32)
        cp = sb(nc, es, "s5_cp", [128, 4, 2, 16], F32)
        dsk = sb(nc, es, "s5_d", [128, 1], F32)
        ramp = sb(nc, es, "s5_ramp", [128, TC + 1], F32)
        sm = sb(nc, es, "s5_sm", [128, 4, 16], F32)
        bbar = sb(nc, es, "s5_bbar", [128, 4, 2, 16], F32)
        tmp16 = sb(nc, es, "s5_t16", [128, 4, 2, 16], F32)
        pad = sb(nc, es, "s5_pad", [128, 4, 4, 128], F32)
        BT = sb(nc, es, "s5_BT", [128, 4, 2, 128], BF16)
        padb = sb(nc, es, "s5_padb", [128, 4, 2, 128], BF16)
        CT = sb(nc, es, "s5_CT", [128, 4, 2, 128], BF16)
        cosT = sb(nc, es, "s5_cos", [128, 4, TC + 1], F32)
        sinT = sb(nc, es, "s5_sin", [128, 4, TC + 1], F32)
        targ = sb(nc, es, "s5_targ", [128, 4, TC + 1], F32)
        ttmp = sb(nc, es, "s5_ttmp", [128, 4, TC + 1], F32)
        tint = sb(nc, es, "s5_tint", [128, 4, TC + 1], I32)
        init = sb(nc, es, "s5_init", [128, 4, 2], F32)
        itmp = sb(nc, es, "s5_itmp", [128, 4, 4], F32)
        xt = sb(nc, es, "s5_xt", [128, 2, KC, 4, 128], BF16)
        uf = sb(nc, es, "s5_uf", [128, TC], F32)
        ub = sb(nc, es, "s5_ub", [128, TC], BF16)
        w1a = sb(nc, es, "s5_w1", [128, 2, 6, TC], F32)
        xs = sb(nc, es, "s5_xs", [128, 4, 2, TC], BF16)
        yo = sb(nc, es, "s5_yo", [128, 3, TC], F32)
        yb16 = sb(nc, es, "s5_yb", [128, 2, TC], BF16)
        pu = ps(nc, es, "s5_pu", [128, TC])
        pbu = [ps(nc, es, "s5_pbu%d" % i, [128, 2, TC]) for i in range(2)]
        py = ps(nc, es, "s5_py", [128, TC])
        ptr = ps(nc, es, "s5_ptr", [128, 4, 128], BF16)
        pb = Buf()
        pg.dma("pool", wss[:, :], wss_d, [], [pb])
        pg.dma("sp", prm[:, :, :], s5p_d, [], [pb])
        pg.dma("sp", bp[:, :, :, :], s5b_d, [], [pb])
        pg.dma("sp", cp[:, :, :, :], s5c_d, [], [pb])
        pg.dma("sp", dsk[:, :], s5d_d, [], [pb])
        pg.dma("sp", ramp[:, :], ramp_d, [], [pb])
        P = [pb]
        ar, ai, ldt = prm[:, :, 0], prm[:, :, 1], prm[:, :, 2]
        dt, mag, ang = sm[:, :, 0], sm[:, :, 1], sm[:, :, 2]
        cosa, sina = sm[:, :, 3], sm[:, :, 4]
        nr, ni, den, cr, ci = sm[:, :, 5], sm[:, :, 6], sm[:, :, 7], sm[:, :, 8], sm[:, :, 9]
        t0, t1, angc = sm[:, :, 10], sm[:, :, 11], sm[:, :, 12]

        def V(fn):
            pg.op("dve", P, P, fn)

        pg.op("act", P, P, lambda e: e.activation(out=dt, in_=ldt, func=AF.Exp))
        V(lambda e: e.tensor_tensor(out=t0, in0=dt, in1=ar, op=ALU.mult))
        pg.op("act", P, P, lambda e: e.activation(out=mag, in_=t0, func=AF.Exp))
        V(lambda e: e.tensor_tensor(out=ang, in0=dt, in1=ai, op=ALU.mult))
        emit_sin(pg, sina, ang, t0, tint[:, :, 0], P)
        V(lambda e: e.tensor_scalar(out=angc, in0=ang, scalar1=math.pi / 2, scalar2=None, op0=ALU.add))
        emit_sin(pg, cosa, angc, t0, tint[:, :, 0], P)
        V(lambda e: e.tensor_tensor(out=nr, in0=mag, in1=cosa, op=ALU.mult))
        V(lambda e: e.tensor_scalar(out=nr, in0=nr, scalar1=-1.0, scalar2=None, op0=ALU.add))
        V(lambda e: e.tensor_tensor(out=ni, in0=mag, in1=sina, op=ALU.mult))
        V(lambda e: e.tensor_tensor(out=den, in0=ar, in1=ar, op=ALU.mult))
        V(lambda e: e.tensor_tensor(out=t0, in0=ai, in1=ai, op=ALU.mult))
        V(lambda e: e.tensor_tensor(out=den, in0=den, in1=t0, op=ALU.add))
        V(lambda e: e.reciprocal(out=den, in_=den))
        V(lambda e: e.tensor_tensor(out=cr, in0=nr, in1=ar, op=ALU.mult))
        V(lambda e: e.tensor_tensor(out=t0, in0=ni, in1=ai, op=ALU.mult))
        V(lambda e: e.tensor_tensor(out=cr, in0=cr, in1=t0, op=ALU.add))
        V(lambda e: e.tensor_tensor(out=cr, in0=cr, in1=den, op=ALU.mult))
        V(lambda e: e.tensor_tensor(out=ci, in0=ni, in1=ar, op=ALU.mult))
        V(lambda e: e.tensor_tensor(out=t0, in0=nr, in1=ai, op=ALU.mult))
        V(lambda e: e.tensor_tensor(out=ci, in0=ci, in1=t0, op=ALU.subtract))
        V(lambda e: e.tensor_tensor(out=ci, in0=ci, in1=den, op=ALU.mult))
        s5stage = globals().get("S5_STAGE", "full")
        if s5stage == "A":
            pg.barrier()
            return
        crb = cr.unsqueeze(2).broadcast_to([128, 4, 16])
        cib = ci.unsqueeze(2).broadcast_to([128, 4, 16])
        V(lambda e: e.tensor_tensor(out=bbar[:, :, 0, :], in0=bp[:, :, 0, :], in1=crb, op=ALU.mult))
        V(lambda e: e.tensor_tensor(out=tmp16[:, :, 0, :], in0=bp[:, :, 1, :], in1=cib, op=ALU.mult))
        V(lambda e: e.tensor_tensor(out=bbar[:, :, 0, :], in0=bbar[:, :, 0, :], in1=tmp16[:, :, 0, :], op=ALU.subtract))
        V(lambda e: e.tensor_tensor(out=bbar[:, :, 1, :], in0=bp[:, :, 1, :], in1=crb, op=ALU.mult))
        V(lambda e: e.tensor_tensor(out=tmp16[:, :, 1, :], in0=bp[:, :, 0, :], in1=cib, op=ALU.mult))
        V(lambda e: e.tensor_tensor(out=bbar[:, :, 1, :], in0=bbar[:, :, 1, :], in1=tmp16[:, :, 1, :], op=ALU.add))
        V(lambda e: e.memset(pad[:, :, :, :], 0.0))
        for q in range(4):
            for half in range(2):
                psl = slice(half * 64, (half + 1) * 64)
                csl = slice((2 * q + half) * 16, (2 * q + half + 1) * 16)
                V(lambda e, q=q, psl=psl, csl=csl: e.tensor_copy(out=pad[psl, q, 0, csl], in_=bbar[psl, q, 0, :]))
                V(lambda e, q=q, psl=psl, csl=csl: e.tensor_copy(out=pad[psl, q, 1, csl], in_=bbar[psl, q, 1, :]))
                V(lambda e, q=q, psl=psl, csl=csl: e.tensor_copy(out=pad[psl, q, 2, csl], in_=cp[psl, q, 0, :]))
                V(lambda e, q=q, psl=psl, csl=csl: e.tensor_scalar(out=pad[psl, q, 3, csl], in0=cp[psl, q, 1, :], scalar1=-1.0,
                                                                   scalar2=None, op0=ALU.mult))
        V(lambda e: e.tensor_copy(out=CT[:, :, :, :], in_=pad[:, :, 2:4, :]))
        V(lambda e: e.tensor_copy(out=padb[:, :, :, :], in_=pad[:, :, 0:2, :]))
        trb = Buf()
        for q in range(4):
            for ri in range(2):
                pg.op("pe", P + [consts["buf"]], [trb],
                      lambda e, q=q, ri=ri: e.transpose(ptr[:, q, :], padb[:, q, ri, :], consts["ident_b"][:, :]))
                pg.op("act", [trb], P, lambda e, q=q, ri=ri: e.activation(out=BT[:, q, ri, :], in_=ptr[:, q, :], func=AF.Copy))
        if s5stage == "B":
            pg.barrier()
            return
        for q in range(4):
            V(lambda e, q=q: e.tensor_scalar(out=targ[:, q, :], in0=ramp[:, :], scalar1=ang[:, q:q + 1], scalar2=None, op0=ALU.mult))
        emit_sin(pg, sinT[:, :, :], targ[:, :, :], ttmp[:, :, :], tint[:, :, :], P)
        V(lambda e: e.tensor_scalar(out=targ[:, :, :], in0=targ[:, :, :], scalar1=math.pi / 2, scalar2=None, op0=ALU.add))
        emit_sin(pg, cosT[:, :, :], targ[:, :, :], ttmp[:, :, :], tint[:, :, :], P)
        V(lambda e: e.memset(init[:, :, :], 0.0))
        if s5stage == "C":
            pg.barrier()
            return
        xtb = [Buf(), Buf()]
        ufb, ubb, pub = Buf(), Buf(), Buf()
        pbub = [Buf(), Buf()]
        w1ball = [[Buf() for _ in range(6)] for _ in range(2)]
        xsb = [Buf() for _ in range(4)]
        pyb = Buf()
        yob = Buf()
        ybb = [Buf(), Buf()]
        ib = Buf()
        ib.w = pb.w
        load_xn_tile(pg, g1_d, g1_buf, xt[:, 0], xtb[0], 0)
        nt = globals().get("S5_TILES", SEQ // TT)
        for m in range(nt):
            k = m % 2
            if m + 1 < nt:
                load_xn_tile(pg, g1_d, g1_buf, xt[:, (m + 1) % 2], xtb[(m + 1) % 2], m + 1)
            if s5stage == "D0":
                continue
            for c in range(KC):
                pg.op("pe", [xtb[k], pb], [pub],
                      lambda e, c=c, k=k: e.matmul(pu[:, :], wss[:, c * 128:(c + 1) * 128],
                                                   xt[:, k, c, :, :].rearrange("p r t -> p (r t)"),
                                                   start=(c == 0), stop=(c == KC - 1)))
            pg.op("act", [pub], [ufb], lambda e: e.activation(out=uf[:, :], in_=pu[:, :], func=AF.Copy))
            pg.op("dve", [ufb], [ubb], lambda e: e.tensor_copy(out=ub[:, :], in_=uf[:, :]))
            if s5stage == "D":
                continue
            for q in range(4):
                kb = q % 2
                w1 = w1a[:, kb]
                w1b = w1ball[kb]
                for ri in range(2):
                    pg.op("pe", [ubb, pb], [pbub[kb]],
                          lambda e, q=q, ri=ri, kb=kb: e.matmul(pbu[kb][:, ri, :], BT[:, q, ri, :], ub[:, :], start=True, stop=True))
                cs, sn = cosT[:, q, 0:TC], sinT[:, q, 0:TC]
                bur, bui = pbu[kb][:, 0, :], pbu[kb][:, 1, :]
                pg.op("dve", [pbub[kb], pb], [w1b[0]], lambda e, cs=cs, bur=bur: e.tensor_tensor(out=w1[:, 0, :], in0=bur, in1=cs, op=ALU.mult))
                pg.op("dve", [pbub[kb], pb], [w1b[1]], lambda e, sn=sn, bui=bui: e.tensor_tensor(out=w1[:, 1, :], in0=bui, in1=sn, op=ALU.mult))
                pg.op("pool", [w1b[0], w1b[1]], [w1b[0]], lambda e: e.tensor_tensor(out=w1[:, 0, :], in0=w1[:, 0, :], in1=w1[:, 1, :], op=ALU.add))
                pg.op("dve", [pbub[kb], pb], [w1b[2]], lambda e, cs=cs, bui=bui: e.tensor_tensor(out=w1[:, 2, :], in0=bui, in1=cs, op=ALU.mult))
                pg.op("dve", [pbub[kb], pb], [w1b[1]], lambda e, sn=sn, bur=bur: e.tensor_tensor(out=w1[:, 1, :], in0=bur, in1=sn, op=ALU.mult))
                pg.op("pool", [w1b[2], w1b[1]], [w1b[2]], lambda e: e.tensor_tensor(out=w1[:, 2, :], in0=w1[:, 2, :], in1=w1[:, 1, :], op=ALU.subtract))
                rho = mag[:, q:q + 1].to_broadcast([128, TC])
                pg.op("dve", [w1b[0], ib, pb], [w1b[3]],
                      lambda e, q=q, rho=rho: e.tensor_tensor_scan(out=w1[:, 3, :], data0=rho, data1=w1[:, 0, :],
                                                                   initial=init[:, q, 0:1], op0=ALU.mult, op1=ALU.add))
                pg.op("dve", [w1b[2], ib, pb], [w1b[4]],
                      lambda e, q=q, rho=rho: e.tensor_tensor_scan(out=w1[:, 4, :], data0=rho, data1=w1[:, 2, :],
                                                                   initial=init[:, q, 1:2], op0=ALU.mult, op1=ALU.add))
                wl_r, wl_i = w1[:, 3, TC - 1:TC], w1[:, 4, TC - 1:TC]
                cT, sT = cosT[:, q, TC:TC + 1], sinT[:, q, TC:TC + 1]
                pg.op("dve", [w1b[3], w1b[4], pb], [ib], lambda e, q=q, wl_r=wl_r, cT=cT: e.tensor_tensor(out=itmp[:, q, 0:1], in0=wl_r, in1=cT, op=ALU.mult))
                pg.op("dve", [w1b[3], w1b[4], pb], [ib], lambda e, q=q, wl_i=wl_i, sT=sT: e.tensor_tensor(out=itmp[:, q, 1:2], in0=wl_i, in1=sT, op=ALU.mult))
                pg.op("dve", [w1b[3], w1b[4], pb], [ib], lambda e, q=q, wl_r=wl_r, sT=sT: e.tensor_tensor(out=itmp[:, q, 2:3], in0=wl_r, in1=sT, op=ALU.mult))
                pg.op("dve", [w1b[3], w1b[4], pb], [ib], lambda e, q=q, wl_i=wl_i, cT=cT: e.tensor_tensor(out=itmp[:, q, 3:4], in0=wl_i, in1=cT, op=ALU.mult))
                pg.op("dve", [ib], [ib], lambda e, q=q: e.tensor_tensor(out=init[:, q, 0:1], in0=itmp[:, q, 0:1], in1=itmp[:, q, 1:2], op=ALU.subtract))
                pg.op("dve", [ib], [ib], lambda e, q=q: e.tensor_tensor(out=init[:, q, 1:2], in0=itmp[:, q, 2:3], in1=itmp[:, q, 3:4], op=ALU.add))
                pg.op("pool", [w1b[3], pb], [w1b[0]], lambda e, cs=cs: e.tensor_tensor(out=w1[:, 0, :], in0=w1[:, 3, :], in1=cs, op=ALU.mult))
                pg.op("pool", [w1b[4], pb], [w1b[1]], lambda e, sn=sn: e.tensor_tensor(out=w1[:, 1, :], in0=w1[:, 4, :], in1=sn, op=ALU.mult))
                pg.op("dve", [w1b[0], w1b[1]], [xsb[q]], lambda e, q=q: e.tensor_tensor(out=xs[:, q, 0, :], in0=w1[:, 0, :], in1=w1[:, 1, :], op=ALU.subtract))
                pg.op("pool", [w1b[3], pb], [w1b[2]], lambda e, sn=sn: e.tensor_tensor(out=w1[:, 2, :], in0=w1[:, 3, :], in1=sn, op=ALU.mult))
                pg.op("pool", [w1b[4], pb], [w1b[5]], lambda e, cs=cs: e.tensor_tensor(out=w1[:, 5, :], in0=w1[:, 4, :], in1=cs, op=ALU.mult))
                pg.op("dve", [w1b[2], w1b[5]], [xsb[q]], lambda e, q=q: e.tensor_tensor(out=xs[:, q, 1, :], in0=w1[:, 2, :], in1=w1[:, 5, :], op=ALU.add))
            for q in range(4):
                for ri in range(2):
                    pg.op("pe", [xsb[q], pb], [pyb],
                          lambda e, q=q, ri=ri: e.matmul(py[:, :], CT[:, q, ri, :], xs[:, q, ri, :],
                                                         start=(q == 0 and ri == 0), stop=(q == 3 and ri == 1)))
            pg.op("dve", [pyb, ufb, pb], [yob], lambda e: e.scalar_tensor_tensor(out=yo[:, 0, :], in0=uf[:, :], scalar=dsk[:, 0:1],
                                                                                in1=py[:, :], op0=ALU.mult, op1=ALU.add))
            pg.op("dve", [yob], [yob], lambda e: e.tensor_tensor(out=yo[:, 1, :], in0=yo[:, 0, :], in1=yo[:, 0, :], op=ALU.mult))
            pg.op("dve", [yob], [yob], lambda e: e.tensor_scalar(out=yo[:, 1, :], in0=yo[:, 1, :], scalar1=0.044715, scalar2=1.0,
                                                                 op0=ALU.mult, op1=ALU.add))
            pg.op("dve", [yob], [yob], lambda e: e.tensor_tensor(out=yo[:, 1, :], in0=yo[:, 1, :], in1=yo[:, 0, :], op=ALU.mult))
            pg.op("act", [yob], [yob], lambda e: e.activation(out=yo[:, 2, :], in_=yo[:, 1, :], func=AF.Sigmoid,
                                                              scale=2.0 * math.sqrt(2.0 / math.pi)))
            pg.op("dve", [yob], [ybb[k]], lambda e, k=k: e.tensor_tensor(out=yb16[:, k, :], in0=yo[:, 0, :], in1=yo[:, 2, :], op=ALU.mult))
            pg.dma("sp", pay2_d[2][:, m * TT:(m + 1) * TT], yb16[:, k, :], [ybb[k]], [pay2_buf])
        pg.barrier()


RT = 256
NCH = RT // 64
NEG_EH = -math.exp(-0.5)


def phase_rwkv(pg, nc, consts, g1_d, g1_buf, wrw_d, rwp_d, rwpg_d, rwl_d, rwgn_d, rwm_d, pay2_d, pay2_buf):
    with ExitStack() as es:
        wrw = sb(nc, es, "rw_w", [128, KC, 832], BF16)
        xt = sb(nc, es, "rw_xt", [128, 2, KC, 2, 128], BF16)
        prm = sb(nc, es, "rw_prm", [64, 32], F32)
        mug = sb(nc, es, "rw_mug", [128, 1], F32)
        lw = sb(nc, es, "rw_lw", [128, 3, 192], BF16)
        gn = sb(nc, es, "rw_gn", [64, 2, 3, 64], F32)
        msk = sb(nc, es, "rw_msk", [64, 3, 64], F32)
        rmask = sb(nc, es, "rw_rmask", [64, RT], F32)
        idb = sb(nc, es, "rw_idb", [64, 64], BF16)
        Z = sb(nc, es, "rw_Z", [64, 11, RT + 1], F32)
        ZG = sb(nc, es, "rw_ZG", [128, RT + 1], F32)
        ZS = sb(nc, es, "rw_ZS", [64, 11, RT], F32)
        E = sb(nc, es, "rw_E", [64, 12, RT], F32)
        zgs = sb(nc, es, "rw_zgs", [128, 2, RT], F32)
        tw = sb(nc, es, "rw_tw", [64, 2, RT], BF16)
        sgb = sb(nc, es, "rw_sgb", [128, RT], BF16)
        SIG = sb(nc, es, "rw_SIG", [64, 3, RT], F32)
        AL = sb(nc, es, "rw_AL", [64, 3, RT], F32)
        KKN = sb(nc, es, "rw_KKN", [64, 3, RT], F32)
        KP = sb(nc, es, "rw_KP", [64, 3, RT], F32)
        L = sb(nc, es, "rw_L", [64, 3, RT], F32)
        T1 = sb(nc, es, "rw_T1", [64, 3, RT], F32)
        T2 = sb(nc, es, "rw_T2", [128, 3, RT], F32)
        onesblk = sb(nc, es, "rw_onesblk", [128, 128], F32)
        PCt = sb(nc, es, "rw_PC", [64, 3, NCH], F32)
        arT = sb(nc, es, "rw_arT", [64, 3, NCH, 2, 64], BF16)
        bkT = sb(nc, es, "rw_bkT", [64, 3, NCH, 2, 64], BF16)
        FM = sb(nc, es, "rw_FM", [64, 3, 5, RT], BF16)
        TOK = sb(nc, es, "rw_TOK", [64, NCH, 3, 5, 64], BF16)
        SCm = sb(nc, es, "rw_SCm", [64, 3, NCH, 2, 2, 64], BF16)
        NLt = sb(nc, es, "rw_NL", [64, 3, NCH, 64], BF16)
        MJ = sb(nc, es, "rw_MJ", [64, 2, 3, NCH, 64], BF16)
        NJ = sb(nc, es, "rw_NJ", [64, 2, 3, NCH, 64], BF16)
        Tt = sb(nc, es, "rw_Tt", [64, 2, 3, NCH, 64], BF16)
        AkVb = sb(nc, es, "rw_AkVb", [64, 3, NCH, 64], BF16)
        UVs = sb(nc, es, "rw_UVs", [64, NCH, 3, 64], F32)
        ApT = sb(nc, es, "rw_ApT", [64, 3, NCH, 64], BF16)
        S = sb(nc, es, "rw_S", [64, 3, 64], F32)
        Sb_ = sb(nc, es, "rw_Sb", [64, 3, 64], BF16)
        Ubf = sb(nc, es, "rw_Ubf", [64, 3, 64], BF16)
        Yt = sb(nc, es, "rw_Yt", [64, NCH, 3, 64], F32)
        F1 = sb(nc, es, "rw_F1", [64, NCH, 3, 64], F32)
        F2 = sb(nc, es, "rw_F2", [64, NCH, 3, 64], F32)
        st = sb(nc, es, "rw_st", [64, 4, NCH, 3], F32)
        rk = sb(nc, es, "rw_rk", [64, NCH, 3], F32)
        yfb = sb(nc, es, "rw_yfb", [64, NCH, 192], BF16)
        yoA = sb(nc, es, "rw_yoA", [128, 2, RT], BF16)
        yoB = sb(nc, es, "rw_yoB", [64, 2, RT], BF16)
        B01 = ps(nc, es, "rw_b01", [128, 1024])
        B23 = ps(nc, es, "rw_b23", [128, 1024])
        B4 = ps(nc, es, "rw_b4", [128, 512])
        B56 = ps(nc, es, "rw_b56", [128, 1024])
        bankT = ps(nc, es, "rw_bT", [128, 1024], BF16)
        pb = Buf()
        P = [pb]
        pg.dma("pool", wrw[:, :, :], wrw_d, [], P)
        pg.dma("sp", prm[:, 0:26], rwp_d, [], P)
        pg.dma("sp", mug[:, :], rwpg_d, [], P)
        pg.dma("pool", lw[:, :, :], rwl_d, [], P)
        pg.dma("sp", gn[:, :, :, :], rwgn_d, [], P)
        pg.dma("sp", msk[:, :, :], rwm_d, [], P)
        pg.op("dve", P + [consts["buf"]], P, lambda e: e.tensor_copy(out=idb[:, :], in_=consts["ident_b"][0:64, 0:64]))
        pg.op("dve", P, P, lambda e: e.memset(rmask[:, :], 1.0))
        pg.op("dve", P, P, lambda e: e.memset(rmask[:, :].rearrange("p (c t) -> p c t", t=64)[:, :, 0:1], 0.0))
        pg.op("dve", P, P, lambda e: e.tensor_scalar(out=prm[:, 29:32], in0=prm[:, 20:23], scalar1=-1.0, scalar2=1.0,
                                                     op0=ALU.mult, op1=ALU.add))
        pg.op("dve", P, P, lambda e: e.memset(T2[:, :, :], 0.0))
        pg.op("dve", P, P, lambda e: e.memset(onesblk[:, :], 0.0))
        pg.op("dve", P, P, lambda e: e.memset(onesblk[0:64, 0:64], 1.0))
        pg.op("dve", P, P, lambda e: e.memset(S[:, :, :], 0.0))
        pg.op("dve", P, P, lambda e: e.memset(Sb_[:, :, :], 0.0))
        pg.op("dve", P, P, lambda e: e.memset(Z[:, :, 0:1], 0.0))
        pg.op("dve", P, P, lambda e: e.memset(ZG[:, 0:1], 0.0))
        MU, W0, A0, KK_, KA, RK, OMKA = 0, 11, 14, 17, 20, 23, 29
        xtb = [Buf(), Buf()]
        zb, zgb, zsb, eb = Buf(), Buf(), Buf(), Buf()
        bb = [Buf() for _ in range(8)]
        pjb = [bb[5], bb[6]]
        plb = [bb[2], bb[3], bb[4]]
        b_tw, b_sg, b_sig, b_al, b_kkn, b_kp, b_L, b_t1, b_t2, b_pc = (Buf() for _ in range(10))
        b_ar, b_bk, b_fm, b_tok, b_ptr = Buf(), Buf(), Buf(), Buf(), bb[7]
        b_scp = [bb[5], bb[6]]
        b_np, b_scm, b_nl = bb[4], Buf(), Buf()
        b_mj, b_nj, b_tt = [Buf(), Buf()], [Buf(), Buf()], [Buf(), Buf()]
        b_pm, b_pn, b_pp = [bb[0], bb[1]], [bb[2], bb[3]], [bb[5], bb[6]]
        b_akv, b_uvs, b_apt = Buf(), Buf(), Buf()
        b_S, b_Sb, b_ubf, b_yt = Buf(), Buf(), Buf(), Buf()
        b_pu, b_py, b_pd, b_pg, b_prk = bb[0], bb[1], bb[2], bb[3], bb[4]
        b_f1, b_f2, b_st, b_rk, b_yfb = Buf(), Buf(), Buf(), Buf(), Buf()
        b_yo = [Buf(), Buf()]
        zgsb = Buf()
        pj = [B56[:, 0:256], B56[:, 512:768]]
        PL = [B23[0:64, 0:256], B23[0:64, 512:768], B4[0:64, 0:256]]
        PLF = [B23[:, 0:256], B23[:, 512:768], B4[:, 0:256]]
        p_u = B01[0:64, 0:192].rearrange("p (h v) -> p h v", h=3)
        p_y = B01[0:64, 512:704].rearrange("p (h v) -> p h v", h=3)
        p_d = B23[0:64, 0:192].rearrange("p (h v) -> p h v", h=3)
        p_g = B23[0:64, 768:960]
        p_sc = [B56[0:64, 0:512].rearrange("p (c x) -> p c x", x=256), B56[0:64, 512:1024].rearrange("p (c x) -> p c x", x=256)]
        p_n = B4[0:64, 0:256].rearrange("p (c s) -> p c s", s=64)
        HC = 3 * NCH * 64
        p_m3 = B01[0:64, 0:HC].rearrange("p (h c s) -> p h c s", h=3, s=64)
        p_nn3 = B23[0:64, 0:HC].rearrange("p (h c s) -> p h c s", h=3, s=64)
        p_pp3 = B56[0:64, 0:HC].rearrange("p (h c s) -> p h c s", h=3, s=64)
        p_tok = bankT[0:64, 0:960].rearrange("p (h k f) -> p h k f", h=3, k=5)
        p_tA = bankT[:, 0:NCH * 64].rearrange("p (c t) -> p c t", t=64)
        p_tB = bankT[0:64, 512:512 + NCH * 64].rearrange("p (c t) -> p c t", t=64)

        def bc(ap, shape, axis):
            return ap.unsqueeze(axis).broadcast_to(shape)

        ntile = globals().get("RW_TILES", SEQ // RT)

        def load(i):
            load_xn_tile(pg, g1_d, g1_buf, xt[:, i % 2], xtb[i % 2], i // 2, rs=(2 * (i % 2), 2 * (i % 2) + 1))

        def emit_carry():
            pg.op("dve", [zb], [zb], lambda e: e.tensor_copy(out=Z[:, :, 0:1], in_=Z[:, :, RT:RT + 1]))
            pg.op("dve", [zgb], [zgb], lambda e: e.tensor_copy(out=ZG[:, 0:1], in_=ZG[:, RT:RT + 1]))

        def emit_proj(ti, gs):
            kx = ti % 2
            xr = lambda c: xt[:, kx, c, :, :].rearrange("p r t -> p (r t)")
            for g in gs:
                kk = g % 2
                M = 64 if g < 11 else 128
                c0 = g * 64
                for c in range(KC):
                    pg.op("pe", [xtb[kx], pb], [pjb[kk]],
                          lambda e, c=c, kk=kk, M=M, c0=c0, xr=xr: e.matmul(pj[kk][0:M, :], wrw[:, c, c0:c0 + M], xr(c),
                                                                           start=(c == 0), stop=(c == KC - 1)))
                if g < 11:
                    pg.op("act", [pjb[kk]], [zb], lambda e, g=g, kk=kk: e.activation(out=Z[:, g, 1:RT + 1], in_=pj[kk][0:64, :], func=AF.Copy))
                else:
                    pg.op("act", [pjb[kk]], [zgb], lambda e, kk=kk: e.activation(out=ZG[:, 1:RT + 1], in_=pj[kk][:, :], func=AF.Copy))

        load(0)
        for i in range(ntile):
            k = i % 2
            if i + 1 < ntile:
                load(i + 1)
            if i == 0:
                emit_proj(0, range(12))
            Dt = E[:, 0:11, :]
            pg.op("dve", [zb], [eb], lambda e: e.tensor_tensor(out=Dt, in0=Z[:, :, 0:RT], in1=Z[:, :, 1:RT + 1], op=ALU.subtract))
            pg.op("dve", [eb, pb], [eb], lambda e: e.tensor_tensor(out=Dt, in0=Dt, in1=bc(prm[:, MU:MU + 11], [64, 11, RT], 2), op=ALU.mult))
            pg.op("dve", [eb, zb], [zsb], lambda e: e.tensor_tensor(out=ZS[:, :, :], in0=Dt, in1=Z[:, :, 1:RT + 1], op=ALU.add))
            pg.op("pool", [zgb], [zgsb], lambda e: e.tensor_tensor(out=zgs[:, 0, :], in0=ZG[:, 0:RT], in1=ZG[:, 1:RT + 1], op=ALU.subtract))
            pg.op("dve", [zgsb, zgb, pb], [zgsb], lambda e: e.scalar_tensor_tensor(out=zgs[:, 1, :], in0=zgs[:, 0, :], scalar=mug[:, 0:1],
                                                                                  in1=ZG[:, 1:RT + 1], op0=ALU.mult, op1=ALU.add))
            R, Kx, Vx = ZS[:, 0:3, :], ZS[:, 3:6, :], ZS[:, 6:9, :]
            pg.op("act", [zsb], [b_tw], lambda e: e.activation(out=tw[:, 0, :], in_=ZS[:, 9, :], func=AF.Tanh))
            pg.op("act", [zsb], [b_tw], lambda e: e.activation(out=tw[:, 1, :], in_=ZS[:, 10, :], func=AF.Copy))
            pg.op("act", [zgsb], [b_sg], lambda e: e.activation(out=sgb[:, :], in_=zgs[:, 1, :], func=AF.Sigmoid))
            for h in range(3):
                pg.op("pe", [b_tw, pb], [plb[h]], lambda e, h=h: e.matmul(PL[h], lw[0:64, 0, h * 64:(h + 1) * 64], tw[:, 0, :], start=True, stop=True))
                pg.op("act", [plb[h], pb], [b_sig], lambda e, h=h: e.activation(out=SIG[:, h, :], in_=PL[h], func=AF.Sigmoid, bias=prm[:, W0 + h:W0 + h + 1]))
            for h in range(3):
                pg.op("pe", [b_tw, pb], [plb[h]], lambda e, h=h: e.matmul(PL[h], lw[0:64, 1, h * 64:(h + 1) * 64], tw[:, 1, :], start=True, stop=True))
                pg.op("act", [plb[h], pb], [b_al], lambda e, h=h: e.activation(out=AL[:, h, :], in_=PL[h], func=AF.Sigmoid, bias=prm[:, A0 + h:A0 + h + 1]))
            pg.op("dve", [b_sig], [b_sig], lambda e: e.tensor_scalar(out=SIG[:, :, :], in0=SIG[:, :, :], scalar1=NEG_EH, scalar2=None, op0=ALU.mult))
            pg.op("dve", [zsb, pb], [b_t1], lambda e: e.tensor_tensor(out=T1[:, :, :], in0=Kx, in1=bc(prm[:, KK_:KK_ + 3], [64, 3, RT], 2), op=ALU.mult))
            pg.op("pool", [b_t1], [b_t2], lambda e: e.tensor_tensor(out=T2[0:64, :, :], in0=T1[:, :, :], in1=T1[:, :, :], op=ALU.mult))
            for h in range(3):
                pg.op("pe", [b_t2, pb], [plb[h]], lambda e, h=h: e.matmul(PLF[h], onesblk[:, :], T2[:, h, :], start=True, stop=True))
                pg.op("act", [plb[h]], [b_kkn], lambda e, h=h: e.activation(out=KKN[:, h, :], in_=PL[h], func=AF.Sqrt))
            pg.op("dve", [b_kkn], [b_kkn], lambda e: e.tensor_scalar(out=KKN[:, :, :], in0=KKN[:, :, :], scalar1=1e-12, scalar2=None, op0=ALU.max))
            pg.op("dve", [b_kkn], [b_kkn], lambda e: e.reciprocal(out=KKN[:, :, :], in_=KKN[:, :, :]))
            pg.op("dve", [b_kkn, b_t1], [b_kkn], lambda e: e.tensor_tensor(out=KKN[:, :, :], in0=KKN[:, :, :], in1=T1[:, :, :], op=ALU.mult))
            pg.op("dve", [b_al, pb], [b_kp], lambda e: e.tensor_tensor(out=KP[:, :, :], in0=AL[:, :, :], in1=bc(prm[:, KA:KA + 3], [64, 3, RT], 2), op=ALU.mult))
            pg.op("dve", [b_kp, pb], [b_kp], lambda e: e.tensor_tensor(out=KP[:, :, :], in0=KP[:, :, :], in1=bc(prm[:, OMKA:OMKA + 3], [64, 3, RT], 2), op=ALU.add))
            pg.op("dve", [b_kp, zsb], [b_kp], lambda e: e.tensor_tensor(out=KP[:, :, :], in0=KP[:, :, :], in1=Kx, op=ALU.mult))
            pg.op("pool", [zsb, b_kp, b_t2], [b_t2], lambda e: e.tensor_tensor(out=T2[0:64, :, :], in0=R, in1=KP[:, :, :], op=ALU.mult))
            pg.op("pool", [b_t2, pb], [b_fm], lambda e: e.tensor_tensor(out=FM[:, :, 4, :], in0=T2[0:64, :, :], in1=bc(prm[:, RK:RK + 3], [64, 3, RT], 2), op=ALU.mult))
            pg.op("dve", [b_kkn, b_al, b_t1], [b_t1], lambda e: e.tensor_tensor(out=T1[:, :, :], in0=KKN[:, :, :], in1=AL[:, :, :], op=ALU.mult))
            for h in range(3):
                pg.op("dve", [b_sig, pb], [b_L], lambda e, h=h: e.tensor_tensor_scan(out=L[:, h, :], data0=rmask[:, :], data1=SIG[:, h, :], initial=0.0,
                                                                                      op0=ALU.mult, op1=ALU.add))
            Pin, Pex, Pinv, PCs = E[:, 0:3, :], E[:, 3:6, :], E[:, 6:9, :], E[:, 9:12, :]
            Lc = L[:, :, :].rearrange("p h (c t) -> p h c t", t=64)
            pg.op("act", [b_L], [eb], lambda e: e.activation(out=Pin, in_=L[:, :, :], func=AF.Exp))
            pg.op("act", [b_L], [eb], lambda e: e.activation(out=Pinv, in_=L[:, :, :], func=AF.Exp, scale=-1.0))
            pg.op("dve", [b_L, b_sig, eb], [eb], lambda e: e.tensor_tensor(out=Pex, in0=L[:, :, :], in1=SIG[:, :, :], op=ALU.subtract))
            pg.op("act", [eb], [eb], lambda e: e.activation(out=Pex, in_=Pex, func=AF.Exp))
            pg.op("dve", [b_L, eb], [eb], lambda e: e.tensor_tensor(out=PCs.rearrange("p h (c t) -> p h c t", t=64),
                                                                   in0=Lc[:, :, :, 63:64].broadcast_to([64, 3, NCH, 64]), in1=Lc, op=ALU.subtract))
            pg.op("act", [eb], [eb], lambda e: e.activation(out=PCs, in_=PCs, func=AF.Exp))
            pg.op("act", [b_L], [b_pc], lambda e: e.activation(out=PCt[:, :, :], in_=Lc[:, :, :, 63], func=AF.Exp))
            arv = arT[:, :, :, :, :]
            pg.op("dve", [b_kkn, eb], [b_ar], lambda e: e.scalar_tensor_tensor(out=arT[:, :, :, 0, :], in0=KKN[:, :, :].rearrange("p h (c t) -> p h c t", t=64), scalar=-1.0,
                                                                             in1=Pex.rearrange("p h (c t) -> p h c t", t=64), op0=ALU.mult, op1=ALU.mult))
            pg.op("dve", [zsb, eb], [b_ar], lambda e: e.tensor_tensor(out=arT[:, :, :, 1, :], in0=R.rearrange("p h (c t) -> p h c t", t=64),
                                                                    in1=Pin.rearrange("p h (c t) -> p h c t", t=64), op=ALU.mult))
            pg.op("dve", [b_t1, eb], [b_bk], lambda e: e.tensor_tensor(out=bkT[:, :, :, 0, :], in0=T1[:, :, :].rearrange("p h (c t) -> p h c t", t=64),
                                                                     in1=Pinv.rearrange("p h (c t) -> p h c t", t=64), op=ALU.mult))
            pg.op("dve", [b_kp, eb], [b_bk], lambda e: e.tensor_tensor(out=bkT[:, :, :, 1, :], in0=KP[:, :, :].rearrange("p h (c t) -> p h c t", t=64),
                                                                     in1=Pinv.rearrange("p h (c t) -> p h c t", t=64), op=ALU.mult))
            pg.op("pool", [b_t1, eb], [b_fm], lambda e: e.tensor_tensor(out=FM[:, :, 0, :], in0=T1[:, :, :], in1=PCs, op=ALU.mult))
            pg.op("pool", [b_kp, eb], [b_fm], lambda e: e.tensor_tensor(out=FM[:, :, 1, :], in0=KP[:, :, :], in1=PCs, op=ALU.mult))
            pg.op("pool", [b_ar], [b_fm], lambda e: e.tensor_copy(out=FM[:, :, 2, :].rearrange("p h (c t) -> p h c t", t=64), in_=arT[:, :, :, 0, :]))
            pg.op("pool", [zsb], [b_fm], lambda e: e.tensor_copy(out=FM[:, :, 3, :], in_=Vx))
            for c in range(NCH):
                for h in range(3):
                    for kd in range(5):
                        pg.op("pe", [b_fm, pb], [b_ptr], lambda e, c=c, h=h, kd=kd: e.transpose(p_tok[:, h, kd, :], FM[:, h, kd, c * 64:(c + 1) * 64], idb[:, :]))
                pg.op("act", [b_ptr], [b_tok], lambda e, c=c: e.activation(out=TOK[:, c, :, :, :], in_=p_tok, func=AF.Copy))
            pg.op("dve", [b_tok], [b_rk], lambda e: e.tensor_reduce(out=rk[:, :, :], in_=TOK[:, :, :, 4, :], axis=AX.X, op=ALU.add))
            for h in range(3):
                for half in range(NCH // 2):
                    for cc in range(2):
                        c = half * 2 + cc
                        for kd in range(2):
                            pg.op("pe", [b_bk, b_ar], [b_scp[half]],
                                  lambda e, h=h, c=c, cc=cc, kd=kd, half=half: e.matmul(p_sc[half][:, cc, kd * 128:(kd + 1) * 128], bkT[:, h, c, kd, :],
                                                                                        arT[:, h, c, :, :].rearrange("p a t -> p (a t)"), start=True, stop=True))
                    pg.op("dve", [b_scp[half], pb], [b_scm],
                          lambda e, h=h, half=half: e.tensor_tensor(out=SCm[:, h, half * 2:half * 2 + 2, :, :, :].rearrange("p c k a t -> p (c k) a t"),
                                                                    in0=p_sc[half].rearrange("p c (k a t) -> p (c k) a t", k=2, a=2),
                                                                    in1=msk[:, 0:2, :].unsqueeze(1).broadcast_to([64, 4, 2, 64]), op=ALU.mult))
                for c in range(NCH):
                    pg.op("pe", [b_bk, b_ar], [b_np], lambda e, h=h, c=c: e.matmul(p_n[:, c, :], arT[:, h, c, 0, :], bkT[:, h, c, 0, :], start=True, stop=True))
                pg.op("dve", [b_np, pb], [b_nl], lambda e, h=h: e.tensor_tensor(out=NLt[:, h, :, :], in0=p_n, in1=bc(msk[:, 2, :], [64, NCH, 64], 1), op=ALU.mult))
            hcs = [(h, c) for h in range(3) for c in range(NCH)]
            Mc = lambda h, c: SCm[:, h, c, 0, 0, :]
            Nc = lambda h, c: NLt[:, h, c, :]
            pg.op("dve", [b_scm, pb], [b_tt[0]],
                  lambda e: e.tensor_tensor(out=Tt[:, 0, :, :, :], in0=SCm[:, :, :, 0, 0, :], in1=idb[:, :].unsqueeze(1).unsqueeze(1).broadcast_to([64, 3, NCH, 64]), op=ALU.add))
            tcur = 0
            mrd, nrd = [b_scm], [b_nl]
            for lev in range(5):
                j = lev % 2
                last = (lev == 4)
                for (h, c) in hcs:
                    pg.op("pe", mrd + nrd, b_pn, lambda e, h=h, c=c, Mc=Mc, Nc=Nc: e.matmul(p_nn3[:, h, c, :], Mc(h, c), Nc(h, c), start=True, stop=True))
                if not last:
                    for (h, c) in hcs:
                        pg.op("pe", mrd + nrd, b_pm, lambda e, h=h, c=c, Mc=Mc, Nc=Nc: e.matmul(p_m3[:, h, c, :], Nc(h, c), Mc(h, c), start=True, stop=True))
                pg.op("act", b_pn, [b_nj[j]], lambda e, j=j: e.activation(out=NJ[:, j, :, :, :], in_=p_nn3, func=AF.Copy))
                if not last:
                    pg.op("dve", b_pm, [b_mj[j]], lambda e, j=j: e.tensor_copy(out=MJ[:, j, :, :, :], in_=p_m3))
                Mc = lambda h, c, j=j: MJ[:, j, h, c, :]
                Nc = lambda h, c, j=j: NJ[:, j, h, c, :]
                mrd, nrd = [b_mj[j]], [b_nj[j]]
                for (h, c) in hcs:
                    pg.op("pe", [b_nj[j], b_tt[tcur]], b_pp, lambda e, h=h, c=c, j=j, tcur=tcur: e.matmul(p_pp3[:, h, c, :], NJ[:, j, h, c, :], Tt[:, tcur, h, c, :], start=True, stop=True))
                pg.op("dve", b_pp + [b_tt[tcur]], [b_tt[1 - tcur]], lambda e, tcur=tcur: e.tensor_tensor(out=Tt[:, 1 - tcur, :, :, :], in0=p_pp3, in1=Tt[:, tcur, :, :, :], op=ALU.add))
                tcur = 1 - tcur
            for (h, c) in hcs:
                pg.op("pe", [b_scm, b_tok], b_pm, lambda e, c=c, h=h: e.matmul(p_m3[:, h, c, :], SCm[:, h, c, 1, 0, :], TOK[:, c, h, 3, :], start=True, stop=True))
            pg.op("act", b_pm, [b_akv], lambda e: e.activation(out=AkVb[:, :, :, :], in_=p_m3, func=AF.Copy))
            for (h, c) in hcs:
                pg.op("pe", [b_tt[tcur], b_akv], b_pn, lambda e, c=c, h=h, tcur=tcur: e.matmul(p_nn3[:, h, c, :], Tt[:, tcur, h, c, :], AkVb[:, h, c, :], start=True, stop=True))
            pg.op("act", b_pn, [b_uvs], lambda e: e.activation(out=UVs[:, :, :, :].rearrange("p c h v -> p h c v"), in_=p_nn3, func=AF.Copy))
            for (h, c) in hcs:
                pg.op("pe", [b_tt[tcur], b_tok], b_pp, lambda e, c=c, h=h, tcur=tcur: e.matmul(p_pp3[:, h, c, :], TOK[:, c, h, 2, :], Tt[:, tcur, h, c, :], start=True, stop=True))
            pg.op("dve", b_pp, [b_apt], lambda e: e.tensor_copy(out=ApT[:, :, :, :], in_=p_pp3))
            if i + 1 < ntile:
                emit_carry()
            for c in range(NCH):
                for h in range(3):
                    pg.op("pe", [b_apt, b_Sb], [b_pu], lambda e, c=c, h=h: e.matmul(p_u[:, h, :], ApT[:, h, c, :], Sb_[:, h, :], start=True, stop=True))
                if i + 1 < ntile:
                    emit_proj(i + 1, range(c * (12 // NCH), (c + 1) * (12 // NCH)))
                pg.op("dve", [b_pu, b_uvs], [b_ubf], lambda e, c=c: e.tensor_tensor(out=Ubf[:, :, :], in0=p_u, in1=UVs[:, c, :, :], op=ALU.add))
                for h in range(3):
                    pg.op("pe", [b_ar, b_Sb], [b_py], lambda e, c=c, h=h: e.matmul(p_y[:, h, :], arT[:, h, c, 1, :], Sb_[:, h, :], start=True, stop=False))
                    pg.op("pe", [b_scm, b_tok], [b_py], lambda e, c=c, h=h: e.matmul(p_y[:, h, :], SCm[:, h, c, 1, 1, :], TOK[:, c, h, 3, :], start=False, stop=False))
                    pg.op("pe", [b_scm, b_ubf], [b_py], lambda e, c=c, h=h: e.matmul(p_y[:, h, :], SCm[:, h, c, 0, 1, :], Ubf[:, h, :], start=False, stop=True))
                for h in range(3):
                    pg.op("pe", [b_tok, b_ubf], [b_pd], lambda e, c=c, h=h: e.matmul(p_d[:, h, :], TOK[:, c, h, 0, :], Ubf[:, h, :], start=True, stop=False))
                    pg.op("pe", [b_tok], [b_pd], lambda e, c=c, h=h: e.matmul(p_d[:, h, :], TOK[:, c, h, 1, :], TOK[:, c, h, 3, :], start=False, stop=True))
                pg.op("act", [b_py], [b_yt], lambda e, c=c: e.activation(out=Yt[:, c, :, :], in_=p_y, func=AF.Copy))
                pg.op("dve", [b_S, b_pc], [b_S], lambda e, c=c: e.tensor_tensor(out=S[:, :, :], in0=S[:, :, :], in1=bc(PCt[:, :, c], [64, 3, 64], 2), op=ALU.mult))
                pg.op("dve", [b_S, b_pd], [b_S], lambda e: e.tensor_tensor(out=S[:, :, :], in0=S[:, :, :], in1=p_d, op=ALU.add))
                pg.op("act", [b_S], [b_Sb], lambda e: e.activation(out=Sb_[:, :, :], in_=S[:, :, :], func=AF.Copy))
            Y3 = Yt[:, :, :, :]
            sh = [64, NCH, 3, 64]
            pg.op("dve", [b_yt], [b_st], lambda e: e.tensor_reduce(out=st[:, 0, :, :], in_=Y3, axis=AX.X, op=ALU.add))
            pg.op("dve", [b_st], [b_st], lambda e: e.tensor_scalar(out=st[:, 0, :, :], in0=st[:, 0, :, :], scalar1=1.0 / 64, scalar2=None, op0=ALU.mult))
            pg.op("dve", [b_yt, b_st], [b_f1], lambda e: e.tensor_tensor(out=F1[:, :, :, :], in0=Y3, in1=bc(st[:, 0, :, :], sh, 3), op=ALU.subtract))
            pg.op("pool", [b_f1], [b_f2], lambda e: e.tensor_tensor(out=F2[:, :, :, :], in0=F1[:, :, :, :], in1=F1[:, :, :, :], op=ALU.mult))
            pg.op("dve", [b_f2], [b_st], lambda e: e.tensor_reduce(out=st[:, 1, :, :], in_=F2[:, :, :, :], axis=AX.X, op=ALU.add))
            pg.op("act", [b_st, consts["buf"]], [b_st], lambda e: e.activation(out=st[:, 2, :, :], in_=st[:, 1, :, :], func=AF.Sqrt, scale=1.0 / 64, bias=consts["epsgn"][0:64, :]))
            pg.op("dve", [b_st], [b_st], lambda e: e.reciprocal(out=st[:, 3, :, :], in_=st[:, 2, :, :]))
            pg.op("dve", [b_f1, b_st], [b_f1], lambda e: e.tensor_tensor(out=F1[:, :, :, :], in0=F1[:, :, :, :], in1=bc(st[:, 3, :, :], sh, 3), op=ALU.mult))
            pg.op("dve", [b_f1, pb], [b_f1], lambda e: e.tensor_tensor(out=F1[:, :, :, :], in0=F1[:, :, :, :], in1=bc(gn[:, 0, :, :], sh, 1), op=ALU.mult))
            pg.op("dve", [b_f1, pb], [b_f1], lambda e: e.tensor_tensor(out=F1[:, :, :, :], in0=F1[:, :, :, :], in1=bc(gn[:, 1, :, :], sh, 1), op=ALU.add))
            pg.op("pool", [b_tok, b_rk, b_f2], [b_f2], lambda e: e.tensor_tensor(out=F2[:, :, :, :], in0=TOK[:, :, :, 3, :], in1=bc(rk[:, :, :], sh, 3), op=ALU.mult))
            pg.op("dve", [b_f1, b_f2], [b_f1], lambda e: e.tensor_tensor(out=F1[:, :, :, :], in0=F1[:, :, :, :], in1=F2[:, :, :, :], op=ALU.add))
            for c in range(NCH):
                pg.op("pe", [b_sg, pb], [b_pg], lambda e, c=c: e.matmul(p_g[:, 0:192], sgb[:, c * 64:(c + 1) * 64], lw[:, 2, :], start=True, stop=True))
                pg.op("dve", [b_pg, b_f1], [b_yfb], lambda e, c=c: e.tensor_tensor(out=yfb[:, c, :], in0=F1[:, c, :, :].rearrange("p h v -> p (h v)"), in1=p_g[:, 0:192], op=ALU.mult))
            for c in range(NCH):
                pg.op("pe", [b_yfb, pb], [b_ptr], lambda e, c=c: e.transpose(p_tA[:, c, :], yfb[:, c, 0:128], idb[:, :]))
                pg.op("pe", [b_yfb, pb], [b_ptr], lambda e, c=c: e.transpose(p_tB[:, c, :], yfb[:, c, 128:192], idb[:, :]))
            pg.op("act", [b_ptr], [b_yo[k]], lambda e, k=k: e.activation(out=yoA[:, k, :].rearrange("p (c t) -> p c t", t=64), in_=p_tA, func=AF.Copy))
            pg.op("act", [b_ptr], [b_yo[k]], lambda e, k=k: e.activation(out=yoB[:, k, :].rearrange("p (c t) -> p c t", t=64), in_=p_tB, func=AF.Copy))
            pg.dma("sp", pay2_d[0][:, i * RT:(i + 1) * RT], yoA[:, k, :], [b_yo[k]], [pay2_buf])
            pg.dma("sp", pay2_d[1][:, i * RT:(i + 1) * RT], yoB[:, k, :], [b_yo[k]], [pay2_buf])
        pg.barrier()


NPIECE = 18


def phase_o(pg, nc, consts, x_d, x_dbuf, yda_d, yda_buf, g2_d, g2_buf, sel_d, wglu_d, bglu_d, wout_d, gpost_sb):
    with ExitStack() as es0:
        oT = sb(nc, es0, "o_oT", [128, KC, NT], F32)
        ob = [Buf() for _ in range(KC)]
        with ExitStack() as es:
            yT = sb(nc, es, "o_yT", [128, NPIECE, NT], BF16)
            yb = [Buf() for _ in range(NPIECE)]
            ygT = sb(nc, es, "o_ygT", [128, 4, NT], BF16)
            ygb = [Buf() for _ in range(4)]
            stg = sb(nc, es, "o_stg", [128, 2, 8, 4, 128], BF16)
            stb = [Buf(), Buf()]
            acc = sb(nc, es, "o_acc", [128, 2, NT], F32)
            accb = [Buf(), Buf()]
            sel = sb(nc, es, "o_sel", [128, 4], F32)
            wglu = sb(nc, es, "o_wglu", [128, 4, 512], BF16)
            bglu = sb(nc, es, "o_bglu", [128, 4], F32)
            gate = sb(nc, es, "o_gate", [128, 2, 512], F32)
            gtb = [Buf(), Buf()]
            NW = 2
            wt = sb(nc, es, "o_wt", [128, NW, NPIECE * 128], BF16)
            wtb = [Buf() for _ in range(NW)]
            pp = [ps(nc, es, "o_pp%d" % i, [128, 512]) for i in range(4)]
            ppb = [Buf() for _ in range(4)]
            pb = Buf()
            pg.dma("sp", sel[:, :], sel_d, [], [pb])
            pg.dma("pool", wglu[:, :, :], wglu_d, [], [pb])
            pg.dma("sp", bglu[:, :], bglu_d, [], [pb])
            for h in range(6):
                pg.dma("sp", yT[:, 8 + h, :], yda_d[h], [yda_buf], [yb[8 + h]])
            it = 0

            def select(rp, kchunk, nrows, dst, dstb):
                nonlocal it
                k = it % 2
                it += 1
                pg.dma("sp", stg[0:nrows, k, :, :, :], g2_d[kchunk][rp].rearrange("p (m r t) -> p m r t", r=4, t=128),
                       (list(g2_buf) if isinstance(g2_buf, (list, tuple)) else [g2_buf]), [stb[k]])
                a = acc[0:nrows, k, :].rearrange("p (m t) -> p m t", t=128)
                pg.op("dve", [stb[k], pb], [accb[k]],
                      lambda e, k=k, a=a, nrows=nrows: e.tensor_scalar(out=a, in0=stg[0:nrows, k, :, 0, :], scalar1=sel[0:nrows, 0:1], scalar2=None, op0=ALU.mult))
                for r in range(1, 4):
                    pg.op("dve", [stb[k], pb, accb[k]], [accb[k]],
                          lambda e, k=k, a=a, r=r, nrows=nrows: e.scalar_tensor_tensor(out=a, in0=stg[0:nrows, k, :, r, :], scalar=sel[0:nrows, r:r + 1],
                                                                                       in1=a, op0=ALU.mult, op1=ALU.add))
                pg.op("act", [accb[k]], [dstb], lambda e, k=k, nrows=nrows, dst=dst: e.activation(out=dst[0:nrows, :], in_=acc[0:nrows, k, :], func=AF.Copy))

            for rp in range(4):
                select(rp, 0, 128, yT[:, 2 * rp, :], yb[2 * rp])
                select(rp, 1, 64, yT[:, 2 * rp + 1, :], yb[2 * rp + 1])
                select(rp, 2, 128, ygT[:, rp, :], ygb[rp])
            git = 0
            for cc in range(4):
                for half in range(2):
                    k = git % 4
                    gk = git % 2
                    git += 1
                    tsl = slice(half * 512, (half + 1) * 512)
                    for rp in range(4):
                        pg.op("pe", [ygb[rp], pb], [ppb[k]],
                              lambda e, rp=rp, cc=cc, k=k, tsl=tsl: e.matmul(pp[k][:, :], wglu[:, rp, cc * 128:(cc + 1) * 128], ygT[:, rp, tsl],
                                                                            start=(rp == 0), stop=(rp == 3)))
                    pg.op("act", [ppb[k], pb], [gtb[gk]],
                          lambda e, k=k, gk=gk, cc=cc: e.activation(out=gate[:, gk, :], in_=pp[k][:, :], func=AF.Sigmoid, bias=bglu[:, cc:cc + 1]))
                    pg.op("dve", [gtb[gk], ygb[cc]], [yb[14 + cc]],
                          lambda e, gk=gk, cc=cc, tsl=tsl: e.tensor_tensor(out=yT[:, 14 + cc, tsl], in0=ygT[:, cc, tsl], in1=gate[:, gk, :], op=ALU.mult))
            def load_w(c):
                pg.dma("pool", wt[:, c % NW, :], wout_d[c], [], [wtb[c % NW]])

            load_w(0)
            for c in range(KC):
                if c + 1 < KC:
                    load_w(c + 1)
                s = c % NW
                for half in range(2):
                    k = git % 4
                    git += 1
                    tsl = slice(half * 512, (half + 1) * 512)
                    for pi in range(NPIECE):
                        kr = 64 if (pi < 8 and pi % 2 == 1) else 128
                        pg.op("pe", [wtb[s], yb[pi]], [ppb[k]],
                              lambda e, pi=pi, kr=kr, s=s, k=k, tsl=tsl: e.matmul(pp[k][:, :], wt[0:kr, s, pi * 128:(pi + 1) * 128], yT[0:kr, pi, tsl],
                                                                                 start=(pi == 0), stop=(pi == NPIECE - 1)))
                    pg.op("act", [ppb[k]], [ob[c]],
                          lambda e, k=k, c=c, tsl=tsl: e.activation(out=oT[:, c, tsl], in_=pp[k][:, :], func=AF.Copy))
            pg.barrier()
        emit_postnorm_residual(pg, nc, consts, oT, ob, x_d, x_dbuf, gpost_sb)


L_DEPTH = 2
GROUPS = [[0, 1, 2, 3], [4, 5, 6, 7]]
PAY2_CHUNK_ROWS = [128, 64, 128]
DATA_GROUPS = {
    "pay": [([r, NT], BF16) for r in PAY_CHUNK_ROWS],
    "g1": [([4 * r, NT], BF16) for r in PAY_CHUNK_ROWS],
    "pay2": [([r, SEQ], BF16) for r in PAY2_CHUNK_ROWS],
    "g2": [([4 * r, SEQ], BF16) for r in PAY2_CHUNK_ROWS],
    "q": [([6, 128, NT], BF16)],
    "yda": [([6, 128, NT], BF16)],
}
WEIGHT_SHAPES = {
    "wgu1": [NF, 128, 2 * KC * 128], "wd1": [KC, 128, NF * 128], "wgu2": [NF, 128, 2 * KC * 128], "wd2": [KC, 128, NF * 128],
    "wqk": [12, 128, KC * 128], "wv": [128, KC * 768], "wrw": [128, KC, 832], "rwp": [64, 26], "rwpg": [128, 1],
    "rwl": [128, 3, 192], "rwgn": [64, 2, 3, 64], "wss": [128, KC * 128], "s5p": [128, 4, 3], "s5b": [128, 4, 2, 16],
    "s5c": [128, 4, 2, 16], "s5d": [128, 1], "lamp": [128, 4, 64], "subw": [128, 128], "wglu": [128, 4, 512],
    "bglu": [128, 4], "wout": [KC, 128, NPIECE * 128],
}
CONST_SHAPES = {"ident": [128, 128], "ramp": [128, TT + 1], "rwm": [64, 3, 64], "sel": [128, 4], "damask": [128, 4, 128],
                "gains": [128, L_DEPTH * 6, KC]}


def lam_init_of(l):
    return 0.8 - 0.6 * math.exp(-0.3 * l)


def build_launch(seq, ins, outs, uses_x):
    nc = bass.Bass("TRN2", target_bir_lowering=False)
    declared = {}

    def ext_in(name, shape, dt=F32):
        if name not in declared:
            declared[name] = nc.dram_tensor(name, list(shape), dt, kind="ExternalInput").ap()
        return declared[name]

    def W(name, l):
        return ext_in("%s_%d" % (name, l), WEIGHT_SHAPES[name])

    def Cn(name):
        return ext_in(name, CONST_SHAPES[name])

    data = {}
    dbuf = {}
    data_in_names = []
    for name, members in DATA_GROUPS.items():
        kind = "ExternalInput" if name in ins else ("ExternalOutput" if name in outs else "Internal")
        data[name] = []
        for k, (shape, dt) in enumerate(members):
            nm = "%s%d" % (name, k)
            data[name].append(nc.dram_tensor(nm, list(shape), dt, kind=kind).ap())
            if name in ins:
                data_in_names.append(nm)
        dbuf[name] = Buf()
    for nm in ("pay_xn", "pay_kv", "g1_xn", "g1_kv", "pay2_rw", "pay2_ss", "g2_rw", "g2_ss"):
        dbuf[nm] = Buf()
    fused = any(ph.startswith("gather") for ph, _ in seq)
    g1v = g1_view(data["g1"])
    g2v = g1_view(data["g2"])
    q_d = data["q"][0]
    yda_d = data["yda"][0]
    with ExitStack() as es:
        pg = Prog(nc, es)
        consts = setup_consts(pg, nc, es, Cn("ident"))
        gains = sb(nc, es, "gains_sb", [128, L_DEPTH * 6, KC], F32)
        pg.dma("sp", gains[:, :, :], Cn("gains"), [], [consts["buf"]])
        for l in range(L_DEPTH):
            for i in (1, 5):
                pg.op("dve", [consts["buf"]], [consts["buf"]],
                      lambda e, l=l, i=i: e.tensor_scalar(out=gains[:, l * 6 + i, :], in0=gains[:, l * 6 + i, :], scalar1=0.5,
                                                          scalar2=None, op0=ALU.mult))
        xb = Buf()
        x_d = None
        if uses_x:
            x_in = ext_in("x_in", [KC, 128, NT])
            x_d = nc.dram_tensor("x_out", [KC, 128, NT], F32, kind="ExternalOutput").ap()
            pg.dma("sp", x_d, x_in, [], [xb])
        pg.barrier()
        G = lambda l, i: gains[:, l * 6 + i, :]
        for (ph, l) in seq:
            if ph == "ffn1":
                phase_ffn(pg, nc, consts, x_d, xb, W("wgu1", l), W("wd1", l), G(l, 0), G(l, 1))
            elif ph == "ffn2":
                phase_ffn(pg, nc, consts, x_d, xb, W("wgu2", l), W("wd2", l), G(l, 4), G(l, 5))
            elif ph == "p":
                if fused:
                    phase_p(pg, nc, consts, x_d, xb, G(l, 2), W("wqk", l), W("wv", l), data["pay"], dbuf["pay_kv"], q_d, dbuf["q"],
                            pay_xn_buf=dbuf["pay_xn"],
                            after_xn=lambda: emit_allgather(pg, nc, data["pay"][0:4], dbuf["pay_xn"], data["g1"][0:4], dbuf["g1_xn"]))
                    emit_allgather(pg, nc, data["pay"][4:8], dbuf["pay_kv"], data["g1"][4:8], dbuf["g1_kv"])
                else:
                    phase_p(pg, nc, consts, x_d, xb, G(l, 2), W("wqk", l), W("wv", l), data["pay"], dbuf["pay"], q_d, dbuf["q"])
            elif ph == "gather":
                pass
            elif ph == "da":
                phase_da(pg, nc, consts, g1v, dbuf["g1_kv"] if fused else dbuf["g1"], q_d, dbuf["q"], Cn("damask"), W("lamp", l), W("subw", l),
                         lam_init_of(l), yda_d, dbuf["yda"])
            elif ph == "rwkv":
                phase_rwkv(pg, nc, consts, g1v, dbuf["g1_xn"] if fused else dbuf["g1"], W("wrw", l), W("rwp", l), W("rwpg", l), W("rwl", l), W("rwgn", l),
                           Cn("rwm"), data["pay2"], dbuf["pay2_rw"] if fused else dbuf["pay2"])
                if fused:
                    emit_allgather(pg, nc, data["pay2"][0:2], dbuf["pay2_rw"], data["g2"][0:2], dbuf["g2_rw"])
            elif ph == "s5":
                phase_s5(pg, nc, consts, g1v, dbuf["g1_xn"] if fused else dbuf["g1"], W("wss", l), W("s5p", l), W("s5b", l), W("s5c", l), W("s5d", l),
                         Cn("ramp"), data["pay2"], dbuf["pay2_ss"] if fused else dbuf["pay2"])
                if fused:
                    emit_allgather(pg, nc, data["pay2"][2:3], dbuf["pay2_ss"], data["g2"][2:3], dbuf["g2_ss"])
            elif ph == "o":
                phase_o(pg, nc, consts, x_d, xb, yda_d, dbuf["yda"], g2v, [dbuf["g2_rw"], dbuf["g2_ss"]] if fused else dbuf["g2"],
                        Cn("sel"), W("wglu", l), W("bglu", l),
                        W("wout", l), G(l, 3))
            else:
                raise ValueError(ph)
        pg.barrier()
    in_names = list(declared.keys()) + data_in_names
    return nc, in_names


def emit_allgather(pg, nc, src, src_buf, dst, dst_buf):
    pg._sync("pool", [src_buf], [dst_buf])
    for s_ap, d_ap in zip(src, dst):
        ins = nc.gpsimd.collective_compute("AllGather", ALU.bypass, replica_groups=GROUPS, ins=[s_ap], outs=[d_ap])
        pg.cc_cnt += 1
        ins.then_inc(pg.sem["cc"], 1)
    src_buf.r["cc"] = pg.cc_cnt
    dst_buf.w = ("cc", pg.cc_cnt)
    dst_buf.r = {}


def _tile_gu(w_gu):
    return np.ascontiguousarray(w_gu.reshape(KC, 128, 2, NF, 128).transpose(3, 1, 2, 0, 4)).reshape(NF, 128, 2 * KC * 128)


def _tile_down(w_down):
    return np.ascontiguousarray(w_down.reshape(NF, 128, KC, 128).transpose(2, 1, 0, 3)).reshape(KC, 128, NF * 128)


def _pm(v):
    return np.ascontiguousarray(np.asarray(v).reshape(KC, 128).T)


def host_shared(inp, l):
    w_in = inp["w_in"][l]
    out = {}
    out["wgu1"] = _tile_gu(inp["ffn1_w_gu"][l])
    out["wd1"] = _tile_down(inp["ffn1_w_down"][l])
    out["wgu2"] = _tile_gu(inp["ffn2_w_gu"][l])
    out["wd2"] = _tile_down(inp["ffn2_w_down"][l])
    c0 = 2560
    wqk = np.empty((12, 128, KC * 128), np.float32)
    for i in range(12):
        col = c0 + i * 128
        wqk[i] = w_in[:, col:col + 128].reshape(KC, 128, 128).transpose(1, 0, 2).reshape(128, KC * 128)
    out["wqk"] = wqk
    out["wv"] = np.ascontiguousarray(w_in[:, c0 + 1536:c0 + 2304].reshape(KC, 128, 768).transpose(1, 0, 2)).reshape(128, KC * 768)
    out["lamp"] = np.ascontiguousarray(np.broadcast_to(
        np.stack([inp["da_lq1"][l], inp["da_lk1"][l], inp["da_lq2"][l], inp["da_lk2"][l]])[None], (128, 4, 64)))
    out["subw"] = np.ascontiguousarray(np.broadcast_to(inp["da_subln_w"][l][None], (128, 128)))
    out["wglu"] = np.ascontiguousarray(inp["ssm_w_glu"][l].reshape(4, 128, 512).transpose(1, 0, 2))
    out["bglu"] = np.ascontiguousarray(inp["ssm_b_glu"][l].reshape(4, 128).T)
    w_out = inp["w_out"][l]
    pieces = []
    for rp in range(4):
        pieces += [(rp * 192, 128), (rp * 192 + 128, 64)]
    pieces += [(768 + h * 128, 128) for h in range(6)] + [(1536 + c * 128, 128) for c in range(4)]
    wo = np.zeros((KC, 128, NPIECE, 128), np.float32)
    for pi, (r0, n) in enumerate(pieces):
        wo[:, :n, pi, :] = w_out[r0:r0 + n, :].reshape(n, KC, 128).transpose(1, 0, 2)
    out["wout"] = wo.reshape(KC, 128, NPIECE * 128)
    return out


def host_percore(inp, l, j):
    w_in = inp["w_in"][l]
    mu = inp["rw_mu"][l]
    hs = [3 * j, 3 * j + 1, 3 * j + 2]
    cols = []
    for base in (0, 768, 1536):
        for h in hs:
            cols += list(range(base + h * 64, base + (h + 1) * 64))
    cols += list(range(2304, 2560))
    cols = np.array(cols)
    out = {}
    out["wrw"] = np.ascontiguousarray(w_in[:, cols].reshape(KC, 128, 832).transpose(1, 0, 2))
    prm = np.zeros((64, 26), np.float32)
    for g in range(11):
        prm[:, g] = mu[cols[g * 64:(g + 1) * 64]]
    for i, h in enumerate(hs):
        sl = slice(h * 64, (h + 1) * 64)
        prm[:, 11 + i] = inp["rw_w0"][l][sl]
        prm[:, 14 + i] = inp["rw_a0"][l][sl]
        prm[:, 17 + i] = inp["rw_k_k"][l][sl]
        prm[:, 20 + i] = inp["rw_k_a"][l][sl]
        prm[:, 23 + i] = inp["rw_r_k"][l][h]
    out["rwp"] = prm
    out["rwpg"] = np.ascontiguousarray(mu[2432:2560].reshape(128, 1))
    own = np.array(sum([list(range(h * 64, (h + 1) * 64)) for h in hs], []))
    rwl = np.zeros((128, 3, 192), np.float32)
    rwl[:64, 0] = inp["rw_w2"][l][:, own]
    rwl[:64, 1] = inp["rw_a2"][l][:, own]
    rwl[:, 2] = inp["rw_g2"][l][:, own]
    out["rwl"] = rwl
    gn = np.zeros((64, 2, 3, 64), np.float32)
    gn[:, 0] = inp["rw_gn_w"][l][own].reshape(3, 64)[None]
    gn[:, 1] = inp["rw_gn_b"][l][own].reshape(3, 64)[None]
    out["rwgn"] = gn
    out["wss"] = np.ascontiguousarray(
        w_in[:, 4864 + j * 128:4864 + (j + 1) * 128].reshape(KC, 128, 128).transpose(1, 0, 2)).reshape(128, KC * 128)
    s5p = np.zeros((128, 4, 3), np.float32)
    s5b = np.zeros((128, 4, 2, 16), np.float32)
    s5c = np.zeros((128, 4, 2, 16), np.float32)
    for q in range(4):
        for half in range(2):
            g = 8 * j + 2 * q + half
            ps_ = slice(half * 64, (half + 1) * 64)
            s5p[ps_, q, 0] = inp["ssm_a_re"][l][g]
            s5p[ps_, q, 1] = inp["ssm_a_im"][l][g]
            s5p[ps_, q, 2] = inp["ssm_log_dt"][l][g]
            s5b[ps_, q, 0] = inp["ssm_b_re"][l][g]
            s5b[ps_, q, 1] = inp["ssm_b_im"][l][g]
            s5c[ps_, q, 0] = inp["ssm_c_re"][l][g].T
            s5c[ps_, q, 1] = inp["ssm_c_im"][l][g].T
    out["s5p"], out["s5b"], out["s5c"] = s5p, s5b, s5c
    out["s5d"] = np.ascontiguousarray(inp["ssm_d"][l][j * 128:(j + 1) * 128].reshape(128, 1))
    return out


def host_consts(inp, j):
    out = {"ident": np.eye(128, dtype=np.float32)}
    out["ramp"] = np.ascontiguousarray(np.broadcast_to(np.arange(TT + 1, dtype=np.float32)[None], (128, TT + 1)))
    s_ = np.arange(64)
    out["rwm"] = np.ascontiguousarray(np.stack([(s_[:, None] < s_[None, :]), (s_[:, None] <= s_[None, :]),
                                                 (s_[None, :] < s_[:, None])], axis=1).astype(np.float32))
    sel = np.zeros((128, 4), np.float32)
    sel[:, j] = 1.0
    out["sel"] = sel
    tri = (np.arange(128)[:, None] <= np.arange(128)[None, :]).astype(np.float32)
    mask = np.zeros((128, 4, 128), np.float32)
    for r in range(4):
        if r < j:
            mask[:, r, :] = 1.0
        elif r == j:
            mask[:, r, :] = tri
    out["damask"] = mask
    names = ["ffn1_pre_g", "ffn1_post_g", "mix_pre_g", "mix_post_g", "ffn2_pre_g", "ffn2_post_g"]
    gains = np.zeros((128, L_DEPTH * 6, KC), np.float32)
    for l in range(L_DEPTH):
        for i, n in enumerate(names):
            gains[:, l * 6 + i, :] = _pm(inp[n][l])
    out["gains"] = gains
    return out


FUSED = True


def kernel(**inputs):
    inp = {k: np.asarray(v) for k, v in inputs.items()}
    x = inp["x"].astype(np.float32, copy=False)
    pool = []
    for c in range(NCORES):
        b, j = c // 4, c % 4
        d = dict(host_consts(inp, j))
        pool.append(d)
    shared = [host_shared(inp, l) for l in range(L_DEPTH)]
    percore = [[host_percore(inp, l, j) for j in range(4)] for l in range(L_DEPTH)]
    for c in range(NCORES):
        j = c % 4
        for l in range(L_DEPTH):
            for k, v in shared[l].items():
                pool[c]["%s_%d" % (k, l)] = v
            for k, v in percore[l][j].items():
                pool[c]["%s_%d" % (k, l)] = v
    xs = []
    for c in range(NCORES):
        b, j = c // 4, c % 4
        t = x[b].reshape(8, 4, 128, D)[:, j].reshape(NT, D)
        xs.append(np.ascontiguousarray(t.T.reshape(KC, 128, NT)))

    def run(seq, ins, outs, uses_x, state):
        nc, in_names = build_launch(seq, ins, outs, uses_x)
        maps = []
        for c in range(NCORES):
            m = {}
            for n in in_names:
                if n == "x_in":
                    m[n] = state["x"][c]
                elif n in state:
                    m[n] = state[n][c]
                else:
                    m[n] = pool[c][n]
            maps.append(m)
        res = run_bass_kernel_spmd(nc, maps, core_ids=list(range(NCORES))).results
        if uses_x:
            state["x"] = [np.asarray(res[c]["x_out"]) for c in range(NCORES)]
        for g in outs:
            for k in range(len(DATA_GROUPS[g])):
                n = "%s%d" % (g, k)
                state[n] = [np.asarray(res[c][n]) for c in range(NCORES)]

    def gather(state, src, dst):
        for k in range(len(DATA_GROUPS[src])):
            sn, dn = "%s%d" % (src, k), "%s%d" % (dst, k)
            state[dn] = [None] * NCORES
            for c in range(NCORES):
                b = c // 4
                state[dn][c] = np.concatenate([state[sn][4 * b + r] for r in range(4)], axis=0)

    state = {"x": xs}
    if FUSED:
        seq = []
        for l in range(L_DEPTH):
            seq += [("ffn1", l), ("p", l), ("gather", l), ("rwkv", l), ("s5", l), ("da", l), ("o", l), ("ffn2", l)]
        run(seq, set(), set(), True, state)
    else:
        run([("ffn1", 0), ("p", 0)], set(), {"pay", "q"}, True, state)
        for l in range(L_DEPTH):
            gather(state, "pay", "g1")
            run([("da", l), ("rwkv", l), ("s5", l)], {"g1", "q"}, {"yda", "pay2"}, False, state)
            gather(state, "pay2", "g2")
            seq = [("o", l), ("ffn2", l)]
            if l + 1 < L_DEPTH:
                seq += [("ffn1", l + 1), ("p", l + 1)]
                run(seq, {"yda", "g2"}, {"pay", "q"}, True, state)
            else:
                run(seq, {"yda", "g2"}, set(), True, state)
    out = np.empty((2, SEQ, D), np.float32)
    for c in range(NCORES):
        b, j = c // 4, c % 4
        t = state["x"][c].reshape(D, NT).T.reshape(8, 128, D)
        out[b].reshape(8, 4, 128, D)[:, j] = t
    return out
```

```python
import math
from contextlib import ExitStack
import numpy as np
import ml_dtypes
import concourse.bass as bass
import concourse.mybir as mybir
from concourse.bass_utils import run_bass_kernel_spmd

F32 = mybir.dt.float32
BF16 = mybir.dt.bfloat16
AF = mybir.ActivationFunctionType
ALU = mybir.AluOpType
AX = mybir.AxisListType

D = 2048
KC = 16
NT = 1024
DFF = 5504
NF = 43
SEQ = 4096
NCORES = 8
EPS = 1e-6


class Buf:
    __slots__ = ("w", "r")

    def __init__(self):
        self.w = None
        self.r = {}


class Prog:
    def __init__(self, nc, es, n_dma_sems=24):
        self.nc = nc
        self.es = es
        self.eng = {"pe": nc.tensor, "act": nc.scalar, "dve": nc.vector, "pool": nc.gpsimd, "sp": nc.sync}
        self.sem = {e: es.enter_context(nc.semaphore("s_" + e)) for e in self.eng}
        self.sem["cc"] = es.enter_context(nc.semaphore("s_cc"))
        self.cc_cnt = 0
        self.cnt = {e: 0 for e in self.eng}
        self.dsem = [es.enter_context(nc.semaphore("d%d" % i)) for i in range(n_dma_sems)]
        self.dval = [0] * n_dma_sems
        self.dpool = {"sp": list(range(0, 12)), "pool": list(range(12, n_dma_sems - 2)),
                      "cc": list(range(n_dma_sems - 2, n_dma_sems))}
        self.dnext = {"sp": 0, "pool": 0, "cc": 0}
        self.waited = {e: {} for e in self.eng}
        self.nins = 0

    def _wait(self, e, tok):
        key, val = tok
        if val <= 0:
            return
        if self.waited[e].get(key, 0) >= val:
            return
        sem = self.sem[key] if isinstance(key, str) else self.dsem[key]
        self.eng[e].wait_ge(sem, val)
        self.waited[e][key] = val

    @staticmethod
    def _toks(w):
        if w is None:
            return ()
        if isinstance(w, list):
            return w
        return (w,)

    def _sync(self, e, reads, writes):
        for b in reads:
            for t in self._toks(b.w):
                self._wait(e, t)
        for b in writes:
            for t in self._toks(b.w):
                if t[0] != e or e != "pe":
                    self._wait(e, t)
            for k, v in b.r.items():
                if k != e or e != "pe":
                    self._wait(e, (k, v))

    def dma_multi(self, q, pairs, reads, writes):
        self._sync(q, reads, writes)
        toks = []
        pool = self.dpool[q]
        for (out, in_) in pairs:
            i = pool[self.dnext[q] % len(pool)]
            self.dnext[q] += 1
            self._wait(q, (i, self.dval[i]))
            ins = self.eng[q].dma_start(out=out, in_=in_)
            self.dval[i] += 16
            self.nins += 1
            ins.then_inc(self.dsem[i], 16)
            toks.append((i, self.dval[i]))
            for b in reads:
                b.r[i] = self.dval[i]
        for b in writes:
            b.w = list(toks)
            b.r = {}

    def op(self, e, reads, writes, fn):
        self._sync(e, reads, writes)
        ins = fn(self.eng[e])
        self.cnt[e] += 1
        self.nins += 1
        ins.then_inc(self.sem[e], 1)
        v = self.cnt[e]
        for b in reads:
            b.r[e] = v
        for b in writes:
            b.w = (e, v)
            b.r = {}

    def dma(self, q, out, in_, reads, writes):
        self._sync(q, reads, writes)
        pool = self.dpool[q]
        i = pool[self.dnext[q] % len(pool)]
        self.dnext[q] += 1
        self._wait(q, (i, self.dval[i]))
        ins = self.eng[q].dma_start(out=out, in_=in_)
        self.dval[i] += 16
        self.nins += 1
        ins.then_inc(self.dsem[i], 16)
        v = self.dval[i]
        for b in reads:
            b.r[i] = v
        for b in writes:
            b.w = (i, v)
            b.r = {}

    def barrier(self):
        for e in self.eng:
            for k in self.eng:
                if k != e:
                    self._wait(e, (k, self.cnt[k]))
            for i in range(len(self.dsem)):
                self._wait(e, (i, self.dval[i]))

    def final_wait(self, bufs):
        for b in bufs:
            for t in self._toks(b.w):
                self._wait("sp", t)


_UNIQ = [0]


def sb(nc, es, name, shape, dt):
    _UNIQ[0] += 1
    return es.enter_context(nc.sbuf_tensor("%s_%d" % (name, _UNIQ[0]), list(shape), dt))


def ps(nc, es, name, shape, dt=F32):
    _UNIQ[0] += 1
    return es.enter_context(nc.psum_tensor("%s_%d" % (name, _UNIQ[0]), list(shape), dt))


def emit_rstd(pg, consts, src, src_bufs, nchunks, ncols, rstd, rstd_buf, sq, sq_bufs, pss, pss_bufs, dim):
    ones = consts["ones_f"]
    nh = ncols // 512
    for c in range(nchunks):
        k = c % 2
        pg.op("act", [src_bufs[c]], [sq_bufs[k]],
              lambda e, c=c, k=k: e.activation(out=sq[:, k, :], in_=src[:, c, :], func=AF.Square))
        for h in range(nh):
            pg.op("pe", [sq_bufs[k], consts["buf"]], [pss_bufs[h]],
                  lambda e, c=c, k=k, h=h: e.matmul(pss[:, h * 512:(h + 1) * 512], ones[:, :],
                                                   sq[:, k, h * 512:(h + 1) * 512],
                                                   start=(c == 0), stop=(c == nchunks - 1)))
    for h in range(nh):
        pg.op("act", [pss_bufs[h]], [rstd_buf],
              lambda e, h=h: e.activation(out=rstd[:, h * 512:(h + 1) * 512], in_=pss[:, h * 512:(h + 1) * 512],
                                          func=AF.Sqrt, scale=1.0 / dim, bias=consts["eps"][:, 0:1]))
    pg.op("dve", [rstd_buf], [rstd_buf],
          lambda e: e.reciprocal(out=rstd[:, :], in_=rstd[:, :]))


def phase_prenorm(pg, nc, consts, x_d, x_dbuf, g_sb, xnT, xn_bufs):
    with ExitStack() as es:
        xT = sb(nc, es, "pn_xT", [128, KC, NT], F32)
        sq = sb(nc, es, "pn_sq", [128, 2, NT], F32)
        rstd = sb(nc, es, "pn_rstd", [128, NT], F32)
        pss = ps(nc, es, "pn_pss", [128, NT])
        xb = [Buf() for _ in range(KC)]
        sqb = [Buf(), Buf()]
        pssb = [Buf(), Buf()]
        rb = Buf()
        for c in range(KC):
            pg.dma("sp", xT[:, c, :], x_d[c], [x_dbuf], [xb[c]])
        emit_rstd(pg, consts, xT, xb, KC, NT, rstd, rb, sq, sqb, pss, pssb, D)
        for c in range(KC):
            pg.op("dve", [xb[c], rb, consts["buf"]], [xn_bufs[c]],
                  lambda e, c=c: e.scalar_tensor_tensor(out=xnT[:, c, :], in0=xT[:, c, :], scalar=g_sb[:, c:c + 1],
                                                        in1=rstd[:, :], op0=ALU.mult, op1=ALU.mult))
        pg.barrier()


def phase_ffn(pg, nc, consts, x_d, x_dbuf, wgu_d, wd_d, gpre_sb, gpost_sb):
    with ExitStack() as es0:
        hT = sb(nc, es0, "f_hT", [128, NF, NT], BF16)
        hb = [[Buf(), Buf()] for _ in range(NF)]
        with ExitStack() as es1:
            xnT = sb(nc, es1, "f_xnT", [128, KC, NT], BF16)
            xnb = [Buf() for _ in range(KC)]
            phase_prenorm(pg, nc, consts, x_d, x_dbuf, gpre_sb, xnT, xnb)
            with ExitStack() as es2:
                NW = 3
                wgu = sb(nc, es2, "f_wgu", [128, NW, 2 * KC * 128], BF16)
                wb = [Buf() for _ in range(NW)]
                sg = sb(nc, es2, "f_sg", [128, 2, 512], BF16)
                sgb = [Buf(), Buf()]
                psg = [ps(nc, es2, "f_psg%d" % i, [128, 512]) for i in range(2)]
                psu = [ps(nc, es2, "f_psu%d" % i, [128, 512]) for i in range(2)]
                psgb = [Buf(), Buf()]
                psub = [Buf(), Buf()]

                def load_w(f):
                    s = f % NW
                    pg.dma("pool", wgu[:, s, :], wgu_d[f], [], [wb[s]])

                for f in range(min(NW - 1, NF)):
                    load_w(f)
                it = 0
                for f in range(NF):
                    if f + NW - 1 < NF:
                        load_w(f + NW - 1)
                    s = f % NW
                    for half in range(2):
                        k = it % 2
                        it += 1
                        tsl = slice(half * 512, (half + 1) * 512)
                        for c in range(KC):
                            pg.op("pe", [wb[s], xnb[c]], [psgb[k]],
                                  lambda e, c=c, s=s, k=k, tsl=tsl: e.matmul(
                                      psg[k][:, :], wgu[:, s, c * 128:(c + 1) * 128], xnT[:, c, tsl],
                                      start=(c == 0), stop=(c == KC - 1)))
                        for c in range(KC):
                            pg.op("pe", [wb[s], xnb[c]], [psub[k]],
                                  lambda e, c=c, s=s, k=k, tsl=tsl: e.matmul(
                                      psu[k][:, :], wgu[:, s, (KC + c) * 128:(KC + c + 1) * 128], xnT[:, c, tsl],
                                      start=(c == 0), stop=(c == KC - 1)))
                        pg.op("act", [psgb[k]], [sgb[k]],
                              lambda e, k=k: e.activation(out=sg[:, k, :], in_=psg[k][:, :], func=AF.Silu))
                        pg.op("dve", [sgb[k], psub[k]], [hb[f][half]],
                              lambda e, k=k, f=f, tsl=tsl: e.tensor_tensor(out=hT[:, f, tsl], in0=sg[:, k, :],
                                                                          in1=psu[k][:, :], op=ALU.mult))
                pg.barrier()
        with ExitStack() as es3:
            oT = sb(nc, es3, "f_oT", [128, KC, NT], F32)
            ob = [Buf() for _ in range(KC)]
            with ExitStack() as es4:
                NWD = 2
                wd = sb(nc, es4, "f_wd", [128, NWD, NF * 128], BF16)
                wdb = [Buf() for _ in range(NWD)]
                pso = [ps(nc, es4, "f_pso%d" % i, [128, 512]) for i in range(4)]
                psob = [Buf() for _ in range(4)]

                def load_wd(c):
                    s = c % NWD
                    pg.dma("pool", wd[:, s, :], wd_d[c], [], [wdb[s]])

                load_wd(0)
                it = 0
                for c in range(KC):
                    if c + 1 < KC:
                        load_wd(c + 1)
                    s = c % NWD
                    for half in range(2):
                        k = it % 4
                        it += 1
                        tsl = slice(half * 512, (half + 1) * 512)
                        for f in range(NF):
                            pg.op("pe", [wdb[s], hb[f][half]], [psob[k]],
                                  lambda e, f=f, s=s, k=k, tsl=tsl: e.matmul(
                                      pso[k][:, :], wd[:, s, f * 128:(f + 1) * 128], hT[:, f, tsl],
                                      start=(f == 0), stop=(f == NF - 1)))
                        pg.op("act", [psob[k]], [ob[c]],
                              lambda e, k=k, c=c, tsl=tsl: e.activation(out=oT[:, c, tsl], in_=pso[k][:, :],
                                                                       func=AF.Copy))
                pg.barrier()
            emit_postnorm_residual(pg, nc, consts, oT, ob, x_d, x_dbuf, gpost_sb)


def emit_postnorm_residual(pg, nc, consts, oT, ob, x_d, x_dbuf, gpost_sb):
    with ExitStack() as es5:
        sq = sb(nc, es5, "f_sq", [128, 2, NT], F32)
        rstd = sb(nc, es5, "f_rstd", [128, NT], F32)
        xc = sb(nc, es5, "f_xc", [128, 3, NT], F32)
        pss = ps(nc, es5, "f_pss", [128, NT])
        sqb = [Buf(), Buf()]
        pssb = [Buf(), Buf()]
        rb = Buf()
        xcb = [Buf() for _ in range(3)]
        emit_rstd(pg, consts, oT, ob, KC, NT, rstd, rb, sq, sqb, pss, pssb, D)
        newbuf = Buf()

        def load_x(c):
            pg.dma("sp", xc[:, c % 3, :], x_d[c], [x_dbuf], [xcb[c % 3]])

        load_x(0)
        load_x(1)
        for c in range(KC):
            k = c % 3
            if c + 2 < KC:
                load_x(c + 2)
            pg.op("dve", [ob[c], rb, consts["buf"]], [ob[c]],
                  lambda e, c=c: e.scalar_tensor_tensor(out=oT[:, c, :], in0=oT[:, c, :],
                                                        scalar=gpost_sb[:, c:c + 1], in1=rstd[:, :],
                                                        op0=ALU.mult, op1=ALU.mult))
            pg.op("pool", [ob[c], xcb[k]], [xcb[k]],
                  lambda e, c=c, k=k: e.tensor_tensor(out=xc[:, k, :], in0=xc[:, k, :], in1=oT[:, c, :],
                                                     op=ALU.add))
            pg.dma("sp", x_d[c], xc[:, k, :], [xcb[k]], [newbuf])
        pg.barrier()
        x_dbuf.w = newbuf.w
        x_dbuf.r = {}


PAY_CHUNK_ROWS = [512, 512, 512, 512, 512, 256, 480, 288]
HD = 128


def pay_xn_rows(pay, c):
    return pay[c // 4][(c % 4) * 128:(c % 4 + 1) * 128, :]


def pay_k_rows(pay, h):
    return pay[4 + h // 4][(h % 4) * 128:(h % 4 + 1) * 128, :]


def _vflat(rows_ap):
    return rows_ap.rearrange("a b -> (a b)").rearrange("(t e) -> t e", e=768)


def pay_v_block(pay, mm):
    k, loc = (6, mm) if mm < 5 else (7, mm - 5)
    return _vflat(pay[k][loc * 96:(loc + 1) * 96, :])


def g1_view(g1):
    return [g.rearrange("(r a) t -> r a t", r=4) for g in g1]


def g1_v_block(g1v, r, mm):
    k, loc = (6, mm) if mm < 5 else (7, mm - 5)
    return _vflat(g1v[k][r, loc * 96:(loc + 1) * 96, :])


def phase_p(pg, nc, consts, x_d, x_dbuf, g_sb, wqk_d, wv_d, pay_d, pay_buf, q_d, q_buf, pay_xn_buf=None, after_xn=None):
    if pay_xn_buf is None:
        pay_xn_buf = pay_buf
    with ExitStack() as es0:
        xnT = sb(nc, es0, "p_xnT", [128, KC, NT], BF16)
        xnb = [Buf() for _ in range(KC)]
        phase_prenorm(pg, nc, consts, x_d, x_dbuf, g_sb, xnT, xnb)
        for c in range(KC):
            pg.dma("sp", pay_xn_rows(pay_d, c), xnT[:, c, :], [xnb[c]], [pay_xn_buf])
        if after_xn is not None:
            after_xn()
        with ExitStack() as es:
            wv = sb(nc, es, "p_wv", [128, KC * 768], BF16)
            wvb = Buf()
            pg.dma("pool", wv[:, :], wv_d, [], [wvb])
            NW = 3
            wq = sb(nc, es, "p_wq", [128, NW, KC * 128], BF16)
            wqb = [Buf() for _ in range(NW)]
            ot = sb(nc, es, "p_ot", [128, 4, 512], BF16)
            otb = [Buf() for _ in range(4)]
            vt = sb(nc, es, "p_vt", [128, 2, 768], BF16)
            vtb = [Buf(), Buf()]
            pp = [ps(nc, es, "p_pp%d" % i, [128, 512]) for i in range(4)]
            ppb = [Buf() for _ in range(4)]

            def load_w(i):
                pg.dma("pool", wq[:, i % NW, :], wqk_d[i], [], [wqb[i % NW]])

            load_w(0)
            load_w(1)
            it = 0
            for i in range(12):
                if i + 2 < 12:
                    load_w(i + 2)
                s = i % NW
                for half in range(2):
                    k = it % 4
                    it += 1
                    tsl = slice(half * 512, (half + 1) * 512)
                    for c in range(KC):
                        pg.op("pe", [wqb[s], xnb[c]], [ppb[k]],
                              lambda e, c=c, s=s, k=k, tsl=tsl: e.matmul(pp[k][:, :], wq[:, s, c * 128:(c + 1) * 128],
                                                                        xnT[:, c, tsl], start=(c == 0), stop=(c == KC - 1)))
                    pg.op("act", [ppb[k]], [otb[k]],
                          lambda e, k=k: e.activation(out=ot[:, k, :], in_=pp[k][:, :], func=AF.Copy))
                    if i < 6:
                        pg.dma("sp", q_d[i][:, tsl], ot[:, k, :], [otb[k]], [q_buf])
                    else:
                        pg.dma("sp", pay_k_rows(pay_d, i - 6)[:, tsl], ot[:, k, :], [otb[k]], [pay_buf])
            for tb in range(8):
                vk = tb % 2
                for (c0, cn) in ((0, 512), (512, 256)):
                    k = it % 4
                    it += 1
                    for c in range(KC):
                        pg.op("pe", [wvb, xnb[c]], [ppb[k]],
                              lambda e, c=c, k=k, tb=tb, c0=c0, cn=cn: e.matmul(
                                  pp[k][:, 0:cn], xnT[:, c, tb * 128:(tb + 1) * 128],
                                  wv[:, c * 768 + c0:c * 768 + c0 + cn], start=(c == 0), stop=(c == KC - 1)))
                    pg.op("act", [ppb[k]], [vtb[vk]],
                          lambda e, k=k, vk=vk, c0=c0, cn=cn: e.activation(out=vt[:, vk, c0:c0 + cn], in_=pp[k][:, 0:cn],
                                                                           func=AF.Copy))
                pg.dma("sp", pay_v_block(pay_d, tb), vt[:, vk, :], [vtb[vk]], [pay_buf])
            pg.barrier()


def phase_da(pg, nc, consts, g1_d, g1_buf, q_d, q_buf, mask_d, lamp_d, subw_d, lam_init, yda_d, yda_buf):
    with ExitStack() as es:
        KT = sb(nc, es, "da_KT", [128, 6, 4, NT], BF16)
        Vt = sb(nc, es, "da_V", [128, 4, 8, 6, 129], BF16)
        QT = sb(nc, es, "da_QT", [128, 6, NT], BF16)
        mask = sb(nc, es, "da_mask", [128, 4, 128], BF16)
        lamp = sb(nc, es, "da_lamp", [128, 4, 64], F32)
        subw = sb(nc, es, "da_subw", [128, 128], F32)
        lam = sb(nc, es, "da_lam", [128, 4], F32)
        ydaT = sb(nc, es, "da_ydaT", [128, 6, NT], BF16)
        PT = sb(nc, es, "da_PT", [128, 2, 2, 4, 128], BF16)
        fin = sb(nc, es, "da_fin", [128, 8, 128], F32)
        sm = sb(nc, es, "da_sm", [128, 16], F32)
        ybf = sb(nc, es, "da_ybf", [128, 2, 128], BF16)
        pS = [ps(nc, es, "da_pS%d" % i, [128, 2, 4, 128]) for i in range(2)]
        pO = [ps(nc, es, "da_pO", [128, 2, 512])]
        pT = ps(nc, es, "da_pT", [128, 2, 128], BF16)
        ktb = [Buf() for _ in range(6)]
        vb = Buf()
        qb = Buf()
        mb = Buf()
        lb = Buf()
        ptb = [Buf(), Buf()]
        psb = [Buf(), Buf()]
        pob = [Buf(), Buf()]
        _pt = Buf()
        ptrb = [_pt, _pt]
        ybb = [Buf(), Buf()]
        finb = Buf()
        ydb = [Buf() for _ in range(6)]
        ident = consts["ident_b"]
        for h in range(6):
            pg.dma("sp", KT[:, h, :, :], g1_d[4 + h // 4][:, (h % 4) * 128:(h % 4 + 1) * 128, :].rearrange("r p t -> p r t"),
                   [g1_buf], [ktb[h]])
        pg.op("pool", [], [vb], lambda e: e.memset(Vt[:, :, :, :, 128:129], 1.0))
        pg.dma_multi("sp", [(Vt[:, r, mm, :, 0:128], g1_v_block(g1_d, r, mm).rearrange("t (h e) -> t h e", h=6))
                            for r in range(4) for mm in range(8)], [g1_buf], [vb])
        pg.dma("sp", QT[:, :, :], q_d.rearrange("h p t -> p h t"), [q_buf], [qb])
        pg.dma("pool", mask[:, :, :], mask_d, [], [mb])
        pg.dma("sp", lamp[:, :, :], lamp_d, [], [lb])
        pg.dma("sp", subw[:, :], subw_d, [], [lb])
        pg.op("dve", [lb], [lb], lambda e: e.tensor_tensor(out=lamp[:, 0, :], in0=lamp[:, 0, :], in1=lamp[:, 1, :], op=ALU.mult))
        pg.op("dve", [lb], [lb], lambda e: e.tensor_tensor(out=lamp[:, 2, :], in0=lamp[:, 2, :], in1=lamp[:, 3, :], op=ALU.mult))
        pg.op("dve", [lb], [lb], lambda e: e.tensor_reduce(out=lam[:, 0:1], in_=lamp[:, 0, :], axis=AX.X, op=ALU.add))
        pg.op("dve", [lb], [lb], lambda e: e.tensor_reduce(out=lam[:, 1:2], in_=lamp[:, 2, :], axis=AX.X, op=ALU.add))
        pg.op("act", [lb], [lb], lambda e: e.activation(out=lam[:, 0:2], in_=lam[:, 0:2], func=AF.Exp))
        pg.op("dve", [lb], [lb], lambda e: e.tensor_tensor(out=lam[:, 2:3], in0=lam[:, 0:1], in1=lam[:, 1:2], op=ALU.subtract))
        pg.op("dve", [lb], [lb], lambda e: e.tensor_scalar(out=lam[:, 2:3], in0=lam[:, 2:3], scalar1=float(lam_init), scalar2=None, op0=ALU.add))
        pg.op("dve", [lb], [lb], lambda e: e.tensor_scalar(out=subw[:, :], in0=subw[:, :], scalar1=float(1.0 - lam_init), scalar2=None, op0=ALU.mult))
        groups = [(h, m, mp) for h in range(6) for m in range(8) for mp in range(m + 1)]

        def emit_S(idx):
            h, m, mp = groups[idx]
            k = idx % 2
            for r in range(4):
                for c in range(2):
                    pg.op("pe", [ktb[h], qb], [psb[k]],
                          lambda e, h=h, r=r, c=c, k=k, mp=mp, m=m: e.matmul(
                              pS[k][:, c, r, :], KT[c * 64:(c + 1) * 64, h, r, mp * 128:(mp + 1) * 128],
                              QT[c * 64:(c + 1) * 64, h, m * 128:(m + 1) * 128], start=True, stop=True))

        emit_S(0)
        oit = 0
        for idx, (h, m, mp) in enumerate(groups):
            k = idx % 2
            ok = 0
            ngroups = m + 1
            if idx + 1 < len(groups):
                emit_S(idx + 1)
            for half in range(2):
                pg.op("act", [psb[k]], [ptb[k]],
                      lambda e, k=k, half=half: e.activation(out=PT[:, k, half, :, :], in_=pS[k][:, half, :, :],
                                                             func=AF.Exp, scale=0.125))
            if mp == m:
                pg.op("dve", [ptb[k], mb], [ptb[k]],
                      lambda e, k=k: e.tensor_tensor(out=PT[:, k, :, :, :], in0=PT[:, k, :, :, :],
                                                     in1=mask[:, :, :].unsqueeze(1).broadcast_to([128, 2, 4, 128]),
                                                     op=ALU.mult))
            for r in range(4):
                for c in range(2):
                    pg.op("pe", [ptb[k], vb], [pob[ok]],
                          lambda e, h=h, r=r, c=c, k=k, mp=mp, ok=ok, ng=ngroups: e.matmul(
                              pO[ok][:, c, 0:129], PT[:, k, c, r, :], Vt[:, r, mp, h, :],
                              start=(mp == 0 and r == 0), stop=(mp == ng - 1 and r == 3)))
            if mp != m:
                continue
            oit += 1
            O = pO[ok]
            pg.op("dve", [pob[ok]], [finb], lambda e, O=O: e.reciprocal(out=sm[:, 0:2], in_=O[:, :, 128]))
            pg.op("dve", [finb, lb], [finb],
                  lambda e: e.tensor_tensor(out=sm[:, 2:3], in0=sm[:, 1:2], in1=lam[:, 2:3], op=ALU.mult))
            pg.op("dve", [pob[ok], finb], [finb],
                  lambda e, O=O: e.tensor_scalar(out=fin[:, 0, :], in0=O[:, 1, 0:128], scalar1=sm[:, 2:3], scalar2=None,
                                                 op0=ALU.mult))
            pg.op("dve", [pob[ok], finb], [finb],
                  lambda e, O=O: e.scalar_tensor_tensor(out=fin[:, 1, :], in0=O[:, 0, 0:128], scalar=sm[:, 0:1],
                                                        in1=fin[:, 0, :], op0=ALU.mult, op1=ALU.subtract))
            pg.op("dve", [finb], [finb],
                  lambda e: e.tensor_tensor(out=fin[:, 2, :], in0=fin[:, 1, :], in1=fin[:, 1, :], op=ALU.mult))
            pg.op("dve", [finb], [finb],
                  lambda e: e.tensor_reduce(out=sm[:, 4:5], in_=fin[:, 2, :], axis=AX.X, op=ALU.add))
            pg.op("act", [finb, consts["buf"]], [finb],
                  lambda e: e.activation(out=sm[:, 5:6], in_=sm[:, 4:5], func=AF.Sqrt, scale=1.0 / 128,
                                         bias=consts["eps5"][:, 0:1]))
            pg.op("dve", [finb], [finb], lambda e: e.reciprocal(out=sm[:, 6:7], in_=sm[:, 5:6]))
            yk = oit % 2
            pg.op("dve", [finb, lb], [ybb[yk]],
                  lambda e, yk=yk: e.scalar_tensor_tensor(out=ybf[:, yk, :], in0=fin[:, 1, :], scalar=sm[:, 6:7],
                                                          in1=subw[:, :], op0=ALU.mult, op1=ALU.mult))
            pg.op("pe", [ybb[yk], consts["buf"]], [ptrb[yk]],
                  lambda e, yk=yk: e.transpose(pT[:, yk, :], ybf[:, yk, :], ident[:, :]))
            pg.op("act", [ptrb[yk]], [ydb[h]],
                  lambda e, yk=yk, h=h, m=m: e.activation(out=ydaT[:, h, m * 128:(m + 1) * 128], in_=pT[:, yk, :],
                                                         func=AF.Copy))
            if m == 7:
                pg.dma("sp", yda_d[h], ydaT[:, h, :], [ydb[h]], [yda_buf])
        pg.barrier()


def setup_consts(pg, nc, es, ident_d):
    cb = Buf()
    ones_f = sb(nc, es, "c_ones_f", [128, 128], F32)
    ones_b = sb(nc, es, "c_ones_b", [128, 128], BF16)
    ident_f = sb(nc, es, "c_ident_f", [128, 128], F32)
    ident_b = sb(nc, es, "c_ident_b", [128, 128], BF16)
    eps = sb(nc, es, "c_eps", [128, 4], F32)
    pg.op("dve", [], [cb], lambda e: e.memset(ones_f[:, :], 1.0))
    pg.op("dve", [], [cb], lambda e: e.memset(ones_b[:, :], 1.0))
    pg.op("dve", [], [cb], lambda e: e.memset(eps[:, 0:1], EPS))
    pg.op("dve", [], [cb], lambda e: e.memset(eps[:, 1:2], 1e-5))
    pg.op("dve", [], [cb], lambda e: e.memset(eps[:, 2:3], 64e-5))
    pg.op("dve", [], [cb], lambda e: e.memset(eps[:, 3:4], 0.0))
    pg.dma("sp", ident_f[:, :], ident_d, [], [cb])
    pg.dma("pool", ident_b[:, :], ident_d, [], [cb])
    return {"ones_f": ones_f, "ones_b": ones_b, "ident_f": ident_f, "ident_b": ident_b, "buf": cb,
            "eps": eps[:, 0:1], "eps5": eps[:, 1:2], "epsgn": eps[:, 2:3], "zero": eps[:, 3:4]}


I32 = mybir.dt.int32
TT = 512


def load_xn_tile(pg, g1_d, g1_buf, xt, xtb, m, rs=(0, 1, 2, 3)):
    pairs = []
    for i, r in enumerate(rs):
        for k4 in range(4):
            pairs.append((xt[:, k4 * 4:(k4 + 1) * 4, i, :],
                          g1_d[k4][r, :, m * 128:(m + 1) * 128].rearrange("(kc p) t -> p kc t", p=128)))
    pg.dma_multi("sp", pairs, [g1_buf], [xtb])


def emit_sin(pg, out, x, tmpf, tmpi, bufs):
    pg.op("dve", bufs, bufs, lambda e: e.tensor_scalar(out=tmpf, in0=x, scalar1=1.0 / (2 * math.pi), scalar2=None, op0=ALU.mult))
    pg.op("dve", bufs, bufs, lambda e: e.tensor_copy(out=tmpi, in_=tmpf))
    pg.op("dve", bufs, bufs, lambda e: e.tensor_copy(out=tmpf, in_=tmpi))
    pg.op("dve", bufs, bufs, lambda e: e.scalar_tensor_tensor(out=tmpf, in0=tmpf, scalar=-2 * math.pi, in1=x, op0=ALU.mult, op1=ALU.add))
    pg.op("dve", bufs, bufs, lambda e: e.tensor_scalar(out=tmpf, in0=tmpf, scalar1=math.pi, scalar2=-math.pi, op0=ALU.min, op1=ALU.max))
    pg.op("act", bufs, bufs, lambda e: e.activation(out=out, in_=tmpf, func=AF.Sin))


def phase_s5(pg, nc, consts, g1_d, g1_buf, wss_d, s5p_d, s5b_d, s5c_d, s5d_d, ramp_d, pay2_d, pay2_buf):
    TC = TT
    with ExitStack() as es:
        wss = sb(nc, es, "s5_w", [128, KC * 128], BF16)
        prm = sb(nc, es, "s5_prm", [128, 4, 3], F32)
        bp = sb(nc, es, "s5_bp", [128, 4, 2, 16], F32)
        cp = sb(nc, es, "s5_cp", [128, 4, 2, 16], F32)
        dsk = sb(nc, es, "s5_d", [128, 1], F32)
        ramp = sb(nc, es, "s5_ramp", [128, TC + 1], F32)
        sm = sb(nc, es, "s5_sm", [128, 4, 16], F32)
        bbar = sb(nc, es, "s5_bbar", [128, 4, 2, 16], F32)
        tmp16 = sb(nc, es, "s5_t16", [128, 4, 2, 16], F32)
        pad = sb(nc, es, "s5_pad", [128, 4, 4, 128], F32)
        BT = sb(nc, es, "s5_BT", [128, 4, 2, 128], BF16)
        padb = sb(nc, es, "s5_padb", [128, 4, 2, 128], BF16)
        CT = sb(nc, es, "s5_CT", [128, 4, 2, 128], BF16)
        cosT = sb(nc, es, "s5_cos", [128, 4, TC + 1], F32)
        sinT = sb(nc, es, "s5_sin", [128, 4, TC + 1], F32)
        targ = sb(nc, es, "s5_targ", [128, 4, TC + 1], F32)
        ttmp = sb(nc, es, "s5_ttmp", [128, 4, TC + 1], F32)
        tint = sb(nc, es, "s5_tint", [128, 4, TC + 1], I32)
        init = sb(nc, es, "s5_init", [128, 4, 2], F32)
        itmp = sb(nc, es, "s5_itmp", [128, 4, 4], F32)
        xt = sb(nc, es, "s5_xt", [128, 2, KC, 4, 128], BF16)
        uf = sb(nc, es, "s5_uf", [128, TC], F32)
        ub = sb(nc, es, "s5_ub", [128, TC], BF16)
        w1a = sb(nc, es, "s5_w1", [128, 2, 6, TC], F32)
        xs = sb(nc, es, "s5_xs", [128, 4, 2, TC], BF16)
        yo = sb(nc, es, "s5_yo", [128, 3, TC], F32)
        yb16 = sb(nc, es, "s5_yb", [128, 2, TC], BF16)
        pu = ps(nc, es, "s5_pu", [128, TC])
        pbu = [ps(nc, es, "s5_pbu%d" % i, [128, 2, TC]) for i in range(2)]
        py = ps(nc, es, "s5_py", [128, TC])
        ptr = ps(nc, es, "s5_ptr", [128, 4, 128], BF16)
        pb = Buf()
        pg.dma("pool", wss[:, :], wss_d, [], [pb])
        pg.dma("sp", prm[:, :, :], s5p_d, [], [pb])
        pg.dma("sp", bp[:, :, :, :], s5b_d, [], [pb])
        pg.dma("sp", cp[:, :, :, :], s5c_d, [], [pb])
        pg.dma("sp", dsk[:, :], s5d_d, [], [pb])
        pg.dma("sp", ramp[:, :], ramp_d, [], [pb])
        P = [pb]
        ar, ai, ldt = prm[:, :, 0], prm[:, :, 1], prm[:, :, 2]
        dt, mag, ang = sm[:, :, 0], sm[:, :, 1], sm[:, :, 2]
        cosa, sina = sm[:, :, 3], sm[:, :, 4]
        nr, ni, den, cr, ci = sm[:, :, 5], sm[:, :, 6], sm[:, :, 7], sm[:, :, 8], sm[:, :, 9]
        t0, t1, angc = sm[:, :, 10], sm[:, :, 11], sm[:, :, 12]

        def V(fn):
            pg.op("dve", P, P, fn)

        pg.op("act", P, P, lambda e: e.activation(out=dt, in_=ldt, func=AF.Exp))
        V(lambda e: e.tensor_tensor(out=t0, in0=dt, in1=ar, op=ALU.mult))
        pg.op("act", P, P, lambda e: e.activation(out=mag, in_=t0, func=AF.Exp))
        V(lambda e: e.tensor_tensor(out=ang, in0=dt, in1=ai, op=ALU.mult))
        emit_sin(pg, sina, ang, t0, tint[:, :, 0], P)
        V(lambda e: e.tensor_scalar(out=angc, in0=ang, scalar1=math.pi / 2, scalar2=None, op0=ALU.add))
        emit_sin(pg, cosa, angc, t0, tint[:, :, 0], P)
        V(lambda e: e.tensor_tensor(out=nr, in0=mag, in1=cosa, op=ALU.mult))
        V(lambda e: e.tensor_scalar(out=nr, in0=nr, scalar1=-1.0, scalar2=None, op0=ALU.add))
        V(lambda e: e.tensor_tensor(out=ni, in0=mag, in1=sina, op=ALU.mult))
        V(lambda e: e.tensor_tensor(out=den, in0=ar, in1=ar, op=ALU.mult))
        V(lambda e: e.tensor_tensor(out=t0, in0=ai, in1=ai, op=ALU.mult))
        V(lambda e: e.tensor_tensor(out=den, in0=den, in1=t0, op=ALU.add))
        V(lambda e: e.reciprocal(out=den, in_=den))
        V(lambda e: e.tensor_tensor(out=cr, in0=nr, in1=ar, op=ALU.mult))
        V(lambda e: e.tensor_tensor(out=t0, in0=ni, in1=ai, op=ALU.mult))
        V(lambda e: e.tensor_tensor(out=cr, in0=cr, in1=t0, op=ALU.add))
        V(lambda e: e.tensor_tensor(out=cr, in0=cr, in1=den, op=ALU.mult))
        V(lambda e: e.tensor_tensor(out=ci, in0=ni, in1=ar, op=ALU.mult))
        V(lambda e: e.tensor_tensor(out=t0, in0=nr, in1=ai, op=ALU.mult))
        V(lambda e: e.tensor_tensor(out=ci, in0=ci, in1=t0, op=ALU.subtract))
        V(lambda e: e.tensor_tensor(out=ci, in0=ci, in1=den, op=ALU.mult))
        s5stage = globals().get("S5_STAGE", "full")
        if s5stage == "A":
            pg.barrier()
            return
        crb = cr.unsqueeze(2).broadcast_to([128, 4, 16])
        cib = ci.unsqueeze(2).broadcast_to([128, 4, 16])
        V(lambda e: e.tensor_tensor(out=bbar[:, :, 0, :], in0=bp[:, :, 0, :], in1=crb, op=ALU.mult))
        V(lambda e: e.tensor_tensor(out=tmp16[:, :, 0, :], in0=bp[:, :, 1, :], in1=cib, op=ALU.mult))
        V(lambda e: e.tensor_tensor(out=bbar[:, :, 0, :], in0=bbar[:, :, 0, :], in1=tmp16[:, :, 0, :], op=ALU.subtract))
        V(lambda e: e.tensor_tensor(out=bbar[:, :, 1, :], in0=bp[:, :, 1, :], in1=crb, op=ALU.mult))
        V(lambda e: e.tensor_tensor(out=tmp16[:, :, 1, :], in0=bp[:, :, 0, :], in1=cib, op=ALU.mult))
        V(lambda e: e.tensor_tensor(out=bbar[:, :, 1, :], in0=bbar[:, :, 1, :], in1=tmp16[:, :, 1, :], op=ALU.add))
        V(lambda e: e.memset(pad[:, :, :, :], 0.0))
        for q in range(4):
            for half in range(2):
                psl = slice(half * 64, (half + 1) * 64)
                csl = slice((2 * q + half) * 16, (2 * q + half + 1) * 16)
                V(lambda e, q=q, psl=psl, csl=csl: e.tensor_copy(out=pad[psl, q, 0, csl], in_=bbar[psl, q, 0, :]))
                V(lambda e, q=q, psl=psl, csl=csl: e.tensor_copy(out=pad[psl, q, 1, csl], in_=bbar[psl, q, 1, :]))
                V(lambda e, q=q, psl=psl, csl=csl: e.tensor_copy(out=pad[psl, q, 2, csl], in_=cp[psl, q, 0, :]))
                V(lambda e, q=q, psl=psl, csl=csl: e.tensor_scalar(out=pad[psl, q, 3, csl], in0=cp[psl, q, 1, :], scalar1=-1.0,
                                                                   scalar2=None, op0=ALU.mult))
        V(lambda e: e.tensor_copy(out=CT[:, :, :, :], in_=pad[:, :, 2:4, :]))
        V(lambda e: e.tensor_copy(out=padb[:, :, :, :], in_=pad[:, :, 0:2, :]))
        trb = Buf()
        for q in range(4):
            for ri in range(2):
                pg.op("pe", P + [consts["buf"]], [trb],
                      lambda e, q=q, ri=ri: e.transpose(ptr[:, q, :], padb[:, q, ri, :], consts["ident_b"][:, :]))
                pg.op("act", [trb], P, lambda e, q=q, ri=ri: e.activation(out=BT[:, q, ri, :], in_=ptr[:, q, :], func=AF.Copy))
        if s5stage == "B":
            pg.barrier()
            return
        for q in range(4):
            V(lambda e, q=q: e.tensor_scalar(out=targ[:, q, :], in0=ramp[:, :], scalar1=ang[:, q:q + 1], scalar2=None, op0=ALU.mult))
        emit_sin(pg, sinT[:, :, :], targ[:, :, :], ttmp[:, :, :], tint[:, :, :], P)
        V(lambda e: e.tensor_scalar(out=targ[:, :, :], in0=targ[:, :, :], scalar1=math.pi / 2, scalar2=None, op0=ALU.add))
        emit_sin(pg, cosT[:, :, :], targ[:, :, :], ttmp[:, :, :], tint[:, :, :], P)
        V(lambda e: e.memset(init[:, :, :], 0.0))
        if s5stage == "C":
            pg.barrier()
            return
        xtb = [Buf(), Buf()]
        ufb, ubb, pub = Buf(), Buf(), Buf()
        pbub = [Buf(), Buf()]
        w1ball = [[Buf() for _ in range(6)] for _ in range(2)]
        xsb = [Buf() for _ in range(4)]
        pyb = Buf()
        yob = Buf()
        ybb = [Buf(), Buf()]
        ib = Buf()
        ib.w = pb.w
        load_xn_tile(pg, g1_d, g1_buf, xt[:, 0], xtb[0], 0)
        nt = globals().get("S5_TILES", SEQ // TT)
        for m in range(nt):
            k = m % 2
            if m + 1 < nt:
                load_xn_tile(pg, g1_d, g1_buf, xt[:, (m + 1) % 2], xtb[(m + 1) % 2], m + 1)
            if s5stage == "D0":
                continue
            for c in range(KC):
                pg.op("pe", [xtb[k], pb], [pub],
                      lambda e, c=c, k=k: e.matmul(pu[:, :], wss[:, c * 128:(c + 1) * 128],
                                                   xt[:, k, c, :, :].rearrange("p r t -> p (r t)"),
                                                   start=(c == 0), stop=(c == KC - 1)))
            pg.op("act", [pub], [ufb], lambda e: e.activation(out=uf[:, :], in_=pu[:, :], func=AF.Copy))
            pg.op("dve", [ufb], [ubb], lambda e: e.tensor_copy(out=ub[:, :], in_=uf[:, :]))
            if s5stage == "D":
                continue
            for q in range(4):
                kb = q % 2
                w1 = w1a[:, kb]
                w1b = w1ball[kb]
                for ri in range(2):
                    pg.op("pe", [ubb, pb], [pbub[kb]],
                          lambda e, q=q, ri=ri, kb=kb: e.matmul(pbu[kb][:, ri, :], BT[:, q, ri, :], ub[:, :], start=True, stop=True))
                cs, sn = cosT[:, q, 0:TC], sinT[:, q, 0:TC]
                bur, bui = pbu[kb][:, 0, :], pbu[kb][:, 1, :]
                pg.op("dve", [pbub[kb], pb], [w1b[0]], lambda e, cs=cs, bur=bur: e.tensor_tensor(out=w1[:, 0, :], in0=bur, in1=cs, op=ALU.mult))
                pg.op("dve", [pbub[kb], pb], [w1b[1]], lambda e, sn=sn, bui=bui: e.tensor_tensor(out=w1[:, 1, :], in0=bui, in1=sn, op=ALU.mult))
                pg.op("pool", [w1b[0], w1b[1]], [w1b[0]], lambda e: e.tensor_tensor(out=w1[:, 0, :], in0=w1[:, 0, :], in1=w1[:, 1, :], op=ALU.add))
                pg.op("dve", [pbub[kb], pb], [w1b[2]], lambda e, cs=cs, bui=bui: e.tensor_tensor(out=w1[:, 2, :], in0=bui, in1=cs, op=ALU.mult))
                pg.op("dve", [pbub[kb], pb], [w1b[1]], lambda e, sn=sn, bur=bur: e.tensor_tensor(out=w1[:, 1, :], in0=bur, in1=sn, op=ALU.mult))
                pg.op("pool", [w1b[2], w1b[1]], [w1b[2]], lambda e: e.tensor_tensor(out=w1[:, 2, :], in0=w1[:, 2, :], in1=w1[:, 1, :], op=ALU.subtract))
                rho = mag[:, q:q + 1].to_broadcast([128, TC])
                pg.op("dve", [w1b[0], ib, pb], [w1b[3]],
                      lambda e, q=q, rho=rho: e.tensor_tensor_scan(out=w1[:, 3, :], data0=rho, data1=w1[:, 0, :],
                                                                   initial=init[:, q, 0:1], op0=ALU.mult, op1=ALU.add))
                pg.op("dve", [w1b[2], ib, pb], [w1b[4]],
                      lambda e, q=q, rho=rho: e.tensor_tensor_scan(out=w1[:, 4, :], data0=rho, data1=w1[:, 2, :],
                                                                   initial=init[:, q, 1:2], op0=ALU.mult, op1=ALU.add))
                wl_r, wl_i = w1[:, 3, TC - 1:TC], w1[:, 4, TC - 1:TC]
                cT, sT = cosT[:, q, TC:TC + 1], sinT[:, q, TC:TC + 1]
                pg.op("dve", [w1b[3], w1b[4], pb], [ib], lambda e, q=q, wl_r=wl_r, cT=cT: e.tensor_tensor(out=itmp[:, q, 0:1], in0=wl_r, in1=cT, op=ALU.mult))
                pg.op("dve", [w1b[3], w1b[4], pb], [ib], lambda e, q=q, wl_i=wl_i, sT=sT: e.tensor_tensor(out=itmp[:, q, 1:2], in0=wl_i, in1=sT, op=ALU.mult))
                pg.op("dve", [w1b[3], w1b[4], pb], [ib], lambda e, q=q, wl_r=wl_r, sT=sT: e.tensor_tensor(out=itmp[:, q, 2:3], in0=wl_r, in1=sT, op=ALU.mult))
                pg.op("dve", [w1b[3], w1b[4], pb], [ib], lambda e, q=q, wl_i=wl_i, cT=cT: e.tensor_tensor(out=itmp[:, q, 3:4], in0=wl_i, in1=cT, op=ALU.mult))
                pg.op("dve", [ib], [ib], lambda e, q=q: e.tensor_tensor(out=init[:, q, 0:1], in0=itmp[:, q, 0:1], in1=itmp[:, q, 1:2], op=ALU.subtract))
                pg.op("dve", [ib], [ib], lambda e, q=q: e.tensor_tensor(out=init[:, q, 1:2], in0=itmp[:, q, 2:3], in1=itmp[:, q, 3:4], op=ALU.add))
                pg.op("pool", [w1b[3], pb], [w1b[0]], lambda e, cs=cs: e.tensor_tensor(out=w1[:, 0, :], in0=w1[:, 3, :], in1=cs, op=ALU.mult))
                pg.op("pool", [w1b[4], pb], [w1b[1]], lambda e, sn=sn: e.tensor_tensor(out=w1[:, 1, :], in0=w1[:, 4, :], in1=sn, op=ALU.mult))
                pg.op("dve", [w1b[0], w1b[1]], [xsb[q]], lambda e, q=q: e.tensor_tensor(out=xs[:, q, 0, :], in0=w1[:, 0, :], in1=w1[:, 1, :], op=ALU.subtract))
                pg.op("pool", [w1b[3], pb], [w1b[2]], lambda e, sn=sn: e.tensor_tensor(out=w1[:, 2, :], in0=w1[:, 3, :], in1=sn, op=ALU.mult))
                pg.op("pool", [w1b[4], pb], [w1b[5]], lambda e, cs=cs: e.tensor_tensor(out=w1[:, 5, :], in0=w1[:, 4, :], in1=cs, op=ALU.mult))
                pg.op("dve", [w1b[2], w1b[5]], [xsb[q]], lambda e, q=q: e.tensor_tensor(out=xs[:, q, 1, :], in0=w1[:, 2, :], in1=w1[:, 5, :], op=ALU.add))
            for q in range(4):
                for ri in range(2):
                    pg.op("pe", [xsb[q], pb], [pyb],
                          lambda e, q=q, ri=ri: e.matmul(py[:, :], CT[:, q, ri, :], xs[:, q, ri, :],
                                                         start=(q == 0 and ri == 0), stop=(q == 3 and ri == 1)))
            pg.op("dve", [pyb, ufb, pb], [yob], lambda e: e.scalar_tensor_tensor(out=yo[:, 0, :], in0=uf[:, :], scalar=dsk[:, 0:1],
                                                                                in1=py[:, :], op0=ALU.mult, op1=ALU.add))
            pg.op("dve", [yob], [yob], lambda e: e.tensor_tensor(out=yo[:, 1, :], in0=yo[:, 0, :], in1=yo[:, 0, :], op=ALU.mult))
            pg.op("dve", [yob], [yob], lambda e: e.tensor_scalar(out=yo[:, 1, :], in0=yo[:, 1, :], scalar1=0.044715, scalar2=1.0,
                                                                 op0=ALU.mult, op1=ALU.add))
            pg.op("dve", [yob], [yob], lambda e: e.tensor_tensor(out=yo[:, 1, :], in0=yo[:, 1, :], in1=yo[:, 0, :], op=ALU.mult))
            pg.op("act", [yob], [yob], lambda e: e.activation(out=yo[:, 2, :], in_=yo[:, 1, :], func=AF.Sigmoid,
                                                              scale=2.0 * math.sqrt(2.0 / math.pi)))
            pg.op("dve", [yob], [ybb[k]], lambda e, k=k: e.tensor_tensor(out=yb16[:, k, :], in0=yo[:, 0, :], in1=yo[:, 2, :], op=ALU.mult))
            pg.dma("sp", pay2_d[2][:, m * TT:(m + 1) * TT], yb16[:, k, :], [ybb[k]], [pay2_buf])
        pg.barrier()


RT = 256
NCH = RT // 64
NEG_EH = -math.exp(-0.5)


def phase_rwkv(pg, nc, consts, g1_d, g1_buf, wrw_d, rwp_d, rwpg_d, rwl_d, rwgn_d, rwm_d, pay2_d, pay2_buf):
    with ExitStack() as es:
        wrw = sb(nc, es, "rw_w", [128, KC, 832], BF16)
        xt = sb(nc, es, "rw_xt", [128, 2, KC, 2, 128], BF16)
        prm = sb(nc, es, "rw_prm", [64, 32], F32)
        mug = sb(nc, es, "rw_mug", [128, 1], F32)
        lw = sb(nc, es, "rw_lw", [128, 3, 192], BF16)
        gn = sb(nc, es, "rw_gn", [64, 2, 3, 64], F32)
        msk = sb(nc, es, "rw_msk", [64, 3, 64], F32)
        rmask = sb(nc, es, "rw_rmask", [64, RT], F32)
        idb = sb(nc, es, "rw_idb", [64, 64], BF16)
        Z = sb(nc, es, "rw_Z", [64, 11, RT + 1], F32)
        ZG = sb(nc, es, "rw_ZG", [128, RT + 1], F32)
        ZS = sb(nc, es, "rw_ZS", [64, 11, RT], F32)
        E = sb(nc, es, "rw_E", [64, 12, RT], F32)
        zgs = sb(nc, es, "rw_zgs", [128, 2, RT], F32)
        tw = sb(nc, es, "rw_tw", [64, 2, RT], BF16)
        sgb = sb(nc, es, "rw_sgb", [128, RT], BF16)
        SIG = sb(nc, es, "rw_SIG", [64, 3, RT], F32)
        AL = sb(nc, es, "rw_AL", [64, 3, RT], F32)
        KKN = sb(nc, es, "rw_KKN", [64, 3, RT], F32)
        KP = sb(nc, es, "rw_KP", [64, 3, RT], F32)
        L = sb(nc, es, "rw_L", [64, 3, RT], F32)
        T1 = sb(nc, es, "rw_T1", [64, 3, RT], F32)
        T2 = sb(nc, es, "rw_T2", [128, 3, RT], F32)
        onesblk = sb(nc, es, "rw_onesblk", [128, 128], F32)
        PCt = sb(nc, es, "rw_PC", [64, 3, NCH], F32)
        arT = sb(nc, es, "rw_arT", [64, 3, NCH, 2, 64], BF16)
        bkT = sb(nc, es, "rw_bkT", [64, 3, NCH, 2, 64], BF16)
        FM = sb(nc, es, "rw_FM", [64, 3, 5, RT], BF16)
        TOK = sb(nc, es, "rw_TOK", [64, NCH, 3, 5, 64], BF16)
        SCm = sb(nc, es, "rw_SCm", [64, 3, NCH, 2, 2, 64], BF16)
        NLt = sb(nc, es, "rw_NL", [64, 3, NCH, 64], BF16)
        MJ = sb(nc, es, "rw_MJ", [64, 2, 3, NCH, 64], BF16)
        NJ = sb(nc, es, "rw_NJ", [64, 2, 3, NCH, 64], BF16)
        Tt = sb(nc, es, "rw_Tt", [64, 2, 3, NCH, 64], BF16)
        AkVb = sb(nc, es, "rw_AkVb", [64, 3, NCH, 64], BF16)
        UVs = sb(nc, es, "rw_UVs", [64, NCH, 3, 64], F32)
        ApT = sb(nc, es, "rw_ApT", [64, 3, NCH, 64], BF16)
        S = sb(nc, es, "rw_S", [64, 3, 64], F32)
        Sb_ = sb(nc, es, "rw_Sb", [64, 3, 64], BF16)
        Ubf = sb(nc, es, "rw_Ubf", [64, 3, 64], BF16)
        Yt = sb(nc, es, "rw_Yt", [64, NCH, 3, 64], F32)
        F1 = sb(nc, es, "rw_F1", [64, NCH, 3, 64], F32)
        F2 = sb(nc, es, "rw_F2", [64, NCH, 3, 64], F32)
        st = sb(nc, es, "rw_st", [64, 4, NCH, 3], F32)
        rk = sb(nc, es, "rw_rk", [64, NCH, 3], F32)
        yfb = sb(nc, es, "rw_yfb", [64, NCH, 192], BF16)
        yoA = sb(nc, es, "rw_yoA", [128, 2, RT], BF16)
        yoB = sb(nc, es, "rw_yoB", [64, 2, RT], BF16)
        B01 = ps(nc, es, "rw_b01", [128, 1024])
        B23 = ps(nc, es, "rw_b23", [128, 1024])
        B4 = ps(nc, es, "rw_b4", [128, 512])
        B56 = ps(nc, es, "rw_b56", [128, 1024])
        bankT = ps(nc, es, "rw_bT", [128, 1024], BF16)
        pb = Buf()
        P = [pb]
        pg.dma("pool", wrw[:, :, :], wrw_d, [], P)
        pg.dma("sp", prm[:, 0:26], rwp_d, [], P)
        pg.dma("sp", mug[:, :], rwpg_d, [], P)
        pg.dma("pool", lw[:, :, :], rwl_d, [], P)
        pg.dma("sp", gn[:, :, :, :], rwgn_d, [], P)
        pg.dma("sp", msk[:, :, :], rwm_d, [], P)
        pg.op("dve", P + [consts["buf"]], P, lambda e: e.tensor_copy(out=idb[:, :], in_=consts["ident_b"][0:64, 0:64]))
        pg.op("dve", P, P, lambda e: e.memset(rmask[:, :], 1.0))
        pg.op("dve", P, P, lambda e: e.memset(rmask[:, :].rearrange("p (c t) -> p c t", t=64)[:, :, 0:1], 0.0))
        pg.op("dve", P, P, lambda e: e.tensor_scalar(out=prm[:, 29:32], in0=prm[:, 20:23], scalar1=-1.0, scalar2=1.0,
                                                     op0=ALU.mult, op1=ALU.add))
        pg.op("dve", P, P, lambda e: e.memset(T2[:, :, :], 0.0))
        pg.op("dve", P, P, lambda e: e.memset(onesblk[:, :], 0.0))
        pg.op("dve", P, P, lambda e: e.memset(onesblk[0:64, 0:64], 1.0))
        pg.op("dve", P, P, lambda e: e.memset(S[:, :, :], 0.0))
        pg.op("dve", P, P, lambda e: e.memset(Sb_[:, :, :], 0.0))
        pg.op("dve", P, P, lambda e: e.memset(Z[:, :, 0:1], 0.0))
        pg.op("dve", P, P, lambda e: e.memset(ZG[:, 0:1], 0.0))
        MU, W0, A0, KK_, KA, RK, OMKA = 0, 11, 14, 17, 20, 23, 29
        xtb = [Buf(), Buf()]
        zb, zgb, zsb, eb = Buf(), Buf(), Buf(), Buf()
        bb = [Buf() for _ in range(8)]
        pjb = [bb[5], bb[6]]
        plb = [bb[2], bb[3], bb[4]]
        b_tw, b_sg, b_sig, b_al, b_kkn, b_kp, b_L, b_t1, b_t2, b_pc = (Buf() for _ in range(10))
        b_ar, b_bk, b_fm, b_tok, b_ptr = Buf(), Buf(), Buf(), Buf(), bb[7]
        b_scp = [bb[5], bb[6]]
        b_np, b_scm, b_nl = bb[4], Buf(), Buf()
        b_mj, b_nj, b_tt = [Buf(), Buf()], [Buf(), Buf()], [Buf(), Buf()]
        b_pm, b_pn, b_pp = [bb[0], bb[1]], [bb[2], bb[3]], [bb[5], bb[6]]
        b_akv, b_uvs, b_apt = Buf(), Buf(), Buf()
        b_S, b_Sb, b_ubf, b_yt = Buf(), Buf(), Buf(), Buf()
        b_pu, b_py, b_pd, b_pg, b_prk = bb[0], bb[1], bb[2], bb[3], bb[4]
        b_f1, b_f2, b_st, b_rk, b_yfb = Buf(), Buf(), Buf(), Buf(), Buf()
        b_yo = [Buf(), Buf()]
        zgsb = Buf()
        pj = [B56[:, 0:256], B56[:, 512:768]]
        PL = [B23[0:64, 0:256], B23[0:64, 512:768], B4[0:64, 0:256]]
        PLF = [B23[:, 0:256], B23[:, 512:768], B4[:, 0:256]]
        p_u = B01[0:64, 0:192].rearrange("p (h v) -> p h v", h=3)
        p_y = B01[0:64, 512:704].rearrange("p (h v) -> p h v", h=3)
        p_d = B23[0:64, 0:192].rearrange("p (h v) -> p h v", h=3)
        p_g = B23[0:64, 768:960]
        p_sc = [B56[0:64, 0:512].rearrange("p (c x) -> p c x", x=256), B56[0:64, 512:1024].rearrange("p (c x) -> p c x", x=256)]
        p_n = B4[0:64, 0:256].rearrange("p (c s) -> p c s", s=64)
        HC = 3 * NCH * 64
        p_m3 = B01[0:64, 0:HC].rearrange("p (h c s) -> p h c s", h=3, s=64)
        p_nn3 = B23[0:64, 0:HC].rearrange("p (h c s) -> p h c s", h=3, s=64)
        p_pp3 = B56[0:64, 0:HC].rearrange("p (h c s) -> p h c s", h=3, s=64)
        p_tok = bankT[0:64, 0:960].rearrange("p (h k f) -> p h k f", h=3, k=5)
        p_tA = bankT[:, 0:NCH * 64].rearrange("p (c t) -> p c t", t=64)
        p_tB = bankT[0:64, 512:512 + NCH * 64].rearrange("p (c t) -> p c t", t=64)

        def bc(ap, shape, axis):
            return ap.unsqueeze(axis).broadcast_to(shape)

        ntile = globals().get("RW_TILES", SEQ // RT)

        def load(i):
            load_xn_tile(pg, g1_d, g1_buf, xt[:, i % 2], xtb[i % 2], i // 2, rs=(2 * (i % 2), 2 * (i % 2) + 1))

        def emit_carry():
            pg.op("dve", [zb], [zb], lambda e: e.tensor_copy(out=Z[:, :, 0:1], in_=Z[:, :, RT:RT + 1]))
            pg.op("dve", [zgb], [zgb], lambda e: e.tensor_copy(out=ZG[:, 0:1], in_=ZG[:, RT:RT + 1]))

        def emit_proj(ti, gs):
            kx = ti % 2
            xr = lambda c: xt[:, kx, c, :, :].rearrange("p r t -> p (r t)")
            for g in gs:
                kk = g % 2
                M = 64 if g < 11 else 128
                c0 = g * 64
                for c in range(KC):
                    pg.op("pe", [xtb[kx], pb], [pjb[kk]],
                          lambda e, c=c, kk=kk, M=M, c0=c0, xr=xr: e.matmul(pj[kk][0:M, :], wrw[:, c, c0:c0 + M], xr(c),
                                                                           start=(c == 0), stop=(c == KC - 1)))
                if g < 11:
                    pg.op("act", [pjb[kk]], [zb], lambda e, g=g, kk=kk: e.activation(out=Z[:, g, 1:RT + 1], in_=pj[kk][0:64, :], func=AF.Copy))
                else:
                    pg.op("act", [pjb[kk]], [zgb], lambda e, kk=kk: e.activation(out=ZG[:, 1:RT + 1], in_=pj[kk][:, :], func=AF.Copy))

        load(0)
        for i in range(ntile):
            k = i % 2
            if i + 1 < ntile:
                load(i + 1)
            if i == 0:
                emit_proj(0, range(12))
            Dt = E[:, 0:11, :]
            pg.op("dve", [zb], [eb], lambda e: e.tensor_tensor(out=Dt, in0=Z[:, :, 0:RT], in1=Z[:, :, 1:RT + 1], op=ALU.subtract))
            pg.op("dve", [eb, pb], [eb], lambda e: e.tensor_tensor(out=Dt, in0=Dt, in1=bc(prm[:, MU:MU + 11], [64, 11, RT], 2), op=ALU.mult))
            pg.op("dve", [eb, zb], [zsb], lambda e: e.tensor_tensor(out=ZS[:, :, :], in0=Dt, in1=Z[:, :, 1:RT + 1], op=ALU.add))
            pg.op("pool", [zgb], [zgsb], lambda e: e.tensor_tensor(out=zgs[:, 0, :], in0=ZG[:, 0:RT], in1=ZG[:, 1:RT + 1], op=ALU.subtract))
            pg.op("dve", [zgsb, zgb, pb], [zgsb], lambda e: e.scalar_tensor_tensor(out=zgs[:, 1, :], in0=zgs[:, 0, :], scalar=mug[:, 0:1],
                                                                                  in1=ZG[:, 1:RT + 1], op0=ALU.mult, op1=ALU.add))
            R, Kx, Vx = ZS[:, 0:3, :], ZS[:, 3:6, :], ZS[:, 6:9, :]
            pg.op("act", [zsb], [b_tw], lambda e: e.activation(out=tw[:, 0, :], in_=ZS[:, 9, :], func=AF.Tanh))
            pg.op("act", [zsb], [b_tw], lambda e: e.activation(out=tw[:, 1, :], in_=ZS[:, 10, :], func=AF.Copy))
            pg.op("act", [zgsb], [b_sg], lambda e: e.activation(out=sgb[:, :], in_=zgs[:, 1, :], func=AF.Sigmoid))
            for h in range(3):
                pg.op("pe", [b_tw, pb], [plb[h]], lambda e, h=h: e.matmul(PL[h], lw[0:64, 0, h * 64:(h + 1) * 64], tw[:, 0, :], start=True, stop=True))
                pg.op("act", [plb[h], pb], [b_sig], lambda e, h=h: e.activation(out=SIG[:, h, :], in_=PL[h], func=AF.Sigmoid, bias=prm[:, W0 + h:W0 + h + 1]))
            for h in range(3):
                pg.op("pe", [b_tw, pb], [plb[h]], lambda e, h=h: e.matmul(PL[h], lw[0:64, 1, h * 64:(h + 1) * 64], tw[:, 1, :], start=True, stop=True))
                pg.op("act", [plb[h], pb], [b_al], lambda e, h=h: e.activation(out=AL[:, h, :], in_=PL[h], func=AF.Sigmoid, bias=prm[:, A0 + h:A0 + h + 1]))
            pg.op("dve", [b_sig], [b_sig], lambda e: e.tensor_scalar(out=SIG[:, :, :], in0=SIG[:, :, :], scalar1=NEG_EH, scalar2=None, op0=ALU.mult))
            pg.op("dve", [zsb, pb], [b_t1], lambda e: e.tensor_tensor(out=T1[:, :, :], in0=Kx, in1=bc(prm[:, KK_:KK_ + 3], [64, 3, RT], 2), op=ALU.mult))
            pg.op("pool", [b_t1], [b_t2], lambda e: e.tensor_tensor(out=T2[0:64, :, :], in0=T1[:, :, :], in1=T1[:, :, :], op=ALU.mult))
            for h in range(3):
                pg.op("pe", [b_t2, pb], [plb[h]], lambda e, h=h: e.matmul(PLF[h], onesblk[:, :], T2[:, h, :], start=True, stop=True))
                pg.op("act", [plb[h]], [b_kkn], lambda e, h=h: e.activation(out=KKN[:, h, :], in_=PL[h], func=AF.Sqrt))
            pg.op("dve", [b_kkn], [b_kkn], lambda e: e.tensor_scalar(out=KKN[:, :, :], in0=KKN[:, :, :], scalar1=1e-12, scalar2=None, op0=ALU.max))
            pg.op("dve", [b_kkn], [b_kkn], lambda e: e.reciprocal(out=KKN[:, :, :], in_=KKN[:, :, :]))
            pg.op("dve", [b_kkn, b_t1], [b_kkn], lambda e: e.tensor_tensor(out=KKN[:, :, :], in0=KKN[:, :, :], in1=T1[:, :, :], op=ALU.mult))
            pg.op("dve", [b_al, pb], [b_kp], lambda e: e.tensor_tensor(out=KP[:, :, :], in0=AL[:, :, :], in1=bc(prm[:, KA:KA + 3], [64, 3, RT], 2), op=ALU.mult))
            pg.op("dve", [b_kp, pb], [b_kp], lambda e: e.tensor_tensor(out=KP[:, :, :], in0=KP[:, :, :], in1=bc(prm[:, OMKA:OMKA + 3], [64, 3, RT], 2), op=ALU.add))
            pg.op("dve", [b_kp, zsb], [b_kp], lambda e: e.tensor_tensor(out=KP[:, :, :], in0=KP[:, :, :], in1=Kx, op=ALU.mult))
            pg.op("pool", [zsb, b_kp, b_t2], [b_t2], lambda e: e.tensor_tensor(out=T2[0:64, :, :], in0=R, in1=KP[:, :, :], op=ALU.mult))
            pg.op("pool", [b_t2, pb], [b_fm], lambda e: e.tensor_tensor(out=FM[:, :, 4, :], in0=T2[0:64, :, :], in1=bc(prm[:, RK:RK + 3], [64, 3, RT], 2), op=ALU.mult))
            pg.op("dve", [b_kkn, b_al, b_t1], [b_t1], lambda e: e.tensor_tensor(out=T1[:, :, :], in0=KKN[:, :, :], in1=AL[:, :, :], op=ALU.mult))
            for h in range(3):
                pg.op("dve", [b_sig, pb], [b_L], lambda e, h=h: e.tensor_tensor_scan(out=L[:, h, :], data0=rmask[:, :], data1=SIG[:, h, :], initial=0.0,
                                                                                      op0=ALU.mult, op1=ALU.add))
            Pin, Pex, Pinv, PCs = E[:, 0:3, :], E[:, 3:6, :], E[:, 6:9, :], E[:, 9:12, :]
            Lc = L[:, :, :].rearrange("p h (c t) -> p h c t", t=64)
            pg.op("act", [b_L], [eb], lambda e: e.activation(out=Pin, in_=L[:, :, :], func=AF.Exp))
            pg.op("act", [b_L], [eb], lambda e: e.activation(out=Pinv, in_=L[:, :, :], func=AF.Exp, scale=-1.0))
            pg.op("dve", [b_L, b_sig, eb], [eb], lambda e: e.tensor_tensor(out=Pex, in0=L[:, :, :], in1=SIG[:, :, :], op=ALU.subtract))
            pg.op("act", [eb], [eb], lambda e: e.activation(out=Pex, in_=Pex, func=AF.Exp))
            pg.op("dve", [b_L, eb], [eb], lambda e: e.tensor_tensor(out=PCs.rearrange("p h (c t) -> p h c t", t=64),
                                                                   in0=Lc[:, :, :, 63:64].broadcast_to([64, 3, NCH, 64]), in1=Lc, op=ALU.subtract))
            pg.op("act", [eb], [eb], lambda e: e.activation(out=PCs, in_=PCs, func=AF.Exp))
            pg.op("act", [b_L], [b_pc], lambda e: e.activation(out=PCt[:, :, :], in_=Lc[:, :, :, 63], func=AF.Exp))
            arv = arT[:, :, :, :, :]
            pg.op("dve", [b_kkn, eb], [b_ar], lambda e: e.scalar_tensor_tensor(out=arT[:, :, :, 0, :], in0=KKN[:, :, :].rearrange("p h (c t) -> p h c t", t=64), scalar=-1.0,
                                                                             in1=Pex.rearrange("p h (c t) -> p h c t", t=64), op0=ALU.mult, op1=ALU.mult))
            pg.op("dve", [zsb, eb], [b_ar], lambda e: e.tensor_tensor(out=arT[:, :, :, 1, :], in0=R.rearrange("p h (c t) -> p h c t", t=64),
                                                                    in1=Pin.rearrange("p h (c t) -> p h c t", t=64), op=ALU.mult))
            pg.op("dve", [b_t1, eb], [b_bk], lambda e: e.tensor_tensor(out=bkT[:, :, :, 0, :], in0=T1[:, :, :].rearrange("p h (c t) -> p h c t", t=64),
                                                                     in1=Pinv.rearrange("p h (c t) -> p h c t", t=64), op=ALU.mult))
            pg.op("dve", [b_kp, eb], [b_bk], lambda e: e.tensor_tensor(out=bkT[:, :, :, 1, :], in0=KP[:, :, :].rearrange("p h (c t) -> p h c t", t=64),
                                                                     in1=Pinv.rearrange("p h (c t) -> p h c t", t=64), op=ALU.mult))
            pg.op("pool", [b_t1, eb], [b_fm], lambda e: e.tensor_tensor(out=FM[:, :, 0, :], in0=T1[:, :, :], in1=PCs, op=ALU.mult))
            pg.op("pool", [b_kp, eb], [b_fm], lambda e: e.tensor_tensor(out=FM[:, :, 1, :], in0=KP[:, :, :], in1=PCs, op=ALU.mult))
            pg.op("pool", [b_ar], [b_fm], lambda e: e.tensor_copy(out=FM[:, :, 2, :].rearrange("p h (c t) -> p h c t", t=64), in_=arT[:, :, :, 0, :]))
            pg.op("pool", [zsb], [b_fm], lambda e: e.tensor_copy(out=FM[:, :, 3, :], in_=Vx))
            for c in range(NCH):
                for h in range(3):
                    for kd in range(5):
                        pg.op("pe", [b_fm, pb], [b_ptr], lambda e, c=c, h=h, kd=kd: e.transpose(p_tok[:, h, kd, :], FM[:, h, kd, c * 64:(c + 1) * 64], idb[:, :]))
                pg.op("act", [b_ptr], [b_tok], lambda e, c=c: e.activation(out=TOK[:, c, :, :, :], in_=p_tok, func=AF.Copy))
            pg.op("dve", [b_tok], [b_rk], lambda e: e.tensor_reduce(out=rk[:, :, :], in_=TOK[:, :, :, 4, :], axis=AX.X, op=ALU.add))
            for h in range(3):
                for half in range(NCH // 2):
                    for cc in range(2):
                        c = half * 2 + cc
                        for kd in range(2):
                            pg.op("pe", [b_bk, b_ar], [b_scp[half]],
                                  lambda e, h=h, c=c, cc=cc, kd=kd, half=half: e.matmul(p_sc[half][:, cc, kd * 128:(kd + 1) * 128], bkT[:, h, c, kd, :],
                                                                                        arT[:, h, c, :, :].rearrange("p a t -> p (a t)"), start=True, stop=True))
                    pg.op("dve", [b_scp[half], pb], [b_scm],
                          lambda e, h=h, half=half: e.tensor_tensor(out=SCm[:, h, half * 2:half * 2 + 2, :, :, :].rearrange("p c k a t -> p (c k) a t"),
                                                                    in0=p_sc[half].rearrange("p c (k a t) -> p (c k) a t", k=2, a=2),
                                                                    in1=msk[:, 0:2, :].unsqueeze(1).broadcast_to([64, 4, 2, 64]), op=ALU.mult))
                for c in range(NCH):
                    pg.op("pe", [b_bk, b_ar], [b_np], lambda e, h=h, c=c: e.matmul(p_n[:, c, :], arT[:, h, c, 0, :], bkT[:, h, c, 0, :], start=True, stop=True))
                pg.op("dve", [b_np, pb], [b_nl], lambda e, h=h: e.tensor_tensor(out=NLt[:, h, :, :], in0=p_n, in1=bc(msk[:, 2, :], [64, NCH, 64], 1), op=ALU.mult))
            hcs = [(h, c) for h in range(3) for c in range(NCH)]
            Mc = lambda h, c: SCm[:, h, c, 0, 0, :]
            Nc = lambda h, c: NLt[:, h, c, :]
            pg.op("dve", [b_scm, pb], [b_tt[0]],
                  lambda e: e.tensor_tensor(out=Tt[:, 0, :, :, :], in0=SCm[:, :, :, 0, 0, :], in1=idb[:, :].unsqueeze(1).unsqueeze(1).broadcast_to([64, 3, NCH, 64]), op=ALU.add))
            tcur = 0
            mrd, nrd = [b_scm], [b_nl]
            for lev in range(5):
                j = lev % 2
                last = (lev == 4)
                for (h, c) in hcs:
                    pg.op("pe", mrd + nrd, b_pn, lambda e, h=h, c=c, Mc=Mc, Nc=Nc: e.matmul(p_nn3[:, h, c, :], Mc(h, c), Nc(h, c), start=True, stop=True))
                if not last:
                    for (h, c) in hcs:
                        pg.op("pe", mrd + nrd, b_pm, lambda e, h=h, c=c, Mc=Mc, Nc=Nc: e.matmul(p_m3[:, h, c, :], Nc(h, c), Mc(h, c), start=True, stop=True))
                pg.op("act", b_pn, [b_nj[j]], lambda e, j=j: e.activation(out=NJ[:, j, :, :, :], in_=p_nn3, func=AF.Copy))
                if not last:
                    pg.op("dve", b_pm, [b_mj[j]], lambda e, j=j: e.tensor_copy(out=MJ[:, j, :, :, :], in_=p_m3))
                Mc = lambda h, c, j=j: MJ[:, j, h, c, :]
                Nc = lambda h, c, j=j: NJ[:, j, h, c, :]
                mrd, nrd = [b_mj[j]], [b_nj[j]]
                for (h, c) in hcs:
                    pg.op("pe", [b_nj[j], b_tt[tcur]], b_pp, lambda e, h=h, c=c, j=j, tcur=tcur: e.matmul(p_pp3[:, h, c, :], NJ[:, j, h, c, :], Tt[:, tcur, h, c, :], start=True, stop=True))
                pg.op("dve", b_pp + [b_tt[tcur]], [b_tt[1 - tcur]], lambda e, tcur=tcur: e.tensor_tensor(out=Tt[:, 1 - tcur, :, :, :], in0=p_pp3, in1=Tt[:, tcur, :, :, :], op=ALU.add))
                tcur = 1 - tcur
            for (h, c) in hcs:
                pg.op("pe", [b_scm, b_tok], b_pm, lambda e, c=c, h=h: e.matmul(p_m3[:, h, c, :], SCm[:, h, c, 1, 0, :], TOK[:, c, h, 3, :], start=True, stop=True))
            pg.op("act", b_pm, [b_akv], lambda e: e.activation(out=AkVb[:, :, :, :], in_=p_m3, func=AF.Copy))
            for (h, c) in hcs:
                pg.op("pe", [b_tt[tcur], b_akv], b_pn, lambda e, c=c, h=h, tcur=tcur: e.matmul(p_nn3[:, h, c, :], Tt[:, tcur, h, c, :], AkVb[:, h, c, :], start=True, stop=True))
            pg.op("act", b_pn, [b_uvs], lambda e: e.activation(out=UVs[:, :, :, :].rearrange("p c h v -> p h c v"), in_=p_nn3, func=AF.Copy))
            for (h, c) in hcs:
                pg.op("pe", [b_tt[tcur], b_tok], b_pp, lambda e, c=c, h=h, tcur=tcur: e.matmul(p_pp3[:, h, c, :], TOK[:, c, h, 2, :], Tt[:, tcur, h, c, :], start=True, stop=True))
            pg.op("dve", b_pp, [b_apt], lambda e: e.tensor_copy(out=ApT[:, :, :, :], in_=p_pp3))
            if i + 1 < ntile:
                emit_carry()
            for c in range(NCH):
                for h in range(3):
                    pg.op("pe", [b_apt, b_Sb], [b_pu], lambda e, c=c, h=h: e.matmul(p_u[:, h, :], ApT[:, h, c, :], Sb_[:, h, :], start=True, stop=True))
                if i + 1 < ntile:
                    emit_proj(i + 1, range(c * (12 // NCH), (c + 1) * (12 // NCH)))
                pg.op("dve", [b_pu, b_uvs], [b_ubf], lambda e, c=c: e.tensor_tensor(out=Ubf[:, :, :], in0=p_u, in1=UVs[:, c, :, :], op=ALU.add))
                for h in range(3):
                    pg.op("pe", [b_ar, b_Sb], [b_py], lambda e, c=c, h=h: e.matmul(p_y[:, h, :], arT[:, h, c, 1, :], Sb_[:, h, :], start=True, stop=False))
                    pg.op("pe", [b_scm, b_tok], [b_py], lambda e, c=c, h=h: e.matmul(p_y[:, h, :], SCm[:, h, c, 1, 1, :], TOK[:, c, h, 3, :], start=False, stop=False))
                    pg.op("pe", [b_scm, b_ubf], [b_py], lambda e, c=c, h=h: e.matmul(p_y[:, h, :], SCm[:, h, c, 0, 1, :], Ubf[:, h, :], start=False, stop=True))
                for h in range(3):
                    pg.op("pe", [b_tok, b_ubf], [b_pd], lambda e, c=c, h=h: e.matmul(p_d[:, h, :], TOK[:, c, h, 0, :], Ubf[:, h, :], start=True, stop=False))
                    pg.op("pe", [b_tok], [b_pd], lambda e, c=c, h=h: e.matmul(p_d[:, h, :], TOK[:, c, h, 1, :], TOK[:, c, h, 3, :], start=False, stop=True))
                pg.op("act", [b_py], [b_yt], lambda e, c=c: e.activation(out=Yt[:, c, :, :], in_=p_y, func=AF.Copy))
                pg.op("dve", [b_S, b_pc], [b_S], lambda e, c=c: e.tensor_tensor(out=S[:, :, :], in0=S[:, :, :], in1=bc(PCt[:, :, c], [64, 3, 64], 2), op=ALU.mult))
                pg.op("dve", [b_S, b_pd], [b_S], lambda e: e.tensor_tensor(out=S[:, :, :], in0=S[:, :, :], in1=p_d, op=ALU.add))
                pg.op("act", [b_S], [b_Sb], lambda e: e.activation(out=Sb_[:, :, :], in_=S[:, :, :], func=AF.Copy))
            Y3 = Yt[:, :, :, :]
            sh = [64, NCH, 3, 64]
            pg.op("dve", [b_yt], [b_st], lambda e: e.tensor_reduce(out=st[:, 0, :, :], in_=Y3, axis=AX.X, op=ALU.add))
            pg.op("dve", [b_st], [b_st], lambda e: e.tensor_scalar(out=st[:, 0, :, :], in0=st[:, 0, :, :], scalar1=1.0 / 64, scalar2=None, op0=ALU.mult))
            pg.op("dve", [b_yt, b_st], [b_f1], lambda e: e.tensor_tensor(out=F1[:, :, :, :], in0=Y3, in1=bc(st[:, 0, :, :], sh, 3), op=ALU.subtract))
            pg.op("pool", [b_f1], [b_f2], lambda e: e.tensor_tensor(out=F2[:, :, :, :], in0=F1[:, :, :, :], in1=F1[:, :, :, :], op=ALU.mult))
            pg.op("dve", [b_f2], [b_st], lambda e: e.tensor_reduce(out=st[:, 1, :, :], in_=F2[:, :, :, :], axis=AX.X, op=ALU.add))
            pg.op("act", [b_st, consts["buf"]], [b_st], lambda e: e.activation(out=st[:, 2, :, :], in_=st[:, 1, :, :], func=AF.Sqrt, scale=1.0 / 64, bias=consts["epsgn"][0:64, :]))
            pg.op("dve", [b_st], [b_st], lambda e: e.reciprocal(out=st[:, 3, :, :], in_=st[:, 2, :, :]))
            pg.op("dve", [b_f1, b_st], [b_f1], lambda e: e.tensor_tensor(out=F1[:, :, :, :], in0=F1[:, :, :, :], in1=bc(st[:, 3, :, :], sh, 3), op=ALU.mult))
            pg.op("dve", [b_f1, pb], [b_f1], lambda e: e.tensor_tensor(out=F1[:, :, :, :], in0=F1[:, :, :, :], in1=bc(gn[:, 0, :, :], sh, 1), op=ALU.mult))
            pg.op("dve", [b_f1, pb], [b_f1], lambda e: e.tensor_tensor(out=F1[:, :, :, :], in0=F1[:, :, :, :], in1=bc(gn[:, 1, :, :], sh, 1), op=ALU.add))
            pg.op("pool", [b_tok, b_rk, b_f2], [b_f2], lambda e: e.tensor_tensor(out=F2[:, :, :, :], in0=TOK[:, :, :, 3, :], in1=bc(rk[:, :, :], sh, 3), op=ALU.mult))
            pg.op("dve", [b_f1, b_f2], [b_f1], lambda e: e.tensor_tensor(out=F1[:, :, :, :], in0=F1[:, :, :, :], in1=F2[:, :, :, :], op=ALU.add))
            for c in range(NCH):
                pg.op("pe", [b_sg, pb], [b_pg], lambda e, c=c: e.matmul(p_g[:, 0:192], sgb[:, c * 64:(c + 1) * 64], lw[:, 2, :], start=True, stop=True))
                pg.op("dve", [b_pg, b_f1], [b_yfb], lambda e, c=c: e.tensor_tensor(out=yfb[:, c, :], in0=F1[:, c, :, :].rearrange("p h v -> p (h v)"), in1=p_g[:, 0:192], op=ALU.mult))
            for c in range(NCH):
                pg.op("pe", [b_yfb, pb], [b_ptr], lambda e, c=c: e.transpose(p_tA[:, c, :], yfb[:, c, 0:128], idb[:, :]))
                pg.op("pe", [b_yfb, pb], [b_ptr], lambda e, c=c: e.transpose(p_tB[:, c, :], yfb[:, c, 128:192], idb[:, :]))
            pg.op("act", [b_ptr], [b_yo[k]], lambda e, k=k: e.activation(out=yoA[:, k, :].rearrange("p (c t) -> p c t", t=64), in_=p_tA, func=AF.Copy))
            pg.op("act", [b_ptr], [b_yo[k]], lambda e, k=k: e.activation(out=yoB[:, k, :].rearrange("p (c t) -> p c t", t=64), in_=p_tB, func=AF.Copy))
            pg.dma("sp", pay2_d[0][:, i * RT:(i + 1) * RT], yoA[:, k, :], [b_yo[k]], [pay2_buf])
            pg.dma("sp", pay2_d[1][:, i * RT:(i + 1) * RT], yoB[:, k, :], [b_yo[k]], [pay2_buf])
        pg.barrier()


NPIECE = 18


def phase_o(pg, nc, consts, x_d, x_dbuf, yda_d, yda_buf, g2_d, g2_buf, sel_d, wglu_d, bglu_d, wout_d, gpost_sb):
    with ExitStack() as es0:
        oT = sb(nc, es0, "o_oT", [128, KC, NT], F32)
        ob = [Buf() for _ in range(KC)]
        with ExitStack() as es:
            yT = sb(nc, es, "o_yT", [128, NPIECE, NT], BF16)
            yb = [Buf() for _ in range(NPIECE)]
            ygT = sb(nc, es, "o_ygT", [128, 4, NT], BF16)
            ygb = [Buf() for _ in range(4)]
            stg = sb(nc, es, "o_stg", [128, 2, 8, 4, 128], BF16)
            stb = [Buf(), Buf()]
            acc = sb(nc, es, "o_acc", [128, 2, NT], F32)
            accb = [Buf(), Buf()]
            sel = sb(nc, es, "o_sel", [128, 4], F32)
            wglu = sb(nc, es, "o_wglu", [128, 4, 512], BF16)
            bglu = sb(nc, es, "o_bglu", [128, 4], F32)
            gate = sb(nc, es, "o_gate", [128, 2, 512], F32)
            gtb = [Buf(), Buf()]
            NW = 2
            wt = sb(nc, es, "o_wt", [128, NW, NPIECE * 128], BF16)
            wtb = [Buf() for _ in range(NW)]
            pp = [ps(nc, es, "o_pp%d" % i, [128, 512]) for i in range(4)]
            ppb = [Buf() for _ in range(4)]
            pb = Buf()
            pg.dma("sp", sel[:, :], sel_d, [], [pb])
            pg.dma("pool", wglu[:, :, :], wglu_d, [], [pb])
            pg.dma("sp", bglu[:, :], bglu_d, [], [pb])
            for h in range(6):
                pg.dma("sp", yT[:, 8 + h, :], yda_d[h], [yda_buf], [yb[8 + h]])
            it = 0

            def select(rp, kchunk, nrows, dst, dstb):
                nonlocal it
                k = it % 2
                it += 1
                pg.dma("sp", stg[0:nrows, k, :, :, :], g2_d[kchunk][rp].rearrange("p (m r t) -> p m r t", r=4, t=128),
                       (list(g2_buf) if isinstance(g2_buf, (list, tuple)) else [g2_buf]), [stb[k]])
                a = acc[0:nrows, k, :].rearrange("p (m t) -> p m t", t=128)
                pg.op("dve", [stb[k], pb], [accb[k]],
                      lambda e, k=k, a=a, nrows=nrows: e.tensor_scalar(out=a, in0=stg[0:nrows, k, :, 0, :], scalar1=sel[0:nrows, 0:1], scalar2=None, op0=ALU.mult))
                for r in range(1, 4):
                    pg.op("dve", [stb[k], pb, accb[k]], [accb[k]],
                          lambda e, k=k, a=a, r=r, nrows=nrows: e.scalar_tensor_tensor(out=a, in0=stg[0:nrows, k, :, r, :], scalar=sel[0:nrows, r:r + 1],
                                                                                       in1=a, op0=ALU.mult, op1=ALU.add))
                pg.op("act", [accb[k]], [dstb], lambda e, k=k, nrows=nrows, dst=dst: e.activation(out=dst[0:nrows, :], in_=acc[0:nrows, k, :], func=AF.Copy))

            for rp in range(4):
                select(rp, 0, 128, yT[:, 2 * rp, :], yb[2 * rp])
                select(rp, 1, 64, yT[:, 2 * rp + 1, :], yb[2 * rp + 1])
                select(rp, 2, 128, ygT[:, rp, :], ygb[rp])
            git = 0
            for cc in range(4):
                for half in range(2):
                    k = git % 4
                    gk = git % 2
                    git += 1
                    tsl = slice(half * 512, (half + 1) * 512)
                    for rp in range(4):
                        pg.op("pe", [ygb[rp], pb], [ppb[k]],
                              lambda e, rp=rp, cc=cc, k=k, tsl=tsl: e.matmul(pp[k][:, :], wglu[:, rp, cc * 128:(cc + 1) * 128], ygT[:, rp, tsl],
                                                                            start=(rp == 0), stop=(rp == 3)))
                    pg.op("act", [ppb[k], pb], [gtb[gk]],
                          lambda e, k=k, gk=gk, cc=cc: e.activation(out=gate[:, gk, :], in_=pp[k][:, :], func=AF.Sigmoid, bias=bglu[:, cc:cc + 1]))
                    pg.op("dve", [gtb[gk], ygb[cc]], [yb[14 + cc]],
                          lambda e, gk=gk, cc=cc, tsl=tsl: e.tensor_tensor(out=yT[:, 14 + cc, tsl], in0=ygT[:, cc, tsl], in1=gate[:, gk, :], op=ALU.mult))
            def load_w(c):
                pg.dma("pool", wt[:, c % NW, :], wout_d[c], [], [wtb[c % NW]])

            load_w(0)
            for c in range(KC):
                if c + 1 < KC:
                    load_w(c + 1)
                s = c % NW
                for half in range(2):
                    k = git % 4
                    git += 1
                    tsl = slice(half * 512, (half + 1) * 512)
                    for pi in range(NPIECE):
                        kr = 64 if (pi < 8 and pi % 2 == 1) else 128
                        pg.op("pe", [wtb[s], yb[pi]], [ppb[k]],
                              lambda e, pi=pi, kr=kr, s=s, k=k, tsl=tsl: e.matmul(pp[k][:, :], wt[0:kr, s, pi * 128:(pi + 1) * 128], yT[0:kr, pi, tsl],
                                                                                 start=(pi == 0), stop=(pi == NPIECE - 1)))
                    pg.op("act", [ppb[k]], [ob[c]],
                          lambda e, k=k, c=c, tsl=tsl: e.activation(out=oT[:, c, tsl], in_=pp[k][:, :], func=AF.Copy))
            pg.barrier()
        emit_postnorm_residual(pg, nc, consts, oT, ob, x_d, x_dbuf, gpost_sb)


L_DEPTH = 2
GROUPS = [[0, 1, 2, 3], [4, 5, 6, 7]]
PAY2_CHUNK_ROWS = [128, 64, 128]
DATA_GROUPS = {
    "pay": [([r, NT], BF16) for r in PAY_CHUNK_ROWS],
    "g1": [([4 * r, NT], BF16) for r in PAY_CHUNK_ROWS],
    "pay2": [([r, SEQ], BF16) for r in PAY2_CHUNK_ROWS],
    "g2": [([4 * r, SEQ], BF16) for r in PAY2_CHUNK_ROWS],
    "q": [([6, 128, NT], BF16)],
    "yda": [([6, 128, NT], BF16)],
}
WEIGHT_SHAPES = {
    "wgu1": [NF, 128, 2 * KC * 128], "wd1": [KC, 128, NF * 128], "wgu2": [NF, 128, 2 * KC * 128], "wd2": [KC, 128, NF * 128],
    "wqk": [12, 128, KC * 128], "wv": [128, KC * 768], "wrw": [128, KC, 832], "rwp": [64, 26], "rwpg": [128, 1],
    "rwl": [128, 3, 192], "rwgn": [64, 2, 3, 64], "wss": [128, KC * 128], "s5p": [128, 4, 3], "s5b": [128, 4, 2, 16],
    "s5c": [128, 4, 2, 16], "s5d": [128, 1], "lamp": [128, 4, 64], "subw": [128, 128], "wglu": [128, 4, 512],
    "bglu": [128, 4], "wout": [KC, 128, NPIECE * 128],
}
CONST_SHAPES = {"ident": [128, 128], "ramp": [128, TT + 1], "rwm": [64, 3, 64], "sel": [128, 4], "damask": [128, 4, 128],
                "gains": [128, L_DEPTH * 6, KC]}


def lam_init_of(l):
    return 0.8 - 0.6 * math.exp(-0.3 * l)


def build_launch(seq, ins, outs, uses_x):
    nc = bass.Bass("TRN2", target_bir_lowering=False)
    declared = {}

    def ext_in(name, shape, dt=F32):
        if name not in declared:
            declared[name] = nc.dram_tensor(name, list(shape), dt, kind="ExternalInput").ap()
        return declared[name]

    def W(name, l):
        return ext_in("%s_%d" % (name, l), WEIGHT_SHAPES[name])

    def Cn(name):
        return ext_in(name, CONST_SHAPES[name])

    data = {}
    dbuf = {}
    data_in_names = []
    for name, members in DATA_GROUPS.items():
        kind = "ExternalInput" if name in ins else ("ExternalOutput" if name in outs else "Internal")
        data[name] = []
        for k, (shape, dt) in enumerate(members):
            nm = "%s%d" % (name, k)
            data[name].append(nc.dram_tensor(nm, list(shape), dt, kind=kind).ap())
            if name in ins:
                data_in_names.append(nm)
        dbuf[name] = Buf()
    for nm in ("pay_xn", "pay_kv", "g1_xn", "g1_kv", "pay2_rw", "pay2_ss", "g2_rw", "g2_ss"):
        dbuf[nm] = Buf()
    fused = any(ph.startswith("gather") for ph, _ in seq)
    g1v = g1_view(data["g1"])
    g2v = g1_view(data["g2"])
    q_d = data["q"][0]
    yda_d = data["yda"][0]
    with ExitStack() as es:
        pg = Prog(nc, es)
        consts = setup_consts(pg, nc, es, Cn("ident"))
        gains = sb(nc, es, "gains_sb", [128, L_DEPTH * 6, KC], F32)
        pg.dma("sp", gains[:, :, :], Cn("gains"), [], [consts["buf"]])
        for l in range(L_DEPTH):
            for i in (1, 5):
                pg.op("dve", [consts["buf"]], [consts["buf"]],
                      lambda e, l=l, i=i: e.tensor_scalar(out=gains[:, l * 6 + i, :], in0=gains[:, l * 6 + i, :], scalar1=0.5,
                                                          scalar2=None, op0=ALU.mult))
        xb = Buf()
        x_d = None
        if uses_x:
            x_in = ext_in("x_in", [KC, 128, NT])
            x_d = nc.dram_tensor("x_out", [KC, 128, NT], F32, kind="ExternalOutput").ap()
            pg.dma("sp", x_d, x_in, [], [xb])
        pg.barrier()
        G = lambda l, i: gains[:, l * 6 + i, :]
        for (ph, l) in seq:
            if ph == "ffn1":
                phase_ffn(pg, nc, consts, x_d, xb, W("wgu1", l), W("wd1", l), G(l, 0), G(l, 1))
            elif ph == "ffn2":
                phase_ffn(pg, nc, consts, x_d, xb, W("wgu2", l), W("wd2", l), G(l, 4), G(l, 5))
            elif ph == "p":
                if fused:
                    phase_p(pg, nc, consts, x_d, xb, G(l, 2), W("wqk", l), W("wv", l), data["pay"], dbuf["pay_kv"], q_d, dbuf["q"],
                            pay_xn_buf=dbuf["pay_xn"],
                            after_xn=lambda: emit_allgather(pg, nc, data["pay"][0:4], dbuf["pay_xn"], data["g1"][0:4], dbuf["g1_xn"]))
                    emit_allgather(pg, nc, data["pay"][4:8], dbuf["pay_kv"], data["g1"][4:8], dbuf["g1_kv"])
                else:
                    phase_p(pg, nc, consts, x_d, xb, G(l, 2), W("wqk", l), W("wv", l), data["pay"], dbuf["pay"], q_d, dbuf["q"])
            elif ph == "gather":
                pass
            elif ph == "da":
                phase_da(pg, nc, consts, g1v, dbuf["g1_kv"] if fused else dbuf["g1"], q_d, dbuf["q"], Cn("damask"), W("lamp", l), W("subw", l),
                         lam_init_of(l), yda_d, dbuf["yda"])
            elif ph == "rwkv":
                phase_rwkv(pg, nc, consts, g1v, dbuf["g1_xn"] if fused else dbuf["g1"], W("wrw", l), W("rwp", l), W("rwpg", l), W("rwl", l), W("rwgn", l),
                           Cn("rwm"), data["pay2"], dbuf["pay2_rw"] if fused else dbuf["pay2"])
                if fused:
                    emit_allgather(pg, nc, data["pay2"][0:2], dbuf["pay2_rw"], data["g2"][0:2], dbuf["g2_rw"])
            elif ph == "s5":
                phase_s5(pg, nc, consts, g1v, dbuf["g1_xn"] if fused else dbuf["g1"], W("wss", l), W("s5p", l), W("s5b", l), W("s5c", l), W("s5d", l),
                         Cn("ramp"), data["pay2"], dbuf["pay2_ss"] if fused else dbuf["pay2"])
                if fused:
                    emit_allgather(pg, nc, data["pay2"][2:3], dbuf["pay2_ss"], data["g2"][2:3], dbuf["g2_ss"])
            elif ph == "o":
                phase_o(pg, nc, consts, x_d, xb, yda_d, dbuf["yda"], g2v, [dbuf["g2_rw"], dbuf["g2_ss"]] if fused else dbuf["g2"],
                        Cn("sel"), W("wglu", l), W("bglu", l),
                        W("wout", l), G(l, 3))
            else:
                raise ValueError(ph)
        pg.barrier()
    in_names = list(declared.keys()) + data_in_names
    return nc, in_names


def emit_allgather(pg, nc, src, src_buf, dst, dst_buf):
    pg._sync("pool", [src_buf], [dst_buf])
    for s_ap, d_ap in zip(src, dst):
        ins = nc.gpsimd.collective_compute("AllGather", ALU.bypass, replica_groups=GROUPS, ins=[s_ap], outs=[d_ap])
        pg.cc_cnt += 1
        ins.then_inc(pg.sem["cc"], 1)
    src_buf.r["cc"] = pg.cc_cnt
    dst_buf.w = ("cc", pg.cc_cnt)
    dst_buf.r = {}


def _tile_gu(w_gu):
    return np.ascontiguousarray(w_gu.reshape(KC, 128, 2, NF, 128).transpose(3, 1, 2, 0, 4)).reshape(NF, 128, 2 * KC * 128)


def _tile_down(w_down):
    return np.ascontiguousarray(w_down.reshape(NF, 128, KC, 128).transpose(2, 1, 0, 3)).reshape(KC, 128, NF * 128)


def _pm(v):
    return np.ascontiguousarray(np.asarray(v).reshape(KC, 128).T)


def host_shared(inp, l):
    w_in = inp["w_in"][l]
    out = {}
    out["wgu1"] = _tile_gu(inp["ffn1_w_gu"][l])
    out["wd1"] = _tile_down(inp["ffn1_w_down"][l])
    out["wgu2"] = _tile_gu(inp["ffn2_w_gu"][l])
    out["wd2"] = _tile_down(inp["ffn2_w_down"][l])
    c0 = 2560
    wqk = np.empty((12, 128, KC * 128), np.float32)
    for i in range(12):
        col = c0 + i * 128
        wqk[i] = w_in[:, col:col + 128].reshape(KC, 128, 128).transpose(1, 0, 2).reshape(128, KC * 128)
    out["wqk"] = wqk
    out["wv"] = np.ascontiguousarray(w_in[:, c0 + 1536:c0 + 2304].reshape(KC, 128, 768).transpose(1, 0, 2)).reshape(128, KC * 768)
    out["lamp"] = np.ascontiguousarray(np.broadcast_to(
        np.stack([inp["da_lq1"][l], inp["da_lk1"][l], inp["da_lq2"][l], inp["da_lk2"][l]])[None], (128, 4, 64)))
    out["subw"] = np.ascontiguousarray(np.broadcast_to(inp["da_subln_w"][l][None], (128, 128)))
    out["wglu"] = np.ascontiguousarray(inp["ssm_w_glu"][l].reshape(4, 128, 512).transpose(1, 0, 2))
    out["bglu"] = np.ascontiguousarray(inp["ssm_b_glu"][l].reshape(4, 128).T)
    w_out = inp["w_out"][l]
    pieces = []
    for rp in range(4):
        pieces += [(rp * 192, 128), (rp * 192 + 128, 64)]
    pieces += [(768 + h * 128, 128) for h in range(6)] + [(1536 + c * 128, 128) for c in range(4)]
    wo = np.zeros((KC, 128, NPIECE, 128), np.float32)
    for pi, (r0, n) in enumerate(pieces):
        wo[:, :n, pi, :] = w_out[r0:r0 + n, :].reshape(n, KC, 128).transpose(1, 0, 2)
    out["wout"] = wo.reshape(KC, 128, NPIECE * 128)
    return out


def host_percore(inp, l, j):
    w_in = inp["w_in"][l]
    mu = inp["rw_mu"][l]
    hs = [3 * j, 3 * j + 1, 3 * j + 2]
    cols = []
    for base in (0, 768, 1536):
        for h in hs:
            cols += list(range(base + h * 64, base + (h + 1) * 64))
    cols += list(range(2304, 2560))
    cols = np.array(cols)
    out = {}
    out["wrw"] = np.ascontiguousarray(w_in[:, cols].reshape(KC, 128, 832).transpose(1, 0, 2))
    prm = np.zeros((64, 26), np.float32)
    for g in range(11):
        prm[:, g] = mu[cols[g * 64:(g + 1) * 64]]
    for i, h in enumerate(hs):
        sl = slice(h * 64, (h + 1) * 64)
        prm[:, 11 + i] = inp["rw_w0"][l][sl]
        prm[:, 14 + i] = inp["rw_a0"][l][sl]
        prm[:, 17 + i] = inp["rw_k_k"][l][sl]
        prm[:, 20 + i] = inp["rw_k_a"][l][sl]
        prm[:, 23 + i] = inp["rw_r_k"][l][h]
    out["rwp"] = prm
    out["rwpg"] = np.ascontiguousarray(mu[2432:2560].reshape(128, 1))
    own = np.array(sum([list(range(h * 64, (h + 1) * 64)) for h in hs], []))
    rwl = np.zeros((128, 3, 192), np.float32)
    rwl[:64, 0] = inp["rw_w2"][l][:, own]
    rwl[:64, 1] = inp["rw_a2"][l][:, own]
    rwl[:, 2] = inp["rw_g2"][l][:, own]
    out["rwl"] = rwl
    gn = np.zeros((64, 2, 3, 64), np.float32)
    gn[:, 0] = inp["rw_gn_w"][l][own].reshape(3, 64)[None]
    gn[:, 1] = inp["rw_gn_b"][l][own].reshape(3, 64)[None]
    out["rwgn"] = gn
    out["wss"] = np.ascontiguousarray(
        w_in[:, 4864 + j * 128:4864 + (j + 1) * 128].reshape(KC, 128, 128).transpose(1, 0, 2)).reshape(128, KC * 128)
    s5p = np.zeros((128, 4, 3), np.float32)
    s5b = np.zeros((128, 4, 2, 16), np.float32)
    s5c = np.zeros((128, 4, 2, 16), np.float32)
    for q in range(4):
        for half in range(2):
            g = 8 * j + 2 * q + half
            ps_ = slice(half * 64, (half + 1) * 64)
            s5p[ps_, q, 0] = inp["ssm_a_re"][l][g]
            s5p[ps_, q, 1] = inp["ssm_a_im"][l][g]
            s5p[ps_, q, 2] = inp["ssm_log_dt"][l][g]
            s5b[ps_, q, 0] = inp["ssm_b_re"][l][g]
            s5b[ps_, q, 1] = inp["ssm_b_im"][l][g]
            s5c[ps_, q, 0] = inp["ssm_c_re"][l][g].T
            s5c[ps_, q, 1] = inp["ssm_c_im"][l][g].T
    out["s5p"], out["s5b"], out["s5c"] = s5p, s5b, s5c
    out["s5d"] = np.ascontiguousarray(inp["ssm_d"][l][j * 128:(j + 1) * 128].reshape(128, 1))
    return out


def host_consts(inp, j):
    out = {"ident": np.eye(128, dtype=np.float32)}
    out["ramp"] = np.ascontiguousarray(np.broadcast_to(np.arange(TT + 1, dtype=np.float32)[None], (128, TT + 1)))
    s_ = np.arange(64)
    out["rwm"] = np.ascontiguousarray(np.stack([(s_[:, None] < s_[None, :]), (s_[:, None] <= s_[None, :]),
                                                 (s_[None, :] < s_[:, None])], axis=1).astype(np.float32))
    sel = np.zeros((128, 4), np.float32)
    sel[:, j] = 1.0
    out["sel"] = sel
    tri = (np.arange(128)[:, None] <= np.arange(128)[None, :]).astype(np.float32)
    mask = np.zeros((128, 4, 128), np.float32)
    for r in range(4):
        if r < j:
            mask[:, r, :] = 1.0
        elif r == j:
            mask[:, r, :] = tri
    out["damask"] = mask
    names = ["ffn1_pre_g", "ffn1_post_g", "mix_pre_g", "mix_post_g", "ffn2_pre_g", "ffn2_post_g"]
    gains = np.zeros((128, L_DEPTH * 6, KC), np.float32)
    for l in range(L_DEPTH):
        for i, n in enumerate(names):
            gains[:, l * 6 + i, :] = _pm(inp[n][l])
    out["gains"] = gains
    return out


FUSED = True


def kernel(**inputs):
    inp = {k: np.asarray(v) for k, v in inputs.items()}
    x = inp["x"].astype(np.float32, copy=False)
    pool = []
    for c in range(NCORES):
        b, j = c // 4, c % 4
        d = dict(host_consts(inp, j))
        pool.append(d)
    shared = [host_shared(inp, l) for l in range(L_DEPTH)]
    percore = [[host_percore(inp, l, j) for j in range(4)] for l in range(L_DEPTH)]
    for c in range(NCORES):
        j = c % 4
        for l in range(L_DEPTH):
            for k, v in shared[l].items():
                pool[c]["%s_%d" % (k, l)] = v
            for k, v in percore[l][j].items():
                pool[c]["%s_%d" % (k, l)] = v
    xs = []
    for c in range(NCORES):
        b, j = c // 4, c % 4
        t = x[b].reshape(8, 4, 128, D)[:, j].reshape(NT, D)
        xs.append(np.ascontiguousarray(t.T.reshape(KC, 128, NT)))

    def run(seq, ins, outs, uses_x, state):
        nc, in_names = build_launch(seq, ins, outs, uses_x)
        maps = []
        for c in range(NCORES):
            m = {}
            for n in in_names:
                if n == "x_in":
                    m[n] = state["x"][c]
                elif n in state:
                    m[n] = state[n][c]
                else:
                    m[n] = pool[c][n]
            maps.append(m)
        res = run_bass_kernel_spmd(nc, maps, core_ids=list(range(NCORES))).results
        if uses_x:
            state["x"] = [np.asarray(res[c]["x_out"]) for c in range(NCORES)]
        for g in outs:
            for k in range(len(DATA_GROUPS[g])):
                n = "%s%d" % (g, k)
                state[n] = [np.asarray(res[c][n]) for c in range(NCORES)]

    def gather(state, src, dst):
        for k in range(len(DATA_GROUPS[src])):
            sn, dn = "%s%d" % (src, k), "%s%d" % (dst, k)
            state[dn] = [None] * NCORES
            for c in range(NCORES):
                b = c // 4
                state[dn][c] = np.concatenate([state[sn][4 * b + r] for r in range(4)], axis=0)

    state = {"x": xs}
    if FUSED:
        seq = []
        for l in range(L_DEPTH):
            seq += [("ffn1", l), ("p", l), ("gather", l), ("rwkv", l), ("s5", l), ("da", l), ("o", l), ("ffn2", l)]
        run(seq, set(), set(), True, state)
    else:
        run([("ffn1", 0), ("p", 0)], set(), {"pay", "q"}, True, state)
        for l in range(L_DEPTH):
            gather(state, "pay", "g1")
            run([("da", l), ("rwkv", l), ("s5", l)], {"g1", "q"}, {"yda", "pay2"}, False, state)
            gather(state, "pay2", "g2")
            seq = [("o", l), ("ffn2", l)]
            if l + 1 < L_DEPTH:
                seq += [("ffn1", l + 1), ("p", l + 1)]
                run(seq, {"yda", "g2"}, {"pay", "q"}, True, state)
            else:
                run(seq, {"yda", "g2"}, set(), True, state)
    out = np.empty((2, SEQ, D), np.float32)
    for c in range(NCORES):
        b, j = c // 4, c % 4
        t = state["x"][c].reshape(D, NT).T.reshape(8, 128, D)
        out[b].reshape(8, 4, 128, D)[:, j] = t
    return out
```
